# Optimizing a Trainium2 kernel written in Bass

```python
import jax, jax.numpy as jnp
from jax import lax
import numpy as np

D_MODEL = 2048
BATCH = 1
SEQ = 16384
DEPTH = 1

ATT_HEADS = 16
ATT_KV_HEADS = 2
ATT_HEAD_DIM = 64
ATT_GROUP = ATT_HEADS // ATT_KV_HEADS
ATT_WIDTH = ATT_HEADS * ATT_HEAD_DIM
ATT_KV_WIDTH = ATT_KV_HEADS * ATT_HEAD_DIM
WINDOW = 128
BLOCK = 128
ROPE_THETA = 10000.0

GLA_HEADS = 4
GLA_WIDTH = D_MODEL // 2
GLA_KEY_WIDTH = GLA_WIDTH // 2
GLA_DK = GLA_KEY_WIDTH // GLA_HEADS
GLA_DV = GLA_WIDTH // GLA_HEADS
GLA_RANK = 16
GLA_NORMALIZER = 16.0
GLA_CHUNK = 16

MIX_WIDTH = ATT_WIDTH + GLA_WIDTH
SPLIT_SIZES = (ATT_WIDTH, ATT_KV_WIDTH, ATT_KV_WIDTH, ATT_WIDTH,
               GLA_KEY_WIDTH, GLA_KEY_WIDTH, GLA_WIDTH, GLA_WIDTH, GLA_RANK)
IN_WIDTH = sum(SPLIT_SIZES)

DN_ALPHA = (2.0 * DEPTH) ** 0.25
DN_BETA = (8.0 * DEPTH) ** -0.25
LN_EPS = 1e-5
RMS_EPS = 1e-5

kernel_name = "hymba_swa_sink_gla_deepnorm"


def _split_points():
    pts, acc = [], 0
    for s in SPLIT_SIZES[:-1]:
        acc += s
        pts.append(acc)
    return pts


def _rope_tables(seq):
    half = ATT_HEAD_DIM // 2
    inv_freq = 1.0 / (ROPE_THETA ** (jnp.arange(0, half, dtype=jnp.float32) * 2.0 / ATT_HEAD_DIM))
    ang = jnp.arange(seq, dtype=jnp.float32)[:, None] * inv_freq[None, :]
    return jnp.cos(ang)[:, None, :], jnp.sin(ang)[:, None, :]


def _rope(t, cos, sin):
    half = t.shape[-1] // 2
    t1 = t[..., :half].astype(jnp.float32)
    t2 = t[..., half:].astype(jnp.float32)
    return jnp.concatenate([t1 * cos - t2 * sin, t2 * cos + t1 * sin], axis=-1).astype(t.dtype)


def _swa_sink_attention(q, k, v, sinks):
    B, S, _, D = q.shape
    nb = S // BLOCK
    qb = q.reshape(B, nb, BLOCK, ATT_KV_HEADS, ATT_GROUP, D)
    kb = k.reshape(B, nb, BLOCK, ATT_KV_HEADS, D)
    vb = v.reshape(B, nb, BLOCK, ATT_KV_HEADS, D)

    def with_prev(t):
        prev = jnp.pad(t[:, :-1], ((0, 0), (1, 0), (0, 0), (0, 0), (0, 0)))
        return jnp.concatenate([prev, t], axis=2)

    kk, vv = with_prev(kb), with_prev(vb)
    s = jnp.einsum('bnqhgd,bnkhd->bnhgqk', qb, kk).astype(jnp.float32) * (D ** -0.5)
    qi = jnp.arange(BLOCK)[:, None]
    kj = jnp.arange(2 * BLOCK)[None, :]
    dist = qi + BLOCK - kj
    band = (dist >= 0) & (dist < WINDOW)
    mask = band[None] & ((jnp.arange(nb)[:, None, None] > 0) | (kj[None] >= BLOCK))
    s = jnp.where(mask[None, :, None, None], s, -jnp.inf)
    sink = sinks.astype(jnp.float32).reshape(ATT_KV_HEADS, ATT_GROUP)[None, None, :, :, None, None]
    m = jnp.maximum(jnp.max(s, axis=-1, keepdims=True), sink)
    p = jnp.exp(s - m)
    p = p / (jnp.sum(p, axis=-1, keepdims=True) + jnp.exp(sink - m))
    o = jnp.einsum('bnhgqk,bnkhd->bnqhgd', p.astype(v.dtype), vv)
    return o.reshape(B, S, ATT_HEADS * D)


def _gla_chunked(q, k, v, g):
    B, S, H, DK = q.shape
    DV = v.shape[-1]
    C = GLA_CHUNK
    N = S // C
    f32 = jnp.float32
    qc = (q.astype(f32) * (DK ** -0.5)).reshape(B, N, C, H, DK)
    kc = k.astype(f32).reshape(B, N, C, H, DK)
    vc = v.astype(f32).reshape(B, N, C, H, DV)
    b = jnp.cumsum(g.astype(f32).reshape(B, N, C, H, DK), axis=2)

    idx = jnp.arange(C)
    causal = (idx[:, None] >= idx[None, :])[:, :, None, None]
    diff = b[:, :, :, None] - b[:, :, None, :]
    decay = jnp.exp(jnp.where(causal, diff, -jnp.inf))
    A = jnp.einsum('bnihd,bnjhd,bnijhd->bnhij', qc, kc, decay)
    o_intra = jnp.einsum('bnhij,bnjhv->bnihv', A, vc)

    b_last = b[:, :, -1]
    q_dec = qc * jnp.exp(b)
    k_dec = kc * jnp.exp(b_last[:, :, None] - b)
    tot = jnp.exp(b_last)

    def step(state, inp):
        qd, kd, vv, tt = inp
        o = jnp.einsum('bchk,bhkv->bchv', qd, state)
        state = state * tt[..., None] + jnp.einsum('bchk,bchv->bhkv', kd, vv)
        return state, o

    sw = lambda a: jnp.moveaxis(a, 1, 0)
    state0 = jnp.zeros((B, H, DK, DV), f32)
    _, o_inter = lax.scan(step, state0, (sw(q_dec), sw(k_dec), sw(vc), sw(tot)))
    o = o_intra + jnp.moveaxis(o_inter, 0, 1)
    return o.reshape(B, S, H, DV)


def _layer_norm(z, g, b):
    zf = z.astype(jnp.float32)
    mu = jnp.mean(zf, axis=-1, keepdims=True)
    var = jnp.mean(jnp.square(zf - mu), axis=-1, keepdims=True)
    return ((zf - mu) * lax.rsqrt(var + LN_EPS) * g.astype(jnp.float32) + b.astype(jnp.float32)).astype(z.dtype)


def setup_inputs(seed: int = 0) -> dict:
    key = jax.random.key(seed)
    ks = jax.random.split(key, 9)
    x = jax.random.normal(ks[0], (BATCH, SEQ, D_MODEL), jnp.float32)
    col_scale = np.concatenate([
        np.ones(ATT_WIDTH + ATT_KV_WIDTH), np.full(ATT_KV_WIDTH, DN_BETA), np.ones(ATT_WIDTH),
        np.ones(2 * GLA_KEY_WIDTH), np.full(GLA_WIDTH, DN_BETA), np.ones(GLA_WIDTH + GLA_RANK)]).astype(np.float32)
    w_in = jax.random.normal(ks[1], (DEPTH, D_MODEL, IN_WIDTH), jnp.float32) * (D_MODEL ** -0.5) * jnp.asarray(col_scale)
    w_gk_up = jax.random.normal(ks[2], (DEPTH, GLA_RANK, GLA_KEY_WIDTH), jnp.float32) * (GLA_RANK ** -0.5)
    b_gk = 0.1 * jax.random.normal(ks[3], (DEPTH, GLA_KEY_WIDTH), jnp.float32)
    attn_sinks = jax.random.normal(ks[4], (DEPTH, ATT_HEADS), jnp.float32)
    gla_norm_w = 1.0 + 0.02 * jax.random.normal(ks[5], (DEPTH, GLA_DV), jnp.float32)
    w_out = jax.random.normal(ks[6], (DEPTH, MIX_WIDTH, D_MODEL), jnp.float32) * (MIX_WIDTH ** -0.5) * DN_BETA
    ln_g = 1.0 + 0.02 * jax.random.normal(ks[7], (DEPTH, D_MODEL), jnp.float32)
    ln_b = 0.02 * jax.random.normal(ks[8], (DEPTH, D_MODEL), jnp.float32)
    return {"x": x, "w_in": w_in, "w_gk_up": w_gk_up, "b_gk": b_gk, "attn_sinks": attn_sinks,
            "gla_norm_w": gla_norm_w, "w_out": w_out, "ln_g": ln_g, "ln_b": ln_b}


def reference(x, w_in, w_gk_up, b_gk, attn_sinks, gla_norm_w, w_out, ln_g, ln_b):
    B, S, _ = x.shape
    cos, sin = _rope_tables(S)
    pts = _split_points()
    for l in range(DEPTH):
        h = jnp.einsum('bsd,de->bse', x, w_in[l])
        q_a, k_a, v_a, z_a, q_g, k_g, v_g, z_g, gk_lr = jnp.split(h, pts, axis=-1)

        qa = _rope(q_a.reshape(B, S, ATT_HEADS, ATT_HEAD_DIM), cos, sin)
        ka = _rope(k_a.reshape(B, S, ATT_KV_HEADS, ATT_HEAD_DIM), cos, sin)
        va = v_a.reshape(B, S, ATT_KV_HEADS, ATT_HEAD_DIM)
        y_a = _swa_sink_attention(qa, ka, va, attn_sinks[l]) * jax.nn.silu(z_a)

        gk = jnp.einsum('bsr,rk->bsk', gk_lr, w_gk_up[l]) + b_gk[l]
        gk = jax.nn.log_sigmoid(gk.astype(jnp.float32)) / GLA_NORMALIZER
        o_g = _gla_chunked(q_g.reshape(B, S, GLA_HEADS, GLA_DK),
                           k_g.reshape(B, S, GLA_HEADS, GLA_DK),
                           v_g.reshape(B, S, GLA_HEADS, GLA_DV),
                           gk.reshape(B, S, GLA_HEADS, GLA_DK))
        o_g = o_g * lax.rsqrt(jnp.mean(jnp.square(o_g), axis=-1, keepdims=True) + RMS_EPS) * gla_norm_w[l].astype(jnp.float32)
        y_g = (o_g.astype(x.dtype) * jax.nn.silu(z_g.reshape(B, S, GLA_HEADS, GLA_DV))).reshape(B, S, GLA_WIDTH)

        y = jnp.einsum('bse,ed->bsd', jnp.concatenate([y_a, y_g], axis=-1), w_out[l])
        x = _layer_norm(DN_ALPHA * x + y, ln_g[l], ln_b[l])
    return x
```

```python
import contextlib
import math
import numpy as np
import concourse.bass as bass
import concourse.mybir as mybir
from concourse.bass_utils import run_bass_kernel_spmd

F32 = mybir.dt.float32
BF16 = mybir.dt.bfloat16
AF = mybir.ActivationFunctionType
ALU = mybir.AluOpType
AX = mybir.AxisListType

NCORES = 8
D = 2048
SEQ = 16384
OWN = SEQ // NCORES
NT = OWN // 128
NPRE = (SEQ - OWN) // 128
TP = 2
TOK = TP * 128
KC = D // 128
WCOLS = 5504
C_QA, C_KA, C_VA, C_ZA, C_VG, C_ZG, C_QG, C_KG, C_GK = 0, 1024, 1152, 1280, 2304, 3328, 4352, 4864, 5376
ALPHA = 2.0 ** 0.25
LN_EPS = 1e-5
RMS_EPS = 1e-5


class Prog:
    def __init__(self, nc, stack):
        self.nc = nc
        self.stack = stack
        self.engs = {"pe": nc.tensor, "act": nc.scalar, "dve": nc.vector, "pool": nc.gpsimd, "sp": nc.sync}
        self.sem = {}
        self.cnt = {}
        self.waited = {}
        self.lastw = {}
        self.readers = {}
        self.ninst = {e: 0 for e in self.engs}
        for e in ("pe", "act", "dve", "pool"):
            self._mksem("E_" + e)

    def _mksem(self, name):
        self.sem[name] = self.stack.enter_context(self.nc.semaphore(name))
        self.cnt[name] = 0

    def _deps(self, reads, writes):
        deps = {}

        def add(tok):
            if tok is None:
                return
            s, v = tok
            if deps.get(s, 0) < v:
                deps[s] = v
        for r in reads:
            add(self.lastw.get(r))
        for w in writes:
            add(self.lastw.get(w))
            for t in self.readers.get(w, ()):
                add(t)
        return deps

    def _wait(self, eng, deps):
        e = self.engs[eng]
        for s, v in deps.items():
            if self.waited.get((eng, s), 0) >= v:
                continue
            if eng == "pe" and s == "E_pe":
                continue
            e.wait_ge(self.sem[s], v)
            self.ninst[eng] += 1
            self.waited[(eng, s)] = v

    def _commit(self, tok, reads, writes):
        for w in writes:
            self.lastw[w] = tok
            self.readers[w] = []
        for r in reads:
            self.readers.setdefault(r, []).append(tok)

    def op(self, eng, fns, reads=(), writes=()):
        if callable(fns):
            fns = [fns]
        self._wait(eng, self._deps(reads, writes))
        ins = None
        for f in fns:
            ins = f()
            self.ninst[eng] += 1
        s = "E_" + eng
        self.cnt[s] += 1
        ins.then_inc(self.sem[s], 1)
        self._commit((s, self.cnt[s]), reads, writes)

    def dma(self, eng, slot, fns, reads=(), writes=()):
        if callable(fns):
            fns = [fns]
        s = "D_" + slot
        if s not in self.sem:
            self._mksem(s)
        self._wait(eng, self._deps(reads, writes))
        for f in fns:
            ins = f()
            self.ninst[eng] += 1
            self.cnt[s] += 16
            ins.then_inc(self.sem[s], 16)
        self._commit((s, self.cnt[s]), reads, writes)

    def barrier(self):
        for eng in ("pe", "act", "dve", "pool", "sp"):
            e = self.engs[eng]
            for s, v in self.cnt.items():
                if v > 0 and self.waited.get((eng, s), 0) < v:
                    e.wait_ge(self.sem[s], v)
                    self.waited[(eng, s)] = v

    def finish(self, eng="sp"):
        e = self.engs[eng]
        for s, v in self.cnt.items():
            if v > 0 and self.waited.get((eng, s), 0) < v:
                e.wait_ge(self.sem[s], v)
                self.waited[(eng, s)] = v


class _Stop(Exception):
    pass


STAGE = 99
SKIPK = False
DBG = 0


def _stage(n):
    if STAGE < n:
        raise _Stop()


def build(npre=NPRE, nt=NT):
    nc = bass.Bass("TRN2", target_bir_lowering=False)
    din = lambda name, shape: nc.dram_tensor(name, shape, F32, kind="ExternalInput").ap()
    xT = din("xT", [128, KC, (nt + 1) * 128])
    xpre = din("xpre", [128, KC, max(npre, 1) * 128])
    xtok = din("xtok", [nt, 128, D])
    w_in = din("w_in", [128, KC, WCOLS])
    w_out = din("w_out", [128, KC, D])
    w_aug = din("w_aug", [32, 512])
    cs = din("cs", [128, nt + 1, 2, 32])
    masks = din("masks", [128, 2, 2, 128])
    consts = din("consts", [128, 3, 128])
    sinks = din("sinks", [128, 16])
    normw = din("normw", [128, 256])
    lng = din("lng", [128, D])
    lnb = din("lnb", [128, D])
    out = nc.dram_tensor("out", [nt, 128, D], F32, kind="ExternalOutput").ap()
    w_in16 = nc.dram_tensor("w_in16", [128, KC, WCOLS], BF16, kind="Internal").ap()
    w_out16 = nc.dram_tensor("w_out16", [128, KC, D], BF16, kind="Internal").ap()

    with contextlib.ExitStack() as st:
        P = Prog(nc, st)
        try:
            _build_body(nc, st, P, npre, nt, locals())
        except _Stop:
            pass
        P.finish("sp")
    return nc


def _build_body(nc, st, P, npre, nt, env):
    xT, xpre, xtok, w_in, w_out, w_aug, cs, masks, consts, sinks, normw, lng, lnb, out, w_in16, w_out16 = [env[k] for k in ('xT', 'xpre', 'xtok', 'w_in', 'w_out', 'w_aug', 'cs', 'masks', 'consts', 'sinks', 'normw', 'lng', 'lnb', 'out', 'w_in16', 'w_out16')]
    if True:
        sb = lambda name, shape, dt: st.enter_context(nc.sbuf_tensor(name, shape, dt))
        psb = lambda name, dt=F32: st.enter_context(nc.psum_tensor(name, [128, 512 if dt == F32 else 1024], dt))
        V, A, T, G = nc.vector, nc.scalar, nc.tensor, nc.gpsimd

        B = [psb("B%d" % i) for i in range(4)]
        B4 = psb("B4", BF16)
        B5, B6, B7 = psb("B5"), psb("B6"), psb("B7")

        ident = sb("ident", [128, 128], BF16)
        uincl = sb("uincl", [128, 128], F32)
        uincl16 = sb("uincl16", [128, 128], BF16)
        ones16 = sb("ones16", [128, 128], BF16)
        mask16 = sb("mask16", [128, 2, 2, 128], BF16)
        sinkb = sb("sinkb", [128, 16], F32)
        normwb = sb("normwb", [128, 256], F32)
        lngb = sb("lngb", [128, D], F32)
        lnbb = sb("lnbb", [128, D], F32)
        cst = sb("cst", [128, nt + 1, 2, 32], F32)
        waug = sb("waug", [32, 512], F32)
        S32 = sb("S32", [128, 1024], F32)
        S16 = sb("S16", [128, 1024], BF16)
        gkT = sb("gkT", [128, TOK], F32)
        e1 = sb("e1", [128, 512], F32)
        spl = sb("spl", [128, 512], F32)
        nbl = sb("nbl", [128, 4], F32)
        tot = sb("tot", [128, 4], F32)
        ekd = sb("ekd", [128, 4, 128], F32)
        kdT = sb("kdT", [128, 4, 128], BF16)
        kdec = sb("kdec", [128, 4, 128], BF16)
        wgk = sb("wgk", [128, KC, 128], BF16)

        P.dma("pool", "c_ident", lambda: G.dma_start(out=ident[:], in_=consts[:, 0, :]), writes=["ident"])
        identf = sb("identf", [128, 128], F32)
        P.dma("sp", "c_identf", lambda: nc.sync.dma_start(out=identf[:], in_=consts[:, 0, :]), writes=["identf"])
        P.dma("sp", "c_uincl", lambda: nc.sync.dma_start(out=uincl[:], in_=consts[:, 1, :]), writes=["uincl"])
        P.dma("pool", "c_uincl16", lambda: G.dma_start(out=uincl16[:], in_=consts[:, 1, :]), writes=["uincl16"])
        P.dma("pool", "c_ones", lambda: G.dma_start(out=ones16[:], in_=consts[:, 2, :]), writes=["ones16"])
        P.dma("pool", "c_mask", lambda: G.dma_start(out=mask16[:], in_=masks[:, :, :, :]), writes=["mask16"])
        P.dma("sp", "c_sink", lambda: nc.sync.dma_start(out=sinkb[:], in_=sinks[:, :]), writes=["sinkb"])
        P.dma("sp", "c_normw", lambda: nc.sync.dma_start(out=normwb[:], in_=normw[:, :]), writes=["normwb"])
        P.dma("sp", "c_lng", lambda: nc.sync.dma_start(out=lngb[:], in_=lng[:, :]), writes=["lngb"])
        P.dma("sp", "c_lnb", lambda: nc.sync.dma_start(out=lnbb[:], in_=lnb[:, :]), writes=["lnbb"])
        P.dma("sp", "c_cs", lambda: nc.sync.dma_start(out=cst[:], in_=cs[:, :, :, :]), writes=["cst"])
        P.op("dve", lambda: V.memset(waug[:], 0.0), writes=["waug"])
        P.dma("sp", "c_waug", lambda: nc.sync.dma_start(out=waug[0:16, :], in_=w_aug[0:16, :]), writes=["waug"])
        P.dma("pool", "c_wgk", lambda: G.dma_start(out=wgk[:], in_=w_in[:, :, C_GK:C_GK + 128]), writes=["wgk"])
        P.op("dve", lambda: V.memset(S32[:], 0.0), writes=["S32"])
        P.op("dve", lambda: V.memset(S16[:], 0.0), writes=["S16"])
        ones32 = sb("ones32", [1, 128], F32)
        bgk = sb("bgk", [1, 512], F32)
        P.op("dve", lambda: V.memset(ones32[:], 1.0), writes=["ones32"])
        P.dma("sp", "c_bgk", lambda: nc.sync.dma_start(out=bgk[:], in_=w_aug[16:17, :]), writes=["bgk"])

        WCH = 688
        for i in range(8):
            P.dma("pool", "wcast", (lambda i=i: G.dma_start(out=w_in16[:, :, i * WCH:(i + 1) * WCH],
                                                            in_=w_in[:, :, i * WCH:(i + 1) * WCH])), writes=["w_in16"])
        for i in range(4):
            P.dma("pool", "wcast", (lambda i=i: G.dma_start(out=w_out16[:, :, i * 512:(i + 1) * 512],
                                                            in_=w_out[:, :, i * 512:(i + 1) * 512])), writes=["w_out16"])

        _stage(1)
        def gla_decay(gk_lhsT, tag):
            P.op("pe", [lambda: T.matmul(B[2][:, 0:512], lhsT=gk_lhsT, rhs=waug[0:32, :], start=True, stop=False),
                        lambda: T.matmul(B[2][:, 0:512], lhsT=ones32[0:1, :], rhs=bgk[0:1, :], start=False, stop=True)],
                 reads=["gkT", "waug", "ones32", "bgk"], writes=["B2"])
            P.op("act", lambda: A.activation(out=e1[:], in_=B[2][:, 0:512], func=AF.Exp, scale=-1.0),
                 reads=["B2"], writes=["e1"])
            P.op("act", lambda: A.activation(out=spl[:], in_=e1[:], func=AF.Ln, bias=1.0, scale=1.0),
                 reads=["e1"], writes=["spl"])
            P.op("pe", [(lambda h=h: T.matmul(B[3][:, h * 128:(h + 1) * 128], lhsT=spl[:, h * 128:(h + 1) * 128],
                                              rhs=uincl[:], start=True, stop=True)) for h in range(4)],
                 reads=["spl", "uincl"], writes=["B3"])
            P.op("dve", lambda: V.tensor_scalar(out=nbl[:], in0=B[3][:, 0:512].rearrange("p (h t) -> p h t", h=4)[:, :, 127],
                                                scalar1=-1.0 / 16.0, scalar2=None, op0=ALU.mult),
                 reads=["B3"], writes=["nbl"])
            P.op("act", lambda: A.activation(out=tot[:], in_=nbl[:], func=AF.Exp), reads=["nbl"], writes=["tot"])
            for h in range(4):
                P.op("act", (lambda h=h: A.activation(out=ekd[:, h, :], in_=B[3][:, h * 128:(h + 1) * 128], func=AF.Exp,
                                                      bias=nbl[:, h:h + 1], scale=1.0 / 16.0)),
                     reads=["B3", "nbl"], writes=["ekd"])

        def gla_state_update(kT_src, kT_key, v_rhs, v_key):
            for h in range(4):
                P.op("dve", (lambda h=h: V.tensor_tensor(out=kdT[:, h, :], in0=kT_src(h), in1=ekd[:, h, :], op=ALU.mult)),
                     reads=[kT_key, "ekd"], writes=["kdT"])
            P.op("pe", [(lambda h=h: T.transpose(B4[:, h * 128:(h + 1) * 128], kdT[:, h, :], ident[:])) for h in range(4)],
                 reads=["kdT", "ident"], writes=["B4"])
            P.op("act", lambda: A.copy(out=kdec[:].rearrange("p h t -> p (h t)"), in_=B4[:, 0:512]),
                 reads=["B4"], writes=["kdec"])
            P.op("pe", [(lambda h=h: T.matmul(B[h // 2][:, (h % 2) * 256:(h % 2) * 256 + 256], lhsT=kdec[:, h, :],
                                              rhs=v_rhs(h), start=True, stop=True)) for h in range(4)],
                 reads=["kdec", v_key], writes=["B0", "B1"])
            for h in range(4):
                P.op("dve", (lambda h=h: V.scalar_tensor_tensor(out=S32[:, h * 256:(h + 1) * 256],
                                                                in0=S32[:, h * 256:(h + 1) * 256],
                                                                scalar=tot[:, h:h + 1],
                                                                in1=B[h // 2][:, (h % 2) * 256:(h % 2) * 256 + 256],
                                                                op0=ALU.mult, op1=ALU.add)),
                     reads=["S32", "tot", "B0", "B1"], writes=["S32"])
            P.op("act", lambda: A.copy(out=S16[:], in_=S32[:]), reads=["S32"], writes=["S16"])

        if npre > 0:
            with contextlib.ExitStack() as st2:
                sb2 = lambda name, shape, dt: st2.enter_context(nc.sbuf_tensor(name, shape, dt))
                wkv = sb2("wkv", [128, KC, 1536], BF16)
                xpc = [sb2("xpc%d" % i, [128, KC, 128], BF16) for i in range(2)]
                kpre = sb2("kpre", [128, 512], F32)
                kTp = sb2("kTp", [128, 4, 128], F32)
                vpre = sb2("vpre", [128, 1024], BF16)
                P.dma("pool", "wkv", [lambda: G.dma_start(out=wkv[:, :, 0:512], in_=w_in[:, :, C_KG:C_KG + 512]),
                                      lambda: G.dma_start(out=wkv[:, :, 512:1536], in_=w_in[:, :, C_VG:C_VG + 1024])],
                      writes=["wkv"])
                for c in range(npre):
                    xb = xpc[c % 2]
                    xk = "xpc%d" % (c % 2)
                    P.dma("pool", xk, (lambda c=c, xb=xb: G.dma_start(out=xb[:], in_=xpre[:, :, c * 128:(c + 1) * 128])),
                          writes=[xk])
                    for h in range(4):
                        P.op("pe", [(lambda k=k, h=h: T.matmul(B[0][:, h * 128:(h + 1) * 128],
                                                               lhsT=wkv[:, k, h * 128:(h + 1) * 128], rhs=xb[:, k, :],
                                                               start=(k == 0), stop=(k == KC - 1))) for k in range(KC)],
                             reads=["wkv", xk], writes=["B0"])
                    P.op("act", lambda: A.copy(out=kTp[:].rearrange("p h t -> p (h t)"), in_=B[0][:, 0:512]),
                         reads=["B0"], writes=["kTp"])
                    for j, bk in enumerate((B5, B6)):
                        P.op("pe", [(lambda k=k, j=j, bk=bk: T.matmul(bk[:, 0:512], lhsT=xb[:, k, :],
                                                                      rhs=wkv[:, k, 512 + j * 512:1024 + j * 512],
                                                                      start=(k == 0), stop=(k == KC - 1))) for k in range(KC)],
                             reads=["wkv", xk], writes=["B%d" % (5 + j)])
                    P.op("act", lambda: A.copy(out=vpre[:, 0:512], in_=B5[:, 0:512]), reads=["B5"], writes=["vpre"])
                    P.op("dve", lambda: V.tensor_copy(out=vpre[:, 512:1024], in_=B6[:, 0:512]), reads=["B6"], writes=["vpre"])
                    P.op("pe", [(lambda k=k: T.matmul(B7[:, 0:128], lhsT=wgk[:, k, :], rhs=xb[:, k, :],
                                                      start=(k == 0), stop=(k == KC - 1))) for k in range(KC)],
                         reads=["wgk", xk], writes=["B7"])
                    P.op("act", lambda: A.copy(out=gkT[0:32, 0:128], in_=B7[0:32, 0:128]), reads=["B7"], writes=["gkT"])
                    gla_decay(gkT[0:32, 0:128], "p")
                    gla_state_update(lambda h: kTp[:, h, :], "kTp", lambda h: vpre[:, h * 256:(h + 1) * 256], "vpre")
                P.barrier()

        xTp = sb("xTp", [128, KC, TOK], BF16)
        wblk = [sb("wblk%d" % i, [128, KC, 512], BF16) for i in range(2)]
        q32 = sb("q32", [128, TP, 1024], F32)
        k32 = sb("k32", [128, TP, 128], F32)
        qr = sb("qr", [128, TP, 1024], BF16)
        kdup = sb("kdup", [128, 2, 2, 64], BF16)
        ta = sb("ta", [128, 512], F32)
        tb = sb("tb", [128, 512], F32)
        sza = sb("sza", [128, TP, 1024], BF16)
        szg = sb("szg", [128, TP, 1024], BF16)
        vg = sb("vg", [128, TP, 1024], BF16)
        qgT = sb("qgT", [128, 4, TOK], F32)
        kgT = sb("kgT", [128, 4, TOK], F32)
        kTall = sb("kTall", [128, 2, (nt + 1) * 128], BF16)
        vall = sb("vall", [128, nt + 1, 128], BF16)
        qT = sb("qT", [128, 8, 128], BF16)
        pexp = [sb("pexp%d" % i, [128, 2, 256], BF16) for i in range(2)]
        pT = [sb("pT%d" % i, [128, 2, 2, 128], BF16) for i in range(2)]
        mraw = sb("mraw", [128, 16], F32)
        msc = sb("msc", [128, 16], F32)
        negm = sb("negm", [128, 16], F32)
        dsm = sb("dsm", [128, 16], F32)
        esk = sb("esk", [128, 16], F32)
        den = sb("den", [128, 16], F32)
        rden = sb("rden", [128, 16], F32)
        otmp = sb("otmp", [128, 1024], F32)
        eq = sb("eq", [128, 4, 128], F32)
        ek = sb("ek", [128, 4, 128], F32)
        qdec = sb("qdec", [128, 4, 128], BF16)
        kneg = sb("kneg", [128, 4, 128], BF16)
        atm = sb("atm", [128, 4, 128], BF16)
        sqj = sb("sqj", [128, 256], BF16)
        ss = sb("ss", [128, 4], F32)
        rstd = sb("rstd", [128, 4], F32)
        y16 = sb("y16", [128, D], BF16)
        yT = sb("yT", [128, TP, KC, 128], BF16)
        xres = [sb("xres%d" % i, [128, 512], F32) for i in range(2)]
        r32 = [sb("r32_%d" % i, [128, D], F32) for i in range(TP)]
        lnst = sb("lnst", [128, TP, 8], F32)
        sqj2 = sb("sqj2", [128, D], BF16)

        wslot = [0]

        def load_wblk(src_ap, ncols):
            i = wslot[0] % 2
            wslot[0] += 1
            key = "wblk%d" % i
            P.dma("sp", key, lambda: nc.sync.dma_start(out=wblk[i][:, :, 0:ncols], in_=src_ap), reads=["w_in16", "w_out16"],
                  writes=[key])
            return wblk[i], key

        accs = [0]

        def next_acc():
            i = accs[0] % 2
            accs[0] += 1
            return B[i], "B%d" % i

        def rope(src, dst_fn, nh, slot):
            s4 = src.rearrange("p (h two f) -> p h two f", h=nh, two=2)
            t1, t2 = s4[:, :, 0, :], s4[:, :, 1, :]
            cosb = cst[:, slot, 0, :].unsqueeze(1).to_broadcast([128, nh, 32])
            sinb = cst[:, slot, 1, :].unsqueeze(1).to_broadcast([128, nh, 32])
            ta3 = ta[:, 0:nh * 32].rearrange("p (h f) -> p h f", h=nh)
            tb3 = tb[:, 0:nh * 32].rearrange("p (h f) -> p h f", h=nh)
            return t1, t2, cosb, sinb, ta3, tb3

        def do_rope(src, src_key, dst1, dst2, dst_key, nh, slot):
            t1, t2, cosb, sinb, ta3, tb3 = rope(src, None, nh, slot)
            P.op("dve", lambda: V.tensor_tensor(out=ta3, in0=t1, in1=cosb, op=ALU.mult), reads=[src_key, "cst"], writes=["ta"])
            P.op("dve", lambda: V.tensor_tensor(out=tb3, in0=t2, in1=sinb, op=ALU.mult), reads=[src_key, "cst"], writes=["tb"])
            P.op("dve", lambda: V.tensor_tensor(out=dst1, in0=ta3, in1=tb3, op=ALU.subtract), reads=["ta", "tb"], writes=[dst_key])
            P.op("dve", lambda: V.tensor_tensor(out=ta3, in0=t2, in1=cosb, op=ALU.mult), reads=[src_key, "cst"], writes=["ta"])
            P.op("dve", lambda: V.tensor_tensor(out=tb3, in0=t1, in1=sinb, op=ALU.mult), reads=[src_key, "cst"], writes=["tb"])
            P.op("dve", lambda: V.tensor_tensor(out=dst2, in0=ta3, in1=tb3, op=ALU.add), reads=["ta", "tb"], writes=[dst_key])

        def k_finish(t_k32, slot):
            kd5 = kdup[:]
            do_rope(t_k32, "k32", kd5[:, :, 0, 0:32], kd5[:, :, 0, 32:64], "kdup", 2, slot)
            P.op("dve", lambda: V.tensor_copy(out=kdup[:, :, 1, :], in_=kdup[:, :, 0, :]), reads=["kdup"], writes=["kdup"])
            P.op("pe", [(lambda kv=kv: T.transpose(B4[:, kv * 128:(kv + 1) * 128],
                                                   kdup[:, kv, :, :].rearrange("p a f -> p (a f)"), ident[:])) for kv in range(2)],
                 reads=["kdup", "ident"], writes=["B4"])
            P.op("dve", lambda: V.tensor_copy(out=kTall[:, :, slot * 128:(slot + 1) * 128],
                                              in_=B4[:, 0:256].rearrange("p (a t) -> p a t", a=2)),
                 reads=["B4"], writes=["kTall"])

        _stage(2)
        wb, wkey = load_wblk(w_in16[:, :, C_KA:C_KA + 256], 256)
        P.dma("pool", "xTp", lambda: G.dma_start(out=xTp[:, :, 0:128], in_=xT[:, :, 0:128]), writes=["xTp"])
        acc, akey = next_acc()
        P.op("pe", [(lambda k=k: T.matmul(acc[:, 0:256], lhsT=xTp[:, k, 0:128], rhs=wb[:, k, 0:256],
                                          start=(k == 0), stop=(k == KC - 1))) for k in range(KC)],
             reads=["xTp", wkey], writes=[akey])
        P.op("act", lambda: A.copy(out=k32[:, 0, :], in_=acc[:, 0:128]), reads=[akey], writes=["k32"])
        P.op("act", lambda: A.copy(out=vall[:, 0, :], in_=acc[:, 128:256]), reads=[akey], writes=["vall"])
        k_finish(k32[:, 0, :], 0)

        _stage(3)
        npass = nt // TP
        for ps_i in range(npass):
            tok0 = (1 + ps_i * TP) * 128
            P.dma("pool", "xTp", lambda: G.dma_start(out=xTp[:, :, :], in_=xT[:, :, tok0:tok0 + TOK]), writes=["xTp"])

            def tm_block(col0, ncols, handler):
                wb, wkey = load_wblk(w_in16[:, :, col0:col0 + ncols], ncols)
                for t in range(TP):
                    acc, akey = next_acc()
                    P.op("pe", [(lambda k=k: T.matmul(acc[:, 0:ncols], lhsT=xTp[:, k, t * 128:(t + 1) * 128],
                                                      rhs=wb[:, k, 0:ncols], start=(k == 0), stop=(k == KC - 1)))
                                for k in range(KC)],
                         reads=["xTp", wkey], writes=[akey])
                    handler(t, acc, akey)

            tm_block(C_QA, 512, lambda t, acc, ak: P.op("act", lambda: A.copy(out=q32[:, t, 0:512], in_=acc[:, 0:512]),
                                                        reads=[ak], writes=["q32"]))
            tm_block(C_QA + 512, 512, lambda t, acc, ak: P.op("act", lambda: A.copy(out=q32[:, t, 512:1024], in_=acc[:, 0:512]),
                                                              reads=[ak], writes=["q32"]))

            _stage(3.2)

            def h_kv(t, acc, ak):
                g = ps_i * TP + t + 1
                P.op("act", lambda: A.copy(out=k32[:, t, :], in_=acc[:, 0:128]), reads=[ak], writes=["k32"])
                P.op("act", lambda: A.copy(out=vall[:, g, :], in_=acc[:, 128:256]), reads=[ak], writes=["vall"])
            tm_block(C_KA, 256, h_kv)
            _stage(3.3)
            for j in range(2):
                tm_block(C_ZA + j * 512, 512,
                         lambda t, acc, ak, j=j: P.op("act", lambda: A.activation(out=sza[:, t, j * 512:(j + 1) * 512],
                                                                                  in_=acc[:, 0:512], func=AF.Silu),
                                                      reads=[ak], writes=["sza"]))
            _stage(3.4)
            for j in range(2):
                tm_block(C_VG + j * 512, 512,
                         lambda t, acc, ak, j=j: P.op("dve", lambda: V.tensor_copy(out=vg[:, t, j * 512:(j + 1) * 512],
                                                                                   in_=acc[:, 0:512]),
                                                      reads=[ak], writes=["vg"]))
            _stage(3.5)
            for j in range(2):
                tm_block(C_ZG + j * 512, 512,
                         lambda t, acc, ak, j=j: P.op("act", lambda: A.activation(out=szg[:, t, j * 512:(j + 1) * 512],
                                                                                  in_=acc[:, 0:512], func=AF.Silu),
                                                      reads=[ak], writes=["szg"]))

            _stage(3.6)
            for (col0, dstT, dkey) in ((C_QG, qgT, "qgT"), (C_KG, kgT, "kgT")):
                wb, wkey = load_wblk(w_in16[:, :, col0:col0 + 512], 512)
                for h in range(4):
                    acc, akey = next_acc()
                    P.op("pe", [(lambda k=k: T.matmul(acc[:, 0:TOK], lhsT=wb[:, k, h * 128:(h + 1) * 128], rhs=xTp[:, k, :],
                                                      start=(k == 0), stop=(k == KC - 1))) for k in range(KC)],
                         reads=["xTp", wkey], writes=[akey])
                    P.op("act", (lambda h=h, acc=acc: A.copy(out=dstT[:, h, :], in_=acc[:, 0:TOK])), reads=[akey], writes=[dkey])
            _stage(3.7)
            acc, akey = next_acc()
            P.op("pe", [(lambda k=k: T.matmul(acc[:, 0:TOK], lhsT=wgk[:, k, :], rhs=xTp[:, k, :],
                                              start=(k == 0), stop=(k == KC - 1))) for k in range(KC)],
                 reads=["xTp", "wgk"], writes=[akey])
            _stage(3.8)
            P.op("act", lambda: A.copy(out=gkT[0:32, 0:TOK], in_=acc[0:32, 0:TOK]), reads=[akey], writes=["gkT"])

            _stage(4)
            for t in range(TP):
                g = ps_i * TP + t + 1
                mi = 0 if (ps_i == 0 and t == 0) else 1
                q4 = qr[:, t, :].rearrange("p (h two f) -> p h two f", h=16, two=2)
                do_rope(q32[:, t, :], "q32", q4[:, :, 0, :], q4[:, :, 1, :], "qr", 16, g)
                _stage(4.1)
                if not SKIPK:
                    k_finish(k32[:, t, :], g)
                _stage(4.2)
                P.op("pe", [(lambda j=j: T.transpose(B4[:, j * 128:(j + 1) * 128], qr[:, t, j * 128:(j + 1) * 128], ident[:]))
                            for j in range(8)], reads=["qr", "ident"], writes=["B4"])
                P.op("dve", lambda: V.tensor_copy(out=qT[:].rearrange("p j t -> p (j t)"), in_=B4[:, 0:1024]),
                     reads=["B4"], writes=["qT"])
                _stage(4.3)
                for hp in range(8):
                    kv = hp // 4
                    c0 = (hp % 2) * 256
                    P.op("pe", [(lambda hh=hh: T.matmul(B[2 + hh][:, c0:c0 + 256],
                                                        lhsT=qT[hh * 64:(hh + 1) * 64, hp, :],
                                                        rhs=kTall[hh * 64:(hh + 1) * 64, kv, (g - 1) * 128:(g + 1) * 128],
                                                        start=True, stop=True)) for hh in range(2)],
                         reads=["qT", "kTall"], writes=["B2", "B3"])
                    _stage(4.4)
                    for hh in range(2):
                        P.op("dve", (lambda hh=hh: V.tensor_reduce(out=mraw[:, 2 * hp + hh:2 * hp + hh + 1],
                                                                   in_=B[2 + hh][:, c0:c0 + 256], axis=AX.X, op=ALU.max)),
                             reads=["B%d" % (2 + hh)], writes=["mraw"])
                    P.op("dve", lambda: V.scalar_tensor_tensor(out=msc[:, 2 * hp:2 * hp + 2], in0=mraw[:, 2 * hp:2 * hp + 2],
                                                               scalar=0.125, in1=sinkb[:, 2 * hp:2 * hp + 2],
                                                               op0=ALU.mult, op1=ALU.max),
                         reads=["mraw", "sinkb"], writes=["msc"])
                    P.op("dve", lambda: V.tensor_scalar(out=negm[:, 2 * hp:2 * hp + 2], in0=msc[:, 2 * hp:2 * hp + 2],
                                                        scalar1=-1.0, scalar2=None, op0=ALU.mult),
                         reads=["msc"], writes=["negm"])
                    _stage(4.5)
                    pe_, pkey = pexp[hp % 2], "pexp%d" % (hp % 2)
                    for hh in range(2):
                        P.op("act", (lambda hh=hh: A.activation(out=pe_[:, hh, :], in_=B[2 + hh][:, c0:c0 + 256],
                                                                func=AF.Exp, bias=negm[:, 2 * hp + hh:2 * hp + hh + 1],
                                                                scale=0.125)),
                             reads=["B%d" % (2 + hh), "negm"], writes=[pkey])
                    _stage(4.6)
                    P.op("pe", [(lambda hh=hh, kt=kt: T.transpose(B4[:, (hh * 2 + kt) * 128:(hh * 2 + kt + 1) * 128],
                                                                  pe_[:, hh, kt * 128:(kt + 1) * 128], ident[:]))
                                for hh in range(2) for kt in range(2)], reads=[pkey, "ident"], writes=["B4"])
                    pt_, ptkey = pT[hp % 2], "pT%d" % (hp % 2)
                    P.op("dve", lambda: V.tensor_tensor(out=pt_[:], in0=B4[:, 0:512].rearrange("p (h k q) -> p h k q", h=2, k=2),
                                                        in1=mask16[:, mi, :, :].unsqueeze(1).to_broadcast([128, 2, 2, 128]),
                                                        op=ALU.mult), reads=["B4", "mask16"], writes=[ptkey])
                    _stage(4.7)
                    obk, okey = (B5, "B5") if hp < 4 else (B6, "B6")
                    mm = []
                    for hh in range(2):
                        hl = (2 * hp + hh) % 8
                        for kt in range(2):
                            mm.append(lambda hh=hh, kt=kt, hl=hl: T.matmul(obk[:, hl * 64:(hl + 1) * 64], lhsT=pt_[:, hh, kt, :],
                                                                           rhs=vall[:, g - 1 + kt, kv * 64:(kv + 1) * 64],
                                                                           start=(kt == 0), stop=(kt == 1)))
                        for kt in range(2):
                            mm.append(lambda hh=hh, kt=kt: T.matmul(B7[:, 2 * hp + hh:2 * hp + hh + 1], lhsT=pt_[:, hh, kt, :],
                                                                    rhs=ones16[:, 0:1], start=(kt == 0), stop=(kt == 1)))
                    P.op("pe", mm, reads=[ptkey, "vall", "ones16"], writes=[okey, "B7"])
                _stage(4.8)
                P.op("dve", lambda: V.tensor_tensor(out=dsm[:], in0=sinkb[:], in1=msc[:], op=ALU.subtract),
                     reads=["sinkb", "msc"], writes=["dsm"])
                P.op("act", lambda: A.activation(out=esk[:], in_=dsm[:], func=AF.Exp), reads=["dsm"], writes=["esk"])
                P.op("dve", lambda: V.tensor_tensor(out=den[:], in0=B7[:, 0:16], in1=esk[:], op=ALU.add),
                     reads=["B7", "esk"], writes=["den"])
                P.op("dve", lambda: V.reciprocal(out=rden[:], in_=den[:]), reads=["den"], writes=["rden"])
                for j, (obk, okey) in enumerate(((B5, "B5"), (B6, "B6"))):
                    P.op("dve", (lambda j=j, obk=obk: V.tensor_tensor(
                        out=otmp[:, j * 512:(j + 1) * 512].rearrange("p (h d) -> p h d", h=8),
                        in0=obk[:, 0:512].rearrange("p (h d) -> p h d", h=8),
                        in1=rden[:, j * 8:(j + 1) * 8].unsqueeze(2).to_broadcast([128, 8, 64]), op=ALU.mult)),
                        reads=[okey, "rden"], writes=["otmp"])
                P.op("dve", lambda: V.tensor_tensor(out=y16[:, 0:1024], in0=otmp[:], in1=sza[:, t, :], op=ALU.mult),
                     reads=["otmp", "sza"], writes=["y16"])

                _stage(5)
                gla_decay(gkT[0:32, t * 128:(t + 1) * 128], "m")
                P.op("act", lambda: A.activation(out=eq[:].rearrange("p h t -> p (h t)"), in_=B[3][:, 0:512], func=AF.Exp,
                                                 scale=-1.0 / 16.0), reads=["B3"], writes=["eq"])
                P.op("act", lambda: A.activation(out=ek[:].rearrange("p h t -> p (h t)"), in_=B[3][:, 0:512], func=AF.Exp,
                                                 scale=1.0 / 16.0), reads=["B3"], writes=["ek"])
                P.op("dve", lambda: V.scalar_tensor_tensor(out=qdec[:], in0=qgT[:, :, t * 128:(t + 1) * 128],
                                                           scalar=128.0 ** -0.5, in1=eq[:], op0=ALU.mult, op1=ALU.mult),
                     reads=["qgT", "eq"], writes=["qdec"])
                P.op("dve", lambda: V.tensor_tensor(out=kneg[:], in0=kgT[:, :, t * 128:(t + 1) * 128], in1=ek[:], op=ALU.mult),
                     reads=["kgT", "ek"], writes=["kneg"])
                P.op("pe", [(lambda h=h: T.matmul(B[2][:, h * 128:(h + 1) * 128], lhsT=kneg[:, h, :], rhs=qdec[:, h, :],
                                                  start=True, stop=True)) for h in range(4)],
                     reads=["kneg", "qdec"], writes=["B2"])
                P.op("dve", lambda: V.tensor_tensor(out=atm[:], in0=B[2][:, 0:512].rearrange("p (h i) -> p h i", h=4),
                                                    in1=uincl16[:].unsqueeze(1).to_broadcast([128, 4, 128]), op=ALU.mult),
                     reads=["B2", "uincl16"], writes=["atm"])
                mm = []
                for h in range(4):
                    obk = B5 if h < 2 else B6
                    sl = slice((h % 2) * 256, (h % 2) * 256 + 256)
                    mm.append(lambda h=h, obk=obk, sl=sl: T.matmul(obk[:, sl], lhsT=atm[:, h, :], rhs=vg[:, t, h * 256:(h + 1) * 256],
                                                                   start=True, stop=False))
                    mm.append(lambda h=h, obk=obk, sl=sl: T.matmul(obk[:, sl], lhsT=qdec[:, h, :], rhs=S16[:, h * 256:(h + 1) * 256],
                                                                   start=False, stop=True))
                P.op("pe", mm, reads=["atm", "vg", "qdec", "S16"], writes=["B5", "B6"])
                gla_state_update(lambda h: kgT[:, h, t * 128:(t + 1) * 128], "kgT",
                                 lambda h: vg[:, t, h * 256:(h + 1) * 256], "vg")
                P.op("dve", lambda: V.memset(ss[:], 0.0), writes=["ss"])
                for h in range(4):
                    obk, okey = (B5, "B5") if h < 2 else (B6, "B6")
                    sl = slice((h % 2) * 256, (h % 2) * 256 + 256)
                    P.op("act", (lambda h=h, obk=obk, sl=sl: A.activation(out=sqj[:], in_=obk[:, sl], func=AF.Square,
                                                                          accum_out=ss[:, h:h + 1])),
                         reads=[okey, "ss"], writes=["sqj", "ss"])
                P.op("dve", lambda: V.tensor_scalar(out=rstd[:], in0=ss[:], scalar1=1.0 / 256.0, scalar2=RMS_EPS,
                                                    op0=ALU.mult, op1=ALU.add), reads=["ss"], writes=["rstd"])
                P.op("act", lambda: A.activation(out=rstd[:], in_=rstd[:], func=AF.Sqrt), reads=["rstd"], writes=["rstd"])
                P.op("dve", lambda: V.reciprocal(out=rstd[:], in_=rstd[:]), reads=["rstd"], writes=["rstd"])
                for h in range(4):
                    obk, okey = (B5, "B5") if h < 2 else (B6, "B6")
                    sl = slice((h % 2) * 256, (h % 2) * 256 + 256)
                    P.op("dve", (lambda h=h, obk=obk, sl=sl: V.scalar_tensor_tensor(out=otmp[:, h * 256:(h + 1) * 256], in0=obk[:, sl],
                                                                                    scalar=rstd[:, h:h + 1], in1=normwb[:],
                                                                                    op0=ALU.mult, op1=ALU.mult)),
                         reads=[okey, "rstd", "normwb"], writes=["otmp"])
                P.op("dve", lambda: V.tensor_tensor(out=y16[:, 1024:2048], in0=otmp[:], in1=szg[:, t, :], op=ALU.mult),
                     reads=["otmp", "szg"], writes=["y16"])
                _stage(6)
                for qd in range(2):
                    P.op("pe", [(lambda j=j: T.transpose(B4[:, (j % 8) * 128:(j % 8 + 1) * 128], y16[:, j * 128:(j + 1) * 128], ident[:]))
                                for j in range(qd * 8, qd * 8 + 8)], reads=["y16", "ident"], writes=["B4"])
                    P.op("act", (lambda qd=qd: A.copy(out=yT[:, t, qd * 8:(qd + 1) * 8, :].rearrange("p j t -> p (j t)"),
                                                      in_=B4[:, 0:1024])), reads=["B4"], writes=["yT"])

            _stage(7)
            P.op("dve", lambda: V.memset(lnst[:], 0.0), writes=["lnst"])
            for cb in range(4):
                wbo, wkeyo = load_wblk(w_out16[:, :, cb * 512:(cb + 1) * 512], 512)
                for t in range(TP):
                    gt = ps_i * TP + t
                    xi = (cb * TP + t) % 2
                    xs, xkey = xres[xi], "xres%d" % xi
                    P.dma("sp", xkey, lambda: nc.sync.dma_start(out=xs[:], in_=xtok[gt, :, cb * 512:(cb + 1) * 512]), writes=[xkey])
                    acc, akey = next_acc()
                    P.op("pe", [(lambda k=k: T.matmul(acc[:, 0:512], lhsT=yT[:, t, k, :], rhs=wbo[:, k, :],
                                                      start=(k == 0), stop=(k == KC - 1))) for k in range(KC)],
                         reads=["yT", wkeyo], writes=[akey])
                    P.op("dve", lambda: V.scalar_tensor_tensor(out=r32[t][:, cb * 512:(cb + 1) * 512], in0=xs[:], scalar=ALPHA,
                                                               in1=acc[:, 0:512], op0=ALU.mult, op1=ALU.add,
                                                               accum_out=lnst[:, t, cb:cb + 1]),
                         reads=[xkey, akey, "lnst"], writes=["r32_%d" % t, "lnst"])
            for t in range(TP):
                gt = ps_i * TP + t
                rt, rkey = r32[t], "r32_%d" % t
                L = lambda a, b: lnst[:, t, a:b]
                P.op("dve", lambda: V.tensor_reduce(out=L(4, 5), in_=L(0, 4), axis=AX.X, op=ALU.add),
                     reads=["lnst"], writes=["lnst"])
                P.op("dve", lambda: V.tensor_scalar(out=L(5, 6), in0=L(4, 5), scalar1=-1.0 / D, scalar2=None, op0=ALU.mult),
                     reads=["lnst"], writes=["lnst"])
                P.op("act", lambda: A.activation(out=sqj2[:], in_=rt[:], func=AF.Square, bias=L(5, 6), scale=1.0,
                                                 accum_out=L(6, 7)), reads=[rkey, "lnst"], writes=["sqj2", "lnst"])
                P.op("dve", lambda: V.tensor_scalar(out=L(7, 8), in0=L(6, 7), scalar1=1.0 / D, scalar2=LN_EPS,
                                                    op0=ALU.mult, op1=ALU.add), reads=["lnst"], writes=["lnst"])
                P.op("act", lambda: A.activation(out=L(7, 8), in_=L(7, 8), func=AF.Sqrt), reads=["lnst"], writes=["lnst"])
                P.op("dve", lambda: V.reciprocal(out=L(7, 8), in_=L(7, 8)), reads=["lnst"], writes=["lnst"])
                P.op("dve", lambda: V.tensor_scalar(out=rt[:], in0=rt[:], scalar1=L(5, 6), scalar2=L(7, 8),
                                                    op0=ALU.add, op1=ALU.mult), reads=[rkey, "lnst"], writes=[rkey])
                P.op("dve", lambda: V.tensor_tensor(out=rt[:], in0=rt[:], in1=lngb[:], op=ALU.mult), reads=[rkey, "lngb"], writes=[rkey])
                P.op("dve", lambda: V.tensor_tensor(out=rt[:], in0=rt[:], in1=lnbb[:], op=ALU.add), reads=[rkey, "lnbb"], writes=[rkey])
                P.dma("sp", "out%d" % t, lambda: nc.sync.dma_start(out=out[gt, :, :], in_=rt[:]), reads=[rkey])


def _kmajor(a):
    n = a.shape[1]
    return np.ascontiguousarray(a.reshape(KC, 128, n).transpose(1, 0, 2))


def _host_consts():
    ident = np.eye(128, dtype=np.float32)
    uincl = (np.arange(128)[:, None] <= np.arange(128)[None, :]).astype(np.float32)
    ones = np.ones((128, 128), np.float32)
    consts = np.ascontiguousarray(np.stack([ident, uincl, ones], axis=1))
    k = np.arange(128)[:, None, None]
    kt = np.arange(2)[None, :, None]
    q = np.arange(128)[None, None, :]
    kj = kt * 128 + k
    reg = ((kj > q) & (kj <= q + 128)).astype(np.float32)
    first = reg * (kj >= 128)
    return consts, reg, first


def kernel(x, w_in, w_gk_up, b_gk, attn_sinks, gla_norm_w, w_out, ln_g, ln_b, _ncores=NCORES, _npre=NPRE):
    x = np.asarray(x, np.float32)[0]
    w = np.asarray(w_in, np.float32)[0]
    perm = np.concatenate([np.arange(0, 1024), np.arange(1024, 1152), np.arange(1152, 1280), np.arange(1280, 2304),
                           np.arange(3328, 4352), np.arange(4352, 5376), np.arange(2304, 2816), np.arange(2816, 3328),
                           np.arange(5376, 5392)])
    w_l = _kmajor(np.concatenate([w[:, perm], np.zeros((D, 112), np.float32)], axis=1))
    wo_l = _kmajor(np.asarray(w_out, np.float32)[0])
    waug = np.zeros((32, 512), np.float32)
    waug[0:16] = np.asarray(w_gk_up, np.float32)[0]
    waug[16] = np.asarray(b_gk, np.float32)[0]
    consts, mreg, mfirst = _host_consts()
    sinks = np.ascontiguousarray(np.broadcast_to(np.asarray(attn_sinks, np.float32)[0][None, :], (128, 16)))
    normw = np.ascontiguousarray(np.broadcast_to(np.asarray(gla_norm_w, np.float32)[0][None, :], (128, 256)))
    lng = np.ascontiguousarray(np.broadcast_to(np.asarray(ln_g, np.float32)[0][None, :], (128, D)))
    lnb = np.ascontiguousarray(np.broadcast_to(np.asarray(ln_b, np.float32)[0][None, :], (128, D)))
    inv_freq = (1.0 / (10000.0 ** (np.arange(0, 32, dtype=np.float32) * 2.0 / 64.0))).astype(np.float32)
    xTfull = np.ascontiguousarray(x.T)
    in_maps = []
    for c in range(_ncores):
        s0 = c * OWN
        xt = np.zeros((D, OWN + 128), np.float32)
        if c > 0:
            xt[:, 0:128] = xTfull[:, s0 - 128:s0]
        xt[:, 128:] = xTfull[:, s0:s0 + OWN]
        npre_tok = _npre * 128
        xp = np.zeros((D, max(npre_tok, 128)), np.float32)
        if _npre > 0 and s0 > 0:
            xp[:, npre_tok - s0:] = xTfull[:, 0:s0]
        pos = (np.arange(s0 - 128, s0 + OWN)).astype(np.float32)
        ang = pos[:, None] * inv_freq[None, :]
        cs = np.stack([np.cos(ang), np.sin(ang)], axis=1).astype(np.float32)
        cs = np.ascontiguousarray(cs.reshape(NT + 1, 128, 2, 32).transpose(1, 0, 2, 3))
        masks = np.ascontiguousarray(np.stack([mfirst if c == 0 else mreg, mreg], axis=1))
        in_maps.append({
            "xT": _kmajor(xt), "xpre": _kmajor(xp),
            "xtok": np.ascontiguousarray(x[s0:s0 + OWN].reshape(NT, 128, D)),
            "w_in": w_l, "w_out": wo_l, "w_aug": waug, "cs": cs, "masks": masks, "consts": consts,
            "sinks": sinks, "normw": normw, "lng": lng, "lnb": lnb,
        })
    nc = build(npre=_npre, nt=NT)
    res = run_bass_kernel_spmd(nc, in_maps, core_ids=list(range(_ncores)))
    outs = [np.asarray(r["out"]).reshape(OWN, D) for r in res.results]
    return np.concatenate(outs, axis=0)[None].astype(np.float32)
```

```python
import contextlib
import math
import numpy as np
import concourse.bass as bass
import concourse.mybir as mybir
from concourse.bass_utils import run_bass_kernel_spmd

F32 = mybir.dt.float32
BF16 = mybir.dt.bfloat16
AF = mybir.ActivationFunctionType
ALU = mybir.AluOpType
AX = mybir.AxisListType

NCORES = 8
D = 2048
SEQ = 16384
OWN = SEQ // NCORES
NT = OWN // 128
NPRE = (SEQ - OWN) // 128
TP = 2
TOK = TP * 128
KC = D // 128
WCOLS = 5504
C_QA, C_KA, C_VA, C_ZA, C_VG, C_ZG, C_QG, C_KG, C_GK = 0, 1024, 1152, 1280, 2304, 3328, 4352, 4864, 5376
ALPHA = 2.0 ** 0.25
LN_EPS = 1e-5
RMS_EPS = 1e-5


class Prog:
    def __init__(self, nc, stack):
        self.nc = nc
        self.stack = stack
        self.engs = {"pe": nc.tensor, "act": nc.scalar, "dve": nc.vector, "pool": nc.gpsimd, "sp": nc.sync}
        self.sem = {}
        self.cnt = {}
        self.waited = {}
        self.lastw = {}
        self.readers = {}
        self.ninst = {e: 0 for e in self.engs}
        for e in ("pe", "act", "dve", "pool"):
            self._mksem("E_" + e)

    def _mksem(self, name):
        self.sem[name] = self.stack.enter_context(self.nc.semaphore(name))
        self.cnt[name] = 0

    def _deps(self, reads, writes):
        deps = {}

        def add(tok):
            if tok is None:
                return
            s, v = tok
            if deps.get(s, 0) < v:
                deps[s] = v
        for r in reads:
            add(self.lastw.get(r))
        for w in writes:
            add(self.lastw.get(w))
            for t in self.readers.get(w, ()):
                add(t)
        return deps

    def _wait(self, eng, deps):
        e = self.engs[eng]
        for s, v in deps.items():
            if self.waited.get((eng, s), 0) >= v:
                continue
            if eng == "pe" and s == "E_pe":
                continue
            e.wait_ge(self.sem[s], v)
            self.ninst[eng] += 1
            self.waited[(eng, s)] = v

    def _commit(self, tok, reads, writes):
        for w in writes:
            self.lastw[w] = tok
            self.readers[w] = []
        for r in reads:
            self.readers.setdefault(r, []).append(tok)

    PSUM_KEYS = frozenset("B%d" % i for i in range(8))

    def op(self, eng, fns, reads=(), writes=()):
        if callable(fns):
            fns = [fns]
        writes = list(writes) + [r for r in reads if r in self.PSUM_KEYS and r not in writes]
        self._wait(eng, self._deps(reads, writes))
        ins = None
        for f in fns:
            ins = f()
            self.ninst[eng] += 1
        s = "E_" + eng
        self.cnt[s] += 1
        ins.then_inc(self.sem[s], 1)
        self._commit((s, self.cnt[s]), reads, writes)

    def dma(self, eng, slot, fns, reads=(), writes=()):
        if callable(fns):
            fns = [fns]
        s = "D_" + slot
        if s not in self.sem:
            self._mksem(s)
        self._wait(eng, self._deps(reads, writes))
        for f in fns:
            ins = f()
            self.ninst[eng] += 1
            self.cnt[s] += 16
            ins.then_inc(self.sem[s], 16)
        self._commit((s, self.cnt[s]), reads, writes)

    def barrier(self):
        for eng in ("pe", "act", "dve", "pool", "sp"):
            e = self.engs[eng]
            for s, v in self.cnt.items():
                if v > 0 and self.waited.get((eng, s), 0) < v:
                    e.wait_ge(self.sem[s], v)
                    self.waited[(eng, s)] = v

    def finish(self, eng="sp"):
        e = self.engs[eng]
        for s, v in self.cnt.items():
            if v > 0 and self.waited.get((eng, s), 0) < v:
                e.wait_ge(self.sem[s], v)
                self.waited[(eng, s)] = v


class _Stop(Exception):
    pass


STAGE = 99
SKIPK = False
DBG = 0


def _stage(n):
    if STAGE < n:
        raise _Stop()


def build(npre=NPRE, nt=NT):
    nc = bass.Bass("TRN2", target_bir_lowering=False)
    din = lambda name, shape: nc.dram_tensor(name, shape, F32, kind="ExternalInput").ap()
    xT = din("xT", [128, KC, (nt + 1) * 128])
    xpre = din("xpre", [max(npre, 1), 128, KC * 128])
    xtok = din("xtok", [nt, 128, D])
    w_in = din("w_in", [128, KC, WCOLS])
    w_out = din("w_out", [128, KC, D])
    w_aug = din("w_aug", [32, 512])
    cs = din("cs", [128, nt + 1, 2, 32])
    masks = din("masks", [128, 2, 2, 128])
    consts = din("consts", [128, 4, 128])
    sinks = din("sinks", [128, 16])
    normw = din("normw", [128, 256])
    lng = din("lng", [128, D])
    lnb = din("lnb", [128, D])
    out = nc.dram_tensor("out", [nt, 128, D], F32, kind="ExternalOutput").ap()
    w_in16 = nc.dram_tensor("w_in16", [128, KC, WCOLS], BF16, kind="Internal").ap()
    w_out16 = nc.dram_tensor("w_out16", [128, KC, D], BF16, kind="Internal").ap()

    with contextlib.ExitStack() as st:
        P = Prog(nc, st)
        try:
            _build_body(nc, st, P, npre, nt, locals())
        except _Stop:
            pass
        P.finish("sp")
    return nc


def _build_body(nc, st, P, npre, nt, env):
    xT, xpre, xtok, w_in, w_out, w_aug, cs, masks, consts, sinks, normw, lng, lnb, out, w_in16, w_out16 = [env[k] for k in ('xT', 'xpre', 'xtok', 'w_in', 'w_out', 'w_aug', 'cs', 'masks', 'consts', 'sinks', 'normw', 'lng', 'lnb', 'out', 'w_in16', 'w_out16')]
    if True:
        sb = lambda name, shape, dt: st.enter_context(nc.sbuf_tensor(name, shape, dt))
        psb = lambda name, dt=F32: st.enter_context(nc.psum_tensor(name, [128, 512 if dt == F32 else 1024], dt))
        V, A, T, G = nc.vector, nc.scalar, nc.tensor, nc.gpsimd

        B = [psb("B%d" % i) for i in range(4)]
        B4 = psb("B4", BF16)
        B5, B6, B7 = psb("B5"), psb("B6"), psb("B7")

        ident = sb("ident", [128, 128], BF16)
        uincl = sb("uincl", [128, 128], F32)
        uincl16 = sb("uincl16", [128, 128], BF16)
        ones16 = sb("ones16", [128, 128], BF16)
        mask16 = sb("mask16", [128, 2, 2, 128], BF16)
        sinkb = sb("sinkb", [128, 16], F32)
        normwb = sb("normwb", [128, 256], F32)
        lngb = sb("lngb", [128, D], F32)
        lnbb = sb("lnbb", [128, D], F32)
        cst = sb("cst", [128, nt + 1, 2, 32], F32)
        waug = sb("waug", [32, 512], F32)
        S32 = sb("S32", [128, 1024], F32)
        S16 = sb("S16", [128, 1024], BF16)
        gkT = sb("gkT", [128, TOK], F32)
        e1 = sb("e1", [128, 512], F32)
        spl = sb("spl", [128, 512], F32)
        nbl = sb("nbl", [128, 4], F32)
        tot = sb("tot", [128, 4], F32)
        ekd = sb("ekd", [128, 4, 128], F32)
        kdT = sb("kdT", [128, 4, 128], BF16)
        kdec = sb("kdec", [128, 4, 128], BF16)
        wgk = sb("wgk", [128, KC, 128], BF16)

        P.dma("pool", "c_ident", lambda: G.dma_start(out=ident[:], in_=consts[:, 0, :]), writes=["ident"])
        identf = sb("identf", [128, 128], F32)
        P.dma("sp", "c_identf", lambda: nc.sync.dma_start(out=identf[:], in_=consts[:, 0, :]), writes=["identf"])
        P.dma("sp", "c_uincl", lambda: nc.sync.dma_start(out=uincl[:], in_=consts[:, 1, :]), writes=["uincl"])
        P.dma("pool", "c_uincl16", lambda: G.dma_start(out=uincl16[:], in_=consts[:, 1, :]), writes=["uincl16"])
        P.dma("pool", "c_ones", lambda: G.dma_start(out=ones16[:], in_=consts[:, 2, :]), writes=["ones16"])
        P.dma("pool", "c_mask", lambda: G.dma_start(out=mask16[:], in_=masks[:, :, :, :]), writes=["mask16"])
        P.dma("sp", "c_sink", lambda: nc.sync.dma_start(out=sinkb[:], in_=sinks[:, :]), writes=["sinkb"])
        P.dma("sp", "c_normw", lambda: nc.sync.dma_start(out=normwb[:], in_=normw[:, :]), writes=["normwb"])
        P.dma("sp", "c_lng", lambda: nc.sync.dma_start(out=lngb[:], in_=lng[:, :]), writes=["lngb"])
        P.dma("sp", "c_lnb", lambda: nc.sync.dma_start(out=lnbb[:], in_=lnb[:, :]), writes=["lnbb"])
        P.dma("sp", "c_cs", lambda: nc.sync.dma_start(out=cst[:], in_=cs[:, :, :, :]), writes=["cst"])
        P.op("dve", lambda: V.memset(waug[:], 0.0), writes=["waug"])
        P.dma("sp", "c_waug", lambda: nc.sync.dma_start(out=waug[0:16, :], in_=w_aug[0:16, :]), writes=["waug"])
        P.dma("pool", "c_wgk", lambda: G.dma_start(out=wgk[:], in_=w_in[:, :, C_GK:C_GK + 128]), writes=["wgk"])
        P.op("dve", lambda: V.memset(S32[:], 0.0), writes=["S32"])
        P.op("dve", lambda: V.memset(S16[:], 0.0), writes=["S16"])
        ones32 = sb("ones32", [1, 128], F32)
        bgk = sb("bgk", [1, 512], F32)
        P.op("dve", lambda: V.memset(ones32[:], 1.0), writes=["ones32"])
        P.dma("sp", "c_bgk", lambda: nc.sync.dma_start(out=bgk[:], in_=w_aug[16:17, :]), writes=["bgk"])

        WCH = 688
        for i in range(8):
            P.dma("pool", "wcast", (lambda i=i: G.dma_start(out=w_in16[:, :, i * WCH:(i + 1) * WCH],
                                                            in_=w_in[:, :, i * WCH:(i + 1) * WCH])), writes=["w_in16"])
        for i in range(4):
            P.dma("pool", "wcast", (lambda i=i: G.dma_start(out=w_out16[:, :, i * 512:(i + 1) * 512],
                                                            in_=w_out[:, :, i * 512:(i + 1) * 512])), writes=["w_out16"])

        _stage(1)
        def gla_decay(gk_lhsT, tag):
            P.op("pe", [lambda: T.matmul(B[2][:, 0:512], lhsT=gk_lhsT, rhs=waug[0:32, :], start=True, stop=False),
                        lambda: T.matmul(B[2][:, 0:512], lhsT=ones32[0:1, :], rhs=bgk[0:1, :], start=False, stop=True)],
                 reads=["gkT", "waug", "ones32", "bgk"], writes=["B2"])
            P.op("act", lambda: A.activation(out=e1[:], in_=B[2][:, 0:512], func=AF.Exp, scale=-1.0),
                 reads=["B2"], writes=["e1"])
            P.op("act", lambda: A.activation(out=spl[:], in_=e1[:], func=AF.Ln, bias=1.0, scale=1.0),
                 reads=["e1"], writes=["spl"])
            P.op("pe", [(lambda h=h: T.matmul(B[3][:, h * 128:(h + 1) * 128], lhsT=spl[:, h * 128:(h + 1) * 128],
                                              rhs=uincl[:], start=True, stop=True)) for h in range(4)],
                 reads=["spl", "uincl"], writes=["B3"])
            P.op("dve", lambda: V.tensor_scalar(out=nbl[:], in0=B[3][:, 0:512].rearrange("p (h t) -> p h t", h=4)[:, :, 127],
                                                scalar1=-1.0 / 16.0, scalar2=None, op0=ALU.mult),
                 reads=["B3"], writes=["nbl"])
            P.op("act", lambda: A.activation(out=tot[:], in_=nbl[:], func=AF.Exp), reads=["nbl"], writes=["tot"])
            for h in range(4):
                P.op("act", (lambda h=h: A.activation(out=ekd[:, h, :], in_=B[3][:, h * 128:(h + 1) * 128], func=AF.Exp,
                                                      bias=nbl[:, h:h + 1], scale=1.0 / 16.0)),
                     reads=["B3", "nbl"], writes=["ekd"])

        def gla_state_update(kT_src, kT_key, v_rhs, v_key):
            for h in range(4):
                P.op("dve", (lambda h=h: V.tensor_tensor(out=kdT[:, h, :], in0=kT_src(h), in1=ekd[:, h, :], op=ALU.mult)),
                     reads=[kT_key, "ekd"], writes=["kdT"])
            P.op("pe", [(lambda h=h: T.transpose(B4[:, h * 128:(h + 1) * 128], kdT[:, h, :], ident[:])) for h in range(4)],
                 reads=["kdT", "ident"], writes=["B4"])
            P.op("act", lambda: A.copy(out=kdec[:].rearrange("p h t -> p (h t)"), in_=B4[:, 0:512]),
                 reads=["B4"], writes=["kdec"])
            P.op("pe", [(lambda h=h: T.matmul(B[h // 2][:, (h % 2) * 256:(h % 2) * 256 + 256], lhsT=kdec[:, h, :],
                                              rhs=v_rhs(h), start=True, stop=True)) for h in range(4)],
                 reads=["kdec", v_key], writes=["B0", "B1"])
            for h in range(4):
                P.op("dve", (lambda h=h: V.scalar_tensor_tensor(out=S32[:, h * 256:(h + 1) * 256],
                                                                in0=S32[:, h * 256:(h + 1) * 256],
                                                                scalar=tot[:, h:h + 1],
                                                                in1=B[h // 2][:, (h % 2) * 256:(h % 2) * 256 + 256],
                                                                op0=ALU.mult, op1=ALU.add)),
                     reads=["S32", "tot", "B0", "B1"], writes=["S32"])
            P.op("act", lambda: A.copy(out=S16[:], in_=S32[:]), reads=["S32"], writes=["S16"])

        if npre > 0:
            with contextlib.ExitStack() as st2:
                sb2 = lambda name, shape, dt: st2.enter_context(nc.sbuf_tensor(name, shape, dt))
                wkv = sb2("wkv", [128, KC, 1536], BF16)
                xpc = [sb2("xpc%d" % i, [128, KC * 128], BF16) for i in range(3)]
                vpre = [sb2("vpre%d" % i, [128, 1024], BF16) for i in range(2)]
                esuf = sb2("esuf", [128, 512], F32)
                kdp = sb2("kdp", [128, 512], BF16)
                usuf = sb2("usuf", [128, 128], F32)
                onesf = sb2("onesf", [128, 1], F32)
                B4f = B4[:, :].bitcast(F32)
                P.dma("sp", "c_usuf", lambda: nc.sync.dma_start(out=usuf[:], in_=consts[:, 3, :]), writes=["usuf"])
                P.op("dve", lambda: V.memset(onesf[:], 1.0), writes=["onesf"])
                P.dma("pool", "wkv", [lambda: G.dma_start(out=wkv[:, :, 0:512], in_=w_in[:, :, C_KG:C_KG + 512]),
                                      lambda: G.dma_start(out=wkv[:, :, 512:1536], in_=w_in[:, :, C_VG:C_VG + 1024])],
                      writes=["wkv"])

                def p_load(c):
                    xk = "xpc%d" % (c % 3)
                    P.dma("pool", xk, lambda: G.dma_start(out=xpc[c % 3][:], in_=xpre[c, :, :]), writes=[xk])

                def p_inproj(c):
                    xb = xpc[c % 3][:].rearrange("p (k t) -> p k t", k=KC)
                    xk = "xpc%d" % (c % 3)
                    sl = c % 2
                    bk, bkey = B[sl], "B%d" % sl
                    P.op("pe", [(lambda k=k: T.matmul(bk[:, 0:512], lhsT=xb[:, k, :], rhs=wkv[:, k, 0:512],
                                                      start=(k == 0), stop=(k == KC - 1))) for k in range(KC)],
                         reads=["wkv", xk], writes=[bkey])
                    yield

                    def vgrp(j, vb):
                        P.op("pe", [(lambda k=k: T.matmul(vb[:, 0:512], lhsT=xb[:, k, :],
                                                          rhs=wkv[:, k, 512 + j * 512:1024 + j * 512],
                                                          start=(k == 0), stop=(k == KC - 1))) for k in range(KC)],
                             reads=["wkv", xk], writes=["B%d" % (5 + j)])
                        if j == 0:
                            P.op("act", lambda: A.copy(out=vpre[sl][:, 0:512], in_=B5[:, 0:512]), reads=["B5"], writes=["vpre%d" % sl])
                        else:
                            P.op("dve", lambda: V.tensor_copy(out=vpre[sl][:, 512:1024], in_=B6[:, 0:512]), reads=["B6"],
                                 writes=["vpre%d" % sl])
                    vgrp(0, B5)
                    yield
                    P.op("pe", [(lambda k=k: T.matmul(B7[:, 0:128], lhsT=wgk[:, k, :], rhs=xb[:, k, :],
                                                      start=(k == 0), stop=(k == KC - 1))) for k in range(KC)],
                         reads=["wgk", xk], writes=["B7"])
                    P.op("act", lambda: A.copy(out=gkT[0:32, sl * 128:(sl + 1) * 128], in_=B7[0:32, 0:128]),
                         reads=["B7"], writes=["gkT%d" % sl])
                    vgrp(1, B6)
                    yield

                def p_tail(c):
                    sl = c % 2
                    bk, bkey = B[sl], "B%d" % sl
                    P.op("pe", [lambda: T.matmul(B[2][:, 0:512], lhsT=gkT[0:32, sl * 128:(sl + 1) * 128], rhs=waug[0:32, :],
                                                 start=True, stop=False),
                                lambda: T.matmul(B[2][:, 0:512], lhsT=ones32[0:1, :], rhs=bgk[0:1, :], start=False, stop=True)],
                         reads=["gkT%d" % sl, "waug", "ones32", "bgk"], writes=["B2"])
                    P.op("act", lambda: A.activation(out=e1[:], in_=B[2][:, 0:512], func=AF.Exp, scale=-1.0),
                         reads=["B2"], writes=["e1"])
                    P.op("act", lambda: A.activation(out=spl[:], in_=e1[:], func=AF.Ln, bias=1.0, scale=1.0),
                         reads=["e1"], writes=["spl"])
                    yield
                    P.op("pe", lambda: T.matmul(B[3][:, 0:512], lhsT=usuf[:], rhs=spl[:], start=True, stop=True),
                         reads=["usuf", "spl"], writes=["B3"])
                    P.op("pe", [(lambda h=h: T.matmul(B[2][:, h:h + 1], lhsT=spl[:, h * 128:(h + 1) * 128], rhs=onesf[:, 0:1],
                                                      start=True, stop=True)) for h in range(4)],
                         reads=["spl", "onesf"], writes=["B2"])
                    P.op("act", lambda: A.activation(out=esuf[:], in_=B[3][:, 0:512], func=AF.Exp, scale=-1.0 / 16.0),
                         reads=["B3"], writes=["esuf"])
                    P.op("act", lambda: A.activation(out=tot[:], in_=B[2][:, 0:4], func=AF.Exp, scale=-1.0 / 16.0),
                         reads=["B2"], writes=["tot"])
                    P.op("dve", lambda: V.tensor_tensor(out=kdp[:], in0=bk[:, 0:512], in1=esuf[:], op=ALU.mult),
                         reads=[bkey, "esuf"], writes=["kdp"])
                    yield
                    P.op("pe", [(lambda h=h: T.matmul((B4f if h < 2 else B[2])[:, (h % 2) * 256:(h % 2) * 256 + 256],
                                                      lhsT=kdp[:, h * 128:(h + 1) * 128],
                                                      rhs=vpre[sl][:, h * 256:(h + 1) * 256], start=True, stop=True)) for h in range(4)],
                         reads=["kdp", "vpre%d" % sl], writes=["B4", "B2"])
                    for h in range(4):
                        src = (B4f if h < 2 else B[2])[:, (h % 2) * 256:(h % 2) * 256 + 256]
                        P.op("dve", (lambda h=h, src=src: V.scalar_tensor_tensor(out=S32[:, h * 256:(h + 1) * 256],
                                                                                 in0=S32[:, h * 256:(h + 1) * 256],
                                                                                 scalar=tot[:, h:h + 1], in1=src,
                                                                                 op0=ALU.mult, op1=ALU.add)),
                             reads=["S32", "tot", "B4", "B2"], writes=["S32"])
                    yield

                def run_zip(*gens):
                    gens = list(gens)
                    while gens:
                        for gn in list(gens):
                            try:
                                next(gn)
                            except StopIteration:
                                gens.remove(gn)

                p_load(0)
                if npre > 1:
                    p_load(1)
                run_zip(p_inproj(0))
                for c in range(npre):
                    if c + 2 < npre:
                        p_load(c + 2)
                    if c + 1 < npre:
                        run_zip(p_tail(c), p_inproj(c + 1))
                    else:
                        run_zip(p_tail(c))
                P.op("act", lambda: A.copy(out=S16[:], in_=S32[:]), reads=["S32"], writes=["S16"])
                P.barrier()

        xTp = sb("xTp", [128, KC, TOK], BF16)
        wblk = [sb("wblk%d" % i, [128, KC, 512], BF16) for i in range(2)]
        q32 = sb("q32", [128, TP, 1024], F32)
        k32 = sb("k32", [128, TP, 128], F32)
        qr = sb("qr", [128, TP, 1024], BF16)
        kdup = sb("kdup", [128, 2, 2, 64], BF16)
        ta = sb("ta", [128, 512], F32)
        tb = sb("tb", [128, 512], F32)
        sza = sb("sza", [128, TP, 1024], BF16)
        szg = sb("szg", [128, TP, 1024], BF16)
        vg = sb("vg", [128, TP, 1024], BF16)
        qgT = sb("qgT", [128, 4, TOK], F32)
        kgT = sb("kgT", [128, 4, TOK], F32)
        kTall = sb("kTall", [128, 2, (nt + 1) * 128], BF16)
        vall = sb("vall", [128, nt + 1, 128], BF16)
        qT = sb("qT", [128, 8, 128], BF16)
        pexp = [sb("pexp%d" % i, [128, 2, 256], BF16) for i in range(2)]
        pT = [sb("pT%d" % i, [128, 2, 2, 128], BF16) for i in range(2)]
        mraw = sb("mraw", [128, 16], F32)
        msc = sb("msc", [128, 16], F32)
        negm = sb("negm", [128, 16], F32)
        dsm = sb("dsm", [128, 16], F32)
        esk = sb("esk", [128, 16], F32)
        den = sb("den", [128, 16], F32)
        rden = sb("rden", [128, 16], F32)
        otmp = sb("otmp", [128, 1024], F32)
        eq = sb("eq", [128, 4, 128], F32)
        ek = sb("ek", [128, 4, 128], F32)
        qdec = sb("qdec", [128, 4, 128], BF16)
        kneg = sb("kneg", [128, 4, 128], BF16)
        atm = sb("atm", [128, 4, 128], BF16)
        sqj = sb("sqj", [128, 256], BF16)
        ss = sb("ss", [128, 4], F32)
        rstd = sb("rstd", [128, 4], F32)
        y16 = sb("y16", [128, D], BF16)
        yT = sb("yT", [128, TP, KC, 128], BF16)
        xres = [sb("xres%d" % i, [128, 512], F32) for i in range(2)]
        r32 = [sb("r32_%d" % i, [128, D], F32) for i in range(TP)]
        lnst = sb("lnst", [128, TP, 8], F32)
        sqj2 = sb("sqj2", [128, D], BF16)

        wslot = [0]

        def load_wblk(src_ap, ncols):
            i = wslot[0] % 2
            wslot[0] += 1
            key = "wblk%d" % i
            P.dma("sp", key, lambda: nc.sync.dma_start(out=wblk[i][:, :, 0:ncols], in_=src_ap), reads=["w_in16", "w_out16"],
                  writes=[key])
            return wblk[i], key

        accs = [0]

        def next_acc():
            i = accs[0] % 2
            accs[0] += 1
            return B[i], "B%d" % i

        def rope(src, dst_fn, nh, slot):
            s4 = src.rearrange("p (h two f) -> p h two f", h=nh, two=2)
            t1, t2 = s4[:, :, 0, :], s4[:, :, 1, :]
            cosb = cst[:, slot, 0, :].unsqueeze(1).to_broadcast([128, nh, 32])
            sinb = cst[:, slot, 1, :].unsqueeze(1).to_broadcast([128, nh, 32])
            ta3 = ta[:, 0:nh * 32].rearrange("p (h f) -> p h f", h=nh)
            tb3 = tb[:, 0:nh * 32].rearrange("p (h f) -> p h f", h=nh)
            return t1, t2, cosb, sinb, ta3, tb3

        def do_rope(src, src_key, dst1, dst2, dst_key, nh, slot):
            t1, t2, cosb, sinb, ta3, tb3 = rope(src, None, nh, slot)
            P.op("dve", lambda: V.tensor_tensor(out=ta3, in0=t1, in1=cosb, op=ALU.mult), reads=[src_key, "cst"], writes=["ta"])
            P.op("dve", lambda: V.tensor_tensor(out=tb3, in0=t2, in1=sinb, op=ALU.mult), reads=[src_key, "cst"], writes=["tb"])
            P.op("dve", lambda: V.tensor_tensor(out=dst1, in0=ta3, in1=tb3, op=ALU.subtract), reads=["ta", "tb"], writes=[dst_key])
            P.op("dve", lambda: V.tensor_tensor(out=ta3, in0=t2, in1=cosb, op=ALU.mult), reads=[src_key, "cst"], writes=["ta"])
            P.op("dve", lambda: V.tensor_tensor(out=tb3, in0=t1, in1=sinb, op=ALU.mult), reads=[src_key, "cst"], writes=["tb"])
            P.op("dve", lambda: V.tensor_tensor(out=dst2, in0=ta3, in1=tb3, op=ALU.add), reads=["ta", "tb"], writes=[dst_key])

        def k_finish(t_k32, slot):
            kd5 = kdup[:]
            do_rope(t_k32, "k32", kd5[:, :, 0, 0:32], kd5[:, :, 0, 32:64], "kdup", 2, slot)
            P.op("dve", lambda: V.tensor_copy(out=kdup[:, :, 1, :], in_=kdup[:, :, 0, :]), reads=["kdup"], writes=["kdup"])
            P.op("pe", [(lambda kv=kv: T.transpose(B4[:, kv * 128:(kv + 1) * 128],
                                                   kdup[:, kv, :, :].rearrange("p a f -> p (a f)"), ident[:])) for kv in range(2)],
                 reads=["kdup", "ident"], writes=["B4"])
            P.op("dve", lambda: V.tensor_copy(out=kTall[:, :, slot * 128:(slot + 1) * 128],
                                              in_=B4[:, 0:256].rearrange("p (a t) -> p a t", a=2)),
                 reads=["B4"], writes=["kTall"])

        _stage(2)
        wb, wkey = load_wblk(w_in16[:, :, C_KA:C_KA + 256], 256)
        P.dma("pool", "xTp", lambda: G.dma_start(out=xTp[:, :, 0:128], in_=xT[:, :, 0:128]), writes=["xTp"])
        acc, akey = next_acc()
        P.op("pe", [(lambda k=k: T.matmul(acc[:, 0:256], lhsT=xTp[:, k, 0:128], rhs=wb[:, k, 0:256],
                                          start=(k == 0), stop=(k == KC - 1))) for k in range(KC)],
             reads=["xTp", wkey], writes=[akey])
        P.op("act", lambda: A.copy(out=k32[:, 0, :], in_=acc[:, 0:128]), reads=[akey], writes=["k32"])
        P.op("act", lambda: A.copy(out=vall[:, 0, :], in_=acc[:, 128:256]), reads=[akey], writes=["vall"])
        k_finish(k32[:, 0, :], 0)

        _stage(3)
        npass = nt // TP
        for ps_i in range(npass):
            tok0 = (1 + ps_i * TP) * 128
            P.dma("pool", "xTp", lambda: G.dma_start(out=xTp[:, :, :], in_=xT[:, :, tok0:tok0 + TOK]), writes=["xTp"])

            def tm_block(col0, ncols, handler):
                wb, wkey = load_wblk(w_in16[:, :, col0:col0 + ncols], ncols)
                for t in range(TP):
                    acc, akey = next_acc()
                    P.op("pe", [(lambda k=k: T.matmul(acc[:, 0:ncols], lhsT=xTp[:, k, t * 128:(t + 1) * 128],
                                                      rhs=wb[:, k, 0:ncols], start=(k == 0), stop=(k == KC - 1)))
                                for k in range(KC)],
                         reads=["xTp", wkey], writes=[akey])
                    handler(t, acc, akey)

            tm_block(C_QA, 512, lambda t, acc, ak: P.op("act", lambda: A.copy(out=q32[:, t, 0:512], in_=acc[:, 0:512]),
                                                        reads=[ak], writes=["q32"]))
            tm_block(C_QA + 512, 512, lambda t, acc, ak: P.op("act", lambda: A.copy(out=q32[:, t, 512:1024], in_=acc[:, 0:512]),
                                                              reads=[ak], writes=["q32"]))

            _stage(3.2)

            def h_kv(t, acc, ak):
                g = ps_i * TP + t + 1
                P.op("act", lambda: A.copy(out=k32[:, t, :], in_=acc[:, 0:128]), reads=[ak], writes=["k32"])
                P.op("act", lambda: A.copy(out=vall[:, g, :], in_=acc[:, 128:256]), reads=[ak], writes=["vall"])
            tm_block(C_KA, 256, h_kv)
            _stage(3.3)
            for j in range(2):
                tm_block(C_ZA + j * 512, 512,
                         lambda t, acc, ak, j=j: P.op("act", lambda: A.activation(out=sza[:, t, j * 512:(j + 1) * 512],
                                                                                  in_=acc[:, 0:512], func=AF.Silu),
                                                      reads=[ak], writes=["sza"]))
            _stage(3.4)
            for j in range(2):
                tm_block(C_VG + j * 512, 512,
                         lambda t, acc, ak, j=j: P.op("dve", lambda: V.tensor_copy(out=vg[:, t, j * 512:(j + 1) * 512],
                                                                                   in_=acc[:, 0:512]),
                                                      reads=[ak], writes=["vg"]))
            _stage(3.5)
            for j in range(2):
                tm_block(C_ZG + j * 512, 512,
                         lambda t, acc, ak, j=j: P.op("act", lambda: A.activation(out=szg[:, t, j * 512:(j + 1) * 512],
                                                                                  in_=acc[:, 0:512], func=AF.Silu),
                                                      reads=[ak], writes=["szg"]))

            _stage(3.6)
            for (col0, dstT, dkey) in ((C_QG, qgT, "qgT"), (C_KG, kgT, "kgT")):
                wb, wkey = load_wblk(w_in16[:, :, col0:col0 + 512], 512)
                for h in range(4):
                    acc, akey = next_acc()
                    P.op("pe", [(lambda k=k: T.matmul(acc[:, 0:TOK], lhsT=wb[:, k, h * 128:(h + 1) * 128], rhs=xTp[:, k, :],
                                                      start=(k == 0), stop=(k == KC - 1))) for k in range(KC)],
                         reads=["xTp", wkey], writes=[akey])
                    P.op("act", (lambda h=h, acc=acc: A.copy(out=dstT[:, h, :], in_=acc[:, 0:TOK])), reads=[akey], writes=[dkey])
            _stage(3.7)
            acc, akey = next_acc()
            P.op("pe", [(lambda k=k: T.matmul(acc[:, 0:TOK], lhsT=wgk[:, k, :], rhs=xTp[:, k, :],
                                              start=(k == 0), stop=(k == KC - 1))) for k in range(KC)],
                 reads=["xTp", "wgk"], writes=[akey])
            _stage(3.8)
            P.op("act", lambda: A.copy(out=gkT[0:32, 0:TOK], in_=acc[0:32, 0:TOK]), reads=[akey], writes=["gkT"])

            _stage(4)
            for t in range(TP):
                g = ps_i * TP + t + 1
                mi = 0 if (ps_i == 0 and t == 0) else 1
                q4 = qr[:, t, :].rearrange("p (h two f) -> p h two f", h=16, two=2)
                do_rope(q32[:, t, :], "q32", q4[:, :, 0, :], q4[:, :, 1, :], "qr", 16, g)
                _stage(4.1)
                if not SKIPK:
                    k_finish(k32[:, t, :], g)
                _stage(4.2)
                P.op("pe", [(lambda j=j: T.transpose(B4[:, j * 128:(j + 1) * 128], qr[:, t, j * 128:(j + 1) * 128], ident[:]))
                            for j in range(8)], reads=["qr", "ident"], writes=["B4"])
                P.op("dve", lambda: V.tensor_copy(out=qT[:].rearrange("p j t -> p (j t)"), in_=B4[:, 0:1024]),
                     reads=["B4"], writes=["qT"])
                _stage(4.3)
                for hp in range(8):
                    kv = hp // 4
                    c0 = (hp % 2) * 256
                    P.op("pe", [(lambda hh=hh: T.matmul(B[2 + hh][:, c0:c0 + 256],
                                                        lhsT=qT[hh * 64:(hh + 1) * 64, hp, :],
                                                        rhs=kTall[hh * 64:(hh + 1) * 64, kv, (g - 1) * 128:(g + 1) * 128],
                                                        start=True, stop=True)) for hh in range(2)],
                         reads=["qT", "kTall"], writes=["B2", "B3"])
                    _stage(4.4)
                    for hh in range(2):
                        P.op("dve", (lambda hh=hh: V.tensor_reduce(out=mraw[:, 2 * hp + hh:2 * hp + hh + 1],
                                                                   in_=B[2 + hh][:, c0:c0 + 256], axis=AX.X, op=ALU.max)),
                             reads=["B%d" % (2 + hh)], writes=["mraw"])
                    P.op("dve", lambda: V.scalar_tensor_tensor(out=msc[:, 2 * hp:2 * hp + 2], in0=mraw[:, 2 * hp:2 * hp + 2],
                                                               scalar=0.125, in1=sinkb[:, 2 * hp:2 * hp + 2],
                                                               op0=ALU.mult, op1=ALU.max),
                         reads=["mraw", "sinkb"], writes=["msc"])
                    P.op("dve", lambda: V.tensor_scalar(out=negm[:, 2 * hp:2 * hp + 2], in0=msc[:, 2 * hp:2 * hp + 2],
                                                        scalar1=-1.0, scalar2=None, op0=ALU.mult),
                         reads=["msc"], writes=["negm"])
                    _stage(4.5)
                    pe_, pkey = pexp[hp % 2], "pexp%d" % (hp % 2)
                    for hh in range(2):
                        P.op("act", (lambda hh=hh: A.activation(out=pe_[:, hh, :], in_=B[2 + hh][:, c0:c0 + 256],
                                                                func=AF.Exp, bias=negm[:, 2 * hp + hh:2 * hp + hh + 1],
                                                                scale=0.125)),
                             reads=["B%d" % (2 + hh), "negm"], writes=[pkey])
                    _stage(4.6)
                    P.op("pe", [(lambda hh=hh, kt=kt: T.transpose(B4[:, (hh * 2 + kt) * 128:(hh * 2 + kt + 1) * 128],
                                                                  pe_[:, hh, kt * 128:(kt + 1) * 128], ident[:]))
                                for hh in range(2) for kt in range(2)], reads=[pkey, "ident"], writes=["B4"])
                    pt_, ptkey = pT[hp % 2], "pT%d" % (hp % 2)
                    P.op("dve", lambda: V.tensor_tensor(out=pt_[:], in0=B4[:, 0:512].rearrange("p (h k q) -> p h k q", h=2, k=2),
                                                        in1=mask16[:, mi, :, :].unsqueeze(1).to_broadcast([128, 2, 2, 128]),
                                                        op=ALU.mult), reads=["B4", "mask16"], writes=[ptkey])
                    _stage(4.7)
                    obk, okey = (B5, "B5") if hp < 4 else (B6, "B6")
                    mm = []
                    for hh in range(2):
                        hl = (2 * hp + hh) % 8
                        for kt in range(2):
                            mm.append(lambda hh=hh, kt=kt, hl=hl: T.matmul(obk[:, hl * 64:(hl + 1) * 64], lhsT=pt_[:, hh, kt, :],
                                                                           rhs=vall[:, g - 1 + kt, kv * 64:(kv + 1) * 64],
                                                                           start=(kt == 0), stop=(kt == 1)))
                        for kt in range(2):
                            mm.append(lambda hh=hh, kt=kt: T.matmul(B7[:, 2 * hp + hh:2 * hp + hh + 1], lhsT=pt_[:, hh, kt, :],
                                                                    rhs=ones16[:, 0:1], start=(kt == 0), stop=(kt == 1)))
                    P.op("pe", mm, reads=[ptkey, "vall", "ones16"], writes=[okey, "B7"])
                _stage(4.8)
                P.op("dve", lambda: V.tensor_tensor(out=dsm[:], in0=sinkb[:], in1=msc[:], op=ALU.subtract),
                     reads=["sinkb", "msc"], writes=["dsm"])
                P.op("act", lambda: A.activation(out=esk[:], in_=dsm[:], func=AF.Exp), reads=["dsm"], writes=["esk"])
                P.op("dve", lambda: V.tensor_tensor(out=den[:], in0=B7[:, 0:16], in1=esk[:], op=ALU.add),
                     reads=["B7", "esk"], writes=["den"])
                P.op("dve", lambda: V.reciprocal(out=rden[:], in_=den[:]), reads=["den"], writes=["rden"])
                for j, (obk, okey) in enumerate(((B5, "B5"), (B6, "B6"))):
                    P.op("dve", (lambda j=j, obk=obk: V.tensor_tensor(
                        out=otmp[:, j * 512:(j + 1) * 512].rearrange("p (h d) -> p h d", h=8),
                        in0=obk[:, 0:512].rearrange("p (h d) -> p h d", h=8),
                        in1=rden[:, j * 8:(j + 1) * 8].unsqueeze(2).to_broadcast([128, 8, 64]), op=ALU.mult)),
                        reads=[okey, "rden"], writes=["otmp"])
                P.op("dve", lambda: V.tensor_tensor(out=y16[:, 0:1024], in0=otmp[:], in1=sza[:, t, :], op=ALU.mult),
                     reads=["otmp", "sza"], writes=["y16"])

                _stage(5)
                gla_decay(gkT[0:32, t * 128:(t + 1) * 128], "m")
                P.op("act", lambda: A.activation(out=eq[:].rearrange("p h t -> p (h t)"), in_=B[3][:, 0:512], func=AF.Exp,
                                                 scale=-1.0 / 16.0), reads=["B3"], writes=["eq"])
                P.op("act", lambda: A.activation(out=ek[:].rearrange("p h t -> p (h t)"), in_=B[3][:, 0:512], func=AF.Exp,
                                                 scale=1.0 / 16.0), reads=["B3"], writes=["ek"])
                P.op("dve", lambda: V.scalar_tensor_tensor(out=qdec[:], in0=qgT[:, :, t * 128:(t + 1) * 128],
                                                           scalar=128.0 ** -0.5, in1=eq[:], op0=ALU.mult, op1=ALU.mult),
                     reads=["qgT", "eq"], writes=["qdec"])
                P.op("dve", lambda: V.tensor_tensor(out=kneg[:], in0=kgT[:, :, t * 128:(t + 1) * 128], in1=ek[:], op=ALU.mult),
                     reads=["kgT", "ek"], writes=["kneg"])
                P.op("pe", [(lambda h=h: T.matmul(B[2][:, h * 128:(h + 1) * 128], lhsT=kneg[:, h, :], rhs=qdec[:, h, :],
                                                  start=True, stop=True)) for h in range(4)],
                     reads=["kneg", "qdec"], writes=["B2"])
                P.op("dve", lambda: V.tensor_tensor(out=atm[:], in0=B[2][:, 0:512].rearrange("p (h i) -> p h i", h=4),
                                                    in1=uincl16[:].unsqueeze(1).to_broadcast([128, 4, 128]), op=ALU.mult),
                     reads=["B2", "uincl16"], writes=["atm"])
                mm = []
                for h in range(4):
                    obk = B5 if h < 2 else B6
                    sl = slice((h % 2) * 256, (h % 2) * 256 + 256)
                    mm.append(lambda h=h, obk=obk, sl=sl: T.matmul(obk[:, sl], lhsT=atm[:, h, :], rhs=vg[:, t, h * 256:(h + 1) * 256],
                                                                   start=True, stop=False))
                    mm.append(lambda h=h, obk=obk, sl=sl: T.matmul(obk[:, sl], lhsT=qdec[:, h, :], rhs=S16[:, h * 256:(h + 1) * 256],
                                                                   start=False, stop=True))
                P.op("pe", mm, reads=["atm", "vg", "qdec", "S16"], writes=["B5", "B6"])
                gla_state_update(lambda h: kgT[:, h, t * 128:(t + 1) * 128], "kgT",
                                 lambda h: vg[:, t, h * 256:(h + 1) * 256], "vg")
                P.op("dve", lambda: V.memset(ss[:], 0.0), writes=["ss"])
                for h in range(4):
                    obk, okey = (B5, "B5") if h < 2 else (B6, "B6")
                    sl = slice((h % 2) * 256, (h % 2) * 256 + 256)
                    P.op("act", (lambda h=h, obk=obk, sl=sl: A.activation(out=sqj[:], in_=obk[:, sl], func=AF.Square,
                                                                          accum_out=ss[:, h:h + 1])),
                         reads=[okey, "ss"], writes=["sqj", "ss"])
                P.op("dve", lambda: V.tensor_scalar(out=rstd[:], in0=ss[:], scalar1=1.0 / 256.0, scalar2=RMS_EPS,
                                                    op0=ALU.mult, op1=ALU.add), reads=["ss"], writes=["rstd"])
                P.op("act", lambda: A.activation(out=rstd[:], in_=rstd[:], func=AF.Sqrt), reads=["rstd"], writes=["rstd"])
                P.op("dve", lambda: V.reciprocal(out=rstd[:], in_=rstd[:]), reads=["rstd"], writes=["rstd"])
                for h in range(4):
                    obk, okey = (B5, "B5") if h < 2 else (B6, "B6")
                    sl = slice((h % 2) * 256, (h % 2) * 256 + 256)
                    P.op("dve", (lambda h=h, obk=obk, sl=sl: V.scalar_tensor_tensor(out=otmp[:, h * 256:(h + 1) * 256], in0=obk[:, sl],
                                                                                    scalar=rstd[:, h:h + 1], in1=normwb[:],
                                                                                    op0=ALU.mult, op1=ALU.mult)),
                         reads=[okey, "rstd", "normwb"], writes=["otmp"])
                P.op("dve", lambda: V.tensor_tensor(out=y16[:, 1024:2048], in0=otmp[:], in1=szg[:, t, :], op=ALU.mult),
                     reads=["otmp", "szg"], writes=["y16"])
                _stage(6)
                for qd in range(2):
                    P.op("pe", [(lambda j=j: T.transpose(B4[:, (j % 8) * 128:(j % 8 + 1) * 128], y16[:, j * 128:(j + 1) * 128], ident[:]))
                                for j in range(qd * 8, qd * 8 + 8)], reads=["y16", "ident"], writes=["B4"])
                    P.op("act", (lambda qd=qd: A.copy(out=yT[:, t, qd * 8:(qd + 1) * 8, :].rearrange("p j t -> p (j t)"),
                                                      in_=B4[:, 0:1024])), reads=["B4"], writes=["yT"])

            _stage(7)
            P.op("dve", lambda: V.memset(lnst[:], 0.0), writes=["lnst"])
            for cb in range(4):
                wbo, wkeyo = load_wblk(w_out16[:, :, cb * 512:(cb + 1) * 512], 512)
                for t in range(TP):
                    gt = ps_i * TP + t
                    xi = (cb * TP + t) % 2
                    xs, xkey = xres[xi], "xres%d" % xi
                    P.dma("sp", xkey, lambda: nc.sync.dma_start(out=xs[:], in_=xtok[gt, :, cb * 512:(cb + 1) * 512]), writes=[xkey])
                    acc, akey = next_acc()
                    P.op("pe", [(lambda k=k: T.matmul(acc[:, 0:512], lhsT=yT[:, t, k, :], rhs=wbo[:, k, :],
                                                      start=(k == 0), stop=(k == KC - 1))) for k in range(KC)],
                         reads=["yT", wkeyo], writes=[akey])
                    P.op("dve", lambda: V.scalar_tensor_tensor(out=r32[t][:, cb * 512:(cb + 1) * 512], in0=xs[:], scalar=ALPHA,
                                                               in1=acc[:, 0:512], op0=ALU.mult, op1=ALU.add,
                                                               accum_out=lnst[:, t, cb:cb + 1]),
                         reads=[xkey, akey, "lnst"], writes=["r32_%d" % t, "lnst"])
            for t in range(TP):
                gt = ps_i * TP + t
                rt, rkey = r32[t], "r32_%d" % t
                L = lambda a, b: lnst[:, t, a:b]
                P.op("dve", lambda: V.tensor_reduce(out=L(4, 5), in_=L(0, 4), axis=AX.X, op=ALU.add),
                     reads=["lnst"], writes=["lnst"])
                P.op("dve", lambda: V.tensor_scalar(out=L(5, 6), in0=L(4, 5), scalar1=-1.0 / D, scalar2=None, op0=ALU.mult),
                     reads=["lnst"], writes=["lnst"])
                P.op("act", lambda: A.activation(out=sqj2[:], in_=rt[:], func=AF.Square, bias=L(5, 6), scale=1.0,
                                                 accum_out=L(6, 7)), reads=[rkey, "lnst"], writes=["sqj2", "lnst"])
                P.op("dve", lambda: V.tensor_scalar(out=L(7, 8), in0=L(6, 7), scalar1=1.0 / D, scalar2=LN_EPS,
                                                    op0=ALU.mult, op1=ALU.add), reads=["lnst"], writes=["lnst"])
                P.op("act", lambda: A.activation(out=L(7, 8), in_=L(7, 8), func=AF.Sqrt), reads=["lnst"], writes=["lnst"])
                P.op("dve", lambda: V.reciprocal(out=L(7, 8), in_=L(7, 8)), reads=["lnst"], writes=["lnst"])
                P.op("dve", lambda: V.tensor_scalar(out=rt[:], in0=rt[:], scalar1=L(5, 6), scalar2=L(7, 8),
                                                    op0=ALU.add, op1=ALU.mult), reads=[rkey, "lnst"], writes=[rkey])
                P.op("dve", lambda: V.tensor_tensor(out=rt[:], in0=rt[:], in1=lngb[:], op=ALU.mult), reads=[rkey, "lngb"], writes=[rkey])
                P.op("dve", lambda: V.tensor_tensor(out=rt[:], in0=rt[:], in1=lnbb[:], op=ALU.add), reads=[rkey, "lnbb"], writes=[rkey])
                P.dma("sp", "out%d" % t, lambda: nc.sync.dma_start(out=out[gt, :, :], in_=rt[:]), reads=[rkey])


def _kmajor(a):
    n = a.shape[1]
    return np.ascontiguousarray(a.reshape(KC, 128, n).transpose(1, 0, 2))


def _host_consts():
    ident = np.eye(128, dtype=np.float32)
    uincl = (np.arange(128)[:, None] <= np.arange(128)[None, :]).astype(np.float32)
    ones = np.ones((128, 128), np.float32)
    usuf = (np.arange(128)[:, None] > np.arange(128)[None, :]).astype(np.float32)
    consts = np.ascontiguousarray(np.stack([ident, uincl, ones, usuf], axis=1))
    k = np.arange(128)[:, None, None]
    kt = np.arange(2)[None, :, None]
    q = np.arange(128)[None, None, :]
    kj = kt * 128 + k
    reg = ((kj > q) & (kj <= q + 128)).astype(np.float32)
    first = reg * (kj >= 128)
    return consts, reg, first


def kernel(x, w_in, w_gk_up, b_gk, attn_sinks, gla_norm_w, w_out, ln_g, ln_b, _ncores=NCORES, _npre=NPRE):
    x = np.asarray(x, np.float32)[0]
    w = np.asarray(w_in, np.float32)[0]
    perm = np.concatenate([np.arange(0, 1024), np.arange(1024, 1152), np.arange(1152, 1280), np.arange(1280, 2304),
                           np.arange(3328, 4352), np.arange(4352, 5376), np.arange(2304, 2816), np.arange(2816, 3328),
                           np.arange(5376, 5392)])
    w_l = _kmajor(np.concatenate([w[:, perm], np.zeros((D, 112), np.float32)], axis=1))
    wo_l = _kmajor(np.asarray(w_out, np.float32)[0])
    waug = np.zeros((32, 512), np.float32)
    waug[0:16] = np.asarray(w_gk_up, np.float32)[0]
    waug[16] = np.asarray(b_gk, np.float32)[0]
    consts, mreg, mfirst = _host_consts()
    sinks = np.ascontiguousarray(np.broadcast_to(np.asarray(attn_sinks, np.float32)[0][None, :], (128, 16)))
    normw = np.ascontiguousarray(np.broadcast_to(np.asarray(gla_norm_w, np.float32)[0][None, :], (128, 256)))
    lng = np.ascontiguousarray(np.broadcast_to(np.asarray(ln_g, np.float32)[0][None, :], (128, D)))
    lnb = np.ascontiguousarray(np.broadcast_to(np.asarray(ln_b, np.float32)[0][None, :], (128, D)))
    inv_freq = (1.0 / (10000.0 ** (np.arange(0, 32, dtype=np.float32) * 2.0 / 64.0))).astype(np.float32)
    xTfull = np.ascontiguousarray(x.T)
    in_maps = []
    for c in range(_ncores):
        s0 = c * OWN
        xt = np.zeros((D, OWN + 128), np.float32)
        if c > 0:
            xt[:, 0:128] = xTfull[:, s0 - 128:s0]
        xt[:, 128:] = xTfull[:, s0:s0 + OWN]
        npre_tok = _npre * 128
        xp = np.zeros((D, max(npre_tok, 128)), np.float32)
        if _npre > 0 and s0 > 0:
            xp[:, npre_tok - s0:] = xTfull[:, 0:s0]
        pos = (np.arange(s0 - 128, s0 + OWN)).astype(np.float32)
        ang = pos[:, None] * inv_freq[None, :]
        cs = np.stack([np.cos(ang), np.sin(ang)], axis=1).astype(np.float32)
        cs = np.ascontiguousarray(cs.reshape(NT + 1, 128, 2, 32).transpose(1, 0, 2, 3))
        masks = np.ascontiguousarray(np.stack([mfirst if c == 0 else mreg, mreg], axis=1))
        in_maps.append({
            "xT": _kmajor(xt),
            "xpre": np.ascontiguousarray(_kmajor(xp).reshape(128, KC, max(_npre, 1), 128).transpose(2, 0, 1, 3)
                                         .reshape(max(_npre, 1), 128, KC * 128)),
            "xtok": np.ascontiguousarray(x[s0:s0 + OWN].reshape(NT, 128, D)),
            "w_in": w_l, "w_out": wo_l, "w_aug": waug, "cs": cs, "masks": masks, "consts": consts,
            "sinks": sinks, "normw": normw, "lng": lng, "lnb": lnb,
        })
    nc = build(npre=_npre, nt=NT)
    res = run_bass_kernel_spmd(nc, in_maps, core_ids=list(range(_ncores)))
    outs = [np.asarray(r["out"]).reshape(OWN, D) for r in res.results]
    return np.concatenate(outs, axis=0)[None].astype(np.float32)
```

```python
import contextlib
import math
import numpy as np
import concourse.bass as bass
import concourse.mybir as mybir
from concourse.bass_utils import run_bass_kernel_spmd

F32 = mybir.dt.float32
BF16 = mybir.dt.bfloat16
AF = mybir.ActivationFunctionType
ALU = mybir.AluOpType
AX = mybir.AxisListType

NCORES = 8
D = 2048
SEQ = 16384
OWN = SEQ // NCORES
NT = OWN // 128
NPRE = (SEQ - OWN) // 128
TP = 2
TOK = TP * 128
KC = D // 128
WCOLS = 5408
C_QA, C_KA, C_VA, C_ZA, C_VG, C_ZG, C_QG, C_KG, C_GK = 0, 1024, 1152, 1280, 2304, 3328, 4352, 4864, 5376
ALPHA = 2.0 ** 0.25
WBLOCKS = [("in", C_QA, 512), ("in", C_QA + 512, 512), ("in", C_KA, 256), ("in", C_ZA, 512), ("in", C_ZA + 512, 512),
           ("in", C_VG, 512), ("in", C_VG + 512, 512), ("in", C_ZG, 512), ("in", C_ZG + 512, 512),
           ("in", C_QG, 512), ("in", C_KG, 512),
           ("out", 0, 512), ("out", 512, 512), ("out", 1024, 512), ("out", 1536, 512)]
WB_IDX = {(src, c0): i for i, (src, c0, n) in enumerate(WBLOCKS)}
LN_EPS = 1e-5
RMS_EPS = 1e-5


class Prog:
    def __init__(self, nc, stack):
        self.nc = nc
        self.stack = stack
        self.engs = {"pe": nc.tensor, "act": nc.scalar, "dve": nc.vector, "pool": nc.gpsimd, "sp": nc.sync}
        self.sem = {}
        self.cnt = {}
        self.waited = {}
        self.lastw = {}
        self.readers = {}
        self.ninst = {e: 0 for e in self.engs}
        for e in ("pe", "act", "dve", "pool"):
            self._mksem("E_" + e)

    def _mksem(self, name):
        self.sem[name] = self.stack.enter_context(self.nc.semaphore(name))
        self.cnt[name] = 0

    def _deps(self, reads, writes):
        deps = {}

        def add(tok):
            if tok is None:
                return
            s, v = tok
            if deps.get(s, 0) < v:
                deps[s] = v
        for r in reads:
            add(self.lastw.get(r))
        for w in writes:
            add(self.lastw.get(w))
            for t in self.readers.get(w, ()):
                add(t)
        return deps

    def _wait(self, eng, deps):
        e = self.engs[eng]
        for s, v in deps.items():
            if self.waited.get((eng, s), 0) >= v:
                continue
            if eng == "pe" and s == "E_pe":
                continue
            e.wait_ge(self.sem[s], v)
            self.ninst[eng] += 1
            self.waited[(eng, s)] = v

    def _commit(self, tok, reads, writes):
        for w in writes:
            self.lastw[w] = tok
            self.readers[w] = []
        for r in reads:
            self.readers.setdefault(r, []).append(tok)

    PSUM_KEYS = frozenset("B%d" % i for i in range(8))

    def op(self, eng, fns, reads=(), writes=()):
        if callable(fns):
            fns = [fns]
        writes = list(writes) + [r for r in reads if r in self.PSUM_KEYS and r not in writes]
        self._wait(eng, self._deps(reads, writes))
        ins = None
        for f in fns:
            ins = f()
            self.ninst[eng] += 1
        s = "E_" + eng
        self.cnt[s] += 1
        ins.then_inc(self.sem[s], 1)
        self._commit((s, self.cnt[s]), reads, writes)

    def dma(self, eng, slot, fns, reads=(), writes=()):
        if callable(fns):
            fns = [fns]
        s = "D_" + slot
        if s not in self.sem:
            self._mksem(s)
        self._wait(eng, self._deps(reads, writes))
        for f in fns:
            ins = f()
            self.ninst[eng] += 1
            self.cnt[s] += 16
            ins.then_inc(self.sem[s], 16)
        self._commit((s, self.cnt[s]), reads, writes)

    def barrier(self):
        for eng in ("pe", "act", "dve", "pool", "sp"):
            e = self.engs[eng]
            for s, v in self.cnt.items():
                if v > 0 and self.waited.get((eng, s), 0) < v:
                    e.wait_ge(self.sem[s], v)
                    self.waited[(eng, s)] = v

    def finish(self, eng="sp"):
        e = self.engs[eng]
        for s, v in self.cnt.items():
            if v > 0 and self.waited.get((eng, s), 0) < v:
                e.wait_ge(self.sem[s], v)
                self.waited[(eng, s)] = v


class _Stop(Exception):
    pass


STAGE = 99
SKIPK = False
DBG = 0


def _stage(n):
    if STAGE < n:
        raise _Stop()


def build(npre=NPRE, nt=NT):
    nc = bass.Bass("TRN2", target_bir_lowering=False)
    din = lambda name, shape: nc.dram_tensor(name, shape, F32, kind="ExternalInput").ap()
    xT = din("xT", [128, KC, (nt + 1) * 128])
    xpre = din("xpre", [max(npre, 1), 128, KC * 128])
    xtok = din("xtok", [nt, 128, D])
    w_in = din("w_in", [128, KC, WCOLS])
    w_out = din("w_out", [128, KC, D])
    w_aug = din("w_aug", [32, 512])
    cs = din("cs", [128, nt + 1, 2, 32])
    masks = din("masks", [128, 2, 2, 128])
    consts = din("consts", [128, 4, 128])
    sinks = din("sinks", [128, 16])
    normw = din("normw", [128, 256])
    lng = din("lng", [128, D])
    lnb = din("lnb", [128, D])
    out = nc.dram_tensor("out", [nt, 128, D], F32, kind="ExternalOutput").ap()
    wsc = nc.dram_tensor("wsc", [len(WBLOCKS), 128, KC * 512], BF16, kind="Internal").ap()

    with contextlib.ExitStack() as st:
        P = Prog(nc, st)
        try:
            _build_body(nc, st, P, npre, nt, locals())
        except _Stop:
            pass
        P.finish("sp")
    return nc


def _build_body(nc, st, P, npre, nt, env):
    xT, xpre, xtok, w_in, w_out, w_aug, cs, masks, consts, sinks, normw, lng, lnb, out, wsc = [env[k] for k in ('xT', 'xpre', 'xtok', 'w_in', 'w_out', 'w_aug', 'cs', 'masks', 'consts', 'sinks', 'normw', 'lng', 'lnb', 'out', 'wsc')]
    if True:
        sb = lambda name, shape, dt: st.enter_context(nc.sbuf_tensor(name, shape, dt))
        psb = lambda name, dt=F32: st.enter_context(nc.psum_tensor(name, [128, 512 if dt == F32 else 1024], dt))
        V, A, T, G = nc.vector, nc.scalar, nc.tensor, nc.gpsimd

        B = [psb("B%d" % i) for i in range(4)]
        B4 = psb("B4", BF16)
        B5, B6, B7 = psb("B5"), psb("B6"), psb("B7")

        ident = sb("ident", [128, 128], BF16)
        uincl = sb("uincl", [128, 128], F32)
        uincl16 = sb("uincl16", [128, 128], BF16)
        ones16 = sb("ones16", [128, 128], BF16)
        mask16 = sb("mask16", [128, 2, 2, 128], BF16)
        sinkb = sb("sinkb", [128, 16], F32)
        normwb = sb("normwb", [128, 256], F32)
        lngb = sb("lngb", [128, D], F32)
        lnbb = sb("lnbb", [128, D], F32)
        cst = sb("cst", [128, nt + 1, 2, 32], F32)
        waug = sb("waug", [32, 512], F32)
        S32 = sb("S32", [128, 1024], F32)
        S16 = sb("S16", [128, 1024], BF16)
        gkT = sb("gkT", [128, TOK], F32)
        e1 = sb("e1", [128, 512], F32)
        spl = sb("spl", [128, 512], F32)
        nbl = sb("nbl", [128, 4], F32)
        tot = sb("tot", [128, 4], F32)
        ekd = sb("ekd", [128, 4, 128], F32)
        kdT = sb("kdT", [128, 4, 128], BF16)
        kdec = sb("kdec", [128, 4, 128], BF16)
        wgk = sb("wgk", [128, KC, 32], BF16)

        P.dma("pool", "c_ident", lambda: G.dma_start(out=ident[:], in_=consts[:, 0, :]), writes=["ident"])
        identf = sb("identf", [128, 128], F32)
        P.dma("sp", "c_identf", lambda: nc.sync.dma_start(out=identf[:], in_=consts[:, 0, :]), writes=["identf"])
        P.dma("sp", "c_uincl", lambda: nc.sync.dma_start(out=uincl[:], in_=consts[:, 1, :]), writes=["uincl"])
        P.dma("pool", "c_uincl16", lambda: G.dma_start(out=uincl16[:], in_=consts[:, 1, :]), writes=["uincl16"])
        P.dma("pool", "c_ones", lambda: G.dma_start(out=ones16[:], in_=consts[:, 2, :]), writes=["ones16"])
        P.dma("pool", "c_mask", lambda: G.dma_start(out=mask16[:], in_=masks[:, :, :, :]), writes=["mask16"])
        P.dma("sp", "c_sink", lambda: nc.sync.dma_start(out=sinkb[:], in_=sinks[:, :]), writes=["sinkb"])
        P.dma("sp", "c_normw", lambda: nc.sync.dma_start(out=normwb[:], in_=normw[:, :]), writes=["normwb"])
        P.dma("sp", "c_lng", lambda: nc.sync.dma_start(out=lngb[:], in_=lng[:, :]), writes=["lngb"])
        P.dma("sp", "c_lnb", lambda: nc.sync.dma_start(out=lnbb[:], in_=lnb[:, :]), writes=["lnbb"])
        P.dma("sp", "c_cs", lambda: nc.sync.dma_start(out=cst[:], in_=cs[:, :, :, :]), writes=["cst"])
        P.op("dve", lambda: V.memset(waug[:], 0.0), writes=["waug"])
        P.dma("sp", "c_waug", lambda: nc.sync.dma_start(out=waug[0:16, :], in_=w_aug[0:16, :]), writes=["waug"])
        P.dma("pool", "c_wgk", lambda: G.dma_start(out=wgk[:], in_=w_in[:, :, C_GK:C_GK + 32]), writes=["wgk"])
        P.op("dve", lambda: V.memset(S32[:], 0.0), writes=["S32"])
        P.op("dve", lambda: V.memset(S16[:], 0.0), writes=["S16"])
        ones32 = sb("ones32", [1, 128], F32)
        bgk = sb("bgk", [1, 512], F32)
        P.op("dve", lambda: V.memset(ones32[:], 1.0), writes=["ones32"])
        P.dma("sp", "c_bgk", lambda: nc.sync.dma_start(out=bgk[:], in_=w_aug[16:17, :]), writes=["bgk"])

        def wsc_view(bi):
            n = WBLOCKS[bi][2]
            return wsc[bi, :, 0:KC * n].rearrange("p (k c) -> p k c", k=KC)

        for bi, (src, c0, n) in enumerate(WBLOCKS):
            srcap = (w_in if src == "in" else w_out)[:, :, c0:c0 + n]
            P.dma("pool", "wcast", (lambda bi=bi, srcap=srcap: G.dma_start(out=wsc_view(bi), in_=srcap)), writes=["wsc"])
        _stage(1)
        def gla_decay(gk_lhsT, tag):
            P.op("pe", [lambda: T.matmul(B[2][:, 0:512], lhsT=gk_lhsT, rhs=waug[0:32, :], start=True, stop=False),
                        lambda: T.matmul(B[2][:, 0:512], lhsT=ones32[0:1, :], rhs=bgk[0:1, :], start=False, stop=True)],
                 reads=["gkT", "waug", "ones32", "bgk"], writes=["B2"])
            P.op("act", lambda: A.activation(out=e1[:], in_=B[2][:, 0:512], func=AF.Exp, scale=-1.0),
                 reads=["B2"], writes=["e1"])
            P.op("act", lambda: A.activation(out=spl[:], in_=e1[:], func=AF.Ln, bias=1.0, scale=1.0),
                 reads=["e1"], writes=["spl"])
            P.op("pe", [(lambda h=h: T.matmul(B[3][:, h * 128:(h + 1) * 128], lhsT=spl[:, h * 128:(h + 1) * 128],
                                              rhs=uincl[:], start=True, stop=True)) for h in range(4)],
                 reads=["spl", "uincl"], writes=["B3"])
            P.op("dve", lambda: V.tensor_scalar(out=nbl[:], in0=B[3][:, 0:512].rearrange("p (h t) -> p h t", h=4)[:, :, 127],
                                                scalar1=-1.0 / 16.0, scalar2=None, op0=ALU.mult),
                 reads=["B3"], writes=["nbl"])
            P.op("act", lambda: A.activation(out=tot[:], in_=nbl[:], func=AF.Exp), reads=["nbl"], writes=["tot"])
            for h in range(4):
                P.op("act", (lambda h=h: A.activation(out=ekd[:, h, :], in_=B[3][:, h * 128:(h + 1) * 128], func=AF.Exp,
                                                      bias=nbl[:, h:h + 1], scale=1.0 / 16.0)),
                     reads=["B3", "nbl"], writes=["ekd"])

        def gla_state_update(kT_src, kT_key, v_rhs, v_key):
            for h in range(4):
                P.op("dve", (lambda h=h: V.tensor_tensor(out=kdT[:, h, :], in0=kT_src(h), in1=ekd[:, h, :], op=ALU.mult)),
                     reads=[kT_key, "ekd"], writes=["kdT"])
            P.op("pe", [(lambda h=h: T.transpose(B4[:, h * 128:(h + 1) * 128], kdT[:, h, :], ident[:])) for h in range(4)],
                 reads=["kdT", "ident"], writes=["B4"])
            P.op("act", lambda: A.copy(out=kdec[:].rearrange("p h t -> p (h t)"), in_=B4[:, 0:512]),
                 reads=["B4"], writes=["kdec"])
            P.op("pe", [(lambda h=h: T.matmul(B[h // 2][:, (h % 2) * 256:(h % 2) * 256 + 256], lhsT=kdec[:, h, :],
                                              rhs=v_rhs(h), start=True, stop=True)) for h in range(4)],
                 reads=["kdec", v_key], writes=["B0", "B1"])
            for h in range(4):
                P.op("dve", (lambda h=h: V.scalar_tensor_tensor(out=S32[:, h * 256:(h + 1) * 256],
                                                                in0=S32[:, h * 256:(h + 1) * 256],
                                                                scalar=tot[:, h:h + 1],
                                                                in1=B[h // 2][:, (h % 2) * 256:(h % 2) * 256 + 256],
                                                                op0=ALU.mult, op1=ALU.add)),
                     reads=["S32", "tot", "B0", "B1"], writes=["S32"])
            P.op("act", lambda: A.copy(out=S16[:], in_=S32[:]), reads=["S32"], writes=["S16"])

        if npre > 0:
            with contextlib.ExitStack() as st2:
                sb2 = lambda name, shape, dt: st2.enter_context(nc.sbuf_tensor(name, shape, dt))
                wkv = sb2("wkv", [128, KC, 1536], BF16)
                xpc = [sb2("xpc%d" % i, [128, KC * 128], BF16) for i in range(3)]
                vpre = [sb2("vpre%d" % i, [128, 1024], BF16) for i in range(2)]
                esuf = sb2("esuf", [128, 512], F32)
                kdp = sb2("kdp", [128, 512], BF16)
                usuf = sb2("usuf", [128, 128], F32)
                onesf = sb2("onesf", [128, 1], F32)
                B4f = B4[:, :].bitcast(F32)
                P.dma("sp", "c_usuf", lambda: nc.sync.dma_start(out=usuf[:], in_=consts[:, 3, :]), writes=["usuf"])
                P.op("dve", lambda: V.memset(onesf[:], 1.0), writes=["onesf"])
                P.dma("pool", "wkv", [lambda: G.dma_start(out=wkv[:, :, 0:512], in_=w_in[:, :, C_KG:C_KG + 512]),
                                      lambda: G.dma_start(out=wkv[:, :, 512:1536], in_=w_in[:, :, C_VG:C_VG + 1024])],
                      writes=["wkv"])

                def p_load(c):
                    xk = "xpc%d" % (c % 3)
                    P.dma("pool", xk, lambda: G.dma_start(out=xpc[c % 3][:], in_=xpre[c, :, :]), writes=[xk])

                def p_inproj(c):
                    xb = xpc[c % 3][:].rearrange("p (k t) -> p k t", k=KC)
                    xk = "xpc%d" % (c % 3)
                    sl = c % 2
                    bk, bkey = B[sl], "B%d" % sl
                    P.op("pe", [(lambda k=k: T.matmul(bk[:, 0:512], lhsT=xb[:, k, :], rhs=wkv[:, k, 0:512],
                                                      start=(k == 0), stop=(k == KC - 1))) for k in range(KC)],
                         reads=["wkv", xk], writes=[bkey])
                    yield

                    def vgrp(j, vb):
                        P.op("pe", [(lambda k=k: T.matmul(vb[:, 0:512], lhsT=xb[:, k, :],
                                                          rhs=wkv[:, k, 512 + j * 512:1024 + j * 512],
                                                          start=(k == 0), stop=(k == KC - 1))) for k in range(KC)],
                             reads=["wkv", xk], writes=["B%d" % (5 + j)])
                        if j == 0:
                            P.op("act", lambda: A.copy(out=vpre[sl][:, 0:512], in_=B5[:, 0:512]), reads=["B5"], writes=["vpre%d" % sl])
                        else:
                            P.op("dve", lambda: V.tensor_copy(out=vpre[sl][:, 512:1024], in_=B6[:, 0:512]), reads=["B6"],
                                 writes=["vpre%d" % sl])
                    vgrp(0, B5)
                    yield
                    P.op("pe", [(lambda k=k: T.matmul(B7[0:32, 0:128], lhsT=wgk[:, k, :], rhs=xb[:, k, :],
                                                      start=(k == 0), stop=(k == KC - 1))) for k in range(KC)],
                         reads=["wgk", xk], writes=["B7"])
                    P.op("act", lambda: A.copy(out=gkT[0:32, sl * 128:(sl + 1) * 128], in_=B7[0:32, 0:128]),
                         reads=["B7"], writes=["gkT%d" % sl])
                    vgrp(1, B6)
                    yield

                def p_tail(c):
                    sl = c % 2
                    bk, bkey = B[sl], "B%d" % sl
                    P.op("pe", [lambda: T.matmul(B[2][:, 0:512], lhsT=gkT[0:32, sl * 128:(sl + 1) * 128], rhs=waug[0:32, :],
                                                 start=True, stop=False),
                                lambda: T.matmul(B[2][:, 0:512], lhsT=ones32[0:1, :], rhs=bgk[0:1, :], start=False, stop=True)],
                         reads=["gkT%d" % sl, "waug", "ones32", "bgk"], writes=["B2"])
                    P.op("act", lambda: A.activation(out=e1[:], in_=B[2][:, 0:512], func=AF.Exp, scale=-1.0),
                         reads=["B2"], writes=["e1"])
                    P.op("act", lambda: A.activation(out=spl[:], in_=e1[:], func=AF.Ln, bias=1.0, scale=1.0),
                         reads=["e1"], writes=["spl"])
                    yield
                    P.op("pe", lambda: T.matmul(B[3][:, 0:512], lhsT=usuf[:], rhs=spl[:], start=True, stop=True),
                         reads=["usuf", "spl"], writes=["B3"])
                    P.op("pe", [(lambda h=h: T.matmul(B[2][:, h:h + 1], lhsT=spl[:, h * 128:(h + 1) * 128], rhs=onesf[:, 0:1],
                                                      start=True, stop=True)) for h in range(4)],
                         reads=["spl", "onesf"], writes=["B2"])
                    P.op("act", lambda: A.activation(out=esuf[:], in_=B[3][:, 0:512], func=AF.Exp, scale=-1.0 / 16.0),
                         reads=["B3"], writes=["esuf"])
                    P.op("act", lambda: A.activation(out=tot[:], in_=B[2][:, 0:4], func=AF.Exp, scale=-1.0 / 16.0),
                         reads=["B2"], writes=["tot"])
                    P.op("dve", lambda: V.tensor_tensor(out=kdp[:], in0=bk[:, 0:512], in1=esuf[:], op=ALU.mult),
                         reads=[bkey, "esuf"], writes=["kdp"])
                    yield
                    P.op("pe", [(lambda h=h: T.matmul((B4f if h < 2 else B[2])[:, (h % 2) * 256:(h % 2) * 256 + 256],
                                                      lhsT=kdp[:, h * 128:(h + 1) * 128],
                                                      rhs=vpre[sl][:, h * 256:(h + 1) * 256], start=True, stop=True)) for h in range(4)],
                         reads=["kdp", "vpre%d" % sl], writes=["B4", "B2"])
                    for h in range(4):
                        src = (B4f if h < 2 else B[2])[:, (h % 2) * 256:(h % 2) * 256 + 256]
                        P.op("dve", (lambda h=h, src=src: V.scalar_tensor_tensor(out=S32[:, h * 256:(h + 1) * 256],
                                                                                 in0=S32[:, h * 256:(h + 1) * 256],
                                                                                 scalar=tot[:, h:h + 1], in1=src,
                                                                                 op0=ALU.mult, op1=ALU.add)),
                             reads=["S32", "tot", "B4", "B2"], writes=["S32"])
                    yield

                def run_zip(*gens):
                    gens = list(gens)
                    while gens:
                        for gn in list(gens):
                            try:
                                next(gn)
                            except StopIteration:
                                gens.remove(gn)

                p_load(0)
                if npre > 1:
                    p_load(1)
                run_zip(p_inproj(0))
                for c in range(npre):
                    if c + 2 < npre:
                        p_load(c + 2)
                    if c + 1 < npre:
                        run_zip(p_tail(c), p_inproj(c + 1))
                    else:
                        run_zip(p_tail(c))
                P.op("act", lambda: A.copy(out=S16[:], in_=S32[:]), reads=["S32"], writes=["S16"])
                P.barrier()

        xTpb = [sb("xTp%d" % i, [128, KC, TOK], BF16) for i in range(2)]
        xTp = xTpb[0]
        wblk = [sb("wblk%d" % i, [128, KC, 512], BF16) for i in range(2)]
        q32 = sb("q32", [128, TP, 1024], F32)
        k32 = sb("k32", [128, TP, 128], F32)
        qr = sb("qr", [128, TP, 1024], BF16)
        kr = sb("kr", [128, 128], BF16)
        ta = sb("ta", [128, 512], F32)
        tb = sb("tb", [128, 512], F32)
        sza = sb("sza", [128, TP, 1024], BF16)
        szg = sb("szg", [128, TP, 1024], BF16)
        vg = sb("vg", [128, TP, 1024], BF16)
        qgT = sb("qgT", [128, 4, TOK], F32)
        kgT = sb("kgT", [128, 4, TOK], F32)
        kTall = sb("kTall", [64, 2, (nt + 1) * 128], BF16)
        vall = sb("vall", [128, nt + 1, 128], BF16)
        qT = sb("qT", [64, 16, 128], BF16)
        pexp = [sb("pexp%d" % i, [128, 2, 256], BF16) for i in range(2)]
        pT = [sb("pT%d" % i, [128, 2, 2, 128], BF16) for i in range(2)]
        mraw = sb("mraw", [128, 16], F32)
        msc = sb("msc", [128, 16], F32)
        negm = sb("negm", [128, 16], F32)
        dsm = sb("dsm", [128, 16], F32)
        esk = sb("esk", [128, 16], F32)
        den = sb("den", [128, 16], F32)
        rden = sb("rden", [128, 16], F32)
        otmp = sb("otmp", [128, 1024], F32)
        eq = sb("eq", [128, 4, 128], F32)
        ek = sb("ek", [128, 4, 128], F32)
        qdec = sb("qdec", [128, 4, 128], BF16)
        kneg = sb("kneg", [128, 4, 128], BF16)
        atm = sb("atm", [128, 4, 128], BF16)
        sqj = sb("sqj", [128, 256], BF16)
        ss = sb("ss", [128, 4], F32)
        rstd = sb("rstd", [128, 4], F32)
        y16 = sb("y16", [128, TP, D], BF16)
        yT = sb("yT", [128, TP, KC, 128], BF16)
        xres = [sb("xres%d" % i, [128, 512], F32) for i in range(4)]
        r32 = [sb("r32_%d" % i, [128, D], F32) for i in range(TP)]
        lnst = sb("lnst", [128, TP, 8], F32)

        wslot = [0]

        def load_wblk(src, c0):
            bi = WB_IDX[(src, c0)]
            ncols = WBLOCKS[bi][2]
            i = wslot[0] % 2
            wslot[0] += 1
            key = "wblk%d" % i
            P.dma("sp", key, lambda: nc.sync.dma_start(out=wblk[i][:, :, 0:ncols], in_=wsc_view(bi)), reads=["wsc"], writes=[key])
            return wblk[i], key

        accs = [0]

        def next_acc():
            i = accs[0] % 2
            accs[0] += 1
            return B[i], "B%d" % i

        def run(gen):
            for _ in gen:
                pass

        def run_mix(fg, bg, ratio=2):
            fg_done = bg_done = False
            while not (fg_done and bg_done):
                for _ in range(ratio):
                    if fg_done:
                        break
                    try:
                        r = next(fg)
                        if r == "drain":
                            for _ in bg:
                                pass
                            bg_done = True
                    except StopIteration:
                        fg_done = True
                if not bg_done:
                    try:
                        next(bg)
                    except StopIteration:
                        bg_done = True

        def chain(*gens):
            for g_ in gens:
                yield from g_

        def do_rope(E, ename, src, src_key, dst1, dst2, dst_key, nh, slot):
            s4 = src.rearrange("p (h two f) -> p h two f", h=nh, two=2)
            t1, t2 = s4[:, :, 0, :], s4[:, :, 1, :]
            cosb = cst[:, slot, 0, :].unsqueeze(1).to_broadcast([128, nh, 32])
            sinb = cst[:, slot, 1, :].unsqueeze(1).to_broadcast([128, nh, 32])
            ta3 = ta[:, 0:nh * 32].rearrange("p (h f) -> p h f", h=nh)
            tb3 = tb[:, 0:nh * 32].rearrange("p (h f) -> p h f", h=nh)
            P.op(ename, lambda: E.tensor_tensor(out=ta3, in0=t1, in1=cosb, op=ALU.mult), reads=[src_key, "cst"], writes=["ta"])
            P.op(ename, lambda: E.tensor_tensor(out=tb3, in0=t2, in1=sinb, op=ALU.mult), reads=[src_key, "cst"], writes=["tb"])
            P.op(ename, lambda: E.tensor_tensor(out=dst1, in0=ta3, in1=tb3, op=ALU.subtract), reads=["ta", "tb"], writes=[dst_key])
            P.op(ename, lambda: E.tensor_tensor(out=ta3, in0=t2, in1=cosb, op=ALU.mult), reads=[src_key, "cst"], writes=["ta"])
            P.op(ename, lambda: E.tensor_tensor(out=tb3, in0=t1, in1=sinb, op=ALU.mult), reads=[src_key, "cst"], writes=["tb"])
            P.op(ename, lambda: E.tensor_tensor(out=dst2, in0=ta3, in1=tb3, op=ALU.add), reads=["ta", "tb"], writes=[dst_key])

        def k_finish(t_k32, slot):
            kr4 = kr[:].rearrange("p (h two f) -> p h two f", h=2, two=2)
            do_rope(G, "pool", t_k32, "k32", kr4[:, :, 0, :], kr4[:, :, 1, :], "kr", 2, slot)
            P.op("pe", [(lambda kv=kv: T.transpose(B4[0:64, kv * 128:(kv + 1) * 128], kr[:, kv * 64:(kv + 1) * 64], ident[:]))
                        for kv in range(2)], reads=["kr", "ident"], writes=["B4"])
            P.op("dve", lambda: V.tensor_copy(out=kTall[:, :, slot * 128:(slot + 1) * 128],
                                              in_=B4[0:64, 0:256].rearrange("p (a t) -> p a t", a=2)),
                 reads=["B4"], writes=["kTall"])

        def load_x(ps_i):
            tok0 = (1 + ps_i * TP) * 128
            xk = "xTp%d" % (ps_i % 2)
            P.dma("pool", xk, lambda: G.dma_start(out=xTpb[ps_i % 2][:, :, :], in_=xT[:, :, tok0:tok0 + TOK]), writes=[xk])

        wb, wkey = load_wblk("in", C_KA)
        P.dma("pool", "xTp1", lambda: G.dma_start(out=xTpb[1][:, :, 0:128], in_=xT[:, :, 0:128]), writes=["xTp1"])
        acc, akey = next_acc()
        P.op("pe", [(lambda k=k: T.matmul(acc[:, 0:256], lhsT=xTpb[1][:, k, 0:128], rhs=wb[:, k, 0:256],
                                          start=(k == 0), stop=(k == KC - 1))) for k in range(KC)],
             reads=["xTp1", wkey], writes=[akey])
        P.op("act", lambda: A.copy(out=k32[:, 0, :], in_=acc[:, 0:128]), reads=[akey], writes=["k32"])
        P.op("act", lambda: A.copy(out=vall[:, 0, :], in_=acc[:, 128:256]), reads=[akey], writes=["vall"])
        k_finish(k32[:, 0, :], 0)
        load_x(0)

        npass = nt // TP
        pending_stores = []
        for ps_i in range(npass):
            def bg_tm(blocks, pi=None):
                pi = ps_i if pi is None else pi
                xb_, xk_ = xTpb[pi % 2], "xTp%d" % (pi % 2)
                for col0, ncols, handler in blocks:
                    wb, wkey = load_wblk("in", col0)
                    for t in range(TP):
                        acc, akey = next_acc()
                        P.op("pe", [(lambda k=k: T.matmul(acc[:, 0:ncols], lhsT=xb_[:, k, t * 128:(t + 1) * 128],
                                                          rhs=wb[:, k, 0:ncols], start=(k == 0), stop=(k == KC - 1)))
                                    for k in range(KC)], reads=[xk_, wkey], writes=[akey])
                        handler(t, acc, akey)
                        yield

            def bg_fm():
                xb_, xk_ = xTpb[ps_i % 2], "xTp%d" % (ps_i % 2)
                for (col0, dstT, dkey) in ((C_QG, qgT, "qgT"), (C_KG, kgT, "kgT")):
                    wb, wkey = load_wblk("in", col0)
                    for h in range(4):
                        acc, akey = next_acc()
                        P.op("pe", [(lambda k=k: T.matmul(acc[:, 0:TOK], lhsT=wb[:, k, h * 128:(h + 1) * 128], rhs=xb_[:, k, :],
                                                          start=(k == 0), stop=(k == KC - 1))) for k in range(KC)],
                             reads=[xk_, wkey], writes=[akey])
                        P.op("act", lambda: A.copy(out=dstT[:, h, :], in_=acc[:, 0:TOK]), reads=[akey], writes=[dkey])
                        yield
                acc, akey = next_acc()
                P.op("pe", [(lambda k=k: T.matmul(acc[0:32, 0:TOK], lhsT=wgk[:, k, :], rhs=xb_[:, k, :],
                                                  start=(k == 0), stop=(k == KC - 1))) for k in range(KC)],
                     reads=[xk_, "wgk"], writes=[akey])
                P.op("act", lambda: A.copy(out=gkT[0:32, 0:TOK], in_=acc[0:32, 0:TOK]), reads=[akey], writes=["gkT"])
                yield

            def bg_outproj(t):
                gt = ps_i * TP + t
                for cb in range(4):
                    P.dma("sp", "xres%d" % cb, (lambda cb=cb: nc.sync.dma_start(out=xres[cb][:], in_=xtok[gt, :, cb * 512:(cb + 1) * 512])),
                          writes=["xres%d" % cb])
                for cb in range(4):
                    wbo, wkeyo = load_wblk("out", cb * 512)
                    xs, xkey = xres[cb], "xres%d" % cb
                    acc, akey = next_acc()
                    P.op("pe", [(lambda k=k: T.matmul(acc[:, 0:512], lhsT=yT[:, t, k, :], rhs=wbo[:, k, :],
                                                      start=(k == 0), stop=(k == KC - 1))) for k in range(KC)],
                         reads=["yT%d" % t, wkeyo], writes=[akey])
                    P.op("dve", lambda: V.scalar_tensor_tensor(out=r32[t][:, cb * 512:(cb + 1) * 512], in0=xs[:], scalar=ALPHA,
                                                               in1=acc[:, 0:512], op0=ALU.mult, op1=ALU.add,
                                                               accum_out=lnst[:, t, cb:cb + 1]),
                         reads=[xkey, akey, "lnst%d" % t], writes=["r32_%d" % t, "lnst%d" % t])
                    yield

            def h_q(j):
                return lambda t, acc, ak: P.op("act", lambda: A.copy(out=q32[:, t, j * 512:(j + 1) * 512], in_=acc[:, 0:512]),
                                               reads=[ak], writes=["q32_%d" % t])

            def h_kv(t, acc, ak, pi=None):
                g = (ps_i if pi is None else pi) * TP + t + 1
                P.op("act", lambda: A.copy(out=k32[:, t, :], in_=acc[:, 0:128]), reads=[ak], writes=["k32"])
                P.op("act", lambda: A.copy(out=vall[:, g, :], in_=acc[:, 128:256]), reads=[ak], writes=["vall"])

            def h_silu(dst, dkey, j):
                return lambda t, acc, ak: P.op("act", lambda: A.activation(out=dst[:, t, j * 512:(j + 1) * 512], in_=acc[:, 0:512],
                                                                           func=AF.Silu), reads=[ak], writes=[dkey + "%d" % t])

            def h_vg(j):
                return lambda t, acc, ak: P.op("dve", lambda: V.tensor_copy(out=vg[:, t, j * 512:(j + 1) * 512], in_=acc[:, 0:512]),
                                               reads=[ak], writes=["vg%d" % t])

            def attn(t):
                g = ps_i * TP + t + 1
                mi = 0 if (ps_i == 0 and t == 0) else 1
                q4 = qr[:, t, :].rearrange("p (h two f) -> p h two f", h=16, two=2)
                do_rope(G, "pool", q32[:, t, :], "q32_%d" % t, q4[:, :, 0, :], q4[:, :, 1, :], "qr%d" % t, 16, g)
                k_finish(k32[:, t, :], g)
                yield
                for half in range(2):
                    P.op("pe", [(lambda j=j: T.transpose(B4[0:64, j * 128:(j + 1) * 128],
                                                         qr[:, t, (half * 8 + j) * 64:(half * 8 + j + 1) * 64], ident[:]))
                                for j in range(8)], reads=["qr%d" % t, "ident"], writes=["B4"])
                    P.op("dve", lambda: V.tensor_copy(out=qT[:, half * 8:(half + 1) * 8, :].rearrange("p j t -> p (j t)"),
                                                      in_=B4[0:64, 0:1024]), reads=["B4"], writes=["qT"])
                    yield

                def scores(hp):
                    sbk, skey = B[2 + hp % 2], "B%d" % (2 + hp % 2)
                    kv = hp // 4
                    P.op("pe", [(lambda hh=hh: T.matmul(sbk[:, hh * 256:(hh + 1) * 256], lhsT=qT[:, 2 * hp + hh, :],
                                                        rhs=kTall[:, kv, (g - 1) * 128:(g + 1) * 128],
                                                        start=True, stop=True)) for hh in range(2)],
                         reads=["qT", "kTall"], writes=[skey])

                scores(0)
                for hp in range(8):
                    sbk, skey = B[2 + hp % 2], "B%d" % (2 + hp % 2)
                    kv = hp // 4
                    c2 = slice(2 * hp, 2 * hp + 2)
                    P.op("dve", lambda: V.tensor_reduce(out=mraw[:, c2], in_=sbk[:, 0:512].rearrange("p (h n) -> p h n", h=2),
                                                        axis=AX.X, op=ALU.max), reads=[skey], writes=["mraw"])
                    P.op("dve", lambda: V.scalar_tensor_tensor(out=msc[:, c2], in0=mraw[:, c2], scalar=0.125, in1=sinkb[:, c2],
                                                               op0=ALU.mult, op1=ALU.max),
                         reads=["mraw", "sinkb"], writes=["msc"])
                    P.op("dve", lambda: V.tensor_scalar(out=negm[:, c2], in0=msc[:, c2], scalar1=-1.0, scalar2=None, op0=ALU.mult),
                         reads=["msc"], writes=["negm"])
                    pe_, pkey = pexp[hp % 2], "pexp%d" % (hp % 2)
                    for hh in range(2):
                        P.op("act", (lambda hh=hh: A.activation(out=pe_[:, hh, :], in_=sbk[:, hh * 256:(hh + 1) * 256],
                                                                func=AF.Exp, bias=negm[:, 2 * hp + hh:2 * hp + hh + 1], scale=0.125)),
                             reads=[skey, "negm"], writes=[pkey])
                    if hp + 1 < 8:
                        scores(hp + 1)
                    yield
                    P.op("pe", [(lambda hh=hh, kt=kt: T.transpose(B4[:, (hh * 2 + kt) * 128:(hh * 2 + kt + 1) * 128],
                                                                  pe_[:, hh, kt * 128:(kt + 1) * 128], ident[:]))
                                for hh in range(2) for kt in range(2)], reads=[pkey, "ident"], writes=["B4"])
                    pt_, ptkey = pT[hp % 2], "pT%d" % (hp % 2)
                    P.op("dve", lambda: V.tensor_tensor(out=pt_[:], in0=B4[:, 0:512].rearrange("p (h k q) -> p h k q", h=2, k=2),
                                                        in1=mask16[:, mi, :, :].unsqueeze(1).to_broadcast([128, 2, 2, 128]),
                                                        op=ALU.mult), reads=["B4", "mask16"], writes=[ptkey])
                    obk, okey = (B5, "B5") if hp < 4 else (B6, "B6")
                    mm = []
                    for hh in range(2):
                        hl = (2 * hp + hh) % 8
                        for kt in range(2):
                            mm.append(lambda hh=hh, kt=kt, hl=hl: T.matmul(obk[:, hl * 64:(hl + 1) * 64], lhsT=pt_[:, hh, kt, :],
                                                                           rhs=vall[:, g - 1 + kt, kv * 64:(kv + 1) * 64],
                                                                           start=(kt == 0), stop=(kt == 1)))
                        for kt in range(2):
                            mm.append(lambda hh=hh, kt=kt: T.matmul(B7[:, 2 * hp + hh:2 * hp + hh + 1], lhsT=pt_[:, hh, kt, :],
                                                                    rhs=ones16[:, 0:1], start=(kt == 0), stop=(kt == 1)))
                    P.op("pe", mm, reads=[ptkey, "vall", "ones16"], writes=[okey, "B7"])
                    yield
                P.op("dve", lambda: V.tensor_tensor(out=dsm[:], in0=sinkb[:], in1=msc[:], op=ALU.subtract),
                     reads=["sinkb", "msc"], writes=["dsm"])
                P.op("act", lambda: A.activation(out=esk[:], in_=dsm[:], func=AF.Exp), reads=["dsm"], writes=["esk"])
                P.op("dve", lambda: V.tensor_tensor(out=den[:], in0=B7[:, 0:16], in1=esk[:], op=ALU.add),
                     reads=["B7", "esk"], writes=["den"])
                P.op("dve", lambda: V.reciprocal(out=rden[:], in_=den[:]), reads=["den"], writes=["rden"])
                for j, (obk, okey) in enumerate(((B5, "B5"), (B6, "B6"))):
                    P.op("dve", (lambda j=j, obk=obk: V.tensor_tensor(
                        out=otmp[:, j * 512:(j + 1) * 512].rearrange("p (h d) -> p h d", h=8),
                        in0=obk[:, 0:512].rearrange("p (h d) -> p h d", h=8),
                        in1=rden[:, j * 8:(j + 1) * 8].unsqueeze(2).to_broadcast([128, 8, 64]), op=ALU.mult)),
                        reads=[okey, "rden"], writes=["otmp"])
                yield "drain"
                P.op("dve", lambda: V.tensor_tensor(out=y16[:, t, 0:1024], in0=otmp[:], in1=sza[:, t, :], op=ALU.mult),
                     reads=["otmp", "sza%d" % t], writes=["y16_%d" % t])
                yield

            def gla(t):
                ts_ = slice(t * 128, (t + 1) * 128)
                P.op("pe", [lambda: T.matmul(B[2][:, 0:512], lhsT=gkT[0:32, ts_], rhs=waug[0:32, :], start=True, stop=False),
                            lambda: T.matmul(B[2][:, 0:512], lhsT=ones32[0:1, :], rhs=bgk[0:1, :], start=False, stop=True)],
                     reads=["gkT", "waug", "ones32", "bgk"], writes=["B2"])
                yield
                P.op("act", lambda: A.activation(out=e1[:], in_=B[2][:, 0:512], func=AF.Exp, scale=-1.0), reads=["B2"], writes=["e1"])
                P.op("act", lambda: A.activation(out=spl[:], in_=e1[:], func=AF.Ln, bias=1.0, scale=1.0), reads=["e1"], writes=["spl"])
                P.op("pe", [(lambda h=h: T.matmul(B[3][:, h * 128:(h + 1) * 128], lhsT=spl[:, h * 128:(h + 1) * 128],
                                                  rhs=uincl[:], start=True, stop=True)) for h in range(4)],
                     reads=["spl", "uincl"], writes=["B3"])
                yield
                P.op("dve", lambda: V.tensor_scalar(out=nbl[:], in0=B[3][:, 0:512].rearrange("p (h t) -> p h t", h=4)[:, :, 127],
                                                    scalar1=-1.0 / 16.0, scalar2=None, op0=ALU.mult), reads=["B3"], writes=["nbl"])
                P.op("act", lambda: A.activation(out=eq[:].rearrange("p h t -> p (h t)"), in_=B[3][:, 0:512], func=AF.Exp,
                                                 scale=-1.0 / 16.0), reads=["B3"], writes=["eq"])
                P.op("act", lambda: A.activation(out=ek[:].rearrange("p h t -> p (h t)"), in_=B[3][:, 0:512], func=AF.Exp,
                                                 scale=1.0 / 16.0), reads=["B3"], writes=["ek"])
                P.op("act", lambda: A.activation(out=tot[:], in_=nbl[:], func=AF.Exp), reads=["nbl"], writes=["tot"])
                for h in range(4):
                    P.op("act", (lambda h=h: A.activation(out=ekd[:, h, :], in_=B[3][:, h * 128:(h + 1) * 128], func=AF.Exp,
                                                          bias=nbl[:, h:h + 1], scale=1.0 / 16.0)),
                         reads=["B3", "nbl"], writes=["ekd"])
                P.op("dve", lambda: V.scalar_tensor_tensor(out=qdec[:], in0=qgT[:, :, ts_], scalar=128.0 ** -0.5, in1=eq[:],
                                                           op0=ALU.mult, op1=ALU.mult), reads=["qgT", "eq"], writes=["qdec"])
                P.op("dve", lambda: V.tensor_tensor(out=kneg[:], in0=kgT[:, :, ts_], in1=ek[:], op=ALU.mult),
                     reads=["kgT", "ek"], writes=["kneg"])
                P.op("pe", [(lambda h=h: T.matmul(B[2][:, h * 128:(h + 1) * 128], lhsT=kneg[:, h, :], rhs=qdec[:, h, :],
                                                  start=True, stop=True)) for h in range(4)],
                     reads=["kneg", "qdec"], writes=["B2"])
                P.op("dve", lambda: V.tensor_tensor(out=kdT[:], in0=kgT[:, :, ts_], in1=ekd[:], op=ALU.mult),
                     reads=["kgT", "ekd"], writes=["kdT"])
                P.op("pe", [(lambda h=h: T.transpose(B4[:, h * 128:(h + 1) * 128], kdT[:, h, :], ident[:])) for h in range(4)],
                     reads=["kdT", "ident"], writes=["B4"])
                yield
                P.op("dve", lambda: V.tensor_tensor(out=atm[:], in0=B[2][:, 0:512].rearrange("p (h i) -> p h i", h=4),
                                                    in1=uincl16[:].unsqueeze(1).to_broadcast([128, 4, 128]), op=ALU.mult),
                     reads=["B2", "uincl16"], writes=["atm"])
                P.op("act", lambda: A.copy(out=kdec[:].rearrange("p h t -> p (h t)"), in_=B4[:, 0:512]), reads=["B4"], writes=["kdec"])
                mm = []
                for h in range(4):
                    obk = B5 if h < 2 else B6
                    sl = slice((h % 2) * 256, (h % 2) * 256 + 256)
                    mm.append(lambda h=h, obk=obk, sl=sl: T.matmul(obk[:, sl], lhsT=atm[:, h, :], rhs=vg[:, t, h * 256:(h + 1) * 256],
                                                                   start=True, stop=False))
                    mm.append(lambda h=h, obk=obk, sl=sl: T.matmul(obk[:, sl], lhsT=qdec[:, h, :], rhs=S16[:, h * 256:(h + 1) * 256],
                                                                   start=False, stop=True))
                P.op("pe", mm, reads=["atm", "vg%d" % t, "qdec", "S16"], writes=["B5", "B6"])
                P.op("pe", [(lambda h=h: T.matmul((B7 if h < 2 else B[3])[:, (h % 2) * 256:(h % 2) * 256 + 256], lhsT=kdec[:, h, :],
                                                  rhs=vg[:, t, h * 256:(h + 1) * 256], start=True, stop=True)) for h in range(4)],
                     reads=["kdec", "vg%d" % t], writes=["B7", "B3"])
                yield
                P.op("dve", lambda: V.memset(ss[:], 0.0), writes=["ss"])
                for h in range(4):
                    obk, okey = (B5, "B5") if h < 2 else (B6, "B6")
                    sl = slice((h % 2) * 256, (h % 2) * 256 + 256)
                    P.op("act", (lambda h=h, obk=obk, sl=sl: A.activation(out=sqj[:], in_=obk[:, sl], func=AF.Square,
                                                                          accum_out=ss[:, h:h + 1])),
                         reads=[okey, "ss"], writes=["sqj", "ss"])
                P.op("dve", lambda: V.tensor_scalar(out=rstd[:], in0=ss[:], scalar1=1.0 / 256.0, scalar2=RMS_EPS,
                                                    op0=ALU.mult, op1=ALU.add), reads=["ss"], writes=["rstd"])
                P.op("act", lambda: A.activation(out=rstd[:], in_=rstd[:], func=AF.Sqrt), reads=["rstd"], writes=["rstd"])
                P.op("dve", lambda: V.reciprocal(out=rstd[:], in_=rstd[:]), reads=["rstd"], writes=["rstd"])
                for h in range(4):
                    obk, okey = (B5, "B5") if h < 2 else (B6, "B6")
                    sl = slice((h % 2) * 256, (h % 2) * 256 + 256)
                    P.op("dve", (lambda h=h, obk=obk, sl=sl: V.scalar_tensor_tensor(out=otmp[:, h * 256:(h + 1) * 256], in0=obk[:, sl],
                                                                                    scalar=rstd[:, h:h + 1], in1=normwb[:],
                                                                                    op0=ALU.mult, op1=ALU.mult)),
                         reads=[okey, "rstd", "normwb"], writes=["otmp"])
                P.op("dve", lambda: V.tensor_tensor(out=y16[:, t, 1024:2048], in0=otmp[:], in1=szg[:, t, :], op=ALU.mult),
                     reads=["otmp", "szg%d" % t], writes=["y16_%d" % t])
                for h in range(4):
                    src = (B7 if h < 2 else B[3])[:, (h % 2) * 256:(h % 2) * 256 + 256]
                    P.op("dve", (lambda h=h, src=src: V.scalar_tensor_tensor(out=S32[:, h * 256:(h + 1) * 256],
                                                                             in0=S32[:, h * 256:(h + 1) * 256],
                                                                             scalar=tot[:, h:h + 1], in1=src,
                                                                             op0=ALU.mult, op1=ALU.add)),
                         reads=["S32", "tot", "B7", "B3"], writes=["S32"])
                P.op("act", lambda: A.copy(out=S16[:], in_=S32[:]), reads=["S32"], writes=["S16"])
                yield
                for qd in range(2):
                    P.op("pe", [(lambda j=j: T.transpose(B4[:, (j % 8) * 128:(j % 8 + 1) * 128], y16[:, t, j * 128:(j + 1) * 128], ident[:]))
                                for j in range(qd * 8, qd * 8 + 8)], reads=["y16_%d" % t, "ident"], writes=["B4"])
                    P.op("act", (lambda qd=qd: A.copy(out=yT[:, t, qd * 8:(qd + 1) * 8, :].rearrange("p j t -> p (j t)"),
                                                      in_=B4[:, 0:1024])), reads=["B4"], writes=["yT%d" % t])
                    yield

            def layer_norm(t):
                gt = ps_i * TP + t
                rt, rkey, lk = r32[t], "r32_%d" % t, "lnst%d" % t
                L = lambda a, b: lnst[:, t, a:b]
                P.op("dve", lambda: V.tensor_reduce(out=L(4, 5), in_=L(0, 4), axis=AX.X, op=ALU.add), reads=[lk], writes=[lk])
                P.op("dve", lambda: V.tensor_scalar(out=L(5, 6), in0=L(4, 5), scalar1=-1.0 / D, scalar2=None, op0=ALU.mult),
                     reads=[lk], writes=[lk])
                P.op("act", lambda: A.activation(out=y16[:, t, :], in_=rt[:], func=AF.Square, bias=L(5, 6), scale=1.0,
                                                 accum_out=L(6, 7)), reads=[rkey, lk], writes=["y16_%d" % t, lk])
                P.op("dve", lambda: V.tensor_scalar(out=L(7, 8), in0=L(6, 7), scalar1=1.0 / D, scalar2=LN_EPS,
                                                    op0=ALU.mult, op1=ALU.add), reads=[lk], writes=[lk])
                P.op("act", lambda: A.activation(out=L(7, 8), in_=L(7, 8), func=AF.Sqrt), reads=[lk], writes=[lk])
                P.op("dve", lambda: V.reciprocal(out=L(7, 8), in_=L(7, 8)), reads=[lk], writes=[lk])
                P.op("dve", lambda: V.tensor_scalar(out=rt[:], in0=rt[:], scalar1=L(5, 6), scalar2=L(7, 8),
                                                    op0=ALU.add, op1=ALU.mult), reads=[rkey, lk], writes=[rkey])
                P.op("dve", lambda: V.tensor_tensor(out=rt[:], in0=rt[:], in1=lngb[:], op=ALU.mult), reads=[rkey, "lngb"], writes=[rkey])
                P.op("dve", lambda: V.tensor_tensor(out=rt[:], in0=rt[:], in1=lnbb[:], op=ALU.add), reads=[rkey, "lnbb"], writes=[rkey])
                pending_stores.append((t, gt))

            def flush_stores():
                while pending_stores:
                    t_, gt_ = pending_stores.pop(0)
                    P.dma("sp", "out%d" % t_, (lambda t_=t_, gt_=gt_: nc.sync.dma_start(out=out[gt_, :, :], in_=r32[t_][:])),
                          reads=["r32_%d" % t_])

            def s1_blocks(pi):
                return [(C_QA, 512, h_q(0)), (C_QA + 512, 512, h_q(1)),
                        (C_KA, 256, (lambda t, acc, ak, pi=pi: h_kv(t, acc, ak, pi)))]

            if ps_i == 0:
                run(bg_tm(s1_blocks(0), 0))
            if ps_i + 1 < npass:
                load_x(ps_i + 1)
            run_mix(attn(0), bg_tm([(C_ZA, 512, h_silu(sza, "sza", 0)), (C_ZA + 512, 512, h_silu(sza, "sza", 1)),
                                    (C_VG, 512, h_vg(0)), (C_VG + 512, 512, h_vg(1))]), ratio=2)
            flush_stores()
            run_mix(attn(1), chain(bg_tm([(C_ZG, 512, h_silu(szg, "szg", 0)), (C_ZG + 512, 512, h_silu(szg, "szg", 1))]),
                                   bg_fm()), ratio=2)
            for t in range(TP):
                P.op("dve", (lambda t=t: V.memset(lnst[:, t, :], 0.0)), writes=["lnst%d" % t])
            if ps_i + 1 < npass:
                run_mix(gla(0), bg_tm(s1_blocks(ps_i + 1), ps_i + 1), ratio=1)
            else:
                run(gla(0))
            run_mix(gla(1), bg_outproj(0), ratio=1)
            run(bg_outproj(1))
            layer_norm(0)
            layer_norm(1)
        flush_stores()


def _kmajor(a):
    n = a.shape[1]
    return np.ascontiguousarray(a.reshape(KC, 128, n).transpose(1, 0, 2))


def _host_consts():
    ident = np.eye(128, dtype=np.float32)
    uincl = (np.arange(128)[:, None] <= np.arange(128)[None, :]).astype(np.float32)
    ones = np.ones((128, 128), np.float32)
    usuf = (np.arange(128)[:, None] > np.arange(128)[None, :]).astype(np.float32)
    consts = np.ascontiguousarray(np.stack([ident, uincl, ones, usuf], axis=1))
    k = np.arange(128)[:, None, None]
    kt = np.arange(2)[None, :, None]
    q = np.arange(128)[None, None, :]
    kj = kt * 128 + k
    reg = ((kj > q) & (kj <= q + 128)).astype(np.float32)
    first = reg * (kj >= 128)
    return consts, reg, first


def kernel(x, w_in, w_gk_up, b_gk, attn_sinks, gla_norm_w, w_out, ln_g, ln_b, _ncores=NCORES, _npre=NPRE):
    x = np.asarray(x, np.float32)[0]
    w = np.asarray(w_in, np.float32)[0]
    perm = np.concatenate([np.arange(0, 1024), np.arange(1024, 1152), np.arange(1152, 1280), np.arange(1280, 2304),
                           np.arange(3328, 4352), np.arange(4352, 5376), np.arange(2304, 2816), np.arange(2816, 3328),
                           np.arange(5376, 5392)])
    w_l = _kmajor(np.concatenate([w[:, perm], np.zeros((D, 16), np.float32)], axis=1))
    wo_l = _kmajor(np.asarray(w_out, np.float32)[0])
    waug = np.zeros((32, 512), np.float32)
    waug[0:16] = np.asarray(w_gk_up, np.float32)[0]
    waug[16] = np.asarray(b_gk, np.float32)[0]
    consts, mreg, mfirst = _host_consts()
    sinks = np.ascontiguousarray(np.broadcast_to(np.asarray(attn_sinks, np.float32)[0][None, :], (128, 16)))
    normw = np.ascontiguousarray(np.broadcast_to(np.asarray(gla_norm_w, np.float32)[0][None, :], (128, 256)))
    lng = np.ascontiguousarray(np.broadcast_to(np.asarray(ln_g, np.float32)[0][None, :], (128, D)))
    lnb = np.ascontiguousarray(np.broadcast_to(np.asarray(ln_b, np.float32)[0][None, :], (128, D)))
    inv_freq = (1.0 / (10000.0 ** (np.arange(0, 32, dtype=np.float32) * 2.0 / 64.0))).astype(np.float32)
    xTfull = np.ascontiguousarray(x.T)
    in_maps = []
    for c in range(_ncores):
        s0 = c * OWN
        xt = np.zeros((D, OWN + 128), np.float32)
        if c > 0:
            xt[:, 0:128] = xTfull[:, s0 - 128:s0]
        xt[:, 128:] = xTfull[:, s0:s0 + OWN]
        npre_tok = _npre * 128
        xp = np.zeros((D, max(npre_tok, 128)), np.float32)
        if _npre > 0 and s0 > 0:
            xp[:, npre_tok - s0:] = xTfull[:, 0:s0]
        pos = (np.arange(s0 - 128, s0 + OWN)).astype(np.float32)
        ang = pos[:, None] * inv_freq[None, :]
        cs = np.stack([np.cos(ang), np.sin(ang)], axis=1).astype(np.float32)
        cs = np.ascontiguousarray(cs.reshape(NT + 1, 128, 2, 32).transpose(1, 0, 2, 3))
        masks = np.ascontiguousarray(np.stack([mfirst if c == 0 else mreg, mreg], axis=1))
        in_maps.append({
            "xT": _kmajor(xt),
            "xpre": np.ascontiguousarray(_kmajor(xp).reshape(128, KC, max(_npre, 1), 128).transpose(2, 0, 1, 3)
                                         .reshape(max(_npre, 1), 128, KC * 128)),
            "xtok": np.ascontiguousarray(x[s0:s0 + OWN].reshape(NT, 128, D)),
            "w_in": w_l, "w_out": wo_l, "w_aug": waug, "cs": cs, "masks": masks, "consts": consts,
            "sinks": sinks, "normw": normw, "lng": lng, "lnb": lnb,
        })
    nc = build(npre=_npre, nt=NT)
    res = run_bass_kernel_spmd(nc, in_maps, core_ids=list(range(_ncores)))
    outs = [np.asarray(r["out"]).reshape(OWN, D) for r in res.results]
    return np.concatenate(outs, axis=0)[None].astype(np.float32)
```

```python
import contextlib
import math
import numpy as np
import concourse.bass as bass
import concourse.mybir as mybir
from concourse.bass_utils import run_bass_kernel_spmd

F32 = mybir.dt.float32
BF16 = mybir.dt.bfloat16
AF = mybir.ActivationFunctionType
ALU = mybir.AluOpType
AX = mybir.AxisListType

NCORES = 8
D = 2048
SEQ = 16384
OWN = SEQ // NCORES
NT = OWN // 128
NPRE = (SEQ - OWN) // 128
TP = 2
TOK = TP * 128
KC = D // 128
WCOLS = 5408
C_QA, C_KA, C_VA, C_ZA, C_VG, C_ZG, C_QG, C_KG, C_GK = 0, 1024, 1152, 1280, 2304, 3328, 4352, 4864, 5376
ALPHA = 2.0 ** 0.25
WBLOCKS = [("in", C_QA, 512), ("in", C_QA + 512, 512), ("in", C_KA, 256), ("in", C_ZA, 512), ("in", C_ZA + 512, 512),
           ("in", C_VG, 512), ("in", C_VG + 512, 512), ("in", C_ZG, 512), ("in", C_ZG + 512, 512),
           ("in", C_QG, 512), ("in", C_KG, 512),
           ("out", 0, 512), ("out", 512, 512), ("out", 1024, 512), ("out", 1536, 512)]
WB_IDX = {(src, c0): i for i, (src, c0, n) in enumerate(WBLOCKS)}
LN_EPS = 1e-5
RMS_EPS = 1e-5


class Prog:
    def __init__(self, nc, stack):
        self.nc = nc
        self.stack = stack
        self.engs = {"pe": nc.tensor, "act": nc.scalar, "dve": nc.vector, "pool": nc.gpsimd, "sp": nc.sync}
        self.sem = {}
        self.cnt = {}
        self.waited = {}
        self.lastw = {}
        self.readers = {}
        self.ninst = {e: 0 for e in self.engs}
        for e in ("pe", "act", "dve", "pool"):
            self._mksem("E_" + e)

    def _mksem(self, name):
        self.sem[name] = self.stack.enter_context(self.nc.semaphore(name))
        self.cnt[name] = 0

    def _deps(self, reads, writes):
        deps = {}

        def add(tok):
            if tok is None:
                return
            s, v = tok
            if deps.get(s, 0) < v:
                deps[s] = v
        for r in reads:
            add(self.lastw.get(r))
        for w in writes:
            add(self.lastw.get(w))
            for t in self.readers.get(w, ()):
                add(t)
        return deps

    def _wait(self, eng, deps):
        e = self.engs[eng]
        for s, v in deps.items():
            if self.waited.get((eng, s), 0) >= v:
                continue
            if eng == "pe" and s == "E_pe":
                continue
            e.wait_ge(self.sem[s], v)
            self.ninst[eng] += 1
            self.waited[(eng, s)] = v

    def _commit(self, tok, reads, writes):
        for w in writes:
            self.lastw[w] = tok
            self.readers[w] = []
        for r in reads:
            self.readers.setdefault(r, []).append(tok)

    PSUM_KEYS = frozenset("B%d" % i for i in range(8))

    def op(self, eng, fns, reads=(), writes=()):
        if callable(fns):
            fns = [fns]
        writes = list(writes) + [r for r in reads if r in self.PSUM_KEYS and r not in writes]
        self._wait(eng, self._deps(reads, writes))
        ins = None
        for f in fns:
            ins = f()
            self.ninst[eng] += 1
        s = "E_" + eng
        self.cnt[s] += 1
        ins.then_inc(self.sem[s], 1)
        self._commit((s, self.cnt[s]), reads, writes)

    def dma(self, eng, slot, fns, reads=(), writes=()):
        if callable(fns):
            fns = [fns]
        s = "D_" + slot
        if s not in self.sem:
            self._mksem(s)
        self._wait(eng, self._deps(reads, writes))
        for f in fns:
            ins = f()
            self.ninst[eng] += 1
            self.cnt[s] += 16
            ins.then_inc(self.sem[s], 16)
        self._commit((s, self.cnt[s]), reads, writes)

    def barrier(self):
        for eng in ("pe", "act", "dve", "pool", "sp"):
            e = self.engs[eng]
            for s, v in self.cnt.items():
                if v > 0 and self.waited.get((eng, s), 0) < v:
                    e.wait_ge(self.sem[s], v)
                    self.waited[(eng, s)] = v

    def finish(self, eng="sp"):
        e = self.engs[eng]
        for s, v in self.cnt.items():
            if v > 0 and self.waited.get((eng, s), 0) < v:
                e.wait_ge(self.sem[s], v)
                self.waited[(eng, s)] = v


class _Stop(Exception):
    pass


STAGE = 99
SKIPK = False
DBG = 0


def _stage(n):
    if STAGE < n:
        raise _Stop()


def build(npre=NPRE, nt=NT):
    nc = bass.Bass("TRN2", target_bir_lowering=False)
    din = lambda name, shape: nc.dram_tensor(name, shape, F32, kind="ExternalInput").ap()
    xT = din("xT", [128, KC, (nt + 1) * 128])
    xpre = din("xpre", [max(npre, 1), 128, KC * 128])
    xtok = din("xtok", [nt, 128, D])
    w_in = din("w_in", [128, KC, WCOLS])
    w_out = din("w_out", [128, KC, D])
    w_aug = din("w_aug", [32, 512])
    cs = din("cs", [128, nt + 1, 2, 32])
    masks = din("masks", [128, 2, 2, 128])
    consts = din("consts", [128, 4, 128])
    sinks = din("sinks", [128, 16])
    normw = din("normw", [128, 256])
    lng = din("lng", [128, D])
    lnb = din("lnb", [128, D])
    out = nc.dram_tensor("out", [nt, 128, D], F32, kind="ExternalOutput").ap()
    wsc = nc.dram_tensor("wsc", [len(WBLOCKS), 128, KC * 512], BF16, kind="Internal").ap()

    with contextlib.ExitStack() as st:
        P = Prog(nc, st)
        try:
            _build_body(nc, st, P, npre, nt, locals())
        except _Stop:
            pass
        P.finish("sp")
    return nc


def _build_body(nc, st, P, npre, nt, env):
    xT, xpre, xtok, w_in, w_out, w_aug, cs, masks, consts, sinks, normw, lng, lnb, out, wsc = [env[k] for k in ('xT', 'xpre', 'xtok', 'w_in', 'w_out', 'w_aug', 'cs', 'masks', 'consts', 'sinks', 'normw', 'lng', 'lnb', 'out', 'wsc')]
    if True:
        sb = lambda name, shape, dt: st.enter_context(nc.sbuf_tensor(name, shape, dt))
        psb = lambda name, dt=F32: st.enter_context(nc.psum_tensor(name, [128, 512 if dt == F32 else 1024], dt))
        V, A, T, G = nc.vector, nc.scalar, nc.tensor, nc.gpsimd

        B = [psb("B%d" % i) for i in range(4)]
        B4 = psb("B4", BF16)
        B5, B6, B7 = psb("B5"), psb("B6"), psb("B7")

        ident = sb("ident", [128, 128], BF16)
        uincl = sb("uincl", [128, 128], F32)
        uincl16 = sb("uincl16", [128, 128], BF16)
        ones16 = sb("ones16", [128, 128], BF16)
        mask16 = sb("mask16", [128, 2, 2, 128], BF16)
        sinkb = sb("sinkb", [128, 16], F32)
        normwb = sb("normwb", [128, 256], F32)
        lngb = sb("lngb", [128, D], F32)
        lnbb = sb("lnbb", [128, D], F32)
        cst = sb("cst", [128, nt + 1, 2, 32], F32)
        waug = sb("waug", [32, 512], F32)
        S32 = sb("S32", [128, 1024], F32)
        S16 = sb("S16", [128, 1024], BF16)
        gkT = sb("gkT", [128, TOK], F32)
        e1 = sb("e1", [128, 512], F32)
        spl = sb("spl", [128, 512], F32)
        nbl = sb("nbl", [128, 4], F32)
        tot = sb("tot", [128, 4], F32)
        ekd = sb("ekd", [128, 4, 128], F32)
        kdT = sb("kdT", [128, 4, 128], BF16)
        kdec = sb("kdec", [128, 4, 128], BF16)
        wgk = sb("wgk", [128, KC, 32], BF16)

        P.dma("pool", "c_ident", lambda: G.dma_start(out=ident[:], in_=consts[:, 0, :]), writes=["ident"])
        identf = sb("identf", [128, 128], F32)
        P.dma("sp", "c_identf", lambda: nc.sync.dma_start(out=identf[:], in_=consts[:, 0, :]), writes=["identf"])
        P.dma("sp", "c_uincl", lambda: nc.sync.dma_start(out=uincl[:], in_=consts[:, 1, :]), writes=["uincl"])
        P.dma("pool", "c_uincl16", lambda: G.dma_start(out=uincl16[:], in_=consts[:, 1, :]), writes=["uincl16"])
        P.dma("pool", "c_ones", lambda: G.dma_start(out=ones16[:], in_=consts[:, 2, :]), writes=["ones16"])
        P.dma("pool", "c_mask", lambda: G.dma_start(out=mask16[:], in_=masks[:, :, :, :]), writes=["mask16"])
        P.dma("sp", "c_sink", lambda: nc.sync.dma_start(out=sinkb[:], in_=sinks[:, :]), writes=["sinkb"])
        P.dma("sp", "c_normw", lambda: nc.sync.dma_start(out=normwb[:], in_=normw[:, :]), writes=["normwb"])
        P.dma("sp", "c_lng", lambda: nc.sync.dma_start(out=lngb[:], in_=lng[:, :]), writes=["lngb"])
        P.dma("sp", "c_lnb", lambda: nc.sync.dma_start(out=lnbb[:], in_=lnb[:, :]), writes=["lnbb"])
        P.dma("sp", "c_cs", lambda: nc.sync.dma_start(out=cst[:], in_=cs[:, :, :, :]), writes=["cst"])
        P.op("dve", lambda: V.memset(waug[:], 0.0), writes=["waug"])
        P.dma("sp", "c_waug", lambda: nc.sync.dma_start(out=waug[0:16, :], in_=w_aug[0:16, :]), writes=["waug"])
        P.dma("pool", "c_wgk", lambda: G.dma_start(out=wgk[:], in_=w_in[:, :, C_GK:C_GK + 32]), writes=["wgk"])
        P.op("dve", lambda: V.memset(S32[:], 0.0), writes=["S32"])
        P.op("dve", lambda: V.memset(S16[:], 0.0), writes=["S16"])
        ones32 = sb("ones32", [1, 128], F32)
        bgk = sb("bgk", [1, 512], F32)
        P.op("dve", lambda: V.memset(ones32[:], 1.0), writes=["ones32"])
        P.dma("sp", "c_bgk", lambda: nc.sync.dma_start(out=bgk[:], in_=w_aug[16:17, :]), writes=["bgk"])

        def wsc_view(bi):
            n = WBLOCKS[bi][2]
            return wsc[bi, :, 0:KC * n].rearrange("p (k c) -> p k c", k=KC)

        for bi, (src, c0, n) in enumerate(WBLOCKS):
            srcap = (w_in if src == "in" else w_out)[:, :, c0:c0 + n]
            P.dma("pool", "wcast", (lambda bi=bi, srcap=srcap: G.dma_start(out=wsc_view(bi), in_=srcap)), writes=["wsc"])
        _stage(1)
        def gla_decay(gk_lhsT, tag):
            P.op("pe", [lambda: T.matmul(B[2][:, 0:512], lhsT=gk_lhsT, rhs=waug[0:32, :], start=True, stop=False),
                        lambda: T.matmul(B[2][:, 0:512], lhsT=ones32[0:1, :], rhs=bgk[0:1, :], start=False, stop=True)],
                 reads=["gkT", "waug", "ones32", "bgk"], writes=["B2"])
            P.op("act", lambda: A.activation(out=e1[:], in_=B[2][:, 0:512], func=AF.Exp, scale=-1.0),
                 reads=["B2"], writes=["e1"])
            P.op("act", lambda: A.activation(out=spl[:], in_=e1[:], func=AF.Ln, bias=1.0, scale=1.0),
                 reads=["e1"], writes=["spl"])
            P.op("pe", [(lambda h=h: T.matmul(B[3][:, h * 128:(h + 1) * 128], lhsT=spl[:, h * 128:(h + 1) * 128],
                                              rhs=uincl[:], start=True, stop=True)) for h in range(4)],
                 reads=["spl", "uincl"], writes=["B3"])
            P.op("dve", lambda: V.tensor_scalar(out=nbl[:], in0=B[3][:, 0:512].rearrange("p (h t) -> p h t", h=4)[:, :, 127],
                                                scalar1=-1.0 / 16.0, scalar2=None, op0=ALU.mult),
                 reads=["B3"], writes=["nbl"])
            P.op("act", lambda: A.activation(out=tot[:], in_=nbl[:], func=AF.Exp), reads=["nbl"], writes=["tot"])
            for h in range(4):
                P.op("act", (lambda h=h: A.activation(out=ekd[:, h, :], in_=B[3][:, h * 128:(h + 1) * 128], func=AF.Exp,
                                                      bias=nbl[:, h:h + 1], scale=1.0 / 16.0)),
                     reads=["B3", "nbl"], writes=["ekd"])

        def gla_state_update(kT_src, kT_key, v_rhs, v_key):
            for h in range(4):
                P.op("dve", (lambda h=h: V.tensor_tensor(out=kdT[:, h, :], in0=kT_src(h), in1=ekd[:, h, :], op=ALU.mult)),
                     reads=[kT_key, "ekd"], writes=["kdT"])
            P.op("pe", [(lambda h=h: T.transpose(B4[:, h * 128:(h + 1) * 128], kdT[:, h, :], ident[:])) for h in range(4)],
                 reads=["kdT", "ident"], writes=["B4"])
            P.op("act", lambda: A.copy(out=kdec[:].rearrange("p h t -> p (h t)"), in_=B4[:, 0:512]),
                 reads=["B4"], writes=["kdec"])
            P.op("pe", [(lambda h=h: T.matmul(B[h // 2][:, (h % 2) * 256:(h % 2) * 256 + 256], lhsT=kdec[:, h, :],
                                              rhs=v_rhs(h), start=True, stop=True)) for h in range(4)],
                 reads=["kdec", v_key], writes=["B0", "B1"])
            for h in range(4):
                P.op("dve", (lambda h=h: V.scalar_tensor_tensor(out=S32[:, h * 256:(h + 1) * 256],
                                                                in0=S32[:, h * 256:(h + 1) * 256],
                                                                scalar=tot[:, h:h + 1],
                                                                in1=B[h // 2][:, (h % 2) * 256:(h % 2) * 256 + 256],
                                                                op0=ALU.mult, op1=ALU.add)),
                     reads=["S32", "tot", "B0", "B1"], writes=["S32"])
            P.op("act", lambda: A.copy(out=S16[:], in_=S32[:]), reads=["S32"], writes=["S16"])

        if npre > 0:
            with contextlib.ExitStack() as st2:
                sb2 = lambda name, shape, dt: st2.enter_context(nc.sbuf_tensor(name, shape, dt))
                wkv = sb2("wkv", [128, KC, 1536], BF16)
                xpc = [sb2("xpc%d" % i, [128, KC * 128], BF16) for i in range(3)]
                vpre = [sb2("vpre%d" % i, [128, 1024], BF16) for i in range(2)]
                esuf = sb2("esuf", [128, 512], F32)
                kdp = sb2("kdp", [128, 512], BF16)
                usuf = sb2("usuf", [128, 128], F32)
                onesf = sb2("onesf", [128, 1], F32)
                B4f = B4[:, :].bitcast(F32)
                P.dma("sp", "c_usuf", lambda: nc.sync.dma_start(out=usuf[:], in_=consts[:, 3, :]), writes=["usuf"])
                P.op("dve", lambda: V.memset(onesf[:], 1.0), writes=["onesf"])
                P.dma("pool", "wkv", [lambda: G.dma_start(out=wkv[:, :, 0:512], in_=w_in[:, :, C_KG:C_KG + 512]),
                                      lambda: G.dma_start(out=wkv[:, :, 512:1536], in_=w_in[:, :, C_VG:C_VG + 1024])],
                      writes=["wkv"])

                def p_load(c):
                    xk = "xpc%d" % (c % 3)
                    P.dma("pool", xk, lambda: G.dma_start(out=xpc[c % 3][:], in_=xpre[c, :, :]), writes=[xk])

                def p_inproj(c):
                    xb = xpc[c % 3][:].rearrange("p (k t) -> p k t", k=KC)
                    xk = "xpc%d" % (c % 3)
                    sl = c % 2
                    bk, bkey = B[sl], "B%d" % sl
                    P.op("pe", [(lambda k=k: T.matmul(bk[:, 0:512], lhsT=xb[:, k, :], rhs=wkv[:, k, 0:512],
                                                      start=(k == 0), stop=(k == KC - 1))) for k in range(KC)],
                         reads=["wkv", xk], writes=[bkey])
                    yield

                    def vgrp(j, vb):
                        P.op("pe", [(lambda k=k: T.matmul(vb[:, 0:512], lhsT=xb[:, k, :],
                                                          rhs=wkv[:, k, 512 + j * 512:1024 + j * 512],
                                                          start=(k == 0), stop=(k == KC - 1))) for k in range(KC)],
                             reads=["wkv", xk], writes=["B%d" % (5 + j)])
                        if j == 0:
                            P.op("act", lambda: A.copy(out=vpre[sl][:, 0:512], in_=B5[:, 0:512]), reads=["B5"], writes=["vpre%d" % sl])
                        else:
                            P.op("dve", lambda: V.tensor_copy(out=vpre[sl][:, 512:1024], in_=B6[:, 0:512]), reads=["B6"],
                                 writes=["vpre%d" % sl])
                    vgrp(0, B5)
                    yield
                    P.op("pe", [(lambda k=k: T.matmul(B7[0:32, 0:128], lhsT=wgk[:, k, :], rhs=xb[:, k, :],
                                                      start=(k == 0), stop=(k == KC - 1))) for k in range(KC)],
                         reads=["wgk", xk], writes=["B7"])
                    P.op("act", lambda: A.copy(out=gkT[0:32, sl * 128:(sl + 1) * 128], in_=B7[0:32, 0:128]),
                         reads=["B7"], writes=["gkT%d" % sl])
                    vgrp(1, B6)
                    yield

                def p_tail(c):
                    sl = c % 2
                    bk, bkey = B[sl], "B%d" % sl
                    P.op("pe", [lambda: T.matmul(B[2][:, 0:512], lhsT=gkT[0:32, sl * 128:(sl + 1) * 128], rhs=waug[0:32, :],
                                                 start=True, stop=False),
                                lambda: T.matmul(B[2][:, 0:512], lhsT=ones32[0:1, :], rhs=bgk[0:1, :], start=False, stop=True)],
                         reads=["gkT%d" % sl, "waug", "ones32", "bgk"], writes=["B2"])
                    P.op("act", lambda: A.activation(out=e1[:], in_=B[2][:, 0:512], func=AF.Exp, scale=-1.0),
                         reads=["B2"], writes=["e1"])
                    P.op("act", lambda: A.activation(out=spl[:], in_=e1[:], func=AF.Ln, bias=1.0, scale=1.0),
                         reads=["e1"], writes=["spl"])
                    yield
                    P.op("pe", lambda: T.matmul(B[3][:, 0:512], lhsT=usuf[:], rhs=spl[:], start=True, stop=True),
                         reads=["usuf", "spl"], writes=["B3"])
                    P.op("pe", [(lambda h=h: T.matmul(B[2][:, h:h + 1], lhsT=spl[:, h * 128:(h + 1) * 128], rhs=onesf[:, 0:1],
                                                      start=True, stop=True)) for h in range(4)],
                         reads=["spl", "onesf"], writes=["B2"])
                    P.op("act", lambda: A.activation(out=esuf[:], in_=B[3][:, 0:512], func=AF.Exp, scale=-1.0 / 16.0),
                         reads=["B3"], writes=["esuf"])
                    P.op("act", lambda: A.activation(out=tot[:], in_=B[2][:, 0:4], func=AF.Exp, scale=-1.0 / 16.0),
                         reads=["B2"], writes=["tot"])
                    P.op("dve", lambda: V.tensor_tensor(out=kdp[:], in0=bk[:, 0:512], in1=esuf[:], op=ALU.mult),
                         reads=[bkey, "esuf"], writes=["kdp"])
                    yield
                    P.op("pe", [(lambda h=h: T.matmul((B4f if h < 2 else B[2])[:, (h % 2) * 256:(h % 2) * 256 + 256],
                                                      lhsT=kdp[:, h * 128:(h + 1) * 128],
                                                      rhs=vpre[sl][:, h * 256:(h + 1) * 256], start=True, stop=True)) for h in range(4)],
                         reads=["kdp", "vpre%d" % sl], writes=["B4", "B2"])
                    for h in range(4):
                        src = (B4f if h < 2 else B[2])[:, (h % 2) * 256:(h % 2) * 256 + 256]
                        P.op("dve", (lambda h=h, src=src: V.scalar_tensor_tensor(out=S32[:, h * 256:(h + 1) * 256],
                                                                                 in0=S32[:, h * 256:(h + 1) * 256],
                                                                                 scalar=tot[:, h:h + 1], in1=src,
                                                                                 op0=ALU.mult, op1=ALU.add)),
                             reads=["S32", "tot", "B4", "B2"], writes=["S32"])
                    yield

                def run_zip(*gens):
                    gens = list(gens)
                    while gens:
                        for gn in list(gens):
                            try:
                                next(gn)
                            except StopIteration:
                                gens.remove(gn)

                p_load(0)
                if npre > 1:
                    p_load(1)
                run_zip(p_inproj(0))
                for c in range(npre):
                    if c + 2 < npre:
                        p_load(c + 2)
                    if c + 1 < npre:
                        run_zip(p_tail(c), p_inproj(c + 1))
                    else:
                        run_zip(p_tail(c))
                P.op("act", lambda: A.copy(out=S16[:], in_=S32[:]), reads=["S32"], writes=["S16"])
                P.barrier()

        xTpb = [sb("xTp%d" % i, [128, KC, TOK], BF16) for i in range(2)]
        xTp = xTpb[0]
        wblk = [sb("wblk%d" % i, [128, KC, 512], BF16) for i in range(2)]
        q32 = sb("q32", [128, TP, 1024], F32)
        k32 = sb("k32", [128, TP, 128], F32)
        qr = sb("qr", [128, TP, 1024], BF16)
        krb = [sb("kr%d" % i, [128, 128], BF16) for i in range(2)]
        ta = sb("ta", [128, 512], F32)
        tb = sb("tb", [128, 512], F32)
        sza = sb("sza", [128, TP, 1024], BF16)
        szg = sb("szg", [128, TP, 1024], BF16)
        vg = sb("vg", [128, TP, 1024], BF16)
        qgT = sb("qgT", [128, 4, TOK], F32)
        kgT = sb("kgT", [128, 4, TOK], F32)
        kTall = sb("kTall", [64, 2, (nt + 1) * 128], BF16)
        vall = sb("vall", [128, nt + 1, 128], BF16)
        qT = sb("qT", [64, 16, 128], BF16)
        pexp = [sb("pexp%d" % i, [128, 2, 256], BF16) for i in range(2)]
        pT = [sb("pT%d" % i, [128, 2, 2, 128], BF16) for i in range(2)]
        mraw = sb("mraw", [128, 16], F32)
        msc = sb("msc", [128, 16], F32)
        negm = sb("negm", [128, 16], F32)
        dsm = sb("dsm", [128, 16], F32)
        esk = sb("esk", [128, 16], F32)
        den = sb("den", [128, 16], F32)
        rden = sb("rden", [128, 16], F32)
        otmp = sb("otmp", [128, 1024], F32)
        eq = sb("eq", [128, 4, 128], F32)
        ek = sb("ek", [128, 4, 128], F32)
        qdec = sb("qdec", [128, 4, 128], BF16)
        kneg = sb("kneg", [128, 4, 128], BF16)
        atm = sb("atm", [128, 4, 128], BF16)
        sqj = sb("sqj", [128, 256], BF16)
        ss = sb("ss", [128, 4], F32)
        rstd = sb("rstd", [128, 4], F32)
        y16 = sb("y16", [128, TP, D], BF16)
        yT = sb("yT", [128, TP, KC, 128], BF16)
        xres = [sb("xres%d" % i, [128, 512], F32) for i in range(4)]
        r32 = [sb("r32_%d" % i, [128, D], F32) for i in range(TP)]
        lnst = sb("lnst", [128, TP, 8], F32)

        wslot = [0]

        def load_wblk(src, c0):
            bi = WB_IDX[(src, c0)]
            ncols = WBLOCKS[bi][2]
            i = wslot[0] % 2
            wslot[0] += 1
            key = "wblk%d" % i
            P.dma("sp", key, lambda: nc.sync.dma_start(out=wblk[i][:, :, 0:ncols], in_=wsc_view(bi)), reads=["wsc"], writes=[key])
            return wblk[i], key

        accs = [0]

        def next_acc():
            i = accs[0] % 2
            accs[0] += 1
            return B[i], "B%d" % i

        def run(gen):
            for _ in gen:
                pass

        def run_mix(fg, bg, ratio=2, start_after=0):
            fg_done = bg_done = False
            for _ in range(start_after):
                try:
                    next(fg)
                except StopIteration:
                    fg_done = True
                    break
            while not (fg_done and bg_done):
                for _ in range(ratio):
                    if fg_done:
                        break
                    try:
                        r = next(fg)
                        if r == "drain":
                            for _ in bg:
                                pass
                            bg_done = True
                    except StopIteration:
                        fg_done = True
                if not bg_done:
                    try:
                        next(bg)
                    except StopIteration:
                        bg_done = True

        def chain(*gens):
            for g_ in gens:
                yield from g_

        def do_rope(E, ename, src, src_key, dst1, dst2, dst_key, nh, slot):
            s4 = src.rearrange("p (h two f) -> p h two f", h=nh, two=2)
            t1, t2 = s4[:, :, 0, :], s4[:, :, 1, :]
            cosb = cst[:, slot, 0, :].unsqueeze(1).to_broadcast([128, nh, 32])
            sinb = cst[:, slot, 1, :].unsqueeze(1).to_broadcast([128, nh, 32])
            ta3 = ta[:, 0:nh * 32].rearrange("p (h f) -> p h f", h=nh)
            tb3 = tb[:, 0:nh * 32].rearrange("p (h f) -> p h f", h=nh)
            P.op(ename, lambda: E.tensor_tensor(out=ta3, in0=t1, in1=cosb, op=ALU.mult), reads=[src_key, "cst"], writes=["ta"])
            P.op(ename, lambda: E.tensor_tensor(out=tb3, in0=t2, in1=sinb, op=ALU.mult), reads=[src_key, "cst"], writes=["tb"])
            P.op(ename, lambda: E.tensor_tensor(out=dst1, in0=ta3, in1=tb3, op=ALU.subtract), reads=["ta", "tb"], writes=[dst_key])
            P.op(ename, lambda: E.tensor_tensor(out=ta3, in0=t2, in1=cosb, op=ALU.mult), reads=[src_key, "cst"], writes=["ta"])
            P.op(ename, lambda: E.tensor_tensor(out=tb3, in0=t1, in1=sinb, op=ALU.mult), reads=[src_key, "cst"], writes=["tb"])
            P.op(ename, lambda: E.tensor_tensor(out=dst2, in0=ta3, in1=tb3, op=ALU.add), reads=["ta", "tb"], writes=[dst_key])

        def k_rope(t_k32, slot, ti):
            kr4 = krb[ti][:].rearrange("p (h two f) -> p h two f", h=2, two=2)
            do_rope(G, "pool", t_k32, "k32", kr4[:, :, 0, :], kr4[:, :, 1, :], "kr%d" % ti, 2, slot)

        def k_transposes(slot, ti):
            kr = krb[ti]
            P.op("pe", [(lambda kv=kv: T.transpose(B4[0:64, kv * 128:(kv + 1) * 128], kr[:, kv * 64:(kv + 1) * 64], ident[:]))
                        for kv in range(2)], reads=["kr%d" % ti, "ident"], writes=["B4"])
            P.op("dve", lambda: V.tensor_copy(out=kTall[:, :, slot * 128:(slot + 1) * 128],
                                              in_=B4[0:64, 0:256].rearrange("p (a t) -> p a t", a=2)),
                 reads=["B4"], writes=["kTall"])

        def load_x(ps_i):
            tok0 = (1 + ps_i * TP) * 128
            xk = "xTp%d" % (ps_i % 2)
            P.dma("pool", xk, lambda: G.dma_start(out=xTpb[ps_i % 2][:, :, :], in_=xT[:, :, tok0:tok0 + TOK]), writes=[xk])

        wb, wkey = load_wblk("in", C_KA)
        P.dma("pool", "xTp1", lambda: G.dma_start(out=xTpb[1][:, :, 0:128], in_=xT[:, :, 0:128]), writes=["xTp1"])
        acc, akey = next_acc()
        P.op("pe", [(lambda k=k: T.matmul(acc[:, 0:256], lhsT=xTpb[1][:, k, 0:128], rhs=wb[:, k, 0:256],
                                          start=(k == 0), stop=(k == KC - 1))) for k in range(KC)],
             reads=["xTp1", wkey], writes=[akey])
        P.op("act", lambda: A.copy(out=k32[:, 0, :], in_=acc[:, 0:128]), reads=[akey], writes=["k32"])
        P.op("act", lambda: A.copy(out=vall[:, 0, :], in_=acc[:, 128:256]), reads=[akey], writes=["vall"])
        k_rope(k32[:, 0, :], 0, 0)
        k_transposes(0, 0)
        load_x(0)

        npass = nt // TP
        pending_stores = []
        deferred_ln = []
        for ps_i in range(npass):
            def bg_tm(blocks, pi=None):
                pi = ps_i if pi is None else pi
                xb_, xk_ = xTpb[pi % 2], "xTp%d" % (pi % 2)
                for col0, ncols, handler in blocks:
                    wb, wkey = load_wblk("in", col0)
                    for t in range(TP):
                        acc, akey = next_acc()
                        P.op("pe", [(lambda k=k: T.matmul(acc[:, 0:ncols], lhsT=xb_[:, k, t * 128:(t + 1) * 128],
                                                          rhs=wb[:, k, 0:ncols], start=(k == 0), stop=(k == KC - 1)))
                                    for k in range(KC)], reads=[xk_, wkey], writes=[akey])
                        handler(t, acc, akey)
                        yield

            def bg_fm():
                xb_, xk_ = xTpb[ps_i % 2], "xTp%d" % (ps_i % 2)
                for (col0, dstT, dkey) in ((C_QG, qgT, "qgT"), (C_KG, kgT, "kgT")):
                    wb, wkey = load_wblk("in", col0)
                    for h in range(4):
                        acc, akey = next_acc()
                        P.op("pe", [(lambda k=k: T.matmul(acc[:, 0:TOK], lhsT=wb[:, k, h * 128:(h + 1) * 128], rhs=xb_[:, k, :],
                                                          start=(k == 0), stop=(k == KC - 1))) for k in range(KC)],
                             reads=[xk_, wkey], writes=[akey])
                        P.op("act", lambda: A.copy(out=dstT[:, h, :], in_=acc[:, 0:TOK]), reads=[akey], writes=[dkey])
                        yield
                acc, akey = next_acc()
                P.op("pe", [(lambda k=k: T.matmul(acc[0:32, 0:TOK], lhsT=wgk[:, k, :], rhs=xb_[:, k, :],
                                                  start=(k == 0), stop=(k == KC - 1))) for k in range(KC)],
                     reads=[xk_, "wgk"], writes=[akey])
                P.op("act", lambda: A.copy(out=gkT[0:32, 0:TOK], in_=acc[0:32, 0:TOK]), reads=[akey], writes=["gkT"])
                yield

            def bg_outproj(pre):
                for cb in range(4):
                    wbo, wkeyo = pre[cb] if cb < len(pre) else load_wblk("out", cb * 512)
                    for t in range(TP):
                        gt = ps_i * TP + t
                        xi = (cb * TP + t) % 4
                        xs, xkey = xres[xi], "xres%d" % xi
                        P.dma("sp", xkey, lambda: nc.sync.dma_start(out=xs[:], in_=xtok[gt, :, cb * 512:(cb + 1) * 512]), writes=[xkey])
                        acc, akey = next_acc()
                        P.op("pe", [(lambda k=k: T.matmul(acc[:, 0:512], lhsT=yT[:, t, k, :], rhs=wbo[:, k, :],
                                                          start=(k == 0), stop=(k == KC - 1))) for k in range(KC)],
                             reads=["yT%d" % t, wkeyo], writes=[akey])
                        P.op("dve", lambda: V.scalar_tensor_tensor(out=r32[t][:, cb * 512:(cb + 1) * 512], in0=xs[:], scalar=ALPHA,
                                                                   in1=acc[:, 0:512], op0=ALU.mult, op1=ALU.add,
                                                                   accum_out=lnst[:, t, cb:cb + 1]),
                             reads=[xkey, akey, "lnst%d" % t], writes=["r32_%d" % t, "lnst%d" % t])
                        yield

            def h_q(j, pi=None):
                def h(t, acc, ak):
                    P.op("act", lambda: A.copy(out=q32[:, t, j * 512:(j + 1) * 512], in_=acc[:, 0:512]),
                         reads=[ak], writes=["q32_%d" % t])
                    if j == 1:
                        g = (ps_i if pi is None else pi) * TP + t + 1
                        q4 = qr[:, t, :].rearrange("p (h two f) -> p h two f", h=16, two=2)
                        do_rope(G, "pool", q32[:, t, :], "q32_%d" % t, q4[:, :, 0, :], q4[:, :, 1, :], "qr%d" % t, 16, g)
                return h

            def h_kv(t, acc, ak, pi=None):
                g = (ps_i if pi is None else pi) * TP + t + 1
                P.op("act", lambda: A.copy(out=k32[:, t, :], in_=acc[:, 0:128]), reads=[ak], writes=["k32"])
                P.op("act", lambda: A.copy(out=vall[:, g, :], in_=acc[:, 128:256]), reads=[ak], writes=["vall"])
                k_rope(k32[:, t, :], g, t)

            def h_silu(dst, dkey, j):
                return lambda t, acc, ak: P.op("act", lambda: A.activation(out=dst[:, t, j * 512:(j + 1) * 512], in_=acc[:, 0:512],
                                                                           func=AF.Silu), reads=[ak], writes=[dkey + "%d" % t])

            def h_vg(j):
                return lambda t, acc, ak: P.op("dve", lambda: V.tensor_copy(out=vg[:, t, j * 512:(j + 1) * 512], in_=acc[:, 0:512]),
                                               reads=[ak], writes=["vg%d" % t])

            def attn(t):
                g = ps_i * TP + t + 1
                mi = 0 if (ps_i == 0 and t == 0) else 1
                k_transposes(g, t)
                yield
                for half in range(2):
                    P.op("pe", [(lambda j=j: T.transpose(B4[0:64, j * 128:(j + 1) * 128],
                                                         qr[:, t, (half * 8 + j) * 64:(half * 8 + j + 1) * 64], ident[:]))
                                for j in range(8)], reads=["qr%d" % t, "ident"], writes=["B4"])
                    P.op("dve", lambda: V.tensor_copy(out=qT[:, half * 8:(half + 1) * 8, :].rearrange("p j t -> p (j t)"),
                                                      in_=B4[0:64, 0:1024]), reads=["B4"], writes=["qT"])
                    yield

                def scores(hp):
                    sbk, skey = B[2 + hp % 2], "B%d" % (2 + hp % 2)
                    kv = hp // 4
                    P.op("pe", [(lambda hh=hh: T.matmul(sbk[:, hh * 256:(hh + 1) * 256], lhsT=qT[:, 2 * hp + hh, :],
                                                        rhs=kTall[:, kv, (g - 1) * 128:(g + 1) * 128],
                                                        start=True, stop=True)) for hh in range(2)],
                         reads=["qT", "kTall"], writes=[skey])

                scores(0)
                for hp in range(8):
                    sbk, skey = B[2 + hp % 2], "B%d" % (2 + hp % 2)
                    kv = hp // 4
                    c2 = slice(2 * hp, 2 * hp + 2)
                    P.op("dve", lambda: V.tensor_reduce(out=mraw[:, c2], in_=sbk[:, 0:512].rearrange("p (h n) -> p h n", h=2),
                                                        axis=AX.X, op=ALU.max), reads=[skey], writes=["mraw"])
                    P.op("dve", lambda: V.scalar_tensor_tensor(out=msc[:, c2], in0=mraw[:, c2], scalar=0.125, in1=sinkb[:, c2],
                                                               op0=ALU.mult, op1=ALU.max),
                         reads=["mraw", "sinkb"], writes=["msc"])
                    P.op("dve", lambda: V.tensor_scalar(out=negm[:, c2], in0=msc[:, c2], scalar1=-1.0, scalar2=None, op0=ALU.mult),
                         reads=["msc"], writes=["negm"])
                    pe_, pkey = pexp[hp % 2], "pexp%d" % (hp % 2)
                    for hh in range(2):
                        P.op("act", (lambda hh=hh: A.activation(out=pe_[:, hh, :], in_=sbk[:, hh * 256:(hh + 1) * 256],
                                                                func=AF.Exp, bias=negm[:, 2 * hp + hh:2 * hp + hh + 1], scale=0.125)),
                             reads=[skey, "negm"], writes=[pkey])
                    if hp + 1 < 8:
                        scores(hp + 1)
                    yield
                    P.op("pe", [(lambda hh=hh, kt=kt: T.transpose(B4[:, (hh * 2 + kt) * 128:(hh * 2 + kt + 1) * 128],
                                                                  pe_[:, hh, kt * 128:(kt + 1) * 128], ident[:]))
                                for hh in range(2) for kt in range(2)], reads=[pkey, "ident"], writes=["B4"])
                    pt_, ptkey = pT[hp % 2], "pT%d" % (hp % 2)
                    P.op("dve", lambda: V.tensor_tensor(out=pt_[:], in0=B4[:, 0:512].rearrange("p (h k q) -> p h k q", h=2, k=2),
                                                        in1=mask16[:, mi, :, :].unsqueeze(1).to_broadcast([128, 2, 2, 128]),
                                                        op=ALU.mult), reads=["B4", "mask16"], writes=[ptkey])
                    obk, okey = (B5, "B5") if hp < 4 else (B6, "B6")
                    mm = []
                    for hh in range(2):
                        hl = (2 * hp + hh) % 8
                        for kt in range(2):
                            mm.append(lambda hh=hh, kt=kt, hl=hl: T.matmul(obk[:, hl * 64:(hl + 1) * 64], lhsT=pt_[:, hh, kt, :],
                                                                           rhs=vall[:, g - 1 + kt, kv * 64:(kv + 1) * 64],
                                                                           start=(kt == 0), stop=(kt == 1)))
                        for kt in range(2):
                            mm.append(lambda hh=hh, kt=kt: T.matmul(B7[:, 2 * hp + hh:2 * hp + hh + 1], lhsT=pt_[:, hh, kt, :],
                                                                    rhs=ones16[:, 0:1], start=(kt == 0), stop=(kt == 1)))
                    P.op("pe", mm, reads=[ptkey, "vall", "ones16"], writes=[okey, "B7"])
                    yield
                P.op("dve", lambda: V.tensor_tensor(out=dsm[:], in0=sinkb[:], in1=msc[:], op=ALU.subtract),
                     reads=["sinkb", "msc"], writes=["dsm"])
                P.op("act", lambda: A.activation(out=esk[:], in_=dsm[:], func=AF.Exp), reads=["dsm"], writes=["esk"])
                P.op("dve", lambda: V.tensor_tensor(out=den[:], in0=B7[:, 0:16], in1=esk[:], op=ALU.add),
                     reads=["B7", "esk"], writes=["den"])
                P.op("dve", lambda: V.reciprocal(out=rden[:], in_=den[:]), reads=["den"], writes=["rden"])
                for j, (obk, okey) in enumerate(((B5, "B5"), (B6, "B6"))):
                    P.op("dve", (lambda j=j, obk=obk: V.tensor_tensor(
                        out=otmp[:, j * 512:(j + 1) * 512].rearrange("p (h d) -> p h d", h=8),
                        in0=obk[:, 0:512].rearrange("p (h d) -> p h d", h=8),
                        in1=rden[:, j * 8:(j + 1) * 8].unsqueeze(2).to_broadcast([128, 8, 64]), op=ALU.mult)),
                        reads=[okey, "rden"], writes=["otmp"])
                yield "drain"
                P.op("dve", lambda: V.tensor_tensor(out=y16[:, t, 0:1024], in0=otmp[:], in1=sza[:, t, :], op=ALU.mult),
                     reads=["otmp", "sza%d" % t], writes=["y16_%d" % t])
                yield

            def gla(t):
                ts_ = slice(t * 128, (t + 1) * 128)
                P.op("pe", [lambda: T.matmul(B[2][:, 0:512], lhsT=gkT[0:32, ts_], rhs=waug[0:32, :], start=True, stop=False),
                            lambda: T.matmul(B[2][:, 0:512], lhsT=ones32[0:1, :], rhs=bgk[0:1, :], start=False, stop=True)],
                     reads=["gkT", "waug", "ones32", "bgk"], writes=["B2"])
                yield
                P.op("act", lambda: A.activation(out=e1[:], in_=B[2][:, 0:512], func=AF.Exp, scale=-1.0), reads=["B2"], writes=["e1"])
                P.op("act", lambda: A.activation(out=spl[:], in_=e1[:], func=AF.Ln, bias=1.0, scale=1.0), reads=["e1"], writes=["spl"])
                P.op("pe", [(lambda h=h: T.matmul(B[3][:, h * 128:(h + 1) * 128], lhsT=spl[:, h * 128:(h + 1) * 128],
                                                  rhs=uincl[:], start=True, stop=True)) for h in range(4)],
                     reads=["spl", "uincl"], writes=["B3"])
                yield
                P.op("dve", lambda: V.tensor_scalar(out=nbl[:], in0=B[3][:, 0:512].rearrange("p (h t) -> p h t", h=4)[:, :, 127],
                                                    scalar1=-1.0 / 16.0, scalar2=None, op0=ALU.mult), reads=["B3"], writes=["nbl"])
                P.op("act", lambda: A.activation(out=eq[:].rearrange("p h t -> p (h t)"), in_=B[3][:, 0:512], func=AF.Exp,
                                                 scale=-1.0 / 16.0), reads=["B3"], writes=["eq"])
                P.op("act", lambda: A.activation(out=ek[:].rearrange("p h t -> p (h t)"), in_=B[3][:, 0:512], func=AF.Exp,
                                                 scale=1.0 / 16.0), reads=["B3"], writes=["ek"])
                P.op("act", lambda: A.activation(out=tot[:], in_=nbl[:], func=AF.Exp), reads=["nbl"], writes=["tot"])
                for h in range(4):
                    P.op("act", (lambda h=h: A.activation(out=ekd[:, h, :], in_=B[3][:, h * 128:(h + 1) * 128], func=AF.Exp,
                                                          bias=nbl[:, h:h + 1], scale=1.0 / 16.0)),
                         reads=["B3", "nbl"], writes=["ekd"])
                P.op("dve", lambda: V.scalar_tensor_tensor(out=qdec[:], in0=qgT[:, :, ts_], scalar=128.0 ** -0.5, in1=eq[:],
                                                           op0=ALU.mult, op1=ALU.mult), reads=["qgT", "eq"], writes=["qdec"])
                P.op("dve", lambda: V.tensor_tensor(out=kneg[:], in0=kgT[:, :, ts_], in1=ek[:], op=ALU.mult),
                     reads=["kgT", "ek"], writes=["kneg"])
                P.op("pe", [(lambda h=h: T.matmul(B[2][:, h * 128:(h + 1) * 128], lhsT=kneg[:, h, :], rhs=qdec[:, h, :],
                                                  start=True, stop=True)) for h in range(4)],
                     reads=["kneg", "qdec"], writes=["B2"])
                P.op("dve", lambda: V.tensor_tensor(out=kdT[:], in0=kgT[:, :, ts_], in1=ekd[:], op=ALU.mult),
                     reads=["kgT", "ekd"], writes=["kdT"])
                P.op("pe", [(lambda h=h: T.transpose(B4[:, h * 128:(h + 1) * 128], kdT[:, h, :], ident[:])) for h in range(4)],
                     reads=["kdT", "ident"], writes=["B4"])
                yield
                P.op("dve", lambda: V.tensor_tensor(out=atm[:], in0=B[2][:, 0:512].rearrange("p (h i) -> p h i", h=4),
                                                    in1=uincl16[:].unsqueeze(1).to_broadcast([128, 4, 128]), op=ALU.mult),
                     reads=["B2", "uincl16"], writes=["atm"])
                P.op("act", lambda: A.copy(out=kdec[:].rearrange("p h t -> p (h t)"), in_=B4[:, 0:512]), reads=["B4"], writes=["kdec"])
                mm = []
                for h in range(4):
                    obk = B5 if h < 2 else B6
                    sl = slice((h % 2) * 256, (h % 2) * 256 + 256)
                    mm.append(lambda h=h, obk=obk, sl=sl: T.matmul(obk[:, sl], lhsT=atm[:, h, :], rhs=vg[:, t, h * 256:(h + 1) * 256],
                                                                   start=True, stop=False))
                    mm.append(lambda h=h, obk=obk, sl=sl: T.matmul(obk[:, sl], lhsT=qdec[:, h, :], rhs=S16[:, h * 256:(h + 1) * 256],
                                                                   start=False, stop=True))
                P.op("pe", mm, reads=["atm", "vg%d" % t, "qdec", "S16"], writes=["B5", "B6"])
                P.op("pe", [(lambda h=h: T.matmul((B7 if h < 2 else B[3])[:, (h % 2) * 256:(h % 2) * 256 + 256], lhsT=kdec[:, h, :],
                                                  rhs=vg[:, t, h * 256:(h + 1) * 256], start=True, stop=True)) for h in range(4)],
                     reads=["kdec", "vg%d" % t], writes=["B7", "B3"])
                yield
                P.op("dve", lambda: V.memset(ss[:], 0.0), writes=["ss"])
                for h in range(4):
                    obk, okey = (B5, "B5") if h < 2 else (B6, "B6")
                    sl = slice((h % 2) * 256, (h % 2) * 256 + 256)
                    P.op("act", (lambda h=h, obk=obk, sl=sl: A.activation(out=sqj[:], in_=obk[:, sl], func=AF.Square,
                                                                          accum_out=ss[:, h:h + 1])),
                         reads=[okey, "ss"], writes=["sqj", "ss"])
                P.op("dve", lambda: V.tensor_scalar(out=rstd[:], in0=ss[:], scalar1=1.0 / 256.0, scalar2=RMS_EPS,
                                                    op0=ALU.mult, op1=ALU.add), reads=["ss"], writes=["rstd"])
                P.op("act", lambda: A.activation(out=rstd[:], in_=rstd[:], func=AF.Sqrt), reads=["rstd"], writes=["rstd"])
                P.op("dve", lambda: V.reciprocal(out=rstd[:], in_=rstd[:]), reads=["rstd"], writes=["rstd"])
                for h in range(4):
                    obk, okey = (B5, "B5") if h < 2 else (B6, "B6")
                    sl = slice((h % 2) * 256, (h % 2) * 256 + 256)
                    P.op("dve", (lambda h=h, obk=obk, sl=sl: V.scalar_tensor_tensor(out=otmp[:, h * 256:(h + 1) * 256], in0=obk[:, sl],
                                                                                    scalar=rstd[:, h:h + 1], in1=normwb[:],
                                                                                    op0=ALU.mult, op1=ALU.mult)),
                         reads=[okey, "rstd", "normwb"], writes=["otmp"])
                P.op("dve", lambda: V.tensor_tensor(out=y16[:, t, 1024:2048], in0=otmp[:], in1=szg[:, t, :], op=ALU.mult),
                     reads=["otmp", "szg%d" % t], writes=["y16_%d" % t])
                for h in range(4):
                    src = (B7 if h < 2 else B[3])[:, (h % 2) * 256:(h % 2) * 256 + 256]
                    P.op("dve", (lambda h=h, src=src: V.scalar_tensor_tensor(out=S32[:, h * 256:(h + 1) * 256],
                                                                             in0=S32[:, h * 256:(h + 1) * 256],
                                                                             scalar=tot[:, h:h + 1], in1=src,
                                                                             op0=ALU.mult, op1=ALU.add)),
                         reads=["S32", "tot", "B7", "B3"], writes=["S32"])
                P.op("act", lambda: A.copy(out=S16[:], in_=S32[:]), reads=["S32"], writes=["S16"])
                yield
                for qd in range(2):
                    P.op("pe", [(lambda j=j: T.transpose(B4[:, (j % 8) * 128:(j % 8 + 1) * 128], y16[:, t, j * 128:(j + 1) * 128], ident[:]))
                                for j in range(qd * 8, qd * 8 + 8)], reads=["y16_%d" % t, "ident"], writes=["B4"])
                    P.op("act", (lambda qd=qd: A.copy(out=yT[:, t, qd * 8:(qd + 1) * 8, :].rearrange("p j t -> p (j t)"),
                                                      in_=B4[:, 0:1024])), reads=["B4"], writes=["yT%d" % t])
                    yield

            def ln_gen(t, gt):
                rt, rkey, lk = r32[t], "r32_%d" % t, "lnst%d" % t
                L = lambda a, b: lnst[:, t, a:b]
                P.op("dve", lambda: V.tensor_reduce(out=L(4, 5), in_=L(0, 4), axis=AX.X, op=ALU.add), reads=[lk], writes=[lk])
                P.op("dve", lambda: V.tensor_scalar(out=L(5, 6), in0=L(4, 5), scalar1=-1.0 / D, scalar2=None, op0=ALU.mult),
                     reads=[lk], writes=[lk])
                P.op("act", lambda: A.activation(out=y16[:, t, :], in_=rt[:], func=AF.Square, bias=L(5, 6), scale=1.0,
                                                 accum_out=L(6, 7)), reads=[rkey, lk], writes=["y16_%d" % t, lk])
                yield
                P.op("dve", lambda: V.tensor_scalar(out=L(7, 8), in0=L(6, 7), scalar1=1.0 / D, scalar2=LN_EPS,
                                                    op0=ALU.mult, op1=ALU.add), reads=[lk], writes=[lk])
                P.op("act", lambda: A.activation(out=L(7, 8), in_=L(7, 8), func=AF.Sqrt), reads=[lk], writes=[lk])
                P.op("dve", lambda: V.reciprocal(out=L(7, 8), in_=L(7, 8)), reads=[lk], writes=[lk])
                P.op("dve", lambda: V.scalar_tensor_tensor(out=rt[:], in0=rt[:], scalar=L(5, 6), in1=lngb[:],
                                                           op0=ALU.add, op1=ALU.mult), reads=[rkey, lk, "lngb"], writes=[rkey])
                yield
                P.op("dve", lambda: V.scalar_tensor_tensor(out=rt[:], in0=rt[:], scalar=L(7, 8), in1=lnbb[:],
                                                           op0=ALU.mult, op1=ALU.add), reads=[rkey, lk, "lnbb"], writes=[rkey])
                pending_stores.append((t, gt))
                yield

            def flush_stores():
                while pending_stores:
                    t_, gt_ = pending_stores.pop(0)
                    P.dma("sp", "out%d" % t_, (lambda t_=t_, gt_=gt_: nc.sync.dma_start(out=out[gt_, :, :], in_=r32[t_][:])),
                          reads=["r32_%d" % t_])

            def s1_blocks(pi):
                return [(C_QA, 512, h_q(0, pi)), (C_QA + 512, 512, h_q(1, pi)),
                        (C_KA, 256, (lambda t, acc, ak, pi=pi: h_kv(t, acc, ak, pi)))]

            if ps_i == 0:
                run(bg_tm(s1_blocks(0), 0))
            if ps_i + 1 < npass:
                load_x(ps_i + 1)
            run_mix(attn(0), chain(deferred_ln.pop(0) if deferred_ln else iter(()),
                                   bg_tm([(C_ZA, 512, h_silu(sza, "sza", 0)), (C_ZA + 512, 512, h_silu(sza, "sza", 1)),
                                          (C_VG, 512, h_vg(0)), (C_VG + 512, 512, h_vg(1))])), ratio=2)
            flush_stores()
            run_mix(attn(1), chain(deferred_ln.pop(0) if deferred_ln else iter(()),
                                   bg_tm([(C_ZG, 512, h_silu(szg, "szg", 0)), (C_ZG + 512, 512, h_silu(szg, "szg", 1))]),
                                   bg_fm()), ratio=2)
            flush_stores()
            for t in range(TP):
                P.op("dve", (lambda t=t: V.memset(lnst[:, t, :], 0.0)), writes=["lnst%d" % t])
            if ps_i + 1 < npass:
                run_mix(gla(0), bg_tm(s1_blocks(ps_i + 1), ps_i + 1), ratio=1)
            else:
                run(gla(0))
            pre = [load_wblk("out", 0), load_wblk("out", 512)]
            run(gla(1))
            run(bg_outproj(pre))
            deferred_ln.append(ln_gen(0, ps_i * TP + 0))
            deferred_ln.append(ln_gen(1, ps_i * TP + 1))
        while deferred_ln:
            run(deferred_ln.pop(0))
        flush_stores()


def _kmajor(a):
    n = a.shape[1]
    return np.ascontiguousarray(a.reshape(KC, 128, n).transpose(1, 0, 2))


def _host_consts():
    ident = np.eye(128, dtype=np.float32)
    uincl = (np.arange(128)[:, None] <= np.arange(128)[None, :]).astype(np.float32)
    ones = np.ones((128, 128), np.float32)
    usuf = (np.arange(128)[:, None] > np.arange(128)[None, :]).astype(np.float32)
    consts = np.ascontiguousarray(np.stack([ident, uincl, ones, usuf], axis=1))
    k = np.arange(128)[:, None, None]
    kt = np.arange(2)[None, :, None]
    q = np.arange(128)[None, None, :]
    kj = kt * 128 + k
    reg = ((kj > q) & (kj <= q + 128)).astype(np.float32)
    first = reg * (kj >= 128)
    return consts, reg, first


def kernel(x, w_in, w_gk_up, b_gk, attn_sinks, gla_norm_w, w_out, ln_g, ln_b, _ncores=NCORES, _npre=NPRE):
    x = np.asarray(x, np.float32)[0]
    w = np.asarray(w_in, np.float32)[0]
    perm = np.concatenate([np.arange(0, 1024), np.arange(1024, 1152), np.arange(1152, 1280), np.arange(1280, 2304),
                           np.arange(3328, 4352), np.arange(4352, 5376), np.arange(2304, 2816), np.arange(2816, 3328),
                           np.arange(5376, 5392)])
    w_l = _kmajor(np.concatenate([w[:, perm], np.zeros((D, 16), np.float32)], axis=1))
    wo_l = _kmajor(np.asarray(w_out, np.float32)[0])
    waug = np.zeros((32, 512), np.float32)
    waug[0:16] = np.asarray(w_gk_up, np.float32)[0]
    waug[16] = np.asarray(b_gk, np.float32)[0]
    consts, mreg, mfirst = _host_consts()
    sinks = np.ascontiguousarray(np.broadcast_to(np.asarray(attn_sinks, np.float32)[0][None, :], (128, 16)))
    normw = np.ascontiguousarray(np.broadcast_to(np.asarray(gla_norm_w, np.float32)[0][None, :], (128, 256)))
    lng = np.ascontiguousarray(np.broadcast_to(np.asarray(ln_g, np.float32)[0][None, :], (128, D)))
    lnb = np.ascontiguousarray(np.broadcast_to(np.asarray(ln_b, np.float32)[0][None, :], (128, D)))
    inv_freq = (1.0 / (10000.0 ** (np.arange(0, 32, dtype=np.float32) * 2.0 / 64.0))).astype(np.float32)
    xTfull = np.ascontiguousarray(x.T)
    in_maps = []
    for c in range(_ncores):
        s0 = c * OWN
        xt = np.zeros((D, OWN + 128), np.float32)
        if c > 0:
            xt[:, 0:128] = xTfull[:, s0 - 128:s0]
        xt[:, 128:] = xTfull[:, s0:s0 + OWN]
        npre_tok = _npre * 128
        xp = np.zeros((D, max(npre_tok, 128)), np.float32)
        if _npre > 0 and s0 > 0:
            xp[:, npre_tok - s0:] = xTfull[:, 0:s0]
        pos = (np.arange(s0 - 128, s0 + OWN)).astype(np.float32)
        ang = pos[:, None] * inv_freq[None, :]
        cs = np.stack([np.cos(ang), np.sin(ang)], axis=1).astype(np.float32)
        cs = np.ascontiguousarray(cs.reshape(NT + 1, 128, 2, 32).transpose(1, 0, 2, 3))
        masks = np.ascontiguousarray(np.stack([mfirst if c == 0 else mreg, mreg], axis=1))
        in_maps.append({
            "xT": _kmajor(xt),
            "xpre": np.ascontiguousarray(_kmajor(xp).reshape(128, KC, max(_npre, 1), 128).transpose(2, 0, 1, 3)
                                         .reshape(max(_npre, 1), 128, KC * 128)),
            "xtok": np.ascontiguousarray(x[s0:s0 + OWN].reshape(NT, 128, D)),
            "w_in": w_l, "w_out": wo_l, "w_aug": waug, "cs": cs, "masks": masks, "consts": consts,
            "sinks": sinks, "normw": normw, "lng": lng, "lnb": lnb,
        })
    nc = build(npre=_npre, nt=NT)
    res = run_bass_kernel_spmd(nc, in_maps, core_ids=list(range(_ncores)))
    outs = [np.asarray(r["out"]).reshape(OWN, D) for r in res.results]
    return np.concatenate(outs, axis=0)[None].astype(np.float32)
```

```python
import contextlib
import math
import numpy as np
import concourse.bass as bass
import concourse.mybir as mybir
from concourse.bass_utils import run_bass_kernel_spmd

F32 = mybir.dt.float32
BF16 = mybir.dt.bfloat16
AF = mybir.ActivationFunctionType
ALU = mybir.AluOpType
AX = mybir.AxisListType

NCORES = 8
D = 2048
SEQ = 16384
OWN = SEQ // NCORES
NT = OWN // 128
NPRE = (SEQ - OWN) // 128
TP = 2
TOK = TP * 128
KC = D // 128
WCOLS = 5408
C_QA, C_KA, C_VA, C_ZA, C_VG, C_ZG, C_QG, C_KG, C_GK = 0, 1024, 1152, 1280, 2304, 3328, 4352, 4864, 5376
ALPHA = 2.0 ** 0.25
WBLOCKS = [("in", C_QA, 512), ("in", C_QA + 512, 512), ("in", C_KA, 256), ("in", C_ZA, 512), ("in", C_ZA + 512, 512),
           ("in", C_VG, 512), ("in", C_VG + 512, 512), ("in", C_ZG, 512), ("in", C_ZG + 512, 512),
           ("in", C_QG, 512), ("in", C_KG, 512),
           ("out", 0, 512), ("out", 512, 512), ("out", 1024, 512), ("out", 1536, 512)]
WB_IDX = {(src, c0): i for i, (src, c0, n) in enumerate(WBLOCKS)}
LN_EPS = 1e-5
RMS_EPS = 1e-5


class Prog:
    def __init__(self, nc, stack):
        self.nc = nc
        self.stack = stack
        self.engs = {"pe": nc.tensor, "act": nc.scalar, "dve": nc.vector, "pool": nc.gpsimd, "sp": nc.sync}
        self.sem = {}
        self.cnt = {}
        self.waited = {}
        self.lastw = {}
        self.readers = {}
        self.ninst = {e: 0 for e in self.engs}
        for e in ("pe", "act", "dve", "pool"):
            self._mksem("E_" + e)

    def _mksem(self, name):
        self.sem[name] = self.stack.enter_context(self.nc.semaphore(name))
        self.cnt[name] = 0

    def _deps(self, reads, writes):
        deps = {}

        def add(tok):
            if tok is None:
                return
            s, v = tok
            if deps.get(s, 0) < v:
                deps[s] = v
        for r in reads:
            add(self.lastw.get(r))
        for w in writes:
            add(self.lastw.get(w))
            for t in self.readers.get(w, ()):
                add(t)
        return deps

    def _wait(self, eng, deps):
        e = self.engs[eng]
        for s, v in deps.items():
            if self.waited.get((eng, s), 0) >= v:
                continue
            if eng == "pe" and s == "E_pe":
                continue
            e.wait_ge(self.sem[s], v)
            self.ninst[eng] += 1
            self.waited[(eng, s)] = v

    def _commit(self, tok, reads, writes):
        for w in writes:
            self.lastw[w] = tok
            self.readers[w] = []
        for r in reads:
            self.readers.setdefault(r, []).append(tok)

    PSUM_KEYS = frozenset("B%d" % i for i in range(8))

    def op(self, eng, fns, reads=(), writes=()):
        if callable(fns):
            fns = [fns]
        writes = list(writes) + [r for r in reads if r in self.PSUM_KEYS and r not in writes]
        self._wait(eng, self._deps(reads, writes))
        ins = None
        for f in fns:
            ins = f()
            self.ninst[eng] += 1
        s = "E_" + eng
        self.cnt[s] += 1
        ins.then_inc(self.sem[s], 1)
        self._commit((s, self.cnt[s]), reads, writes)

    def dma(self, eng, slot, fns, reads=(), writes=()):
        if callable(fns):
            fns = [fns]
        s = "D_" + slot
        if s not in self.sem:
            self._mksem(s)
        self._wait(eng, self._deps(reads, writes))
        for f in fns:
            ins = f()
            self.ninst[eng] += 1
            self.cnt[s] += 16
            ins.then_inc(self.sem[s], 16)
        self._commit((s, self.cnt[s]), reads, writes)

    def barrier(self):
        for eng in ("pe", "act", "dve", "pool", "sp"):
            e = self.engs[eng]
            for s, v in self.cnt.items():
                if v > 0 and self.waited.get((eng, s), 0) < v:
                    e.wait_ge(self.sem[s], v)
                    self.waited[(eng, s)] = v

    def finish(self, eng="sp"):
        e = self.engs[eng]
        for s, v in self.cnt.items():
            if v > 0 and self.waited.get((eng, s), 0) < v:
                e.wait_ge(self.sem[s], v)
                self.waited[(eng, s)] = v


class _Stop(Exception):
    pass


STAGE = 99
SKIPK = False
DBG = 0


def _stage(n):
    if STAGE < n:
        raise _Stop()


def build(npre=NPRE, nt=NT):
    nc = bass.Bass("TRN2", target_bir_lowering=False)
    din = lambda name, shape: nc.dram_tensor(name, shape, F32, kind="ExternalInput").ap()
    xT = din("xT", [128, KC, (nt + 1) * 128])
    xpre = din("xpre", [max(npre, 1), 128, KC * 128])
    xtok = din("xtok", [nt, 128, D])
    xptok = din("xptok", [max(npre, 1), 128, D])
    w_in = din("w_in", [128, KC, WCOLS])
    w_out = din("w_out", [128, KC, D])
    w_aug = din("w_aug", [32, 512])
    cs = din("cs", [128, nt + 1, 2, 32])
    masks = din("masks", [128, 2, 2, 128])
    consts = din("consts", [128, 4, 128])
    sinks = din("sinks", [128, 16])
    normw = din("normw", [128, 256])
    lng = din("lng", [128, D])
    lnb = din("lnb", [128, D])
    out = nc.dram_tensor("out", [nt, 128, D], F32, kind="ExternalOutput").ap()
    wsc = nc.dram_tensor("wsc", [len(WBLOCKS), 128, KC * 512], BF16, kind="Internal").ap()

    with contextlib.ExitStack() as st:
        P = Prog(nc, st)
        try:
            _build_body(nc, st, P, npre, nt, locals())
        except _Stop:
            pass
        P.finish("sp")
    return nc


def _build_body(nc, st, P, npre, nt, env):
    xT, xpre, xtok, w_in, w_out, w_aug, cs, masks, consts, sinks, normw, lng, lnb, out, wsc, xptok = [env[k] for k in ('xT', 'xpre', 'xtok', 'w_in', 'w_out', 'w_aug', 'cs', 'masks', 'consts', 'sinks', 'normw', 'lng', 'lnb', 'out', 'wsc', 'xptok')]
    if True:
        sb = lambda name, shape, dt: st.enter_context(nc.sbuf_tensor(name, shape, dt))
        psb = lambda name, dt=F32: st.enter_context(nc.psum_tensor(name, [128, 512 if dt == F32 else 1024], dt))
        V, A, T, G = nc.vector, nc.scalar, nc.tensor, nc.gpsimd

        B = [psb("B%d" % i) for i in range(4)]
        BZ = st.enter_context(nc.psum_tensor("BZ", [128, 2048], F32))
        B4 = BZ[:, 0:512].bitcast(BF16)
        B5, B6, B7 = BZ[:, 512:1024], BZ[:, 1024:1536], BZ[:, 1536:2048]

        ident = sb("ident", [128, 128], BF16)
        uincl = sb("uincl", [128, 128], F32)
        uincl16 = sb("uincl16", [128, 128], BF16)
        ones16 = sb("ones16", [128, 128], BF16)
        mask16 = sb("mask16", [128, 2, 2, 128], BF16)
        sinkb = sb("sinkb", [128, 16], F32)
        normwb = sb("normwb", [128, 256], F32)
        lngb = sb("lngb", [128, D], F32)
        lnbb = sb("lnbb", [128, D], F32)
        cst = sb("cst", [128, nt + 1, 2, 32], F32)
        waug = sb("waug", [32, 512], F32)
        S32 = sb("S32", [128, 1024], F32)
        S16 = sb("S16", [128, 1024], BF16)
        gkT = sb("gkT", [128, TOK], F32)
        e1 = sb("e1", [128, 512], F32)
        spl = sb("spl", [128, 512], F32)
        nbl = sb("nbl", [128, 4], F32)
        tot = sb("tot", [128, 4], F32)
        ekd = sb("ekd", [128, 4, 128], F32)
        kdT = sb("kdT", [128, 4, 128], BF16)
        kdec = sb("kdec", [128, 4, 128], BF16)
        wgk = sb("wgk", [128, KC, 32], BF16)

        P.dma("pool", "c_ident", lambda: G.dma_start(out=ident[:], in_=consts[:, 0, :]), writes=["ident"])
        identf = sb("identf", [128, 128], F32)
        P.dma("sp", "c_identf", lambda: nc.sync.dma_start(out=identf[:], in_=consts[:, 0, :]), writes=["identf"])
        P.dma("sp", "c_uincl", lambda: nc.sync.dma_start(out=uincl[:], in_=consts[:, 1, :]), writes=["uincl"])
        P.dma("pool", "c_uincl16", lambda: G.dma_start(out=uincl16[:], in_=consts[:, 1, :]), writes=["uincl16"])
        P.dma("pool", "c_ones", lambda: G.dma_start(out=ones16[:], in_=consts[:, 2, :]), writes=["ones16"])
        P.dma("pool", "c_mask", lambda: G.dma_start(out=mask16[:], in_=masks[:, :, :, :]), writes=["mask16"])
        P.dma("sp", "c_sink", lambda: nc.sync.dma_start(out=sinkb[:], in_=sinks[:, :]), writes=["sinkb"])
        P.dma("sp", "c_normw", lambda: nc.sync.dma_start(out=normwb[:], in_=normw[:, :]), writes=["normwb"])
        P.dma("sp", "c_lng", lambda: nc.sync.dma_start(out=lngb[:], in_=lng[:, :]), writes=["lngb"])
        P.dma("sp", "c_lnb", lambda: nc.sync.dma_start(out=lnbb[:], in_=lnb[:, :]), writes=["lnbb"])
        P.dma("sp", "c_cs", lambda: nc.sync.dma_start(out=cst[:], in_=cs[:, :, :, :]), writes=["cst"])
        P.op("dve", lambda: V.memset(waug[:], 0.0), writes=["waug"])
        P.dma("sp", "c_waug", lambda: nc.sync.dma_start(out=waug[0:17, :], in_=w_aug[0:17, :]), writes=["waug"])
        P.dma("pool", "c_wgk", lambda: G.dma_start(out=wgk[:], in_=w_in[:, :, C_GK:C_GK + 32]), writes=["wgk"])
        P.op("dve", lambda: V.memset(S32[:], 0.0), writes=["S32"])
        P.op("dve", lambda: V.memset(S16[:], 0.0), writes=["S16"])
        P.op("dve", lambda: V.memset(gkT[:], 1.0), writes=["gkT", "gkT0", "gkT1"])

        def wsc_view(bi):
            n = WBLOCKS[bi][2]
            return wsc[bi, :, 0:KC * n].rearrange("p (k c) -> p k c", k=KC)

        for bi, (src, c0, n) in enumerate(WBLOCKS):
            srcap = (w_in if src == "in" else w_out)[:, :, c0:c0 + n]
            P.dma("pool", "wcast", (lambda bi=bi, srcap=srcap: G.dma_start(out=wsc_view(bi), in_=srcap)), writes=["wsc"])
        _stage(1)
        if npre > 0:
            with contextlib.ExitStack() as st2:
                sb2 = lambda name, shape, dt: st2.enter_context(nc.sbuf_tensor(name, shape, dt))
                wk = sb2("wk", [128, KC, 512], BF16)
                xpc = [sb2("xpc%d" % i, [128, KC * 128], BF16) for i in range(3)]
                xtk = [sb2("xtk%d" % i, [128, D], BF16) for i in range(3)]
                Z = sb2("Z", [128, 4, D], F32)
                ksb = [sb2("ksb%d" % i, [128, 512], F32) for i in range(2)]
                esuf = sb2("esuf", [128, 512], F32)
                kdpb = [sb2("kdp%d" % i, [128, 512], BF16) for i in range(2)]
                totb = [sb2("totp%d" % i, [128, 4], F32) for i in range(2)]
                usuf = sb2("usuf", [128, 128], F32)
                onesf = sb2("onesf", [128, 1], F32)
                P.dma("sp", "c_usuf", lambda: nc.sync.dma_start(out=usuf[:], in_=consts[:, 3, :]), writes=["usuf"])
                P.op("dve", lambda: V.memset(onesf[:], 1.0), writes=["onesf"])
                P.op("dve", lambda: V.memset(Z[:].rearrange("p h d -> p (h d)"), 0.0), writes=["Z0", "Z1", "Z2", "Z3"])
                P.dma("pool", "wk", lambda: G.dma_start(out=wk[:], in_=w_in[:, :, C_KG:C_KG + 512]), writes=["wk"])

                def p_load(c):
                    xk, tk = "xpc%d" % (c % 3), "xtk%d" % (c % 3)
                    P.dma("pool", xk, lambda: G.dma_start(out=xpc[c % 3][:], in_=xpre[c, :, :]), writes=[xk])
                    P.dma("pool", tk, lambda: G.dma_start(out=xtk[c % 3][:], in_=xptok[c, :, :]), writes=[tk])

                def p_inproj(c):
                    xb = xpc[c % 3][:].rearrange("p (k t) -> p k t", k=KC)
                    xk = "xpc%d" % (c % 3)
                    sl = c % 2
                    for q2 in range(2):
                        P.op("pe", [(lambda k=k: T.matmul(B[1][0:32, 0:128], lhsT=wgk[:, k, :], rhs=xb[:, k, :],
                                                          start=(k == 0), stop=(k == KC - 1))) for k in range(q2 * 8, q2 * 8 + 8)],
                             reads=["wgk", xk], writes=["B1"])
                        if q2 == 1:
                            P.op("act", lambda: A.copy(out=gkT[0:16, sl * 128:(sl + 1) * 128], in_=B[1][0:16, 0:128]),
                                 reads=["B1"], writes=["gkT%d" % sl])
                        yield
                    for q4 in range(4):
                        P.op("pe", [(lambda k=k: T.matmul(B[0][:, 0:512], lhsT=xb[:, k, :], rhs=wk[:, k, :],
                                                          start=(k == 0), stop=(k == KC - 1))) for k in range(q4 * 4, q4 * 4 + 4)],
                             reads=["wk", xk], writes=["B0"])
                        if q4 == 3:
                            P.op("act", lambda: A.copy(out=ksb[sl][:], in_=B[0][:, 0:512]), reads=["B0"], writes=["ksb%d" % sl])
                        yield

                def p_tail_head(c):
                    sl = c % 2
                    P.op("pe", lambda: T.matmul(B[2][:, 0:512], lhsT=gkT[0:32, sl * 128:(sl + 1) * 128], rhs=waug[0:32, :],
                                                start=True, stop=True), reads=["gkT%d" % sl, "waug"], writes=["B2"])
                    P.op("act", lambda: A.activation(out=e1[:], in_=B[2][:, 0:512], func=AF.Exp, scale=-1.0),
                         reads=["B2"], writes=["e1"])
                    P.op("act", lambda: A.activation(out=spl[:], in_=e1[:], func=AF.Ln, bias=1.0, scale=1.0),
                         reads=["e1"], writes=["spl"])
                    yield
                    P.op("pe", lambda: T.matmul(B[3][:, 0:512], lhsT=usuf[:], rhs=spl[:], start=True, stop=True),
                         reads=["usuf", "spl"], writes=["B3"])
                    P.op("pe", [(lambda h=h: T.matmul(B[2][:, h:h + 1], lhsT=spl[:, h * 128:(h + 1) * 128], rhs=onesf[:, 0:1],
                                                      start=True, stop=True)) for h in range(4)],
                         reads=["spl", "onesf"], writes=["B2"])
                    P.op("act", lambda: A.activation(out=esuf[:], in_=B[3][:, 0:512], func=AF.Exp, scale=-1.0 / 16.0),
                         reads=["B3"], writes=["esuf"])
                    P.op("act", lambda: A.activation(out=totb[sl][:], in_=B[2][:, 0:4], func=AF.Exp, scale=-1.0 / 16.0),
                         reads=["B2"], writes=["totp%d" % sl])
                    P.op("dve", lambda: V.tensor_tensor(out=kdpb[sl][:], in0=ksb[sl][:], in1=esuf[:], op=ALU.mult),
                         reads=["ksb%d" % sl, "esuf"], writes=["kdp%d" % sl])
                    yield

                def p_tail_z(c):
                    xt_, tk = xtk[c % 3], "xtk%d" % (c % 3)
                    sl = c % 2
                    kdp, tot_ = kdpb[sl], totb[sl]
                    for h in range(4):
                        for half in range(2):
                            keys = ["B4", "B5"] if half == 0 else ["B6", "B7"]
                            c0 = half * 1024
                            P.op("pe", [(lambda nb=nb: T.matmul(BZ[:, c0 + nb * 512:c0 + (nb + 1) * 512],
                                                                lhsT=kdp[:, h * 128:(h + 1) * 128],
                                                                rhs=xt_[:, c0 + nb * 512:c0 + (nb + 1) * 512],
                                                                start=True, stop=True)) for nb in range(2)],
                                 reads=["kdp%d" % sl, tk], writes=keys)
                            P.op("dve", lambda: V.scalar_tensor_tensor(out=Z[:, h, c0:c0 + 1024], in0=Z[:, h, c0:c0 + 1024],
                                                                       scalar=tot_[:, h:h + 1], in1=BZ[:, c0:c0 + 1024],
                                                                       op0=ALU.mult, op1=ALU.add),
                                 reads=["Z%d" % h, "totp%d" % sl] + keys, writes=["Z%d" % h])
                            yield

                def run_seq(order, gens):
                    for gi in order:
                        try:
                            next(gens[gi])
                        except StopIteration:
                            pass

                p_load(0)
                if npre > 1:
                    p_load(1)
                for _ in p_inproj(0):
                    pass
                for _ in p_tail_head(0):
                    pass
                for c in range(npre):
                    if c + 2 < npre:
                        p_load(c + 2)
                    more = c + 1 < npre
                    gens = {"Z": p_tail_z(c), "I": p_inproj(c + 1) if more else iter(()),
                            "T": p_tail_head(c + 1) if more else iter(())}
                    run_seq(["Z", "I", "Z", "I", "Z", "T", "Z", "I", "Z", "I", "Z", "I", "Z", "I", "Z", "T", "Z", "I", "T"], gens)
                Zb = sb2("Zb", [128, 4, D], BF16)
                ZT = sb2("ZT", [128, 4, KC, 128], BF16)
                wv = sb2("wv", [128, KC, 1024], BF16)
                P.dma("pool", "wv", lambda: G.dma_start(out=wv[:], in_=w_in[:, :, C_VG:C_VG + 1024]), writes=["wv"])
                for h in range(4):
                    eng, E = ("act", A) if h % 2 == 0 else ("dve", V)
                    if eng == "act":
                        P.op("act", (lambda h=h: A.copy(out=Zb[:, h, :], in_=Z[:, h, :])), reads=["Z%d" % h], writes=["Zb"])
                    else:
                        P.op("dve", (lambda h=h: V.tensor_copy(out=Zb[:, h, :], in_=Z[:, h, :])), reads=["Z%d" % h], writes=["Zb"])
                for h in range(4):
                    for qd in range(2):
                        P.op("pe", [(lambda j=j: T.transpose(B4[:, (j % 8) * 128:(j % 8 + 1) * 128], Zb[:, h, j * 128:(j + 1) * 128], ident[:]))
                                    for j in range(qd * 8, qd * 8 + 8)], reads=["Zb", "ident"], writes=["B4"])
                        P.op("act", (lambda h=h, qd=qd: A.copy(out=ZT[:, h, qd * 8:(qd + 1) * 8, :].rearrange("p j t -> p (j t)"),
                                                               in_=B4[:, 0:1024])), reads=["B4"], writes=["ZT"])
                for h in range(4):
                    bk, bkey = B[h // 2], "B%d" % (h // 2)
                    P.op("pe", [(lambda k=k: T.matmul(bk[:, (h % 2) * 256:(h % 2) * 256 + 256], lhsT=ZT[:, h, k, :],
                                                      rhs=wv[:, k, h * 256:(h + 1) * 256], start=(k == 0), stop=(k == KC - 1)))
                                for k in range(KC)], reads=["ZT", "wv"], writes=[bkey])
                for j in range(2):
                    P.op("dve", (lambda j=j: V.tensor_copy(out=S32[:, j * 512:(j + 1) * 512], in_=B[j][:, 0:512])),
                         reads=["B%d" % j], writes=["S32"])
                P.op("act", lambda: A.copy(out=S16[:], in_=S32[:]), reads=["S32"], writes=["S16"])
                P.barrier()

        xTpb = [sb("xTp%d" % i, [128, KC, TOK], BF16) for i in range(2)]
        xTp = xTpb[0]
        wblk = [sb("wblk%d" % i, [128, KC, 512], BF16) for i in range(2)]
        q32 = sb("q32", [128, TP, 1024], F32)
        k32 = sb("k32", [128, TP, 128], F32)
        qr = sb("qr", [128, TP, 1024], BF16)
        krb = [sb("kr%d" % i, [128, 128], BF16) for i in range(2)]
        ta = sb("ta", [128, 512], F32)
        tb = sb("tb", [128, 512], F32)
        sza = sb("sza", [128, TP, 1024], BF16)
        szg = sb("szg", [128, TP, 1024], BF16)
        vg = sb("vg", [128, TP, 1024], BF16)
        qgT = sb("qgT", [128, 4, TOK], F32)
        kgT = sb("kgT", [128, 4, TOK], F32)
        kTall = sb("kTall", [64, 2, (nt + 1) * 128], BF16)
        vall = sb("vall", [128, nt + 1, 128], BF16)
        qT = sb("qT", [64, 16, 128], BF16)
        pexp = [sb("pexp%d" % i, [128, 2, 256], BF16) for i in range(2)]
        pT = [sb("pT%d" % i, [128, 2, 2, 128], BF16) for i in range(2)]
        mraw = sb("mraw", [128, 16], F32)
        msc = sb("msc", [128, 16], F32)
        negm = sb("negm", [128, 16], F32)
        dsm = sb("dsm", [128, 16], F32)
        esk = sb("esk", [128, 16], F32)
        den = sb("den", [128, 16], F32)
        rden = sb("rden", [128, 16], F32)
        otmp = sb("otmp", [128, 1024], F32)
        eq = sb("eq", [128, 4, 128], F32)
        ek = sb("ek", [128, 4, 128], F32)
        qdec = sb("qdec", [128, 4, 128], BF16)
        kneg = sb("kneg", [128, 4, 128], BF16)
        atm = sb("atm", [128, 4, 128], BF16)
        sqj = sb("sqj", [128, 256], BF16)
        ss = sb("ss", [128, 4], F32)
        rstd = sb("rstd", [128, 4], F32)
        y16 = sb("y16", [128, TP, D], BF16)
        yT = sb("yT", [128, TP, KC, 128], BF16)
        xres = [sb("xres%d" % i, [128, 512], F32) for i in range(4)]
        r32 = [sb("r32_%d" % i, [128, D], F32) for i in range(TP)]
        lnst = sb("lnst", [128, TP, 8], F32)

        wslot = [0]

        def load_wblk(src, c0):
            bi = WB_IDX[(src, c0)]
            ncols = WBLOCKS[bi][2]
            i = wslot[0] % 2
            wslot[0] += 1
            key = "wblk%d" % i
            P.dma("sp", key, lambda: nc.sync.dma_start(out=wblk[i][:, :, 0:ncols], in_=wsc_view(bi)), reads=["wsc"], writes=[key])
            return wblk[i], key

        accs = [0]

        def next_acc():
            i = accs[0] % 2
            accs[0] += 1
            return B[i], "B%d" % i

        def run(gen):
            for _ in gen:
                pass

        def run_mix(fg, bg, ratio=2, start_after=0):
            fg_done = bg_done = False
            for _ in range(start_after):
                try:
                    next(fg)
                except StopIteration:
                    fg_done = True
                    break
            while not (fg_done and bg_done):
                for _ in range(ratio):
                    if fg_done:
                        break
                    try:
                        r = next(fg)
                        if r == "drain":
                            for _ in bg:
                                pass
                            bg_done = True
                    except StopIteration:
                        fg_done = True
                if not bg_done:
                    try:
                        next(bg)
                    except StopIteration:
                        bg_done = True

        def chain(*gens):
            for g_ in gens:
                yield from g_

        def do_rope(E, ename, src, src_key, dst1, dst2, dst_key, nh, slot):
            s4 = src.rearrange("p (h two f) -> p h two f", h=nh, two=2)
            t1, t2 = s4[:, :, 0, :], s4[:, :, 1, :]
            cosb = cst[:, slot, 0, :].unsqueeze(1).to_broadcast([128, nh, 32])
            sinb = cst[:, slot, 1, :].unsqueeze(1).to_broadcast([128, nh, 32])
            ta3 = ta[:, 0:nh * 32].rearrange("p (h f) -> p h f", h=nh)
            tb3 = tb[:, 0:nh * 32].rearrange("p (h f) -> p h f", h=nh)
            P.op(ename, lambda: E.tensor_tensor(out=ta3, in0=t1, in1=cosb, op=ALU.mult), reads=[src_key, "cst"], writes=["ta"])
            P.op(ename, lambda: E.tensor_tensor(out=tb3, in0=t2, in1=sinb, op=ALU.mult), reads=[src_key, "cst"], writes=["tb"])
            P.op(ename, lambda: E.tensor_tensor(out=dst1, in0=ta3, in1=tb3, op=ALU.subtract), reads=["ta", "tb"], writes=[dst_key])
            P.op(ename, lambda: E.tensor_tensor(out=ta3, in0=t2, in1=cosb, op=ALU.mult), reads=[src_key, "cst"], writes=["ta"])
            P.op(ename, lambda: E.tensor_tensor(out=tb3, in0=t1, in1=sinb, op=ALU.mult), reads=[src_key, "cst"], writes=["tb"])
            P.op(ename, lambda: E.tensor_tensor(out=dst2, in0=ta3, in1=tb3, op=ALU.add), reads=["ta", "tb"], writes=[dst_key])

        def k_rope(t_k32, slot, ti):
            kr4 = krb[ti][:].rearrange("p (h two f) -> p h two f", h=2, two=2)
            do_rope(G, "pool", t_k32, "k32", kr4[:, :, 0, :], kr4[:, :, 1, :], "kr%d" % ti, 2, slot)

        def k_transposes(slot, ti):
            kr = krb[ti]
            P.op("pe", [(lambda kv=kv: T.transpose(B4[0:64, kv * 128:(kv + 1) * 128], kr[:, kv * 64:(kv + 1) * 64], ident[:]))
                        for kv in range(2)], reads=["kr%d" % ti, "ident"], writes=["B4"])
            P.op("dve", lambda: V.tensor_copy(out=kTall[:, :, slot * 128:(slot + 1) * 128],
                                              in_=B4[0:64, 0:256].rearrange("p (a t) -> p a t", a=2)),
                 reads=["B4"], writes=["kTall"])

        def load_x(ps_i):
            tok0 = (1 + ps_i * TP) * 128
            xk = "xTp%d" % (ps_i % 2)
            P.dma("pool", xk, lambda: G.dma_start(out=xTpb[ps_i % 2][:, :, :], in_=xT[:, :, tok0:tok0 + TOK]), writes=[xk])

        wb, wkey = load_wblk("in", C_KA)
        P.dma("pool", "xTp1", lambda: G.dma_start(out=xTpb[1][:, :, 0:128], in_=xT[:, :, 0:128]), writes=["xTp1"])
        acc, akey = next_acc()
        P.op("pe", [(lambda k=k: T.matmul(acc[:, 0:256], lhsT=xTpb[1][:, k, 0:128], rhs=wb[:, k, 0:256],
                                          start=(k == 0), stop=(k == KC - 1))) for k in range(KC)],
             reads=["xTp1", wkey], writes=[akey])
        P.op("act", lambda: A.copy(out=k32[:, 0, :], in_=acc[:, 0:128]), reads=[akey], writes=["k32"])
        P.op("act", lambda: A.copy(out=vall[:, 0, :], in_=acc[:, 128:256]), reads=[akey], writes=["vall"])
        k_rope(k32[:, 0, :], 0, 0)
        k_transposes(0, 0)
        load_x(0)

        npass = nt // TP
        pending_stores = []
        deferred_ln = []
        for ps_i in range(npass):
            def bg_tm(blocks, pi=None):
                pi = ps_i if pi is None else pi
                xb_, xk_ = xTpb[pi % 2], "xTp%d" % (pi % 2)
                for col0, ncols, handler in blocks:
                    wb, wkey = load_wblk("in", col0)
                    for t in range(TP):
                        acc, akey = next_acc()
                        P.op("pe", [(lambda k=k: T.matmul(acc[:, 0:ncols], lhsT=xb_[:, k, t * 128:(t + 1) * 128],
                                                          rhs=wb[:, k, 0:ncols], start=(k == 0), stop=(k == KC - 1)))
                                    for k in range(KC)], reads=[xk_, wkey], writes=[akey])
                        handler(t, acc, akey)
                        yield

            def bg_fm():
                xb_, xk_ = xTpb[ps_i % 2], "xTp%d" % (ps_i % 2)
                for (col0, dstT, dkey) in ((C_QG, qgT, "qgT"), (C_KG, kgT, "kgT")):
                    wb, wkey = load_wblk("in", col0)
                    for h in range(4):
                        acc, akey = next_acc()
                        P.op("pe", [(lambda k=k: T.matmul(acc[:, 0:TOK], lhsT=wb[:, k, h * 128:(h + 1) * 128], rhs=xb_[:, k, :],
                                                          start=(k == 0), stop=(k == KC - 1))) for k in range(KC)],
                             reads=[xk_, wkey], writes=[akey])
                        P.op("act", lambda: A.copy(out=dstT[:, h, :], in_=acc[:, 0:TOK]), reads=[akey], writes=[dkey])
                        yield
                acc, akey = next_acc()
                P.op("pe", [(lambda k=k: T.matmul(acc[0:32, 0:TOK], lhsT=wgk[:, k, :], rhs=xb_[:, k, :],
                                                  start=(k == 0), stop=(k == KC - 1))) for k in range(KC)],
                     reads=[xk_, "wgk"], writes=[akey])
                P.op("act", lambda: A.copy(out=gkT[0:16, 0:TOK], in_=acc[0:16, 0:TOK]), reads=[akey], writes=["gkT"])
                yield

            def bg_outproj(pre):
                for cb in range(4):
                    wbo, wkeyo = pre[cb] if cb < len(pre) else load_wblk("out", cb * 512)
                    for t in range(TP):
                        gt = ps_i * TP + t
                        xi = (cb * TP + t) % 4
                        xs, xkey = xres[xi], "xres%d" % xi
                        P.dma("sp", xkey, lambda: nc.sync.dma_start(out=xs[:], in_=xtok[gt, :, cb * 512:(cb + 1) * 512]), writes=[xkey])
                        acc, akey = next_acc()
                        P.op("pe", [(lambda k=k: T.matmul(acc[:, 0:512], lhsT=yT[:, t, k, :], rhs=wbo[:, k, :],
                                                          start=(k == 0), stop=(k == KC - 1))) for k in range(KC)],
                             reads=["yT%d" % t, wkeyo], writes=[akey])
                        P.op("dve", lambda: V.scalar_tensor_tensor(out=r32[t][:, cb * 512:(cb + 1) * 512], in0=xs[:], scalar=ALPHA,
                                                                   in1=acc[:, 0:512], op0=ALU.mult, op1=ALU.add,
                                                                   accum_out=lnst[:, t, cb:cb + 1]),
                             reads=[xkey, akey, "lnst%d" % t], writes=["r32_%d" % t, "lnst%d" % t])
                        yield

            def h_q(j, pi=None):
                def h(t, acc, ak):
                    P.op("act", lambda: A.copy(out=q32[:, t, j * 512:(j + 1) * 512], in_=acc[:, 0:512]),
                         reads=[ak], writes=["q32_%d" % t])
                    if j == 1:
                        g = (ps_i if pi is None else pi) * TP + t + 1
                        q4 = qr[:, t, :].rearrange("p (h two f) -> p h two f", h=16, two=2)
                        do_rope(G, "pool", q32[:, t, :], "q32_%d" % t, q4[:, :, 0, :], q4[:, :, 1, :], "qr%d" % t, 16, g)
                return h

            def h_kv(t, acc, ak, pi=None):
                g = (ps_i if pi is None else pi) * TP + t + 1
                P.op("act", lambda: A.copy(out=k32[:, t, :], in_=acc[:, 0:128]), reads=[ak], writes=["k32"])
                P.op("act", lambda: A.copy(out=vall[:, g, :], in_=acc[:, 128:256]), reads=[ak], writes=["vall"])
                k_rope(k32[:, t, :], g, t)

            def h_silu(dst, dkey, j):
                return lambda t, acc, ak: P.op("act", lambda: A.activation(out=dst[:, t, j * 512:(j + 1) * 512], in_=acc[:, 0:512],
                                                                           func=AF.Silu), reads=[ak], writes=[dkey + "%d" % t])

            def h_vg(j):
                return lambda t, acc, ak: P.op("dve", lambda: V.tensor_copy(out=vg[:, t, j * 512:(j + 1) * 512], in_=acc[:, 0:512]),
                                               reads=[ak], writes=["vg%d" % t])

            def attn(t):
                g = ps_i * TP + t + 1
                mi = 0 if (ps_i == 0 and t == 0) else 1
                k_transposes(g, t)
                yield
                for half in range(2):
                    P.op("pe", [(lambda j=j: T.transpose(B4[0:64, j * 128:(j + 1) * 128],
                                                         qr[:, t, (half * 8 + j) * 64:(half * 8 + j + 1) * 64], ident[:]))
                                for j in range(8)], reads=["qr%d" % t, "ident"], writes=["B4"])
                    P.op("dve", lambda: V.tensor_copy(out=qT[:, half * 8:(half + 1) * 8, :].rearrange("p j t -> p (j t)"),
                                                      in_=B4[0:64, 0:1024]), reads=["B4"], writes=["qT"])
                    yield

                def scores(hp):
                    sbk, skey = B[2 + hp % 2], "B%d" % (2 + hp % 2)
                    kv = hp // 4
                    P.op("pe", [(lambda hh=hh: T.matmul(sbk[:, hh * 256:(hh + 1) * 256], lhsT=qT[:, 2 * hp + hh, :],
                                                        rhs=kTall[:, kv, (g - 1) * 128:(g + 1) * 128],
                                                        start=True, stop=True)) for hh in range(2)],
                         reads=["qT", "kTall"], writes=[skey])

                scores(0)
                for hp in range(8):
                    sbk, skey = B[2 + hp % 2], "B%d" % (2 + hp % 2)
                    kv = hp // 4
                    c2 = slice(2 * hp, 2 * hp + 2)
                    P.op("dve", lambda: V.tensor_reduce(out=mraw[:, c2], in_=sbk[:, 0:512].rearrange("p (h n) -> p h n", h=2),
                                                        axis=AX.X, op=ALU.max), reads=[skey], writes=["mraw"])
                    P.op("dve", lambda: V.scalar_tensor_tensor(out=msc[:, c2], in0=mraw[:, c2], scalar=0.125, in1=sinkb[:, c2],
                                                               op0=ALU.mult, op1=ALU.max),
                         reads=["mraw", "sinkb"], writes=["msc"])
                    P.op("dve", lambda: V.tensor_scalar(out=negm[:, c2], in0=msc[:, c2], scalar1=-1.0, scalar2=None, op0=ALU.mult),
                         reads=["msc"], writes=["negm"])
                    pe_, pkey = pexp[hp % 2], "pexp%d" % (hp % 2)
                    for hh in range(2):
                        P.op("act", (lambda hh=hh: A.activation(out=pe_[:, hh, :], in_=sbk[:, hh * 256:(hh + 1) * 256],
                                                                func=AF.Exp, bias=negm[:, 2 * hp + hh:2 * hp + hh + 1], scale=0.125)),
                             reads=[skey, "negm"], writes=[pkey])
                    if hp + 1 < 8:
                        scores(hp + 1)
                    yield
                    P.op("pe", [(lambda hh=hh, kt=kt: T.transpose(B4[:, (hh * 2 + kt) * 128:(hh * 2 + kt + 1) * 128],
                                                                  pe_[:, hh, kt * 128:(kt + 1) * 128], ident[:]))
                                for hh in range(2) for kt in range(2)], reads=[pkey, "ident"], writes=["B4"])
                    pt_, ptkey = pT[hp % 2], "pT%d" % (hp % 2)
                    P.op("dve", lambda: V.tensor_tensor(out=pt_[:], in0=B4[:, 0:512].rearrange("p (h k q) -> p h k q", h=2, k=2),
                                                        in1=mask16[:, mi, :, :].unsqueeze(1).to_broadcast([128, 2, 2, 128]),
                                                        op=ALU.mult), reads=["B4", "mask16"], writes=[ptkey])
                    obk, okey = (B5, "B5") if hp < 4 else (B6, "B6")
                    mm = []
                    for hh in range(2):
                        hl = (2 * hp + hh) % 8
                        for kt in range(2):
                            mm.append(lambda hh=hh, kt=kt, hl=hl: T.matmul(obk[:, hl * 64:(hl + 1) * 64], lhsT=pt_[:, hh, kt, :],
                                                                           rhs=vall[:, g - 1 + kt, kv * 64:(kv + 1) * 64],
                                                                           start=(kt == 0), stop=(kt == 1)))
                        for kt in range(2):
                            mm.append(lambda hh=hh, kt=kt: T.matmul(B7[:, 2 * hp + hh:2 * hp + hh + 1], lhsT=pt_[:, hh, kt, :],
                                                                    rhs=ones16[:, 0:1], start=(kt == 0), stop=(kt == 1)))
                    P.op("pe", mm, reads=[ptkey, "vall", "ones16"], writes=[okey, "B7"])
                    yield
                P.op("dve", lambda: V.tensor_tensor(out=dsm[:], in0=sinkb[:], in1=msc[:], op=ALU.subtract),
                     reads=["sinkb", "msc"], writes=["dsm"])
                P.op("act", lambda: A.activation(out=esk[:], in_=dsm[:], func=AF.Exp), reads=["dsm"], writes=["esk"])
                P.op("dve", lambda: V.tensor_tensor(out=den[:], in0=B7[:, 0:16], in1=esk[:], op=ALU.add),
                     reads=["B7", "esk"], writes=["den"])
                P.op("dve", lambda: V.reciprocal(out=rden[:], in_=den[:]), reads=["den"], writes=["rden"])
                for j, (obk, okey) in enumerate(((B5, "B5"), (B6, "B6"))):
                    P.op("dve", (lambda j=j, obk=obk: V.tensor_tensor(
                        out=otmp[:, j * 512:(j + 1) * 512].rearrange("p (h d) -> p h d", h=8),
                        in0=obk[:, 0:512].rearrange("p (h d) -> p h d", h=8),
                        in1=rden[:, j * 8:(j + 1) * 8].unsqueeze(2).to_broadcast([128, 8, 64]), op=ALU.mult)),
                        reads=[okey, "rden"], writes=["otmp"])
                yield "drain"
                P.op("dve", lambda: V.tensor_tensor(out=y16[:, t, 0:1024], in0=otmp[:], in1=sza[:, t, :], op=ALU.mult),
                     reads=["otmp", "sza%d" % t], writes=["y16_%d" % t])
                yield

            def gla(t):
                ts_ = slice(t * 128, (t + 1) * 128)
                P.op("pe", lambda: T.matmul(B[2][:, 0:512], lhsT=gkT[0:32, ts_], rhs=waug[0:32, :], start=True, stop=True),
                     reads=["gkT", "waug"], writes=["B2"])
                yield
                P.op("act", lambda: A.activation(out=e1[:], in_=B[2][:, 0:512], func=AF.Exp, scale=-1.0), reads=["B2"], writes=["e1"])
                P.op("act", lambda: A.activation(out=spl[:], in_=e1[:], func=AF.Ln, bias=1.0, scale=1.0), reads=["e1"], writes=["spl"])
                P.op("pe", [(lambda h=h: T.matmul(B[3][:, h * 128:(h + 1) * 128], lhsT=spl[:, h * 128:(h + 1) * 128],
                                                  rhs=uincl[:], start=True, stop=True)) for h in range(4)],
                     reads=["spl", "uincl"], writes=["B3"])
                yield
                P.op("dve", lambda: V.tensor_scalar(out=nbl[:], in0=B[3][:, 0:512].rearrange("p (h t) -> p h t", h=4)[:, :, 127],
                                                    scalar1=-1.0 / 16.0, scalar2=None, op0=ALU.mult), reads=["B3"], writes=["nbl"])
                P.op("act", lambda: A.activation(out=eq[:].rearrange("p h t -> p (h t)"), in_=B[3][:, 0:512], func=AF.Exp,
                                                 scale=-1.0 / 16.0), reads=["B3"], writes=["eq"])
                P.op("act", lambda: A.activation(out=ek[:].rearrange("p h t -> p (h t)"), in_=B[3][:, 0:512], func=AF.Exp,
                                                 scale=1.0 / 16.0), reads=["B3"], writes=["ek"])
                P.op("act", lambda: A.activation(out=tot[:], in_=nbl[:], func=AF.Exp), reads=["nbl"], writes=["tot"])
                for h in range(4):
                    P.op("act", (lambda h=h: A.activation(out=ekd[:, h, :], in_=B[3][:, h * 128:(h + 1) * 128], func=AF.Exp,
                                                          bias=nbl[:, h:h + 1], scale=1.0 / 16.0)),
                         reads=["B3", "nbl"], writes=["ekd"])
                P.op("dve", lambda: V.scalar_tensor_tensor(out=qdec[:], in0=qgT[:, :, ts_], scalar=128.0 ** -0.5, in1=eq[:],
                                                           op0=ALU.mult, op1=ALU.mult), reads=["qgT", "eq"], writes=["qdec"])
                P.op("dve", lambda: V.tensor_tensor(out=kneg[:], in0=kgT[:, :, ts_], in1=ek[:], op=ALU.mult),
                     reads=["kgT", "ek"], writes=["kneg"])
                P.op("pe", [(lambda h=h: T.matmul(B[2][:, h * 128:(h + 1) * 128], lhsT=kneg[:, h, :], rhs=qdec[:, h, :],
                                                  start=True, stop=True)) for h in range(4)],
                     reads=["kneg", "qdec"], writes=["B2"])
                P.op("dve", lambda: V.tensor_tensor(out=kdT[:], in0=kgT[:, :, ts_], in1=ekd[:], op=ALU.mult),
                     reads=["kgT", "ekd"], writes=["kdT"])
                P.op("pe", [(lambda h=h: T.transpose(B4[:, h * 128:(h + 1) * 128], kdT[:, h, :], ident[:])) for h in range(4)],
                     reads=["kdT", "ident"], writes=["B4"])
                yield
                P.op("dve", lambda: V.tensor_tensor(out=atm[:], in0=B[2][:, 0:512].rearrange("p (h i) -> p h i", h=4),
                                                    in1=uincl16[:].unsqueeze(1).to_broadcast([128, 4, 128]), op=ALU.mult),
                     reads=["B2", "uincl16"], writes=["atm"])
                P.op("act", lambda: A.copy(out=kdec[:].rearrange("p h t -> p (h t)"), in_=B4[:, 0:512]), reads=["B4"], writes=["kdec"])
                mm = []
                for h in range(4):
                    obk = B5 if h < 2 else B6
                    sl = slice((h % 2) * 256, (h % 2) * 256 + 256)
                    mm.append(lambda h=h, obk=obk, sl=sl: T.matmul(obk[:, sl], lhsT=atm[:, h, :], rhs=vg[:, t, h * 256:(h + 1) * 256],
                                                                   start=True, stop=False))
                    mm.append(lambda h=h, obk=obk, sl=sl: T.matmul(obk[:, sl], lhsT=qdec[:, h, :], rhs=S16[:, h * 256:(h + 1) * 256],
                                                                   start=False, stop=True))
                P.op("pe", mm, reads=["atm", "vg%d" % t, "qdec", "S16"], writes=["B5", "B6"])
                P.op("pe", [(lambda h=h: T.matmul((B7 if h < 2 else B[3])[:, (h % 2) * 256:(h % 2) * 256 + 256], lhsT=kdec[:, h, :],
                                                  rhs=vg[:, t, h * 256:(h + 1) * 256], start=True, stop=True)) for h in range(4)],
                     reads=["kdec", "vg%d" % t], writes=["B7", "B3"])
                yield
                P.op("dve", lambda: V.memset(ss[:], 0.0), writes=["ss"])
                for h in range(4):
                    obk, okey = (B5, "B5") if h < 2 else (B6, "B6")
                    sl = slice((h % 2) * 256, (h % 2) * 256 + 256)
                    P.op("act", (lambda h=h, obk=obk, sl=sl: A.activation(out=sqj[:], in_=obk[:, sl], func=AF.Square,
                                                                          accum_out=ss[:, h:h + 1])),
                         reads=[okey, "ss"], writes=["sqj", "ss"])
                P.op("dve", lambda: V.tensor_scalar(out=rstd[:], in0=ss[:], scalar1=1.0 / 256.0, scalar2=RMS_EPS,
                                                    op0=ALU.mult, op1=ALU.add), reads=["ss"], writes=["rstd"])
                P.op("act", lambda: A.activation(out=rstd[:], in_=rstd[:], func=AF.Sqrt), reads=["rstd"], writes=["rstd"])
                P.op("dve", lambda: V.reciprocal(out=rstd[:], in_=rstd[:]), reads=["rstd"], writes=["rstd"])
                for h in range(4):
                    obk, okey = (B5, "B5") if h < 2 else (B6, "B6")
                    sl = slice((h % 2) * 256, (h % 2) * 256 + 256)
                    P.op("dve", (lambda h=h, obk=obk, sl=sl: V.scalar_tensor_tensor(out=otmp[:, h * 256:(h + 1) * 256], in0=obk[:, sl],
                                                                                    scalar=rstd[:, h:h + 1], in1=normwb[:],
                                                                                    op0=ALU.mult, op1=ALU.mult)),
                         reads=[okey, "rstd", "normwb"], writes=["otmp"])
                P.op("dve", lambda: V.tensor_tensor(out=y16[:, t, 1024:2048], in0=otmp[:], in1=szg[:, t, :], op=ALU.mult),
                     reads=["otmp", "szg%d" % t], writes=["y16_%d" % t])
                for h in range(4):
                    src = (B7 if h < 2 else B[3])[:, (h % 2) * 256:(h % 2) * 256 + 256]
                    P.op("dve", (lambda h=h, src=src: V.scalar_tensor_tensor(out=S32[:, h * 256:(h + 1) * 256],
                                                                             in0=S32[:, h * 256:(h + 1) * 256],
                                                                             scalar=tot[:, h:h + 1], in1=src,
                                                                             op0=ALU.mult, op1=ALU.add)),
                         reads=["S32", "tot", "B7", "B3"], writes=["S32"])
                P.op("act", lambda: A.copy(out=S16[:], in_=S32[:]), reads=["S32"], writes=["S16"])
                yield
                for qd in range(2):
                    P.op("pe", [(lambda j=j: T.transpose(B4[:, (j % 8) * 128:(j % 8 + 1) * 128], y16[:, t, j * 128:(j + 1) * 128], ident[:]))
                                for j in range(qd * 8, qd * 8 + 8)], reads=["y16_%d" % t, "ident"], writes=["B4"])
                    P.op("act", (lambda qd=qd: A.copy(out=yT[:, t, qd * 8:(qd + 1) * 8, :].rearrange("p j t -> p (j t)"),
                                                      in_=B4[:, 0:1024])), reads=["B4"], writes=["yT%d" % t])
                    yield

            def ln_gen(t, gt):
                rt, rkey, lk = r32[t], "r32_%d" % t, "lnst%d" % t
                L = lambda a, b: lnst[:, t, a:b]
                P.op("dve", lambda: V.tensor_reduce(out=L(4, 5), in_=L(0, 4), axis=AX.X, op=ALU.add), reads=[lk], writes=[lk])
                P.op("dve", lambda: V.tensor_scalar(out=L(5, 6), in0=L(4, 5), scalar1=-1.0 / D, scalar2=None, op0=ALU.mult),
                     reads=[lk], writes=[lk])
                P.op("act", lambda: A.activation(out=y16[:, t, :], in_=rt[:], func=AF.Square, bias=L(5, 6), scale=1.0,
                                                 accum_out=L(6, 7)), reads=[rkey, lk], writes=["y16_%d" % t, lk])
                yield
                P.op("dve", lambda: V.tensor_scalar(out=L(7, 8), in0=L(6, 7), scalar1=1.0 / D, scalar2=LN_EPS,
                                                    op0=ALU.mult, op1=ALU.add), reads=[lk], writes=[lk])
                P.op("act", lambda: A.activation(out=L(7, 8), in_=L(7, 8), func=AF.Sqrt), reads=[lk], writes=[lk])
                P.op("dve", lambda: V.reciprocal(out=L(7, 8), in_=L(7, 8)), reads=[lk], writes=[lk])
                P.op("dve", lambda: V.scalar_tensor_tensor(out=rt[:], in0=rt[:], scalar=L(5, 6), in1=lngb[:],
                                                           op0=ALU.add, op1=ALU.mult), reads=[rkey, lk, "lngb"], writes=[rkey])
                yield
                P.op("dve", lambda: V.scalar_tensor_tensor(out=rt[:], in0=rt[:], scalar=L(7, 8), in1=lnbb[:],
                                                           op0=ALU.mult, op1=ALU.add), reads=[rkey, lk, "lnbb"], writes=[rkey])
                pending_stores.append((t, gt))
                yield

            def flush_stores():
                while pending_stores:
                    t_, gt_ = pending_stores.pop(0)
                    P.dma("sp", "out%d" % t_, (lambda t_=t_, gt_=gt_: nc.sync.dma_start(out=out[gt_, :, :], in_=r32[t_][:])),
                          reads=["r32_%d" % t_])

            def s1_blocks(pi):
                return [(C_QA, 512, h_q(0, pi)), (C_QA + 512, 512, h_q(1, pi)),
                        (C_KA, 256, (lambda t, acc, ak, pi=pi: h_kv(t, acc, ak, pi)))]

            if ps_i == 0:
                run(bg_tm(s1_blocks(0), 0))
            if ps_i + 1 < npass:
                load_x(ps_i + 1)
            run_mix(attn(0), chain(deferred_ln.pop(0) if deferred_ln else iter(()),
                                   bg_tm([(C_ZA, 512, h_silu(sza, "sza", 0)), (C_ZA + 512, 512, h_silu(sza, "sza", 1)),
                                          (C_VG, 512, h_vg(0)), (C_VG + 512, 512, h_vg(1))])), ratio=2)
            flush_stores()
            run_mix(attn(1), chain(deferred_ln.pop(0) if deferred_ln else iter(()),
                                   bg_tm([(C_ZG, 512, h_silu(szg, "szg", 0)), (C_ZG + 512, 512, h_silu(szg, "szg", 1))]),
                                   bg_fm()), ratio=2)
            flush_stores()
            for t in range(TP):
                P.op("dve", (lambda t=t: V.memset(lnst[:, t, :], 0.0)), writes=["lnst%d" % t])
            if ps_i + 1 < npass:
                run_mix(gla(0), bg_tm(s1_blocks(ps_i + 1), ps_i + 1), ratio=1)
            else:
                run(gla(0))
            pre = [load_wblk("out", 0), load_wblk("out", 512)]
            run(gla(1))
            run(bg_outproj(pre))
            deferred_ln.append(ln_gen(0, ps_i * TP + 0))
            deferred_ln.append(ln_gen(1, ps_i * TP + 1))
        while deferred_ln:
            run(deferred_ln.pop(0))
        flush_stores()


def _kmajor(a):
    n = a.shape[1]
    return np.ascontiguousarray(a.reshape(KC, 128, n).transpose(1, 0, 2))


def _host_consts():
    ident = np.eye(128, dtype=np.float32)
    uincl = (np.arange(128)[:, None] <= np.arange(128)[None, :]).astype(np.float32)
    ones = np.ones((128, 128), np.float32)
    usuf = (np.arange(128)[:, None] > np.arange(128)[None, :]).astype(np.float32)
    consts = np.ascontiguousarray(np.stack([ident, uincl, ones, usuf], axis=1))
    k = np.arange(128)[:, None, None]
    kt = np.arange(2)[None, :, None]
    q = np.arange(128)[None, None, :]
    kj = kt * 128 + k
    reg = ((kj > q) & (kj <= q + 128)).astype(np.float32)
    first = reg * (kj >= 128)
    return consts, reg, first


def kernel(x, w_in, w_gk_up, b_gk, attn_sinks, gla_norm_w, w_out, ln_g, ln_b, _ncores=NCORES, _npre=NPRE):
    x = np.asarray(x, np.float32)[0]
    w = np.asarray(w_in, np.float32)[0]
    perm = np.concatenate([np.arange(0, 1024), np.arange(1024, 1152), np.arange(1152, 1280), np.arange(1280, 2304),
                           np.arange(3328, 4352), np.arange(4352, 5376), np.arange(2304, 2816), np.arange(2816, 3328),
                           np.arange(5376, 5392)])
    w_l = _kmajor(np.concatenate([w[:, perm], np.zeros((D, 16), np.float32)], axis=1))
    wo_l = _kmajor(np.asarray(w_out, np.float32)[0])
    waug = np.zeros((32, 512), np.float32)
    waug[0:16] = np.asarray(w_gk_up, np.float32)[0]
    waug[16] = np.asarray(b_gk, np.float32)[0]
    consts, mreg, mfirst = _host_consts()
    sinks = np.ascontiguousarray(np.broadcast_to(np.asarray(attn_sinks, np.float32)[0][None, :], (128, 16)))
    normw = np.ascontiguousarray(np.broadcast_to(np.asarray(gla_norm_w, np.float32)[0][None, :], (128, 256)))
    lng = np.ascontiguousarray(np.broadcast_to(np.asarray(ln_g, np.float32)[0][None, :], (128, D)))
    lnb = np.ascontiguousarray(np.broadcast_to(np.asarray(ln_b, np.float32)[0][None, :], (128, D)))
    inv_freq = (1.0 / (10000.0 ** (np.arange(0, 32, dtype=np.float32) * 2.0 / 64.0))).astype(np.float32)
    xTfull = np.ascontiguousarray(x.T)
    in_maps = []
    for c in range(_ncores):
        s0 = c * OWN
        xt = np.zeros((D, OWN + 128), np.float32)
        if c > 0:
            xt[:, 0:128] = xTfull[:, s0 - 128:s0]
        xt[:, 128:] = xTfull[:, s0:s0 + OWN]
        npre_tok = _npre * 128
        xp = np.zeros((D, max(npre_tok, 128)), np.float32)
        if _npre > 0 and s0 > 0:
            xp[:, npre_tok - s0:] = xTfull[:, 0:s0]
        pos = (np.arange(s0 - 128, s0 + OWN)).astype(np.float32)
        ang = pos[:, None] * inv_freq[None, :]
        cs = np.stack([np.cos(ang), np.sin(ang)], axis=1).astype(np.float32)
        cs = np.ascontiguousarray(cs.reshape(NT + 1, 128, 2, 32).transpose(1, 0, 2, 3))
        masks = np.ascontiguousarray(np.stack([mfirst if c == 0 else mreg, mreg], axis=1))
        in_maps.append({
            "xT": _kmajor(xt),
            "xpre": np.ascontiguousarray(_kmajor(xp).reshape(128, KC, max(_npre, 1), 128).transpose(2, 0, 1, 3)
                                         .reshape(max(_npre, 1), 128, KC * 128)),
            "xtok": np.ascontiguousarray(x[s0:s0 + OWN].reshape(NT, 128, D)),
            "xptok": np.ascontiguousarray(xp.T.reshape(max(_npre, 1), 128, D)),
            "w_in": w_l, "w_out": wo_l, "w_aug": waug, "cs": cs, "masks": masks, "consts": consts,
            "sinks": sinks, "normw": normw, "lng": lng, "lnb": lnb,
        })
    nc = build(npre=_npre, nt=NT)
    res = run_bass_kernel_spmd(nc, in_maps, core_ids=list(range(_ncores)))
    outs = [np.asarray(r["out"]).reshape(OWN, D) for r in res.results]
    return np.concatenate(outs, axis=0)[None].astype(np.float32)
```

```python
import contextlib
import math
import numpy as np
import concourse.bass as bass
import concourse.mybir as mybir
from concourse.bass_utils import run_bass_kernel_spmd

F32 = mybir.dt.float32
BF16 = mybir.dt.bfloat16
AF = mybir.ActivationFunctionType
ALU = mybir.AluOpType
AX = mybir.AxisListType

NCORES = 8
D = 2048
SEQ = 16384
OWN = SEQ // NCORES
NT = OWN // 128
NPRE = (SEQ - OWN) // 128
TP = 2
TOK = TP * 128
KC = D // 128
WCOLS = 5408
C_QA, C_KA, C_VA, C_ZA, C_VG, C_ZG, C_QG, C_KG, C_GK = 0, 1024, 1152, 1280, 2304, 3328, 4352, 4864, 5376
ALPHA = 2.0 ** 0.25
WBLOCKS = [("in", C_QA, 512), ("in", C_QA + 512, 512), ("in", C_KA, 256), ("in", C_ZA, 512), ("in", C_ZA + 512, 512),
           ("in", C_VG, 512), ("in", C_VG + 512, 512), ("in", C_ZG, 512), ("in", C_ZG + 512, 512),
           ("in", C_QG, 512), ("in", C_KG, 512),
           ("out", 0, 512), ("out", 512, 512), ("out", 1024, 512), ("out", 1536, 512)]
WB_IDX = {(src, c0): i for i, (src, c0, n) in enumerate(WBLOCKS)}
LN_EPS = 1e-5
RMS_EPS = 1e-5


class Prog:
    def __init__(self, nc, stack):
        self.nc = nc
        self.stack = stack
        self.engs = {"pe": nc.tensor, "act": nc.scalar, "dve": nc.vector, "pool": nc.gpsimd, "sp": nc.sync}
        self.sem = {}
        self.cnt = {}
        self.waited = {}
        self.lastw = {}
        self.readers = {}
        self.ninst = {e: 0 for e in self.engs}
        for e in ("pe", "act", "dve", "pool"):
            self._mksem("E_" + e)

    def _mksem(self, name):
        self.sem[name] = self.stack.enter_context(self.nc.semaphore(name))
        self.cnt[name] = 0

    def _deps(self, reads, writes):
        deps = {}

        def add(tok):
            if tok is None:
                return
            s, v = tok
            if deps.get(s, 0) < v:
                deps[s] = v
        for r in reads:
            add(self.lastw.get(r))
        for w in writes:
            add(self.lastw.get(w))
            for t in self.readers.get(w, ()):
                add(t)
        return deps

    def _wait(self, eng, deps):
        e = self.engs[eng]
        for s, v in deps.items():
            if self.waited.get((eng, s), 0) >= v:
                continue
            if eng == "pe" and s == "E_pe":
                continue
            e.wait_ge(self.sem[s], v)
            self.ninst[eng] += 1
            self.waited[(eng, s)] = v

    def _commit(self, tok, reads, writes):
        for w in writes:
            self.lastw[w] = tok
            self.readers[w] = []
        for r in reads:
            self.readers.setdefault(r, []).append(tok)

    PSUM_KEYS = frozenset("B%d" % i for i in range(8))

    def op(self, eng, fns, reads=(), writes=()):
        if callable(fns):
            fns = [fns]
        writes = list(writes) + [r for r in reads if r in self.PSUM_KEYS and r not in writes]
        self._wait(eng, self._deps(reads, writes))
        ins = None
        for f in fns:
            ins = f()
            self.ninst[eng] += 1
        s = "E_" + eng
        self.cnt[s] += 1
        ins.then_inc(self.sem[s], 1)
        self._commit((s, self.cnt[s]), reads, writes)

    def dma(self, eng, slot, fns, reads=(), writes=()):
        if callable(fns):
            fns = [fns]
        s = "D_" + slot
        if s not in self.sem:
            self._mksem(s)
        self._wait(eng, self._deps(reads, writes))
        for f in fns:
            ins = f()
            self.ninst[eng] += 1
            self.cnt[s] += 16
            ins.then_inc(self.sem[s], 16)
        self._commit((s, self.cnt[s]), reads, writes)

    def barrier(self):
        for eng in ("pe", "act", "dve", "pool", "sp"):
            e = self.engs[eng]
            for s, v in self.cnt.items():
                if v > 0 and self.waited.get((eng, s), 0) < v:
                    e.wait_ge(self.sem[s], v)
                    self.waited[(eng, s)] = v

    def finish(self, eng="sp"):
        e = self.engs[eng]
        for s, v in self.cnt.items():
            if v > 0 and self.waited.get((eng, s), 0) < v:
                e.wait_ge(self.sem[s], v)
                self.waited[(eng, s)] = v


class _Stop(Exception):
    pass


STAGE = 99
SKIPK = False
DBG = 0


def _stage(n):
    if STAGE < n:
        raise _Stop()


def build(npre=NPRE, nt=NT):
    nc = bass.Bass("TRN2", target_bir_lowering=False)
    din = lambda name, shape: nc.dram_tensor(name, shape, F32, kind="ExternalInput").ap()
    xT = din("xT", [128, KC, (nt + 1) * 128])
    xpre = din("xpre", [max(npre, 1), 128, KC * 128])
    xtok = din("xtok", [nt, 128, D])
    xptok = din("xptok", [max(npre, 1), 128, D])
    w_in = din("w_in", [128, KC, WCOLS])
    w_out = din("w_out", [128, KC, D])
    w_aug = din("w_aug", [32, 512])
    cs = din("cs", [128, nt + 1, 2, 32])
    masks = din("masks", [128, 2, 2, 128])
    consts = din("consts", [128, 4, 128])
    sinks = din("sinks", [128, 16])
    normw = din("normw", [128, 256])
    lng = din("lng", [128, D])
    lnb = din("lnb", [128, D])
    out = nc.dram_tensor("out", [nt, 128, D], F32, kind="ExternalOutput").ap()
    wsc = nc.dram_tensor("wsc", [len(WBLOCKS), 128, KC * 512], BF16, kind="Internal").ap()

    with contextlib.ExitStack() as st:
        P = Prog(nc, st)
        try:
            _build_body(nc, st, P, npre, nt, locals())
        except _Stop:
            pass
        P.finish("sp")
    return nc


def _build_body(nc, st, P, npre, nt, env):
    xT, xpre, xtok, w_in, w_out, w_aug, cs, masks, consts, sinks, normw, lng, lnb, out, wsc, xptok = [env[k] for k in ('xT', 'xpre', 'xtok', 'w_in', 'w_out', 'w_aug', 'cs', 'masks', 'consts', 'sinks', 'normw', 'lng', 'lnb', 'out', 'wsc', 'xptok')]
    if True:
        sb = lambda name, shape, dt: st.enter_context(nc.sbuf_tensor(name, shape, dt))
        psb = lambda name, dt=F32: st.enter_context(nc.psum_tensor(name, [128, 512 if dt == F32 else 1024], dt))
        V, A, T, G = nc.vector, nc.scalar, nc.tensor, nc.gpsimd

        B = [psb("B%d" % i) for i in range(4)]
        BZ = st.enter_context(nc.psum_tensor("BZ", [128, 2048], F32))
        B4 = BZ[:, 0:512].bitcast(BF16)
        B5, B6, B7 = BZ[:, 512:1024], BZ[:, 1024:1536], BZ[:, 1536:2048]

        ident = sb("ident", [128, 128], BF16)
        uincl = sb("uincl", [128, 128], F32)
        uincl16 = sb("uincl16", [128, 128], BF16)
        ones16 = sb("ones16", [128, 128], BF16)
        mask16 = sb("mask16", [128, 2, 2, 128], BF16)
        sinkb = sb("sinkb", [128, 16], F32)
        normwb = sb("normwb", [128, 256], F32)
        lngb = sb("lngb", [128, D], F32)
        lnbb = sb("lnbb", [128, D], F32)
        cst = sb("cst", [128, nt + 1, 2, 32], F32)
        waug = sb("waug", [32, 512], F32)
        S32 = sb("S32", [128, 1024], F32)
        S16 = sb("S16", [128, 1024], BF16)
        gkT = sb("gkT", [128, TOK], F32)
        e1 = sb("e1", [128, 512], F32)
        spl = sb("spl", [128, 512], F32)
        nbl = sb("nbl", [128, 4], F32)
        tot = sb("tot", [128, 4], F32)
        ekd = sb("ekd", [128, 4, 128], F32)
        kdT = sb("kdT", [128, 4, 128], BF16)
        kdec = sb("kdec", [128, 4, 128], BF16)
        wgk = sb("wgk", [128, KC, 32], BF16)

        P.dma("pool", "c_ident", lambda: G.dma_start(out=ident[:], in_=consts[:, 0, :]), writes=["ident"])
        identf = sb("identf", [128, 128], F32)
        P.dma("sp", "c_identf", lambda: nc.sync.dma_start(out=identf[:], in_=consts[:, 0, :]), writes=["identf"])
        P.dma("sp", "c_uincl", lambda: nc.sync.dma_start(out=uincl[:], in_=consts[:, 1, :]), writes=["uincl"])
        P.dma("pool", "c_uincl16", lambda: G.dma_start(out=uincl16[:], in_=consts[:, 1, :]), writes=["uincl16"])
        P.dma("pool", "c_ones", lambda: G.dma_start(out=ones16[:], in_=consts[:, 2, :]), writes=["ones16"])
        P.dma("pool", "c_mask", lambda: G.dma_start(out=mask16[:], in_=masks[:, :, :, :]), writes=["mask16"])
        P.dma("sp", "c_sink", lambda: nc.sync.dma_start(out=sinkb[:], in_=sinks[:, :]), writes=["sinkb"])
        P.dma("sp", "c_normw", lambda: nc.sync.dma_start(out=normwb[:], in_=normw[:, :]), writes=["normwb"])
        P.dma("sp", "c_lng", lambda: nc.sync.dma_start(out=lngb[:], in_=lng[:, :]), writes=["lngb"])
        P.dma("sp", "c_lnb", lambda: nc.sync.dma_start(out=lnbb[:], in_=lnb[:, :]), writes=["lnbb"])
        P.dma("sp", "c_cs", lambda: nc.sync.dma_start(out=cst[:], in_=cs[:, :, :, :]), writes=["cst"])
        P.op("dve", lambda: V.memset(waug[:], 0.0), writes=["waug"])
        P.dma("sp", "c_waug", lambda: nc.sync.dma_start(out=waug[0:17, :], in_=w_aug[0:17, :]), writes=["waug"])
        P.dma("pool", "c_wgk", lambda: G.dma_start(out=wgk[:], in_=w_in[:, :, C_GK:C_GK + 32]), writes=["wgk"])
        P.op("dve", lambda: V.memset(S32[:], 0.0), writes=["S32"])
        P.op("dve", lambda: V.memset(S16[:], 0.0), writes=["S16"])
        P.op("dve", lambda: V.memset(gkT[:], 1.0), writes=["gkT", "gkT0", "gkT1"])

        def wsc_view(bi):
            n = WBLOCKS[bi][2]
            return wsc[bi, :, 0:KC * n].rearrange("p (k c) -> p k c", k=KC)

        wcast_jobs = list(range(len(WBLOCKS)))

        def emit_wcast(n=1):
            for _ in range(n):
                if not wcast_jobs:
                    return
                bi = wcast_jobs.pop(0)
                src, c0, ncol = WBLOCKS[bi]
                srcap = (w_in if src == "in" else w_out)[:, :, c0:c0 + ncol]
                P.dma("pool", "wcast", (lambda bi=bi, srcap=srcap: G.dma_start(out=wsc_view(bi), in_=srcap)), writes=["wsc"])
        _stage(1)
        if npre > 0:
            with contextlib.ExitStack() as st2:
                sb2 = lambda name, shape, dt: st2.enter_context(nc.sbuf_tensor(name, shape, dt))
                wk = sb2("wk", [128, KC, 512], BF16)
                xpc = [sb2("xpc%d" % i, [128, KC * 128], BF16) for i in range(3)]
                xtk = [sb2("xtk%d" % i, [128, D], BF16) for i in range(3)]
                Z = sb2("Z", [128, 4, D], F32)
                ksb = [sb2("ksb%d" % i, [128, 512], F32) for i in range(2)]
                esuf = sb2("esuf", [128, 512], F32)
                kdpb = [sb2("kdp%d" % i, [128, 512], BF16) for i in range(2)]
                totb = [sb2("totp%d" % i, [128, 4], F32) for i in range(2)]
                usuf = sb2("usuf", [128, 128], F32)
                onesf = sb2("onesf", [128, 1], F32)
                P.dma("sp", "c_usuf", lambda: nc.sync.dma_start(out=usuf[:], in_=consts[:, 3, :]), writes=["usuf"])
                P.op("dve", lambda: V.memset(onesf[:], 1.0), writes=["onesf"])
                P.op("dve", lambda: V.memset(Z[:].rearrange("p h d -> p (h d)"), 0.0), writes=["Z0", "Z1", "Z2", "Z3"])
                P.dma("pool", "wk", lambda: G.dma_start(out=wk[:], in_=w_in[:, :, C_KG:C_KG + 512]), writes=["wk"])

                def p_load(c):
                    xk, tk = "xpc%d" % (c % 3), "xtk%d" % (c % 3)
                    P.dma("pool", xk, lambda: G.dma_start(out=xpc[c % 3][:], in_=xpre[c, :, :]), writes=[xk])
                    P.dma("pool", tk, lambda: G.dma_start(out=xtk[c % 3][:], in_=xptok[c, :, :]), writes=[tk])

                def p_inproj(c):
                    xb = xpc[c % 3][:].rearrange("p (k t) -> p k t", k=KC)
                    xk = "xpc%d" % (c % 3)
                    sl = c % 2
                    for q2 in range(2):
                        P.op("pe", [(lambda k=k: T.matmul(B[1][0:32, 0:128], lhsT=wgk[:, k, :], rhs=xb[:, k, :],
                                                          start=(k == 0), stop=(k == KC - 1))) for k in range(q2 * 8, q2 * 8 + 8)],
                             reads=["wgk", xk], writes=["B1"])
                        if q2 == 1:
                            P.op("act", lambda: A.copy(out=gkT[0:16, sl * 128:(sl + 1) * 128], in_=B[1][0:16, 0:128]),
                                 reads=["B1"], writes=["gkT%d" % sl])
                        yield
                    for q4 in range(4):
                        P.op("pe", [(lambda k=k: T.matmul(B[0][:, 0:512], lhsT=xb[:, k, :], rhs=wk[:, k, :],
                                                          start=(k == 0), stop=(k == KC - 1))) for k in range(q4 * 4, q4 * 4 + 4)],
                             reads=["wk", xk], writes=["B0"])
                        if q4 == 3:
                            P.op("act", lambda: A.copy(out=ksb[sl][:], in_=B[0][:, 0:512]), reads=["B0"], writes=["ksb%d" % sl])
                        yield

                def p_tail_head(c):
                    sl = c % 2
                    P.op("pe", lambda: T.matmul(B[2][:, 0:512], lhsT=gkT[0:32, sl * 128:(sl + 1) * 128], rhs=waug[0:32, :],
                                                start=True, stop=True), reads=["gkT%d" % sl, "waug"], writes=["B2"])
                    P.op("act", lambda: A.activation(out=e1[:], in_=B[2][:, 0:512], func=AF.Exp, scale=-1.0),
                         reads=["B2"], writes=["e1"])
                    P.op("act", lambda: A.activation(out=spl[:], in_=e1[:], func=AF.Ln, bias=1.0, scale=1.0),
                         reads=["e1"], writes=["spl"])
                    yield
                    P.op("pe", lambda: T.matmul(B[3][:, 0:512], lhsT=usuf[:], rhs=spl[:], start=True, stop=True),
                         reads=["usuf", "spl"], writes=["B3"])
                    P.op("pe", [(lambda h=h: T.matmul(B[2][:, h:h + 1], lhsT=spl[:, h * 128:(h + 1) * 128], rhs=onesf[:, 0:1],
                                                      start=True, stop=True)) for h in range(4)],
                         reads=["spl", "onesf"], writes=["B2"])
                    P.op("act", lambda: A.activation(out=esuf[:], in_=B[3][:, 0:512], func=AF.Exp, scale=-1.0 / 16.0),
                         reads=["B3"], writes=["esuf"])
                    P.op("act", lambda: A.activation(out=totb[sl][:], in_=B[2][:, 0:4], func=AF.Exp, scale=-1.0 / 16.0),
                         reads=["B2"], writes=["totp%d" % sl])
                    P.op("dve", lambda: V.tensor_tensor(out=kdpb[sl][:], in0=ksb[sl][:], in1=esuf[:], op=ALU.mult),
                         reads=["ksb%d" % sl, "esuf"], writes=["kdp%d" % sl])
                    yield

                def p_tail_z(c):
                    xt_, tk = xtk[c % 3], "xtk%d" % (c % 3)
                    sl = c % 2
                    kdp, tot_ = kdpb[sl], totb[sl]
                    for h in range(4):
                        for half in range(2):
                            keys = ["B4", "B5"] if half == 0 else ["B6", "B7"]
                            c0 = half * 1024
                            P.op("pe", [(lambda nb=nb: T.matmul(BZ[:, c0 + nb * 512:c0 + (nb + 1) * 512],
                                                                lhsT=kdp[:, h * 128:(h + 1) * 128],
                                                                rhs=xt_[:, c0 + nb * 512:c0 + (nb + 1) * 512],
                                                                start=True, stop=True)) for nb in range(2)],
                                 reads=["kdp%d" % sl, tk], writes=keys)
                            P.op("dve", lambda: V.scalar_tensor_tensor(out=Z[:, h, c0:c0 + 1024], in0=Z[:, h, c0:c0 + 1024],
                                                                       scalar=tot_[:, h:h + 1], in1=BZ[:, c0:c0 + 1024],
                                                                       op0=ALU.mult, op1=ALU.add),
                                 reads=["Z%d" % h, "totp%d" % sl] + keys, writes=["Z%d" % h])
                            yield

                def run_seq(order, gens):
                    for gi in order:
                        try:
                            next(gens[gi])
                        except StopIteration:
                            pass

                p_load(0)
                if npre > 1:
                    p_load(1)
                for _ in p_inproj(0):
                    pass
                for _ in p_tail_head(0):
                    pass
                for c in range(npre):
                    if c + 2 < npre:
                        p_load(c + 2)
                    if c % 4 == 1:
                        emit_wcast(1)
                    more = c + 1 < npre
                    gens = {"Z": p_tail_z(c), "I": p_inproj(c + 1) if more else iter(()),
                            "T": p_tail_head(c + 1) if more else iter(())}
                    run_seq(["Z", "I", "Z", "I", "Z", "T", "Z", "I", "Z", "I", "Z", "I", "Z", "I", "Z", "T", "Z", "I", "T"], gens)
                Zb = sb2("Zb", [128, 4, D], BF16)
                ZT = sb2("ZT", [128, 4, KC, 128], BF16)
                wv = sb2("wv", [128, KC, 1024], BF16)
                P.dma("pool", "wv", lambda: G.dma_start(out=wv[:], in_=w_in[:, :, C_VG:C_VG + 1024]), writes=["wv"])
                for h in range(4):
                    eng, E = ("act", A) if h % 2 == 0 else ("dve", V)
                    if eng == "act":
                        P.op("act", (lambda h=h: A.copy(out=Zb[:, h, :], in_=Z[:, h, :])), reads=["Z%d" % h], writes=["Zb"])
                    else:
                        P.op("dve", (lambda h=h: V.tensor_copy(out=Zb[:, h, :], in_=Z[:, h, :])), reads=["Z%d" % h], writes=["Zb"])
                for h in range(4):
                    for qd in range(2):
                        P.op("pe", [(lambda j=j: T.transpose(B4[:, (j % 8) * 128:(j % 8 + 1) * 128], Zb[:, h, j * 128:(j + 1) * 128], ident[:]))
                                    for j in range(qd * 8, qd * 8 + 8)], reads=["Zb", "ident"], writes=["B4"])
                        P.op("act", (lambda h=h, qd=qd: A.copy(out=ZT[:, h, qd * 8:(qd + 1) * 8, :].rearrange("p j t -> p (j t)"),
                                                               in_=B4[:, 0:1024])), reads=["B4"], writes=["ZT"])
                for h in range(4):
                    bk, bkey = B[h // 2], "B%d" % (h // 2)
                    P.op("pe", [(lambda k=k: T.matmul(bk[:, (h % 2) * 256:(h % 2) * 256 + 256], lhsT=ZT[:, h, k, :],
                                                      rhs=wv[:, k, h * 256:(h + 1) * 256], start=(k == 0), stop=(k == KC - 1)))
                                for k in range(KC)], reads=["ZT", "wv"], writes=[bkey])
                for j in range(2):
                    P.op("dve", (lambda j=j: V.tensor_copy(out=S32[:, j * 512:(j + 1) * 512], in_=B[j][:, 0:512])),
                         reads=["B%d" % j], writes=["S32"])
                P.op("act", lambda: A.copy(out=S16[:], in_=S32[:]), reads=["S32"], writes=["S16"])
                P.barrier()

        emit_wcast(len(WBLOCKS))
        xTpb = [sb("xTp%d" % i, [128, KC, TOK], BF16) for i in range(2)]
        xTp = xTpb[0]
        wblk = [sb("wblk%d" % i, [128, KC, 512], BF16) for i in range(2)]
        q32 = sb("q32", [128, TP, 1024], F32)
        k32 = sb("k32", [128, TP, 128], F32)
        qr = sb("qr", [128, TP, 1024], BF16)
        krb = [sb("kr%d" % i, [128, 128], BF16) for i in range(2)]
        ta = sb("ta", [128, 512], F32)
        tb = sb("tb", [128, 512], F32)
        sza = sb("sza", [128, TP, 1024], BF16)
        szg = sb("szg", [128, TP, 1024], BF16)
        vg = sb("vg", [128, TP, 1024], BF16)
        qgT = sb("qgT", [128, 4, TOK], F32)
        kgT = sb("kgT", [128, 4, TOK], F32)
        kTall = sb("kTall", [64, 2, (nt + 1) * 128], BF16)
        vall = sb("vall", [128, nt + 1, 128], BF16)
        qT = sb("qT", [64, 16, 128], BF16)
        pexp = [sb("pexp%d" % i, [128, 2, 256], BF16) for i in range(2)]
        pT = [sb("pT%d" % i, [128, 2, 2, 128], BF16) for i in range(2)]
        mraw = sb("mraw", [128, 16], F32)
        msc = sb("msc", [128, 16], F32)
        negm = sb("negm", [128, 16], F32)
        dsm = sb("dsm", [128, 16], F32)
        esk = sb("esk", [128, 16], F32)
        den = sb("den", [128, 16], F32)
        rden = sb("rden", [128, 16], F32)
        otmp = sb("otmp", [128, 1024], F32)
        eq = sb("eq", [128, 4, 128], F32)
        ek = sb("ek", [128, 4, 128], F32)
        qdec = sb("qdec", [128, 4, 128], BF16)
        kneg = sb("kneg", [128, 4, 128], BF16)
        atm = sb("atm", [128, 4, 128], BF16)
        sqj = sb("sqj", [128, 256], BF16)
        ss = sb("ss", [128, 4], F32)
        rstd = sb("rstd", [128, 4], F32)
        y16 = sb("y16", [128, TP, D], BF16)
        yT = sb("yT", [128, TP, KC, 128], BF16)
        xres = [sb("xres%d" % i, [128, 512], F32) for i in range(4)]
        r32 = [sb("r32_%d" % i, [128, D], F32) for i in range(TP)]
        lnst = sb("lnst", [128, TP, 8], F32)

        wslot = [0]

        def load_wblk(src, c0):
            bi = WB_IDX[(src, c0)]
            ncols = WBLOCKS[bi][2]
            i = wslot[0] % 2
            wslot[0] += 1
            key = "wblk%d" % i
            P.dma("sp", key, lambda: nc.sync.dma_start(out=wblk[i][:, :, 0:ncols], in_=wsc_view(bi)), reads=["wsc"], writes=[key])
            return wblk[i], key

        accs = [0]

        def next_acc():
            i = accs[0] % 2
            accs[0] += 1
            return B[i], "B%d" % i

        def run(gen):
            for _ in gen:
                pass

        def run_mix(fg, bg, ratio=2, start_after=0):
            fg_done = bg_done = False
            for _ in range(start_after):
                try:
                    next(fg)
                except StopIteration:
                    fg_done = True
                    break
            while not (fg_done and bg_done):
                for _ in range(ratio):
                    if fg_done:
                        break
                    try:
                        r = next(fg)
                        if r == "drain":
                            for _ in bg:
                                pass
                            bg_done = True
                    except StopIteration:
                        fg_done = True
                if not bg_done:
                    try:
                        next(bg)
                    except StopIteration:
                        bg_done = True

        def chain(*gens):
            for g_ in gens:
                yield from g_

        def do_rope(E, ename, src, src_key, dst1, dst2, dst_key, nh, slot):
            s4 = src.rearrange("p (h two f) -> p h two f", h=nh, two=2)
            t1, t2 = s4[:, :, 0, :], s4[:, :, 1, :]
            cosb = cst[:, slot, 0, :].unsqueeze(1).to_broadcast([128, nh, 32])
            sinb = cst[:, slot, 1, :].unsqueeze(1).to_broadcast([128, nh, 32])
            ta3 = ta[:, 0:nh * 32].rearrange("p (h f) -> p h f", h=nh)
            tb3 = tb[:, 0:nh * 32].rearrange("p (h f) -> p h f", h=nh)
            P.op(ename, lambda: E.tensor_tensor(out=ta3, in0=t1, in1=cosb, op=ALU.mult), reads=[src_key, "cst"], writes=["ta"])
            P.op(ename, lambda: E.tensor_tensor(out=tb3, in0=t2, in1=sinb, op=ALU.mult), reads=[src_key, "cst"], writes=["tb"])
            P.op(ename, lambda: E.tensor_tensor(out=dst1, in0=ta3, in1=tb3, op=ALU.subtract), reads=["ta", "tb"], writes=[dst_key])
            P.op(ename, lambda: E.tensor_tensor(out=ta3, in0=t2, in1=cosb, op=ALU.mult), reads=[src_key, "cst"], writes=["ta"])
            P.op(ename, lambda: E.tensor_tensor(out=tb3, in0=t1, in1=sinb, op=ALU.mult), reads=[src_key, "cst"], writes=["tb"])
            P.op(ename, lambda: E.tensor_tensor(out=dst2, in0=ta3, in1=tb3, op=ALU.add), reads=["ta", "tb"], writes=[dst_key])

        def k_rope(t_k32, slot, ti):
            kr4 = krb[ti][:].rearrange("p (h two f) -> p h two f", h=2, two=2)
            do_rope(G, "pool", t_k32, "k32", kr4[:, :, 0, :], kr4[:, :, 1, :], "kr%d" % ti, 2, slot)

        def k_transposes(slot, ti):
            kr = krb[ti]
            P.op("pe", [(lambda kv=kv: T.transpose(B4[0:64, kv * 128:(kv + 1) * 128], kr[:, kv * 64:(kv + 1) * 64], ident[:]))
                        for kv in range(2)], reads=["kr%d" % ti, "ident"], writes=["B4"])
            P.op("dve", lambda: V.tensor_copy(out=kTall[:, :, slot * 128:(slot + 1) * 128],
                                              in_=B4[0:64, 0:256].rearrange("p (a t) -> p a t", a=2)),
                 reads=["B4"], writes=["kTall"])

        def load_x(ps_i):
            tok0 = (1 + ps_i * TP) * 128
            xk = "xTp%d" % (ps_i % 2)
            P.dma("pool", xk, lambda: G.dma_start(out=xTpb[ps_i % 2][:, :, :], in_=xT[:, :, tok0:tok0 + TOK]), writes=[xk])

        wb, wkey = load_wblk("in", C_KA)
        P.dma("pool", "xTp1", lambda: G.dma_start(out=xTpb[1][:, :, 0:128], in_=xT[:, :, 0:128]), writes=["xTp1"])
        acc, akey = next_acc()
        P.op("pe", [(lambda k=k: T.matmul(acc[:, 0:256], lhsT=xTpb[1][:, k, 0:128], rhs=wb[:, k, 0:256],
                                          start=(k == 0), stop=(k == KC - 1))) for k in range(KC)],
             reads=["xTp1", wkey], writes=[akey])
        P.op("act", lambda: A.copy(out=k32[:, 0, :], in_=acc[:, 0:128]), reads=[akey], writes=["k32"])
        P.op("act", lambda: A.copy(out=vall[:, 0, :], in_=acc[:, 128:256]), reads=[akey], writes=["vall"])
        k_rope(k32[:, 0, :], 0, 0)
        k_transposes(0, 0)
        load_x(0)

        npass = nt // TP
        pending_stores = []
        deferred_ln = []
        for ps_i in range(npass):
            def bg_tm(blocks, pi=None):
                pi = ps_i if pi is None else pi
                xb_, xk_ = xTpb[pi % 2], "xTp%d" % (pi % 2)
                for col0, ncols, handler in blocks:
                    wb, wkey = load_wblk("in", col0)
                    for t in range(TP):
                        acc, akey = next_acc()
                        P.op("pe", [(lambda k=k: T.matmul(acc[:, 0:ncols], lhsT=xb_[:, k, t * 128:(t + 1) * 128],
                                                          rhs=wb[:, k, 0:ncols], start=(k == 0), stop=(k == KC - 1)))
                                    for k in range(KC)], reads=[xk_, wkey], writes=[akey])
                        handler(t, acc, akey)
                        yield

            def bg_fm():
                xb_, xk_ = xTpb[ps_i % 2], "xTp%d" % (ps_i % 2)
                for (col0, dstT, dkey) in ((C_QG, qgT, "qgT"), (C_KG, kgT, "kgT")):
                    wb, wkey = load_wblk("in", col0)
                    for h in range(4):
                        acc, akey = next_acc()
                        P.op("pe", [(lambda k=k: T.matmul(acc[:, 0:TOK], lhsT=wb[:, k, h * 128:(h + 1) * 128], rhs=xb_[:, k, :],
                                                          start=(k == 0), stop=(k == KC - 1))) for k in range(KC)],
                             reads=[xk_, wkey], writes=[akey])
                        P.op("act", lambda: A.copy(out=dstT[:, h, :], in_=acc[:, 0:TOK]), reads=[akey], writes=[dkey])
                        yield
                acc, akey = next_acc()
                P.op("pe", [(lambda k=k: T.matmul(acc[0:32, 0:TOK], lhsT=wgk[:, k, :], rhs=xb_[:, k, :],
                                                  start=(k == 0), stop=(k == KC - 1))) for k in range(KC)],
                     reads=[xk_, "wgk"], writes=[akey])
                P.op("act", lambda: A.copy(out=gkT[0:16, 0:TOK], in_=acc[0:16, 0:TOK]), reads=[akey], writes=["gkT"])
                yield

            def bg_outproj(pre):
                for cb in range(4):
                    wbo, wkeyo = pre[cb] if cb < len(pre) else load_wblk("out", cb * 512)
                    for t in range(TP):
                        gt = ps_i * TP + t
                        xi = (cb * TP + t) % 4
                        xs, xkey = xres[xi], "xres%d" % xi
                        P.dma("sp", xkey, lambda: nc.sync.dma_start(out=xs[:], in_=xtok[gt, :, cb * 512:(cb + 1) * 512]), writes=[xkey])
                        acc, akey = next_acc()
                        P.op("pe", [(lambda k=k: T.matmul(acc[:, 0:512], lhsT=yT[:, t, k, :], rhs=wbo[:, k, :],
                                                          start=(k == 0), stop=(k == KC - 1))) for k in range(KC)],
                             reads=["yT%d" % t, wkeyo], writes=[akey])
                        P.op("dve", lambda: V.scalar_tensor_tensor(out=r32[t][:, cb * 512:(cb + 1) * 512], in0=xs[:], scalar=ALPHA,
                                                                   in1=acc[:, 0:512], op0=ALU.mult, op1=ALU.add,
                                                                   accum_out=lnst[:, t, cb:cb + 1]),
                             reads=[xkey, akey, "lnst%d" % t], writes=["r32_%d" % t, "lnst%d" % t])
                        yield

            def h_q(j, pi=None):
                def h(t, acc, ak):
                    P.op("act", lambda: A.copy(out=q32[:, t, j * 512:(j + 1) * 512], in_=acc[:, 0:512]),
                         reads=[ak], writes=["q32_%d" % t])
                    if j == 1:
                        g = (ps_i if pi is None else pi) * TP + t + 1
                        q4 = qr[:, t, :].rearrange("p (h two f) -> p h two f", h=16, two=2)
                        do_rope(G, "pool", q32[:, t, :], "q32_%d" % t, q4[:, :, 0, :], q4[:, :, 1, :], "qr%d" % t, 16, g)
                return h

            def h_kv(t, acc, ak, pi=None):
                g = (ps_i if pi is None else pi) * TP + t + 1
                P.op("act", lambda: A.copy(out=k32[:, t, :], in_=acc[:, 0:128]), reads=[ak], writes=["k32"])
                P.op("act", lambda: A.copy(out=vall[:, g, :], in_=acc[:, 128:256]), reads=[ak], writes=["vall"])
                k_rope(k32[:, t, :], g, t)

            def h_silu(dst, dkey, j):
                return lambda t, acc, ak: P.op("act", lambda: A.activation(out=dst[:, t, j * 512:(j + 1) * 512], in_=acc[:, 0:512],
                                                                           func=AF.Silu), reads=[ak], writes=[dkey + "%d" % t])

            def h_vg(j):
                return lambda t, acc, ak: P.op("dve", lambda: V.tensor_copy(out=vg[:, t, j * 512:(j + 1) * 512], in_=acc[:, 0:512]),
                                               reads=[ak], writes=["vg%d" % t])

            def attn(t):
                g = ps_i * TP + t + 1
                mi = 0 if (ps_i == 0 and t == 0) else 1
                k_transposes(g, t)
                yield
                for half in range(2):
                    P.op("pe", [(lambda j=j: T.transpose(B4[0:64, j * 128:(j + 1) * 128],
                                                         qr[:, t, (half * 8 + j) * 64:(half * 8 + j + 1) * 64], ident[:]))
                                for j in range(8)], reads=["qr%d" % t, "ident"], writes=["B4"])
                    P.op("dve", lambda: V.tensor_copy(out=qT[:, half * 8:(half + 1) * 8, :].rearrange("p j t -> p (j t)"),
                                                      in_=B4[0:64, 0:1024]), reads=["B4"], writes=["qT"])
                    yield

                def scores(hp):
                    sbk, skey = B[2 + hp % 2], "B%d" % (2 + hp % 2)
                    kv = hp // 4
                    P.op("pe", [(lambda hh=hh: T.matmul(sbk[:, hh * 256:(hh + 1) * 256], lhsT=qT[:, 2 * hp + hh, :],
                                                        rhs=kTall[:, kv, (g - 1) * 128:(g + 1) * 128],
                                                        start=True, stop=True)) for hh in range(2)],
                         reads=["qT", "kTall"], writes=[skey])

                scores(0)
                for hp in range(8):
                    sbk, skey = B[2 + hp % 2], "B%d" % (2 + hp % 2)
                    kv = hp // 4
                    c2 = slice(2 * hp, 2 * hp + 2)
                    P.op("dve", lambda: V.tensor_reduce(out=mraw[:, c2], in_=sbk[:, 0:512].rearrange("p (h n) -> p h n", h=2),
                                                        axis=AX.X, op=ALU.max), reads=[skey], writes=["mraw"])
                    P.op("dve", lambda: V.scalar_tensor_tensor(out=msc[:, c2], in0=mraw[:, c2], scalar=0.125, in1=sinkb[:, c2],
                                                               op0=ALU.mult, op1=ALU.max),
                         reads=["mraw", "sinkb"], writes=["msc"])
                    P.op("dve", lambda: V.tensor_scalar(out=negm[:, c2], in0=msc[:, c2], scalar1=-1.0, scalar2=None, op0=ALU.mult),
                         reads=["msc"], writes=["negm"])
                    pe_, pkey = pexp[hp % 2], "pexp%d" % (hp % 2)
                    for hh in range(2):
                        P.op("act", (lambda hh=hh: A.activation(out=pe_[:, hh, :], in_=sbk[:, hh * 256:(hh + 1) * 256],
                                                                func=AF.Exp, bias=negm[:, 2 * hp + hh:2 * hp + hh + 1], scale=0.125)),
                             reads=[skey, "negm"], writes=[pkey])
                    if hp + 1 < 8:
                        scores(hp + 1)
                    yield
                    P.op("pe", [(lambda hh=hh, kt=kt: T.transpose(B4[:, (hh * 2 + kt) * 128:(hh * 2 + kt + 1) * 128],
                                                                  pe_[:, hh, kt * 128:(kt + 1) * 128], ident[:]))
                                for hh in range(2) for kt in range(2)], reads=[pkey, "ident"], writes=["B4"])
                    pt_, ptkey = pT[hp % 2], "pT%d" % (hp % 2)
                    P.op("dve", lambda: V.tensor_tensor(out=pt_[:], in0=B4[:, 0:512].rearrange("p (h k q) -> p h k q", h=2, k=2),
                                                        in1=mask16[:, mi, :, :].unsqueeze(1).to_broadcast([128, 2, 2, 128]),
                                                        op=ALU.mult), reads=["B4", "mask16"], writes=[ptkey])
                    obk, okey = (B5, "B5") if hp < 4 else (B6, "B6")
                    mm = []
                    for hh in range(2):
                        hl = (2 * hp + hh) % 8
                        for kt in range(2):
                            mm.append(lambda hh=hh, kt=kt, hl=hl: T.matmul(obk[:, hl * 64:(hl + 1) * 64], lhsT=pt_[:, hh, kt, :],
                                                                           rhs=vall[:, g - 1 + kt, kv * 64:(kv + 1) * 64],
                                                                           start=(kt == 0), stop=(kt == 1)))
                        for kt in range(2):
                            mm.append(lambda hh=hh, kt=kt: T.matmul(B7[:, 2 * hp + hh:2 * hp + hh + 1], lhsT=pt_[:, hh, kt, :],
                                                                    rhs=ones16[:, 0:1], start=(kt == 0), stop=(kt == 1)))
                    P.op("pe", mm, reads=[ptkey, "vall", "ones16"], writes=[okey, "B7"])
                    yield
                P.op("dve", lambda: V.tensor_tensor(out=dsm[:], in0=sinkb[:], in1=msc[:], op=ALU.subtract),
                     reads=["sinkb", "msc"], writes=["dsm"])
                P.op("act", lambda: A.activation(out=esk[:], in_=dsm[:], func=AF.Exp), reads=["dsm"], writes=["esk"])
                P.op("dve", lambda: V.tensor_tensor(out=den[:], in0=B7[:, 0:16], in1=esk[:], op=ALU.add),
                     reads=["B7", "esk"], writes=["den"])
                P.op("dve", lambda: V.reciprocal(out=rden[:], in_=den[:]), reads=["den"], writes=["rden"])
                for j, (obk, okey) in enumerate(((B5, "B5"), (B6, "B6"))):
                    P.op("dve", (lambda j=j, obk=obk: V.tensor_tensor(
                        out=otmp[:, j * 512:(j + 1) * 512].rearrange("p (h d) -> p h d", h=8),
                        in0=obk[:, 0:512].rearrange("p (h d) -> p h d", h=8),
                        in1=rden[:, j * 8:(j + 1) * 8].unsqueeze(2).to_broadcast([128, 8, 64]), op=ALU.mult)),
                        reads=[okey, "rden"], writes=["otmp"])
                yield "drain"
                P.op("dve", lambda: V.tensor_tensor(out=y16[:, t, 0:1024], in0=otmp[:], in1=sza[:, t, :], op=ALU.mult),
                     reads=["otmp", "sza%d" % t], writes=["y16_%d" % t])
                yield

            def gla(t):
                ts_ = slice(t * 128, (t + 1) * 128)
                P.op("pe", lambda: T.matmul(B[2][:, 0:512], lhsT=gkT[0:32, ts_], rhs=waug[0:32, :], start=True, stop=True),
                     reads=["gkT", "waug"], writes=["B2"])
                yield
                P.op("act", lambda: A.activation(out=e1[:], in_=B[2][:, 0:512], func=AF.Exp, scale=-1.0), reads=["B2"], writes=["e1"])
                P.op("act", lambda: A.activation(out=spl[:], in_=e1[:], func=AF.Ln, bias=1.0, scale=1.0), reads=["e1"], writes=["spl"])
                P.op("pe", [(lambda h=h: T.matmul(B[3][:, h * 128:(h + 1) * 128], lhsT=spl[:, h * 128:(h + 1) * 128],
                                                  rhs=uincl[:], start=True, stop=True)) for h in range(4)],
                     reads=["spl", "uincl"], writes=["B3"])
                yield
                P.op("dve", lambda: V.tensor_scalar(out=nbl[:], in0=B[3][:, 0:512].rearrange("p (h t) -> p h t", h=4)[:, :, 127],
                                                    scalar1=-1.0 / 16.0, scalar2=None, op0=ALU.mult), reads=["B3"], writes=["nbl"])
                P.op("act", lambda: A.activation(out=eq[:].rearrange("p h t -> p (h t)"), in_=B[3][:, 0:512], func=AF.Exp,
                                                 scale=-1.0 / 16.0), reads=["B3"], writes=["eq"])
                P.op("act", lambda: A.activation(out=ek[:].rearrange("p h t -> p (h t)"), in_=B[3][:, 0:512], func=AF.Exp,
                                                 scale=1.0 / 16.0), reads=["B3"], writes=["ek"])
                P.op("act", lambda: A.activation(out=tot[:], in_=nbl[:], func=AF.Exp), reads=["nbl"], writes=["tot"])
                for h in range(4):
                    P.op("act", (lambda h=h: A.activation(out=ekd[:, h, :], in_=B[3][:, h * 128:(h + 1) * 128], func=AF.Exp,
                                                          bias=nbl[:, h:h + 1], scale=1.0 / 16.0)),
                         reads=["B3", "nbl"], writes=["ekd"])
                P.op("dve", lambda: V.scalar_tensor_tensor(out=qdec[:], in0=qgT[:, :, ts_], scalar=128.0 ** -0.5, in1=eq[:],
                                                           op0=ALU.mult, op1=ALU.mult), reads=["qgT", "eq"], writes=["qdec"])
                P.op("dve", lambda: V.tensor_tensor(out=kneg[:], in0=kgT[:, :, ts_], in1=ek[:], op=ALU.mult),
                     reads=["kgT", "ek"], writes=["kneg"])
                P.op("pe", [(lambda h=h: T.matmul(B[2][:, h * 128:(h + 1) * 128], lhsT=kneg[:, h, :], rhs=qdec[:, h, :],
                                                  start=True, stop=True)) for h in range(4)],
                     reads=["kneg", "qdec"], writes=["B2"])
                P.op("dve", lambda: V.tensor_tensor(out=kdT[:], in0=kgT[:, :, ts_], in1=ekd[:], op=ALU.mult),
                     reads=["kgT", "ekd"], writes=["kdT"])
                P.op("pe", [(lambda h=h: T.transpose(B4[:, h * 128:(h + 1) * 128], kdT[:, h, :], ident[:])) for h in range(4)],
                     reads=["kdT", "ident"], writes=["B4"])
                yield
                P.op("dve", lambda: V.tensor_tensor(out=atm[:], in0=B[2][:, 0:512].rearrange("p (h i) -> p h i", h=4),
                                                    in1=uincl16[:].unsqueeze(1).to_broadcast([128, 4, 128]), op=ALU.mult),
                     reads=["B2", "uincl16"], writes=["atm"])
                P.op("act", lambda: A.copy(out=kdec[:].rearrange("p h t -> p (h t)"), in_=B4[:, 0:512]), reads=["B4"], writes=["kdec"])
                mm = []
                for h in range(4):
                    obk = B5 if h < 2 else B6
                    sl = slice((h % 2) * 256, (h % 2) * 256 + 256)
                    mm.append(lambda h=h, obk=obk, sl=sl: T.matmul(obk[:, sl], lhsT=atm[:, h, :], rhs=vg[:, t, h * 256:(h + 1) * 256],
                                                                   start=True, stop=False))
                    mm.append(lambda h=h, obk=obk, sl=sl: T.matmul(obk[:, sl], lhsT=qdec[:, h, :], rhs=S16[:, h * 256:(h + 1) * 256],
                                                                   start=False, stop=True))
                P.op("pe", mm, reads=["atm", "vg%d" % t, "qdec", "S16"], writes=["B5", "B6"])
                P.op("pe", [(lambda h=h: T.matmul((B7 if h < 2 else B[3])[:, (h % 2) * 256:(h % 2) * 256 + 256], lhsT=kdec[:, h, :],
                                                  rhs=vg[:, t, h * 256:(h + 1) * 256], start=True, stop=True)) for h in range(4)],
                     reads=["kdec", "vg%d" % t], writes=["B7", "B3"])
                yield
                P.op("dve", lambda: V.memset(ss[:], 0.0), writes=["ss"])
                for h in range(4):
                    obk, okey = (B5, "B5") if h < 2 else (B6, "B6")
                    sl = slice((h % 2) * 256, (h % 2) * 256 + 256)
                    P.op("act", (lambda h=h, obk=obk, sl=sl: A.activation(out=sqj[:], in_=obk[:, sl], func=AF.Square,
                                                                          accum_out=ss[:, h:h + 1])),
                         reads=[okey, "ss"], writes=["sqj", "ss"])
                P.op("dve", lambda: V.tensor_scalar(out=rstd[:], in0=ss[:], scalar1=1.0 / 256.0, scalar2=RMS_EPS,
                                                    op0=ALU.mult, op1=ALU.add), reads=["ss"], writes=["rstd"])
                P.op("act", lambda: A.activation(out=rstd[:], in_=rstd[:], func=AF.Sqrt), reads=["rstd"], writes=["rstd"])
                P.op("dve", lambda: V.reciprocal(out=rstd[:], in_=rstd[:]), reads=["rstd"], writes=["rstd"])
                for h in range(4):
                    obk, okey = (B5, "B5") if h < 2 else (B6, "B6")
                    sl = slice((h % 2) * 256, (h % 2) * 256 + 256)
                    P.op("dve", (lambda h=h, obk=obk, sl=sl: V.scalar_tensor_tensor(out=otmp[:, h * 256:(h + 1) * 256], in0=obk[:, sl],
                                                                                    scalar=rstd[:, h:h + 1], in1=normwb[:],
                                                                                    op0=ALU.mult, op1=ALU.mult)),
                         reads=[okey, "rstd", "normwb"], writes=["otmp"])
                P.op("dve", lambda: V.tensor_tensor(out=y16[:, t, 1024:2048], in0=otmp[:], in1=szg[:, t, :], op=ALU.mult),
                     reads=["otmp", "szg%d" % t], writes=["y16_%d" % t])
                for h in range(4):
                    src = (B7 if h < 2 else B[3])[:, (h % 2) * 256:(h % 2) * 256 + 256]
                    P.op("dve", (lambda h=h, src=src: V.scalar_tensor_tensor(out=S32[:, h * 256:(h + 1) * 256],
                                                                             in0=S32[:, h * 256:(h + 1) * 256],
                                                                             scalar=tot[:, h:h + 1], in1=src,
                                                                             op0=ALU.mult, op1=ALU.add)),
                         reads=["S32", "tot", "B7", "B3"], writes=["S32"])
                P.op("act", lambda: A.copy(out=S16[:], in_=S32[:]), reads=["S32"], writes=["S16"])
                yield
                for qd in range(2):
                    P.op("pe", [(lambda j=j: T.transpose(B4[:, (j % 8) * 128:(j % 8 + 1) * 128], y16[:, t, j * 128:(j + 1) * 128], ident[:]))
                                for j in range(qd * 8, qd * 8 + 8)], reads=["y16_%d" % t, "ident"], writes=["B4"])
                    P.op("act", (lambda qd=qd: A.copy(out=yT[:, t, qd * 8:(qd + 1) * 8, :].rearrange("p j t -> p (j t)"),
                                                      in_=B4[:, 0:1024])), reads=["B4"], writes=["yT%d" % t])
                    yield

            def ln_gen(t, gt):
                rt, rkey, lk = r32[t], "r32_%d" % t, "lnst%d" % t
                L = lambda a, b: lnst[:, t, a:b]
                P.op("dve", lambda: V.tensor_reduce(out=L(4, 5), in_=L(0, 4), axis=AX.X, op=ALU.add), reads=[lk], writes=[lk])
                P.op("dve", lambda: V.tensor_scalar(out=L(5, 6), in0=L(4, 5), scalar1=-1.0 / D, scalar2=None, op0=ALU.mult),
                     reads=[lk], writes=[lk])
                P.op("act", lambda: A.activation(out=y16[:, t, :], in_=rt[:], func=AF.Square, bias=L(5, 6), scale=1.0,
                                                 accum_out=L(6, 7)), reads=[rkey, lk], writes=["y16_%d" % t, lk])
                yield
                P.op("dve", lambda: V.tensor_scalar(out=L(7, 8), in0=L(6, 7), scalar1=1.0 / D, scalar2=LN_EPS,
                                                    op0=ALU.mult, op1=ALU.add), reads=[lk], writes=[lk])
                P.op("act", lambda: A.activation(out=L(7, 8), in_=L(7, 8), func=AF.Sqrt), reads=[lk], writes=[lk])
                P.op("dve", lambda: V.reciprocal(out=L(7, 8), in_=L(7, 8)), reads=[lk], writes=[lk])
                P.op("dve", lambda: V.scalar_tensor_tensor(out=rt[:], in0=rt[:], scalar=L(5, 6), in1=lngb[:],
                                                           op0=ALU.add, op1=ALU.mult), reads=[rkey, lk, "lngb"], writes=[rkey])
                yield
                P.op("dve", lambda: V.scalar_tensor_tensor(out=rt[:], in0=rt[:], scalar=L(7, 8), in1=lnbb[:],
                                                           op0=ALU.mult, op1=ALU.add), reads=[rkey, lk, "lnbb"], writes=[rkey])
                pending_stores.append((t, gt))
                yield

            def flush_stores():
                while pending_stores:
                    t_, gt_ = pending_stores.pop(0)
                    P.dma("sp", "out%d" % t_, (lambda t_=t_, gt_=gt_: nc.sync.dma_start(out=out[gt_, :, :], in_=r32[t_][:])),
                          reads=["r32_%d" % t_])

            def s1_blocks(pi):
                return [(C_QA, 512, h_q(0, pi)), (C_QA + 512, 512, h_q(1, pi)),
                        (C_KA, 256, (lambda t, acc, ak, pi=pi: h_kv(t, acc, ak, pi)))]

            if ps_i == 0:
                run(bg_tm(s1_blocks(0), 0))
            if ps_i + 1 < npass:
                load_x(ps_i + 1)
            run_mix(attn(0), chain(deferred_ln.pop(0) if deferred_ln else iter(()),
                                   bg_tm([(C_ZA, 512, h_silu(sza, "sza", 0)), (C_ZA + 512, 512, h_silu(sza, "sza", 1)),
                                          (C_VG, 512, h_vg(0)), (C_VG + 512, 512, h_vg(1))])), ratio=2)
            flush_stores()
            run_mix(attn(1), chain(deferred_ln.pop(0) if deferred_ln else iter(()),
                                   bg_tm([(C_ZG, 512, h_silu(szg, "szg", 0)), (C_ZG + 512, 512, h_silu(szg, "szg", 1))]),
                                   bg_fm()), ratio=2)
            flush_stores()
            for t in range(TP):
                P.op("dve", (lambda t=t: V.memset(lnst[:, t, :], 0.0)), writes=["lnst%d" % t])
            if ps_i + 1 < npass:
                run_mix(gla(0), bg_tm(s1_blocks(ps_i + 1), ps_i + 1), ratio=1)
            else:
                run(gla(0))
            pre = [load_wblk("out", 0), load_wblk("out", 512)]
            run(gla(1))
            run(bg_outproj(pre))
            deferred_ln.append(ln_gen(0, ps_i * TP + 0))
            deferred_ln.append(ln_gen(1, ps_i * TP + 1))
        while deferred_ln:
            run(deferred_ln.pop(0))
        flush_stores()


def _kmajor(a):
    n = a.shape[1]
    return np.ascontiguousarray(a.reshape(KC, 128, n).transpose(1, 0, 2))


def _host_consts():
    ident = np.eye(128, dtype=np.float32)
    uincl = (np.arange(128)[:, None] <= np.arange(128)[None, :]).astype(np.float32)
    ones = np.ones((128, 128), np.float32)
    usuf = (np.arange(128)[:, None] > np.arange(128)[None, :]).astype(np.float32)
    consts = np.ascontiguousarray(np.stack([ident, uincl, ones, usuf], axis=1))
    k = np.arange(128)[:, None, None]
    kt = np.arange(2)[None, :, None]
    q = np.arange(128)[None, None, :]
    kj = kt * 128 + k
    reg = ((kj > q) & (kj <= q + 128)).astype(np.float32)
    first = reg * (kj >= 128)
    return consts, reg, first


def kernel(x, w_in, w_gk_up, b_gk, attn_sinks, gla_norm_w, w_out, ln_g, ln_b, _ncores=NCORES, _npre=NPRE):
    x = np.asarray(x, np.float32)[0]
    w = np.asarray(w_in, np.float32)[0]
    perm = np.concatenate([np.arange(0, 1024), np.arange(1024, 1152), np.arange(1152, 1280), np.arange(1280, 2304),
                           np.arange(3328, 4352), np.arange(4352, 5376), np.arange(2304, 2816), np.arange(2816, 3328),
                           np.arange(5376, 5392)])
    w_l = _kmajor(np.concatenate([w[:, perm], np.zeros((D, 16), np.float32)], axis=1))
    wo_l = _kmajor(np.asarray(w_out, np.float32)[0])
    waug = np.zeros((32, 512), np.float32)
    waug[0:16] = np.asarray(w_gk_up, np.float32)[0]
    waug[16] = np.asarray(b_gk, np.float32)[0]
    consts, mreg, mfirst = _host_consts()
    sinks = np.ascontiguousarray(np.broadcast_to(np.asarray(attn_sinks, np.float32)[0][None, :], (128, 16)))
    normw = np.ascontiguousarray(np.broadcast_to(np.asarray(gla_norm_w, np.float32)[0][None, :], (128, 256)))
    lng = np.ascontiguousarray(np.broadcast_to(np.asarray(ln_g, np.float32)[0][None, :], (128, D)))
    lnb = np.ascontiguousarray(np.broadcast_to(np.asarray(ln_b, np.float32)[0][None, :], (128, D)))
    inv_freq = (1.0 / (10000.0 ** (np.arange(0, 32, dtype=np.float32) * 2.0 / 64.0))).astype(np.float32)
    xTfull = np.ascontiguousarray(x.T)
    in_maps = []
    for c in range(_ncores):
        s0 = c * OWN
        xt = np.zeros((D, OWN + 128), np.float32)
        if c > 0:
            xt[:, 0:128] = xTfull[:, s0 - 128:s0]
        xt[:, 128:] = xTfull[:, s0:s0 + OWN]
        npre_tok = _npre * 128
        xp = np.zeros((D, max(npre_tok, 128)), np.float32)
        if _npre > 0 and s0 > 0:
            xp[:, npre_tok - s0:] = xTfull[:, 0:s0]
        pos = (np.arange(s0 - 128, s0 + OWN)).astype(np.float32)
        ang = pos[:, None] * inv_freq[None, :]
        cs = np.stack([np.cos(ang), np.sin(ang)], axis=1).astype(np.float32)
        cs = np.ascontiguousarray(cs.reshape(NT + 1, 128, 2, 32).transpose(1, 0, 2, 3))
        masks = np.ascontiguousarray(np.stack([mfirst if c == 0 else mreg, mreg], axis=1))
        in_maps.append({
            "xT": _kmajor(xt),
            "xpre": np.ascontiguousarray(_kmajor(xp).reshape(128, KC, max(_npre, 1), 128).transpose(2, 0, 1, 3)
                                         .reshape(max(_npre, 1), 128, KC * 128)),
            "xtok": np.ascontiguousarray(x[s0:s0 + OWN].reshape(NT, 128, D)),
            "xptok": np.ascontiguousarray(xp.T.reshape(max(_npre, 1), 128, D)),
            "w_in": w_l, "w_out": wo_l, "w_aug": waug, "cs": cs, "masks": masks, "consts": consts,
            "sinks": sinks, "normw": normw, "lng": lng, "lnb": lnb,
        })
    nc = build(npre=_npre, nt=NT)
    res = run_bass_kernel_spmd(nc, in_maps, core_ids=list(range(_ncores)))
    outs = [np.asarray(r["out"]).reshape(OWN, D) for r in res.results]
    return np.concatenate(outs, axis=0)[None].astype(np.float32)
```

```python
import contextlib
import math
import numpy as np
import concourse.bass as bass
import concourse.mybir as mybir
from concourse.bass_utils import run_bass_kernel_spmd

F32 = mybir.dt.float32
BF16 = mybir.dt.bfloat16
AF = mybir.ActivationFunctionType
ALU = mybir.AluOpType
AX = mybir.AxisListType

NCORES = 8
D = 2048
SEQ = 16384
OWN = SEQ // NCORES
NT = OWN // 128
NPRE = (SEQ - OWN) // 128
TP = 2
TOK = TP * 128
KC = D // 128
WCOLS = 5408
C_QA, C_KA, C_VA, C_ZA, C_VG, C_ZG, C_QG, C_KG, C_GK = 0, 1024, 1152, 1280, 2304, 3328, 4352, 4864, 5376
ALPHA = 2.0 ** 0.25
WBLOCKS = [("in", C_QA, 512), ("in", C_QA + 512, 512), ("in", C_KA, 256), ("in", C_ZA, 512), ("in", C_ZA + 512, 512),
           ("in", C_VG, 512), ("in", C_VG + 512, 512), ("in", C_ZG, 512), ("in", C_ZG + 512, 512),
           ("in", C_QG, 512), ("in", C_KG, 512),
           ("out", 0, 512), ("out", 512, 512), ("out", 1024, 512), ("out", 1536, 512)]
WB_IDX = {(src, c0): i for i, (src, c0, n) in enumerate(WBLOCKS)}
LN_EPS = 1e-5
RMS_EPS = 1e-5


class Prog:
    def __init__(self, nc, stack):
        self.nc = nc
        self.stack = stack
        self.engs = {"pe": nc.tensor, "act": nc.scalar, "dve": nc.vector, "pool": nc.gpsimd, "sp": nc.sync}
        self.sem = {}
        self.cnt = {}
        self.waited = {}
        self.lastw = {}
        self.readers = {}
        self.ninst = {e: 0 for e in self.engs}
        for e in ("pe", "act", "dve", "pool"):
            self._mksem("E_" + e)

    def _mksem(self, name):
        self.sem[name] = self.stack.enter_context(self.nc.semaphore(name))
        self.cnt[name] = 0

    def _deps(self, reads, writes):
        deps = {}

        def add(tok):
            if tok is None:
                return
            s, v = tok
            if deps.get(s, 0) < v:
                deps[s] = v
        for r in reads:
            add(self.lastw.get(r))
        for w in writes:
            add(self.lastw.get(w))
            for t in self.readers.get(w, ()):
                add(t)
        return deps

    def _wait(self, eng, deps):
        e = self.engs[eng]
        for s, v in deps.items():
            if self.waited.get((eng, s), 0) >= v:
                continue
            if eng == "pe" and s == "E_pe":
                continue
            e.wait_ge(self.sem[s], v)
            self.ninst[eng] += 1
            self.waited[(eng, s)] = v

    def _commit(self, tok, reads, writes):
        for w in writes:
            self.lastw[w] = tok
            self.readers[w] = []
        for r in reads:
            self.readers.setdefault(r, []).append(tok)

    PSUM_KEYS = frozenset("B%d" % i for i in range(8))

    def op(self, eng, fns, reads=(), writes=()):
        if callable(fns):
            fns = [fns]
        writes = list(writes) + [r for r in reads if r in self.PSUM_KEYS and r not in writes]
        self._wait(eng, self._deps(reads, writes))
        ins = None
        for f in fns:
            ins = f()
            self.ninst[eng] += 1
        s = "E_" + eng
        self.cnt[s] += 1
        ins.then_inc(self.sem[s], 1)
        self._commit((s, self.cnt[s]), reads, writes)

    def dma(self, eng, slot, fns, reads=(), writes=()):
        if callable(fns):
            fns = [fns]
        s = "D_" + slot
        if s not in self.sem:
            self._mksem(s)
        self._wait(eng, self._deps(reads, writes))
        for f in fns:
            ins = f()
            self.ninst[eng] += 1
            self.cnt[s] += 16
            ins.then_inc(self.sem[s], 16)
        self._commit((s, self.cnt[s]), reads, writes)

    def barrier(self):
        for eng in ("pe", "act", "dve", "pool", "sp"):
            e = self.engs[eng]
            for s, v in self.cnt.items():
                if v > 0 and self.waited.get((eng, s), 0) < v:
                    e.wait_ge(self.sem[s], v)
                    self.waited[(eng, s)] = v

    def finish(self, eng="sp"):
        e = self.engs[eng]
        for s, v in self.cnt.items():
            if v > 0 and self.waited.get((eng, s), 0) < v:
                e.wait_ge(self.sem[s], v)
                self.waited[(eng, s)] = v


class _Stop(Exception):
    pass


STAGE = 99
SKIPK = False
DBG = 0


def _stage(n):
    if STAGE < n:
        raise _Stop()


def build(npre=NPRE, nt=NT):
    nc = bass.Bass("TRN2", target_bir_lowering=False)
    din = lambda name, shape: nc.dram_tensor(name, shape, F32, kind="ExternalInput").ap()
    xT = din("xT", [128, KC, (nt + 1) * 128])
    xpre = din("xpre", [max(npre, 1), 128, KC * 128])
    xtok = din("xtok", [nt, 128, D])
    xptok = din("xptok", [max(npre, 1), 128, D])
    w_in = din("w_in", [128, KC, WCOLS])
    w_out = din("w_out", [128, KC, D])
    w_aug = din("w_aug", [32, 512])
    cs = din("cs", [128, nt + 1, 2, 32])
    masks = din("masks", [128, 2, 2, 128])
    consts = din("consts", [128, 4, 128])
    sinks = din("sinks", [128, 16])
    normw = din("normw", [128, 256])
    lng = din("lng", [128, D])
    lnb = din("lnb", [128, D])
    out = nc.dram_tensor("out", [nt, 128, D], F32, kind="ExternalOutput").ap()
    wsc = nc.dram_tensor("wsc", [len(WBLOCKS), 128, KC * 512], BF16, kind="Internal").ap()

    with contextlib.ExitStack() as st:
        P = Prog(nc, st)
        try:
            _build_body(nc, st, P, npre, nt, locals())
        except _Stop:
            pass
        P.finish("sp")
    return nc


def _build_body(nc, st, P, npre, nt, env):
    xT, xpre, xtok, w_in, w_out, w_aug, cs, masks, consts, sinks, normw, lng, lnb, out, wsc, xptok = [env[k] for k in ('xT', 'xpre', 'xtok', 'w_in', 'w_out', 'w_aug', 'cs', 'masks', 'consts', 'sinks', 'normw', 'lng', 'lnb', 'out', 'wsc', 'xptok')]
    if True:
        sb = lambda name, shape, dt: st.enter_context(nc.sbuf_tensor(name, shape, dt))
        psb = lambda name, dt=F32: st.enter_context(nc.psum_tensor(name, [128, 512 if dt == F32 else 1024], dt))
        V, A, T, G = nc.vector, nc.scalar, nc.tensor, nc.gpsimd

        B = [psb("B%d" % i) for i in range(4)]
        BZ = st.enter_context(nc.psum_tensor("BZ", [128, 2048], F32))
        B4 = BZ[:, 0:512].bitcast(BF16)
        B5, B6, B7 = BZ[:, 512:1024], BZ[:, 1024:1536], BZ[:, 1536:2048]

        ident = sb("ident", [128, 128], BF16)
        uincl = sb("uincl", [128, 128], F32)
        uincl16 = sb("uincl16", [128, 128], BF16)
        ones16 = sb("ones16", [128, 128], BF16)
        mask16 = sb("mask16", [128, 2, 2, 128], BF16)
        sinkb = sb("sinkb", [128, 16], F32)
        normwb = sb("normwb", [128, 256], F32)
        lngb = sb("lngb", [128, D], F32)
        lnbb = sb("lnbb", [128, D], F32)
        cst = sb("cst", [128, nt + 1, 2, 32], F32)
        waug = sb("waug", [32, 512], F32)
        S32 = sb("S32", [128, 1024], F32)
        S16 = sb("S16", [128, 1024], BF16)
        gkT = sb("gkT", [128, TOK], F32)
        e1 = sb("e1", [128, 512], F32)
        spl = sb("spl", [128, 512], F32)
        nbl = sb("nbl", [128, 4], F32)
        tot = sb("tot", [128, 4], F32)
        ekd = sb("ekd", [128, 4, 128], F32)
        kdT = sb("kdT", [128, 4, 128], BF16)
        kdec = sb("kdec", [128, 4, 128], BF16)
        wgk = sb("wgk", [128, KC, 32], BF16)

        P.dma("pool", "c_ident", lambda: G.dma_start(out=ident[:], in_=consts[:, 0, :]), writes=["ident"])
        identf = sb("identf", [128, 128], F32)
        P.dma("sp", "c_identf", lambda: nc.sync.dma_start(out=identf[:], in_=consts[:, 0, :]), writes=["identf"])
        P.dma("sp", "c_uincl", lambda: nc.sync.dma_start(out=uincl[:], in_=consts[:, 1, :]), writes=["uincl"])
        P.dma("pool", "c_uincl16", lambda: G.dma_start(out=uincl16[:], in_=consts[:, 1, :]), writes=["uincl16"])
        P.dma("pool", "c_ones", lambda: G.dma_start(out=ones16[:], in_=consts[:, 2, :]), writes=["ones16"])
        P.dma("pool", "c_mask", lambda: G.dma_start(out=mask16[:], in_=masks[:, :, :, :]), writes=["mask16"])
        P.dma("sp", "c_sink", lambda: nc.sync.dma_start(out=sinkb[:], in_=sinks[:, :]), writes=["sinkb"])
        P.dma("sp", "c_normw", lambda: nc.sync.dma_start(out=normwb[:], in_=normw[:, :]), writes=["normwb"])
        P.dma("sp", "c_lng", lambda: nc.sync.dma_start(out=lngb[:], in_=lng[:, :]), writes=["lngb"])
        P.dma("sp", "c_lnb", lambda: nc.sync.dma_start(out=lnbb[:], in_=lnb[:, :]), writes=["lnbb"])
        P.dma("sp", "c_cs", lambda: nc.sync.dma_start(out=cst[:], in_=cs[:, :, :, :]), writes=["cst"])
        P.op("dve", lambda: V.memset(waug[:], 0.0), writes=["waug"])
        P.dma("sp", "c_waug", lambda: nc.sync.dma_start(out=waug[0:17, :], in_=w_aug[0:17, :]), writes=["waug"])
        P.dma("pool", "c_wgk", lambda: G.dma_start(out=wgk[:], in_=w_in[:, :, C_GK:C_GK + 32]), writes=["wgk"])
        P.op("dve", lambda: V.memset(S32[:], 0.0), writes=["S32"])
        P.op("dve", lambda: V.memset(S16[:], 0.0), writes=["S16"])
        P.op("dve", lambda: V.memset(gkT[:], 1.0), writes=["gkT", "gkT0", "gkT1"])

        def wsc_view(bi):
            n = WBLOCKS[bi][2]
            return wsc[bi, :, 0:KC * n].rearrange("p (k c) -> p k c", k=KC)

        wcast_jobs = list(range(len(WBLOCKS)))

        def emit_wcast(n=1):
            for _ in range(n):
                if not wcast_jobs:
                    return
                bi = wcast_jobs.pop(0)
                src, c0, ncol = WBLOCKS[bi]
                srcap = (w_in if src == "in" else w_out)[:, :, c0:c0 + ncol]
                P.dma("pool", "wcast", (lambda bi=bi, srcap=srcap: G.dma_start(out=wsc_view(bi), in_=srcap)), writes=["wsc"])
        _stage(1)
        if npre > 0:
            with contextlib.ExitStack() as st2:
                sb2 = lambda name, shape, dt: st2.enter_context(nc.sbuf_tensor(name, shape, dt))
                wk = sb2("wk", [128, KC, 512], BF16)
                xpc = [sb2("xpc%d" % i, [128, KC * 128], BF16) for i in range(3)]
                xtk = [sb2("xtk%d" % i, [128, D], BF16) for i in range(3)]
                Z = sb2("Z", [128, 4, D], F32)
                ksb = [sb2("ksb%d" % i, [128, 512], F32) for i in range(2)]
                esuf = sb2("esuf", [128, 512], F32)
                kdpb = [sb2("kdp%d" % i, [128, 512], BF16) for i in range(2)]
                totb = [sb2("totp%d" % i, [128, 4], F32) for i in range(2)]
                usuf = sb2("usuf", [128, 128], F32)
                onesf = sb2("onesf", [128, 1], F32)
                P.dma("sp", "c_usuf", lambda: nc.sync.dma_start(out=usuf[:], in_=consts[:, 3, :]), writes=["usuf"])
                P.op("dve", lambda: V.memset(onesf[:], 1.0), writes=["onesf"])
                P.op("dve", lambda: V.memset(Z[:].rearrange("p h d -> p (h d)"), 0.0), writes=["Z0", "Z1", "Z2", "Z3"])
                P.dma("pool", "wk", lambda: G.dma_start(out=wk[:], in_=w_in[:, :, C_KG:C_KG + 512]), writes=["wk"])

                def p_load(c):
                    xk, tk = "xpc%d" % (c % 3), "xtk%d" % (c % 3)
                    P.dma("pool", xk, lambda: G.dma_start(out=xpc[c % 3][:], in_=xpre[c, :, :]), writes=[xk])
                    P.dma("pool", tk, lambda: G.dma_start(out=xtk[c % 3][:], in_=xptok[c, :, :]), writes=[tk])

                def p_inproj(c):
                    xb = xpc[c % 3][:].rearrange("p (k t) -> p k t", k=KC)
                    xk = "xpc%d" % (c % 3)
                    sl = c % 2
                    for q2 in range(2):
                        P.op("pe", [(lambda k=k: T.matmul(B[1][0:32, 0:128], lhsT=wgk[:, k, :], rhs=xb[:, k, :],
                                                          start=(k == 0), stop=(k == KC - 1))) for k in range(q2 * 8, q2 * 8 + 8)],
                             reads=["wgk", xk], writes=["B1"])
                        if q2 == 1:
                            P.op("act", lambda: A.copy(out=gkT[0:16, sl * 128:(sl + 1) * 128], in_=B[1][0:16, 0:128]),
                                 reads=["B1"], writes=["gkT%d" % sl])
                        yield
                    for q4 in range(4):
                        P.op("pe", [(lambda k=k: T.matmul(B[0][:, 0:512], lhsT=xb[:, k, :], rhs=wk[:, k, :],
                                                          start=(k == 0), stop=(k == KC - 1))) for k in range(q4 * 4, q4 * 4 + 4)],
                             reads=["wk", xk], writes=["B0"])
                        if q4 == 3:
                            P.op("act", lambda: A.copy(out=ksb[sl][:], in_=B[0][:, 0:512]), reads=["B0"], writes=["ksb%d" % sl])
                        yield

                def p_tail_head(c):
                    sl = c % 2
                    P.op("pe", lambda: T.matmul(B[2][:, 0:512], lhsT=gkT[0:32, sl * 128:(sl + 1) * 128], rhs=waug[0:32, :],
                                                start=True, stop=True), reads=["gkT%d" % sl, "waug"], writes=["B2"])
                    P.op("act", lambda: A.activation(out=e1[:], in_=B[2][:, 0:512], func=AF.Exp, scale=-1.0),
                         reads=["B2"], writes=["e1"])
                    P.op("act", lambda: A.activation(out=spl[:], in_=e1[:], func=AF.Ln, bias=1.0, scale=1.0),
                         reads=["e1"], writes=["spl"])
                    yield
                    P.op("pe", lambda: T.matmul(B[3][:, 0:512], lhsT=usuf[:], rhs=spl[:], start=True, stop=True),
                         reads=["usuf", "spl"], writes=["B3"])
                    P.op("pe", [(lambda h=h: T.matmul(B[2][:, h:h + 1], lhsT=spl[:, h * 128:(h + 1) * 128], rhs=onesf[:, 0:1],
                                                      start=True, stop=True)) for h in range(4)],
                         reads=["spl", "onesf"], writes=["B2"])
                    P.op("act", lambda: A.activation(out=esuf[:], in_=B[3][:, 0:512], func=AF.Exp, scale=-1.0 / 16.0),
                         reads=["B3"], writes=["esuf"])
                    P.op("act", lambda: A.activation(out=totb[sl][:], in_=B[2][:, 0:4], func=AF.Exp, scale=-1.0 / 16.0),
                         reads=["B2"], writes=["totp%d" % sl])
                    P.op("dve", lambda: V.tensor_tensor(out=kdpb[sl][:], in0=ksb[sl][:], in1=esuf[:], op=ALU.mult),
                         reads=["ksb%d" % sl, "esuf"], writes=["kdp%d" % sl])
                    yield

                def p_tail_z(c):
                    xt_, tk = xtk[c % 3], "xtk%d" % (c % 3)
                    sl = c % 2
                    kdp, tot_ = kdpb[sl], totb[sl]
                    for h in range(4):
                        for half in range(2):
                            keys = ["B4", "B5"] if half == 0 else ["B6", "B7"]
                            c0 = half * 1024
                            P.op("pe", [(lambda nb=nb: T.matmul(BZ[:, c0 + nb * 512:c0 + (nb + 1) * 512],
                                                                lhsT=kdp[:, h * 128:(h + 1) * 128],
                                                                rhs=xt_[:, c0 + nb * 512:c0 + (nb + 1) * 512],
                                                                start=True, stop=True)) for nb in range(2)],
                                 reads=["kdp%d" % sl, tk], writes=keys)
                            P.op("dve", lambda: V.scalar_tensor_tensor(out=Z[:, h, c0:c0 + 1024], in0=Z[:, h, c0:c0 + 1024],
                                                                       scalar=tot_[:, h:h + 1], in1=BZ[:, c0:c0 + 1024],
                                                                       op0=ALU.mult, op1=ALU.add),
                                 reads=["Z%d" % h, "totp%d" % sl] + keys, writes=["Z%d" % h])
                            yield

                def run_seq(order, gens):
                    for gi in order:
                        try:
                            next(gens[gi])
                        except StopIteration:
                            pass

                p_load(0)
                if npre > 1:
                    p_load(1)
                for _ in p_inproj(0):
                    pass
                for _ in p_tail_head(0):
                    pass
                for c in range(npre):
                    if c + 2 < npre:
                        p_load(c + 2)
                    if c % 4 == 1:
                        emit_wcast(1)
                    more = c + 1 < npre
                    gens = {"Z": p_tail_z(c), "I": p_inproj(c + 1) if more else iter(()),
                            "T": p_tail_head(c + 1) if more else iter(())}
                    run_seq(["Z", "I", "Z", "I", "Z", "T", "Z", "I", "Z", "I", "Z", "I", "Z", "I", "Z", "T", "Z", "I", "T"], gens)
                Zb = sb2("Zb", [128, 4, D], BF16)
                ZT = sb2("ZT", [128, 4, KC, 128], BF16)
                wv = sb2("wv", [128, KC, 1024], BF16)
                P.dma("pool", "wv", lambda: G.dma_start(out=wv[:], in_=w_in[:, :, C_VG:C_VG + 1024]), writes=["wv"])
                for h in range(4):
                    eng, E = ("act", A) if h % 2 == 0 else ("dve", V)
                    if eng == "act":
                        P.op("act", (lambda h=h: A.copy(out=Zb[:, h, :], in_=Z[:, h, :])), reads=["Z%d" % h], writes=["Zb"])
                    else:
                        P.op("dve", (lambda h=h: V.tensor_copy(out=Zb[:, h, :], in_=Z[:, h, :])), reads=["Z%d" % h], writes=["Zb"])
                for h in range(4):
                    for qd in range(2):
                        P.op("pe", [(lambda j=j: T.transpose(B4[:, (j % 8) * 128:(j % 8 + 1) * 128], Zb[:, h, j * 128:(j + 1) * 128], ident[:]))
                                    for j in range(qd * 8, qd * 8 + 8)], reads=["Zb", "ident"], writes=["B4"])
                        P.op("act", (lambda h=h, qd=qd: A.copy(out=ZT[:, h, qd * 8:(qd + 1) * 8, :].rearrange("p j t -> p (j t)"),
                                                               in_=B4[:, 0:1024])), reads=["B4"], writes=["ZT"])
                for h in range(4):
                    bk, bkey = B[h // 2], "B%d" % (h // 2)
                    P.op("pe", [(lambda k=k: T.matmul(bk[:, (h % 2) * 256:(h % 2) * 256 + 256], lhsT=ZT[:, h, k, :],
                                                      rhs=wv[:, k, h * 256:(h + 1) * 256], start=(k == 0), stop=(k == KC - 1)))
                                for k in range(KC)], reads=["ZT", "wv"], writes=[bkey])
                for j in range(2):
                    P.op("dve", (lambda j=j: V.tensor_copy(out=S32[:, j * 512:(j + 1) * 512], in_=B[j][:, 0:512])),
                         reads=["B%d" % j], writes=["S32"])
                P.op("act", lambda: A.copy(out=S16[:], in_=S32[:]), reads=["S32"], writes=["S16"])
                P.barrier()

        emit_wcast(len(WBLOCKS))
        xTpb = [sb("xTp%d" % i, [128, KC, TOK], BF16) for i in range(2)]
        xTp = xTpb[0]
        wblk = [sb("wblk%d" % i, [128, KC, 512], BF16) for i in range(2)]
        q32 = sb("q32", [128, TP, 1024], F32)
        k32 = sb("k32", [128, TP, 128], F32)
        qr = sb("qr", [128, TP, 1024], BF16)
        krb = [sb("kr%d" % i, [128, 128], BF16) for i in range(2)]
        ta = sb("ta", [128, 512], F32)
        tb = sb("tb", [128, 512], F32)
        sza = sb("sza", [128, TP, 1024], BF16)
        szg = sb("szg", [128, TP, 1024], BF16)
        vg = sb("vg", [128, TP, 1024], BF16)
        qgT = sb("qgT", [128, 4, TOK], F32)
        kgT = sb("kgT", [128, 4, TOK], F32)
        kTall = sb("kTall", [64, 2, (nt + 1) * 128], BF16)
        vall = sb("vall", [128, nt + 1, 128], BF16)
        qT = sb("qT", [64, 16, 128], BF16)
        pexp = [sb("pexp%d" % i, [128, 2, 256], BF16) for i in range(2)]
        pT = [sb("pT%d" % i, [128, 2, 2, 128], BF16) for i in range(2)]
        mraw = sb("mraw", [128, 16], F32)
        msc = sb("msc", [128, 16], F32)
        negm = sb("negm", [128, 16], F32)
        dsm = sb("dsm", [128, 16], F32)
        esk = sb("esk", [128, 16], F32)
        den = sb("den", [128, 16], F32)
        rden = sb("rden", [128, 16], F32)
        otmp = sb("otmp", [128, 1024], F32)
        eq = sb("eq", [128, 4, 128], F32)
        ek = sb("ek", [128, 4, 128], F32)
        qdec = sb("qdec", [128, 4, 128], BF16)
        kneg = sb("kneg", [128, 4, 128], BF16)
        atm = sb("atm", [128, 4, 128], BF16)
        sqj = sb("sqj", [128, 256], BF16)
        ss = sb("ss", [128, 4], F32)
        rstd = sb("rstd", [128, 4], F32)
        y16 = sb("y16", [128, TP, D], BF16)
        yT = sb("yT", [128, TP, KC, 128], BF16)
        xres = [sb("xres%d" % i, [128, 512], F32) for i in range(4)]
        r32 = [sb("r32_%d" % i, [128, D], F32) for i in range(TP)]
        lnst = sb("lnst", [128, TP, 8], F32)

        wslot = [0]

        def load_wblk(src, c0):
            bi = WB_IDX[(src, c0)]
            ncols = WBLOCKS[bi][2]
            i = wslot[0] % 2
            wslot[0] += 1
            key = "wblk%d" % i
            P.dma("sp", key, lambda: nc.sync.dma_start(out=wblk[i][:, :, 0:ncols], in_=wsc_view(bi)), reads=["wsc"], writes=[key])
            return wblk[i], key

        accs = [0]

        def next_acc():
            i = accs[0] % 2
            accs[0] += 1
            return B[i], "B%d" % i

        def run(gen):
            for _ in gen:
                pass

        def run_mix(fg, bg, ratio=2, start_after=0):
            fg_done = bg_done = False
            for _ in range(start_after):
                try:
                    next(fg)
                except StopIteration:
                    fg_done = True
                    break
            while not (fg_done and bg_done):
                for _ in range(ratio):
                    if fg_done:
                        break
                    try:
                        r = next(fg)
                        if r == "drain":
                            for _ in bg:
                                pass
                            bg_done = True
                    except StopIteration:
                        fg_done = True
                if not bg_done:
                    try:
                        next(bg)
                    except StopIteration:
                        bg_done = True

        def chain(*gens):
            for g_ in gens:
                yield from g_

        def do_rope(E, ename, src, src_key, dst1, dst2, dst_key, nh, slot):
            s4 = src.rearrange("p (h two f) -> p h two f", h=nh, two=2)
            t1, t2 = s4[:, :, 0, :], s4[:, :, 1, :]
            cosb = cst[:, slot, 0, :].unsqueeze(1).to_broadcast([128, nh, 32])
            sinb = cst[:, slot, 1, :].unsqueeze(1).to_broadcast([128, nh, 32])
            ta3 = ta[:, 0:nh * 32].rearrange("p (h f) -> p h f", h=nh)
            tb3 = tb[:, 0:nh * 32].rearrange("p (h f) -> p h f", h=nh)
            P.op(ename, lambda: E.tensor_tensor(out=ta3, in0=t1, in1=cosb, op=ALU.mult), reads=[src_key, "cst"], writes=["ta"])
            P.op(ename, lambda: E.tensor_tensor(out=tb3, in0=t2, in1=sinb, op=ALU.mult), reads=[src_key, "cst"], writes=["tb"])
            P.op(ename, lambda: E.tensor_tensor(out=dst1, in0=ta3, in1=tb3, op=ALU.subtract), reads=["ta", "tb"], writes=[dst_key])
            P.op(ename, lambda: E.tensor_tensor(out=ta3, in0=t2, in1=cosb, op=ALU.mult), reads=[src_key, "cst"], writes=["ta"])
            P.op(ename, lambda: E.tensor_tensor(out=tb3, in0=t1, in1=sinb, op=ALU.mult), reads=[src_key, "cst"], writes=["tb"])
            P.op(ename, lambda: E.tensor_tensor(out=dst2, in0=ta3, in1=tb3, op=ALU.add), reads=["ta", "tb"], writes=[dst_key])

        def k_rope(t_k32, slot, ti):
            kr4 = krb[ti][:].rearrange("p (h two f) -> p h two f", h=2, two=2)
            do_rope(G, "pool", t_k32, "k32", kr4[:, :, 0, :], kr4[:, :, 1, :], "kr%d" % ti, 2, slot)

        def k_transposes(slot, ti):
            kr = krb[ti]
            P.op("pe", [(lambda kv=kv: T.transpose(B4[0:64, kv * 128:(kv + 1) * 128], kr[:, kv * 64:(kv + 1) * 64], ident[:]))
                        for kv in range(2)], reads=["kr%d" % ti, "ident"], writes=["B4"])
            P.op("dve", lambda: V.tensor_copy(out=kTall[:, :, slot * 128:(slot + 1) * 128],
                                              in_=B4[0:64, 0:256].rearrange("p (a t) -> p a t", a=2)),
                 reads=["B4"], writes=["kTall"])

        def load_x(ps_i):
            tok0 = (1 + ps_i * TP) * 128
            xk = "xTp%d" % (ps_i % 2)
            P.dma("pool", xk, lambda: G.dma_start(out=xTpb[ps_i % 2][:, :, :], in_=xT[:, :, tok0:tok0 + TOK]), writes=[xk])

        wb, wkey = load_wblk("in", C_KA)
        P.dma("pool", "xTp1", lambda: G.dma_start(out=xTpb[1][:, :, 0:128], in_=xT[:, :, 0:128]), writes=["xTp1"])
        acc, akey = next_acc()
        P.op("pe", [(lambda k=k: T.matmul(acc[:, 0:256], lhsT=xTpb[1][:, k, 0:128], rhs=wb[:, k, 0:256],
                                          start=(k == 0), stop=(k == KC - 1))) for k in range(KC)],
             reads=["xTp1", wkey], writes=[akey])
        P.op("act", lambda: A.copy(out=k32[:, 0, :], in_=acc[:, 0:128]), reads=[akey], writes=["k32"])
        P.op("act", lambda: A.copy(out=vall[:, 0, :], in_=acc[:, 128:256]), reads=[akey], writes=["vall"])
        k_rope(k32[:, 0, :], 0, 0)
        k_transposes(0, 0)
        load_x(0)

        npass = nt // TP
        pending_stores = []
        deferred_ln = []
        for ps_i in range(npass):
            def bg_tm(blocks, pi=None):
                pi = ps_i if pi is None else pi
                xb_, xk_ = xTpb[pi % 2], "xTp%d" % (pi % 2)
                for col0, ncols, handler in blocks:
                    wb, wkey = load_wblk("in", col0)
                    for t in range(TP):
                        acc, akey = next_acc()
                        P.op("pe", [(lambda k=k: T.matmul(acc[:, 0:ncols], lhsT=xb_[:, k, t * 128:(t + 1) * 128],
                                                          rhs=wb[:, k, 0:ncols], start=(k == 0), stop=(k == KC - 1)))
                                    for k in range(KC)], reads=[xk_, wkey], writes=[akey])
                        handler(t, acc, akey)
                        yield

            def bg_fm():
                xb_, xk_ = xTpb[ps_i % 2], "xTp%d" % (ps_i % 2)
                for (col0, dstT, dkey) in ((C_QG, qgT, "qgT"), (C_KG, kgT, "kgT")):
                    wb, wkey = load_wblk("in", col0)
                    for h in range(4):
                        acc, akey = next_acc()
                        P.op("pe", [(lambda k=k: T.matmul(acc[:, 0:TOK], lhsT=wb[:, k, h * 128:(h + 1) * 128], rhs=xb_[:, k, :],
                                                          start=(k == 0), stop=(k == KC - 1))) for k in range(KC)],
                             reads=[xk_, wkey], writes=[akey])
                        P.op("act", lambda: A.copy(out=dstT[:, h, :], in_=acc[:, 0:TOK]), reads=[akey], writes=[dkey])
                        yield
                acc, akey = next_acc()
                P.op("pe", [(lambda k=k: T.matmul(acc[0:32, 0:TOK], lhsT=wgk[:, k, :], rhs=xb_[:, k, :],
                                                  start=(k == 0), stop=(k == KC - 1))) for k in range(KC)],
                     reads=[xk_, "wgk"], writes=[akey])
                P.op("act", lambda: A.copy(out=gkT[0:16, 0:TOK], in_=acc[0:16, 0:TOK]), reads=[akey], writes=["gkT"])
                yield

            def bg_outproj(pre):
                for cb in range(4):
                    wbo, wkeyo = pre[cb] if cb < len(pre) else load_wblk("out", cb * 512)
                    for t in range(TP):
                        gt = ps_i * TP + t
                        xi = (cb * TP + t) % 4
                        xs, xkey = xres[xi], "xres%d" % xi
                        P.dma("sp", xkey, lambda: nc.sync.dma_start(out=xs[:], in_=xtok[gt, :, cb * 512:(cb + 1) * 512]), writes=[xkey])
                        acc, akey = next_acc()
                        P.op("pe", [(lambda k=k: T.matmul(acc[:, 0:512], lhsT=yT[:, t, k, :], rhs=wbo[:, k, :],
                                                          start=(k == 0), stop=(k == KC - 1))) for k in range(KC)],
                             reads=["yT%d" % t, wkeyo], writes=[akey])
                        P.op("dve", lambda: V.scalar_tensor_tensor(out=r32[t][:, cb * 512:(cb + 1) * 512], in0=xs[:], scalar=ALPHA,
                                                                   in1=acc[:, 0:512], op0=ALU.mult, op1=ALU.add,
                                                                   accum_out=lnst[:, t, cb:cb + 1]),
                             reads=[xkey, akey, "lnst%d" % t], writes=["r32_%d" % t, "lnst%d" % t])
                        yield

            def h_q(j, pi=None):
                def h(t, acc, ak):
                    P.op("act", lambda: A.copy(out=q32[:, t, j * 512:(j + 1) * 512], in_=acc[:, 0:512]),
                         reads=[ak], writes=["q32_%d" % t])
                    if j == 1:
                        g = (ps_i if pi is None else pi) * TP + t + 1
                        q4 = qr[:, t, :].rearrange("p (h two f) -> p h two f", h=16, two=2)
                        do_rope(G, "pool", q32[:, t, :], "q32_%d" % t, q4[:, :, 0, :], q4[:, :, 1, :], "qr%d" % t, 16, g)
                return h

            def h_kv(t, acc, ak, pi=None):
                g = (ps_i if pi is None else pi) * TP + t + 1
                P.op("act", lambda: A.copy(out=k32[:, t, :], in_=acc[:, 0:128]), reads=[ak], writes=["k32"])
                P.op("act", lambda: A.copy(out=vall[:, g, :], in_=acc[:, 128:256]), reads=[ak], writes=["vall"])
                k_rope(k32[:, t, :], g, t)

            def h_silu(dst, dkey, j):
                return lambda t, acc, ak: P.op("act", lambda: A.activation(out=dst[:, t, j * 512:(j + 1) * 512], in_=acc[:, 0:512],
                                                                           func=AF.Silu), reads=[ak], writes=[dkey + "%d" % t])

            def h_vg(j):
                return lambda t, acc, ak: P.op("dve", lambda: V.tensor_copy(out=vg[:, t, j * 512:(j + 1) * 512], in_=acc[:, 0:512]),
                                               reads=[ak], writes=["vg%d" % t])

            def attn(t):
                g = ps_i * TP + t + 1
                mi = 0 if (ps_i == 0 and t == 0) else 1
                k_transposes(g, t)
                yield
                for half in range(2):
                    P.op("pe", [(lambda j=j: T.transpose(B4[0:64, j * 128:(j + 1) * 128],
                                                         qr[:, t, (half * 8 + j) * 64:(half * 8 + j + 1) * 64], ident[:]))
                                for j in range(8)], reads=["qr%d" % t, "ident"], writes=["B4"])
                    P.op("dve", lambda: V.tensor_copy(out=qT[:, half * 8:(half + 1) * 8, :].rearrange("p j t -> p (j t)"),
                                                      in_=B4[0:64, 0:1024]), reads=["B4"], writes=["qT"])
                    yield

                def scores(hp):
                    sbk, skey = B[2 + hp % 2], "B%d" % (2 + hp % 2)
                    kv = hp // 4
                    P.op("pe", [(lambda hh=hh: T.matmul(sbk[:, hh * 256:(hh + 1) * 256], lhsT=qT[:, 2 * hp + hh, :],
                                                        rhs=kTall[:, kv, (g - 1) * 128:(g + 1) * 128],
                                                        start=True, stop=True)) for hh in range(2)],
                         reads=["qT", "kTall"], writes=[skey])

                scores(0)
                for hp in range(8):
                    sbk, skey = B[2 + hp % 2], "B%d" % (2 + hp % 2)
                    kv = hp // 4
                    c2 = slice(2 * hp, 2 * hp + 2)
                    P.op("dve", lambda: V.tensor_reduce(out=mraw[:, c2], in_=sbk[:, 0:512].rearrange("p (h n) -> p h n", h=2),
                                                        axis=AX.X, op=ALU.max), reads=[skey], writes=["mraw"])
                    P.op("dve", lambda: V.scalar_tensor_tensor(out=msc[:, c2], in0=mraw[:, c2], scalar=0.125, in1=sinkb[:, c2],
                                                               op0=ALU.mult, op1=ALU.max),
                         reads=["mraw", "sinkb"], writes=["msc"])
                    P.op("dve", lambda: V.tensor_scalar(out=negm[:, c2], in0=msc[:, c2], scalar1=-1.0, scalar2=None, op0=ALU.mult),
                         reads=["msc"], writes=["negm"])
                    pe_, pkey = pexp[hp % 2], "pexp%d" % (hp % 2)
                    for hh in range(2):
                        P.op("act", (lambda hh=hh: A.activation(out=pe_[:, hh, :], in_=sbk[:, hh * 256:(hh + 1) * 256],
                                                                func=AF.Exp, bias=negm[:, 2 * hp + hh:2 * hp + hh + 1], scale=0.125)),
                             reads=[skey, "negm"], writes=[pkey])
                    if hp + 1 < 8:
                        scores(hp + 1)
                    yield
                    P.op("pe", [(lambda hh=hh, kt=kt: T.transpose(B4[:, (hh * 2 + kt) * 128:(hh * 2 + kt + 1) * 128],
                                                                  pe_[:, hh, kt * 128:(kt + 1) * 128], ident[:]))
                                for hh in range(2) for kt in range(2)], reads=[pkey, "ident"], writes=["B4"])
                    pt_, ptkey = pT[hp % 2], "pT%d" % (hp % 2)
                    P.op("dve", lambda: V.tensor_tensor(out=pt_[:], in0=B4[:, 0:512].rearrange("p (h k q) -> p h k q", h=2, k=2),
                                                        in1=mask16[:, mi, :, :].unsqueeze(1).to_broadcast([128, 2, 2, 128]),
                                                        op=ALU.mult), reads=["B4", "mask16"], writes=[ptkey])
                    obk, okey = (B5, "B5") if hp < 4 else (B6, "B6")
                    mm = []
                    for hh in range(2):
                        hl = (2 * hp + hh) % 8
                        for kt in range(2):
                            mm.append(lambda hh=hh, kt=kt, hl=hl: T.matmul(obk[:, hl * 64:(hl + 1) * 64], lhsT=pt_[:, hh, kt, :],
                                                                           rhs=vall[:, g - 1 + kt, kv * 64:(kv + 1) * 64],
                                                                           start=(kt == 0), stop=(kt == 1)))
                        for kt in range(2):
                            mm.append(lambda hh=hh, kt=kt: T.matmul(B7[:, 2 * hp + hh:2 * hp + hh + 1], lhsT=pt_[:, hh, kt, :],
                                                                    rhs=ones16[:, 0:1], start=(kt == 0), stop=(kt == 1)))
                    P.op("pe", mm, reads=[ptkey, "vall", "ones16"], writes=[okey, "B7"])
                    yield
                P.op("dve", lambda: V.tensor_tensor(out=dsm[:], in0=sinkb[:], in1=msc[:], op=ALU.subtract),
                     reads=["sinkb", "msc"], writes=["dsm"])
                P.op("act", lambda: A.activation(out=esk[:], in_=dsm[:], func=AF.Exp), reads=["dsm"], writes=["esk"])
                P.op("dve", lambda: V.tensor_tensor(out=den[:], in0=B7[:, 0:16], in1=esk[:], op=ALU.add),
                     reads=["B7", "esk"], writes=["den"])
                P.op("dve", lambda: V.reciprocal(out=rden[:], in_=den[:]), reads=["den"], writes=["rden"])
                for j, (obk, okey) in enumerate(((B5, "B5"), (B6, "B6"))):
                    P.op("dve", (lambda j=j, obk=obk: V.tensor_tensor(
                        out=otmp[:, j * 512:(j + 1) * 512].rearrange("p (h d) -> p h d", h=8),
                        in0=obk[:, 0:512].rearrange("p (h d) -> p h d", h=8),
                        in1=rden[:, j * 8:(j + 1) * 8].unsqueeze(2).to_broadcast([128, 8, 64]), op=ALU.mult)),
                        reads=[okey, "rden"], writes=["otmp"])
                yield "drain"
                P.op("dve", lambda: V.tensor_tensor(out=y16[:, t, 0:1024], in0=otmp[:], in1=sza[:, t, :], op=ALU.mult),
                     reads=["otmp", "sza%d" % t], writes=["y16_%d" % t])
                yield

            def gla(t):
                ts_ = slice(t * 128, (t + 1) * 128)
                P.op("pe", lambda: T.matmul(B[2][:, 0:512], lhsT=gkT[0:32, ts_], rhs=waug[0:32, :], start=True, stop=True),
                     reads=["gkT", "waug"], writes=["B2"])
                yield
                P.op("act", lambda: A.activation(out=e1[:], in_=B[2][:, 0:512], func=AF.Exp, scale=-1.0), reads=["B2"], writes=["e1"])
                P.op("act", lambda: A.activation(out=spl[:], in_=e1[:], func=AF.Ln, bias=1.0, scale=1.0), reads=["e1"], writes=["spl"])
                P.op("pe", [(lambda h=h: T.matmul(B[3][:, h * 128:(h + 1) * 128], lhsT=spl[:, h * 128:(h + 1) * 128],
                                                  rhs=uincl[:], start=True, stop=True)) for h in range(4)],
                     reads=["spl", "uincl"], writes=["B3"])
                yield
                P.op("dve", lambda: V.tensor_scalar(out=nbl[:], in0=B[3][:, 0:512].rearrange("p (h t) -> p h t", h=4)[:, :, 127],
                                                    scalar1=-1.0 / 16.0, scalar2=None, op0=ALU.mult), reads=["B3"], writes=["nbl"])
                P.op("act", lambda: A.activation(out=eq[:].rearrange("p h t -> p (h t)"), in_=B[3][:, 0:512], func=AF.Exp,
                                                 scale=-1.0 / 16.0), reads=["B3"], writes=["eq"])
                P.op("act", lambda: A.activation(out=ek[:].rearrange("p h t -> p (h t)"), in_=B[3][:, 0:512], func=AF.Exp,
                                                 scale=1.0 / 16.0), reads=["B3"], writes=["ek"])
                P.op("act", lambda: A.activation(out=tot[:], in_=nbl[:], func=AF.Exp), reads=["nbl"], writes=["tot"])
                for h in range(4):
                    P.op("act", (lambda h=h: A.activation(out=ekd[:, h, :], in_=B[3][:, h * 128:(h + 1) * 128], func=AF.Exp,
                                                          bias=nbl[:, h:h + 1], scale=1.0 / 16.0)),
                         reads=["B3", "nbl"], writes=["ekd"])
                P.op("dve", lambda: V.scalar_tensor_tensor(out=qdec[:], in0=qgT[:, :, ts_], scalar=128.0 ** -0.5, in1=eq[:],
                                                           op0=ALU.mult, op1=ALU.mult), reads=["qgT", "eq"], writes=["qdec"])
                P.op("dve", lambda: V.tensor_tensor(out=kneg[:], in0=kgT[:, :, ts_], in1=ek[:], op=ALU.mult),
                     reads=["kgT", "ek"], writes=["kneg"])
                P.op("pe", [(lambda h=h: T.matmul(B[2][:, h * 128:(h + 1) * 128], lhsT=kneg[:, h, :], rhs=qdec[:, h, :],
                                                  start=True, stop=True)) for h in range(4)],
                     reads=["kneg", "qdec"], writes=["B2"])
                P.op("dve", lambda: V.tensor_tensor(out=kdT[:], in0=kgT[:, :, ts_], in1=ekd[:], op=ALU.mult),
                     reads=["kgT", "ekd"], writes=["kdT"])
                P.op("pe", [(lambda h=h: T.transpose(B4[:, h * 128:(h + 1) * 128], kdT[:, h, :], ident[:])) for h in range(4)],
                     reads=["kdT", "ident"], writes=["B4"])
                yield
                P.op("dve", lambda: V.tensor_tensor(out=atm[:], in0=B[2][:, 0:512].rearrange("p (h i) -> p h i", h=4),
                                                    in1=uincl16[:].unsqueeze(1).to_broadcast([128, 4, 128]), op=ALU.mult),
                     reads=["B2", "uincl16"], writes=["atm"])
                P.op("act", lambda: A.copy(out=kdec[:].rearrange("p h t -> p (h t)"), in_=B4[:, 0:512]), reads=["B4"], writes=["kdec"])
                mm = []
                for h in range(4):
                    obk = B5 if h < 2 else B6
                    sl = slice((h % 2) * 256, (h % 2) * 256 + 256)
                    mm.append(lambda h=h, obk=obk, sl=sl: T.matmul(obk[:, sl], lhsT=atm[:, h, :], rhs=vg[:, t, h * 256:(h + 1) * 256],
                                                                   start=True, stop=False))
                    mm.append(lambda h=h, obk=obk, sl=sl: T.matmul(obk[:, sl], lhsT=qdec[:, h, :], rhs=S16[:, h * 256:(h + 1) * 256],
                                                                   start=False, stop=True))
                P.op("pe", mm, reads=["atm", "vg%d" % t, "qdec", "S16"], writes=["B5", "B6"])
                P.op("pe", [(lambda h=h: T.matmul((B7 if h < 2 else B[3])[:, (h % 2) * 256:(h % 2) * 256 + 256], lhsT=kdec[:, h, :],
                                                  rhs=vg[:, t, h * 256:(h + 1) * 256], start=True, stop=True)) for h in range(4)],
                     reads=["kdec", "vg%d" % t], writes=["B7", "B3"])
                yield
                P.op("dve", lambda: V.memset(ss[:], 0.0), writes=["ss"])
                for h in range(4):
                    obk, okey = (B5, "B5") if h < 2 else (B6, "B6")
                    sl = slice((h % 2) * 256, (h % 2) * 256 + 256)
                    P.op("act", (lambda h=h, obk=obk, sl=sl: A.activation(out=sqj[:], in_=obk[:, sl], func=AF.Square,
                                                                          accum_out=ss[:, h:h + 1])),
                         reads=[okey, "ss"], writes=["sqj", "ss"])
                P.op("act", lambda: A.activation(out=rstd[:], in_=ss[:], func=AF.Ln, bias=RMS_EPS, scale=1.0 / 256.0),
                     reads=["ss"], writes=["rstd"])
                P.op("act", lambda: A.activation(out=rstd[:], in_=rstd[:], func=AF.Exp, scale=-0.5), reads=["rstd"], writes=["rstd"])
                for h in range(4):
                    obk, okey = (B5, "B5") if h < 2 else (B6, "B6")
                    sl = slice((h % 2) * 256, (h % 2) * 256 + 256)
                    P.op("dve", (lambda h=h, obk=obk, sl=sl: V.scalar_tensor_tensor(out=otmp[:, h * 256:(h + 1) * 256], in0=obk[:, sl],
                                                                                    scalar=rstd[:, h:h + 1], in1=normwb[:],
                                                                                    op0=ALU.mult, op1=ALU.mult)),
                         reads=[okey, "rstd", "normwb"], writes=["otmp"])
                P.op("dve", lambda: V.tensor_tensor(out=y16[:, t, 1024:2048], in0=otmp[:], in1=szg[:, t, :], op=ALU.mult),
                     reads=["otmp", "szg%d" % t], writes=["y16_%d" % t])
                for h in range(4):
                    src = (B7 if h < 2 else B[3])[:, (h % 2) * 256:(h % 2) * 256 + 256]
                    P.op("dve", (lambda h=h, src=src: V.scalar_tensor_tensor(out=S32[:, h * 256:(h + 1) * 256],
                                                                             in0=S32[:, h * 256:(h + 1) * 256],
                                                                             scalar=tot[:, h:h + 1], in1=src,
                                                                             op0=ALU.mult, op1=ALU.add)),
                         reads=["S32", "tot", "B7", "B3"], writes=["S32"])
                P.op("act", lambda: A.copy(out=S16[:], in_=S32[:]), reads=["S32"], writes=["S16"])
                yield
                for qd in range(2):
                    P.op("pe", [(lambda j=j: T.transpose(B4[:, (j % 8) * 128:(j % 8 + 1) * 128], y16[:, t, j * 128:(j + 1) * 128], ident[:]))
                                for j in range(qd * 8, qd * 8 + 8)], reads=["y16_%d" % t, "ident"], writes=["B4"])
                    P.op("act", (lambda qd=qd: A.copy(out=yT[:, t, qd * 8:(qd + 1) * 8, :].rearrange("p j t -> p (j t)"),
                                                      in_=B4[:, 0:1024])), reads=["B4"], writes=["yT%d" % t])
                    yield

            def ln_gen(t, gt):
                rt, rkey, lk = r32[t], "r32_%d" % t, "lnst%d" % t
                L = lambda a, b: lnst[:, t, a:b]
                P.op("dve", lambda: V.tensor_reduce(out=L(4, 5), in_=L(0, 4), axis=AX.X, op=ALU.add), reads=[lk], writes=[lk])
                P.op("dve", lambda: V.tensor_scalar(out=L(5, 6), in0=L(4, 5), scalar1=-1.0 / D, scalar2=None, op0=ALU.mult),
                     reads=[lk], writes=[lk])
                P.op("act", lambda: A.activation(out=y16[:, t, :], in_=rt[:], func=AF.Square, bias=L(5, 6), scale=1.0,
                                                 accum_out=L(6, 7)), reads=[rkey, lk], writes=["y16_%d" % t, lk])
                yield
                P.op("act", lambda: A.activation(out=L(7, 8), in_=L(6, 7), func=AF.Ln, bias=LN_EPS, scale=1.0 / D),
                     reads=[lk], writes=[lk])
                P.op("act", lambda: A.activation(out=L(7, 8), in_=L(7, 8), func=AF.Exp, scale=-0.5), reads=[lk], writes=[lk])
                P.op("dve", lambda: V.scalar_tensor_tensor(out=rt[:], in0=rt[:], scalar=L(5, 6), in1=lngb[:],
                                                           op0=ALU.add, op1=ALU.mult), reads=[rkey, lk, "lngb"], writes=[rkey])
                yield
                P.op("dve", lambda: V.scalar_tensor_tensor(out=rt[:], in0=rt[:], scalar=L(7, 8), in1=lnbb[:],
                                                           op0=ALU.mult, op1=ALU.add), reads=[rkey, lk, "lnbb"], writes=[rkey])
                pending_stores.append((t, gt))
                yield

            def flush_stores():
                while pending_stores:
                    t_, gt_ = pending_stores.pop(0)
                    P.dma("sp", "out%d" % t_, (lambda t_=t_, gt_=gt_: nc.sync.dma_start(out=out[gt_, :, :], in_=r32[t_][:])),
                          reads=["r32_%d" % t_])

            def s1_blocks(pi):
                return [(C_QA, 512, h_q(0, pi)), (C_QA + 512, 512, h_q(1, pi)),
                        (C_KA, 256, (lambda t, acc, ak, pi=pi: h_kv(t, acc, ak, pi)))]

            if ps_i == 0:
                run(bg_tm(s1_blocks(0), 0))
            if ps_i + 1 < npass:
                load_x(ps_i + 1)
            run_mix(attn(0), chain(deferred_ln.pop(0) if deferred_ln else iter(()),
                                   bg_tm([(C_ZA, 512, h_silu(sza, "sza", 0)), (C_ZA + 512, 512, h_silu(sza, "sza", 1)),
                                          (C_VG, 512, h_vg(0)), (C_VG + 512, 512, h_vg(1))])), ratio=2)
            flush_stores()
            run_mix(attn(1), chain(deferred_ln.pop(0) if deferred_ln else iter(()),
                                   bg_tm([(C_ZG, 512, h_silu(szg, "szg", 0)), (C_ZG + 512, 512, h_silu(szg, "szg", 1))]),
                                   bg_fm()), ratio=2)
            flush_stores()
            for t in range(TP):
                P.op("dve", (lambda t=t: V.memset(lnst[:, t, :], 0.0)), writes=["lnst%d" % t])
            if ps_i + 1 < npass:
                run_mix(gla(0), bg_tm(s1_blocks(ps_i + 1), ps_i + 1), ratio=1)
            else:
                run(gla(0))
            pre = [load_wblk("out", 0), load_wblk("out", 512)]
            run(gla(1))
            run(bg_outproj(pre))
            deferred_ln.append(ln_gen(0, ps_i * TP + 0))
            deferred_ln.append(ln_gen(1, ps_i * TP + 1))
        while deferred_ln:
            run(deferred_ln.pop(0))
        flush_stores()


def _kmajor(a):
    n = a.shape[1]
    return np.ascontiguousarray(a.reshape(KC, 128, n).transpose(1, 0, 2))


def _host_consts():
    ident = np.eye(128, dtype=np.float32)
    uincl = (np.arange(128)[:, None] <= np.arange(128)[None, :]).astype(np.float32)
    ones = np.ones((128, 128), np.float32)
    usuf = (np.arange(128)[:, None] > np.arange(128)[None, :]).astype(np.float32)
    consts = np.ascontiguousarray(np.stack([ident, uincl, ones, usuf], axis=1))
    k = np.arange(128)[:, None, None]
    kt = np.arange(2)[None, :, None]
    q = np.arange(128)[None, None, :]
    kj = kt * 128 + k
    reg = ((kj > q) & (kj <= q + 128)).astype(np.float32)
    first = reg * (kj >= 128)
    return consts, reg, first


def kernel(x, w_in, w_gk_up, b_gk, attn_sinks, gla_norm_w, w_out, ln_g, ln_b, _ncores=NCORES, _npre=NPRE):
    x = np.asarray(x, np.float32)[0]
    w = np.asarray(w_in, np.float32)[0]
    perm = np.concatenate([np.arange(0, 1024), np.arange(1024, 1152), np.arange(1152, 1280), np.arange(1280, 2304),
                           np.arange(3328, 4352), np.arange(4352, 5376), np.arange(2304, 2816), np.arange(2816, 3328),
                           np.arange(5376, 5392)])
    w_l = _kmajor(np.concatenate([w[:, perm], np.zeros((D, 16), np.float32)], axis=1))
    wo_l = _kmajor(np.asarray(w_out, np.float32)[0])
    waug = np.zeros((32, 512), np.float32)
    waug[0:16] = np.asarray(w_gk_up, np.float32)[0]
    waug[16] = np.asarray(b_gk, np.float32)[0]
    consts, mreg, mfirst = _host_consts()
    sinks = np.ascontiguousarray(np.broadcast_to(np.asarray(attn_sinks, np.float32)[0][None, :], (128, 16)))
    normw = np.ascontiguousarray(np.broadcast_to(np.asarray(gla_norm_w, np.float32)[0][None, :], (128, 256)))
    lng = np.ascontiguousarray(np.broadcast_to(np.asarray(ln_g, np.float32)[0][None, :], (128, D)))
    lnb = np.ascontiguousarray(np.broadcast_to(np.asarray(ln_b, np.float32)[0][None, :], (128, D)))
    inv_freq = (1.0 / (10000.0 ** (np.arange(0, 32, dtype=np.float32) * 2.0 / 64.0))).astype(np.float32)
    xTfull = np.ascontiguousarray(x.T)
    in_maps = []
    for c in range(_ncores):
        s0 = c * OWN
        xt = np.zeros((D, OWN + 128), np.float32)
        if c > 0:
            xt[:, 0:128] = xTfull[:, s0 - 128:s0]
        xt[:, 128:] = xTfull[:, s0:s0 + OWN]
        npre_tok = _npre * 128
        xp = np.zeros((D, max(npre_tok, 128)), np.float32)
        if _npre > 0 and s0 > 0:
            xp[:, npre_tok - s0:] = xTfull[:, 0:s0]
        pos = (np.arange(s0 - 128, s0 + OWN)).astype(np.float32)
        ang = pos[:, None] * inv_freq[None, :]
        cs = np.stack([np.cos(ang), np.sin(ang)], axis=1).astype(np.float32)
        cs = np.ascontiguousarray(cs.reshape(NT + 1, 128, 2, 32).transpose(1, 0, 2, 3))
        masks = np.ascontiguousarray(np.stack([mfirst if c == 0 else mreg, mreg], axis=1))
        in_maps.append({
            "xT": _kmajor(xt),
            "xpre": np.ascontiguousarray(_kmajor(xp).reshape(128, KC, max(_npre, 1), 128).transpose(2, 0, 1, 3)
                                         .reshape(max(_npre, 1), 128, KC * 128)),
            "xtok": np.ascontiguousarray(x[s0:s0 + OWN].reshape(NT, 128, D)),
            "xptok": np.ascontiguousarray(xp.T.reshape(max(_npre, 1), 128, D)),
            "w_in": w_l, "w_out": wo_l, "w_aug": waug, "cs": cs, "masks": masks, "consts": consts,
            "sinks": sinks, "normw": normw, "lng": lng, "lnb": lnb,
        })
    nc = build(npre=_npre, nt=NT)
    res = run_bass_kernel_spmd(nc, in_maps, core_ids=list(range(_ncores)))
    outs = [np.asarray(r["out"]).reshape(OWN, D) for r in res.results]
    return np.concatenate(outs, axis=0)[None].astype(np.float32)
```

```python
import contextlib
import math
import numpy as np
import concourse.bass as bass
import concourse.mybir as mybir
from concourse.bass_utils import run_bass_kernel_spmd

F32 = mybir.dt.float32
BF16 = mybir.dt.bfloat16
AF = mybir.ActivationFunctionType
ALU = mybir.AluOpType
AX = mybir.AxisListType

NCORES = 8
D = 2048
SEQ = 16384
OWN = SEQ // NCORES
NT = OWN // 128
NPRE = (SEQ - OWN) // 128
TP = 2
TOK = TP * 128
KC = D // 128
WCOLS = 5408
C_QA, C_KA, C_VA, C_ZA, C_VG, C_ZG, C_QG, C_KG, C_GK = 0, 1024, 1152, 1280, 2304, 3328, 4352, 4864, 5376
ALPHA = 2.0 ** 0.25
WBLOCKS = [("in", C_QA, 512), ("in", C_QA + 512, 512), ("in", C_KA, 256), ("in", C_ZA, 512), ("in", C_ZA + 512, 512),
           ("in", C_VG, 512), ("in", C_VG + 512, 512), ("in", C_ZG, 512), ("in", C_ZG + 512, 512),
           ("in", C_QG, 512), ("in", C_KG, 512),
           ("out", 0, 512), ("out", 512, 512), ("out", 1024, 512), ("out", 1536, 512)]
WB_IDX = {(src, c0): i for i, (src, c0, n) in enumerate(WBLOCKS)}
LN_EPS = 1e-5
RMS_EPS = 1e-5


class Prog:
    def __init__(self, nc, stack):
        self.nc = nc
        self.stack = stack
        self.engs = {"pe": nc.tensor, "act": nc.scalar, "dve": nc.vector, "pool": nc.gpsimd, "sp": nc.sync}
        self.sem = {}
        self.cnt = {}
        self.waited = {}
        self.lastw = {}
        self.readers = {}
        self.ninst = {e: 0 for e in self.engs}
        for e in ("pe", "act", "dve", "pool"):
            self._mksem("E_" + e)

    def _mksem(self, name):
        self.sem[name] = self.stack.enter_context(self.nc.semaphore(name))
        self.cnt[name] = 0

    def _deps(self, reads, writes):
        deps = {}

        def add(tok):
            if tok is None:
                return
            s, v = tok
            if deps.get(s, 0) < v:
                deps[s] = v
        for r in reads:
            add(self.lastw.get(r))
        for w in writes:
            add(self.lastw.get(w))
            for t in self.readers.get(w, ()):
                add(t)
        return deps

    def _wait(self, eng, deps):
        e = self.engs[eng]
        for s, v in deps.items():
            if self.waited.get((eng, s), 0) >= v:
                continue
            if eng == "pe" and s == "E_pe":
                continue
            e.wait_ge(self.sem[s], v)
            self.ninst[eng] += 1
            self.waited[(eng, s)] = v

    def _commit(self, tok, reads, writes):
        for w in writes:
            self.lastw[w] = tok
            self.readers[w] = []
        for r in reads:
            self.readers.setdefault(r, []).append(tok)

    PSUM_KEYS = frozenset("B%d" % i for i in range(8))

    def op(self, eng, fns, reads=(), writes=()):
        if callable(fns):
            fns = [fns]
        writes = list(writes) + [r for r in reads if r in self.PSUM_KEYS and r not in writes]
        self._wait(eng, self._deps(reads, writes))
        ins = None
        for f in fns:
            ins = f()
            self.ninst[eng] += 1
        s = "E_" + eng
        self.cnt[s] += 1
        ins.then_inc(self.sem[s], 1)
        self._commit((s, self.cnt[s]), reads, writes)

    def dma(self, eng, slot, fns, reads=(), writes=()):
        if callable(fns):
            fns = [fns]
        s = "D_" + slot
        if s not in self.sem:
            self._mksem(s)
        self._wait(eng, self._deps(reads, writes))
        for f in fns:
            ins = f()
            self.ninst[eng] += 1
            self.cnt[s] += 16
            ins.then_inc(self.sem[s], 16)
        self._commit((s, self.cnt[s]), reads, writes)

    def barrier(self):
        for eng in ("pe", "act", "dve", "pool", "sp"):
            e = self.engs[eng]
            for s, v in self.cnt.items():
                if v > 0 and self.waited.get((eng, s), 0) < v:
                    e.wait_ge(self.sem[s], v)
                    self.waited[(eng, s)] = v

    def finish(self, eng="sp"):
        e = self.engs[eng]
        for s, v in self.cnt.items():
            if v > 0 and self.waited.get((eng, s), 0) < v:
                e.wait_ge(self.sem[s], v)
                self.waited[(eng, s)] = v


class _Stop(Exception):
    pass


STAGE = 99
SKIPK = False
DBG = 0


def _stage(n):
    if STAGE < n:
        raise _Stop()


def build(npre=NPRE, nt=NT):
    nc = bass.Bass("TRN2", target_bir_lowering=False)
    din = lambda name, shape: nc.dram_tensor(name, shape, F32, kind="ExternalInput").ap()
    xT = din("xT", [128, KC, (nt + 1) * 128])
    xpre = din("xpre", [max(npre, 1), 128, KC * 128])
    xtok = din("xtok", [nt, 128, D])
    xptok = din("xptok", [max(npre, 1), 128, D])
    w_in = din("w_in", [128, KC, WCOLS])
    w_out = din("w_out", [128, KC, D])
    w_aug = din("w_aug", [32, 512])
    cs = din("cs", [128, nt + 1, 2, 32])
    masks = din("masks", [128, 2, 2, 128])
    consts = din("consts", [128, 4, 128])
    sinks = din("sinks", [128, 16])
    normw = din("normw", [128, 256])
    lng = din("lng", [128, D])
    lnb = din("lnb", [128, D])
    out = nc.dram_tensor("out", [nt, 128, D], F32, kind="ExternalOutput").ap()
    wsc = nc.dram_tensor("wsc", [len(WBLOCKS), 128, KC * 512], BF16, kind="Internal").ap()

    with contextlib.ExitStack() as st:
        P = Prog(nc, st)
        try:
            _build_body(nc, st, P, npre, nt, locals())
        except _Stop:
            pass
        P.finish("sp")
    return nc


def _build_body(nc, st, P, npre, nt, env):
    xT, xpre, xtok, w_in, w_out, w_aug, cs, masks, consts, sinks, normw, lng, lnb, out, wsc, xptok = [env[k] for k in ('xT', 'xpre', 'xtok', 'w_in', 'w_out', 'w_aug', 'cs', 'masks', 'consts', 'sinks', 'normw', 'lng', 'lnb', 'out', 'wsc', 'xptok')]
    if True:
        sb = lambda name, shape, dt: st.enter_context(nc.sbuf_tensor(name, shape, dt))
        psb = lambda name, dt=F32: st.enter_context(nc.psum_tensor(name, [128, 512 if dt == F32 else 1024], dt))
        V, A, T, G = nc.vector, nc.scalar, nc.tensor, nc.gpsimd

        B = [psb("B%d" % i) for i in range(4)]
        BZ = st.enter_context(nc.psum_tensor("BZ", [128, 2048], F32))
        B4 = BZ[:, 0:512].bitcast(BF16)
        B5, B6, B7 = BZ[:, 512:1024], BZ[:, 1024:1536], BZ[:, 1536:2048]

        ident = sb("ident", [128, 128], BF16)
        uincl = sb("uincl", [128, 128], F32)
        uincl16 = sb("uincl16", [128, 128], BF16)
        ones16 = sb("ones16", [128, 128], BF16)
        mask16 = sb("mask16", [128, 2, 2, 128], BF16)
        sinkb = sb("sinkb", [128, 16], F32)
        normwb = sb("normwb", [128, 256], F32)
        lngb = sb("lngb", [128, D], F32)
        lnbb = sb("lnbb", [128, D], F32)
        cst = sb("cst", [128, nt + 1, 2, 32], F32)
        waug = sb("waug", [32, 512], F32)
        S32 = sb("S32", [128, 1024], F32)
        S16 = sb("S16", [128, 1024], BF16)
        gkT = sb("gkT", [128, TOK], F32)
        e1 = sb("e1", [128, 512], F32)
        spl = sb("spl", [128, 512], F32)
        nbl = sb("nbl", [128, 4], F32)
        tot = sb("tot", [128, 4], F32)
        ekd = sb("ekd", [128, 4, 128], F32)
        kdT = sb("kdT", [128, 4, 128], BF16)
        kdec = sb("kdec", [128, 4, 128], BF16)
        wgk = sb("wgk", [128, KC, 32], BF16)

        P.dma("pool", "c_ident", lambda: G.dma_start(out=ident[:], in_=consts[:, 0, :]), writes=["ident"])
        identf = sb("identf", [128, 128], F32)
        P.dma("sp", "c_identf", lambda: nc.sync.dma_start(out=identf[:], in_=consts[:, 0, :]), writes=["identf"])
        P.dma("sp", "c_uincl", lambda: nc.sync.dma_start(out=uincl[:], in_=consts[:, 1, :]), writes=["uincl"])
        P.dma("pool", "c_uincl16", lambda: G.dma_start(out=uincl16[:], in_=consts[:, 1, :]), writes=["uincl16"])
        P.dma("pool", "c_ones", lambda: G.dma_start(out=ones16[:], in_=consts[:, 2, :]), writes=["ones16"])
        P.dma("pool", "c_mask", lambda: G.dma_start(out=mask16[:], in_=masks[:, :, :, :]), writes=["mask16"])
        P.dma("sp", "c_sink", lambda: nc.sync.dma_start(out=sinkb[:], in_=sinks[:, :]), writes=["sinkb"])
        negsink = sb("negsink", [128, 16], F32)
        P.op("dve", lambda: V.tensor_scalar(out=negsink[:], in0=sinkb[:], scalar1=-1.0, scalar2=None, op0=ALU.mult),
             reads=["sinkb"], writes=["negsink"])
        P.dma("sp", "c_normw", lambda: nc.sync.dma_start(out=normwb[:], in_=normw[:, :]), writes=["normwb"])
        P.dma("sp", "c_lng", lambda: nc.sync.dma_start(out=lngb[:], in_=lng[:, :]), writes=["lngb"])
        P.dma("sp", "c_lnb", lambda: nc.sync.dma_start(out=lnbb[:], in_=lnb[:, :]), writes=["lnbb"])
        P.dma("sp", "c_cs", lambda: nc.sync.dma_start(out=cst[:], in_=cs[:, :, :, :]), writes=["cst"])
        P.op("dve", lambda: V.memset(waug[:], 0.0), writes=["waug"])
        P.dma("sp", "c_waug", lambda: nc.sync.dma_start(out=waug[0:17, :], in_=w_aug[0:17, :]), writes=["waug"])
        P.dma("pool", "c_wgk", lambda: G.dma_start(out=wgk[:], in_=w_in[:, :, C_GK:C_GK + 32]), writes=["wgk"])
        P.op("dve", lambda: V.memset(S32[:], 0.0), writes=["S32"])
        P.op("dve", lambda: V.memset(S16[:], 0.0), writes=["S16"])
        P.op("dve", lambda: V.memset(gkT[:], 1.0), writes=["gkT", "gkT0", "gkT1"])

        def wsc_view(bi):
            n = WBLOCKS[bi][2]
            return wsc[bi, :, 0:KC * n].rearrange("p (k c) -> p k c", k=KC)

        wcast_jobs = list(range(len(WBLOCKS)))

        def emit_wcast(n=1):
            for _ in range(n):
                if not wcast_jobs:
                    return
                bi = wcast_jobs.pop(0)
                src, c0, ncol = WBLOCKS[bi]
                srcap = (w_in if src == "in" else w_out)[:, :, c0:c0 + ncol]
                P.dma("pool", "wcast", (lambda bi=bi, srcap=srcap: G.dma_start(out=wsc_view(bi), in_=srcap)), writes=["wsc"])
        _stage(1)
        if npre > 0:
            with contextlib.ExitStack() as st2:
                sb2 = lambda name, shape, dt: st2.enter_context(nc.sbuf_tensor(name, shape, dt))
                wk = sb2("wk", [128, KC, 512], BF16)
                xpc = [sb2("xpc%d" % i, [128, KC * 128], BF16) for i in range(3)]
                xtk = [sb2("xtk%d" % i, [128, D], BF16) for i in range(3)]
                Z = sb2("Z", [128, 4, D], F32)
                ksb = [sb2("ksb%d" % i, [128, 512], F32) for i in range(2)]
                esuf = sb2("esuf", [128, 512], F32)
                kdpb = [sb2("kdp%d" % i, [128, 512], BF16) for i in range(2)]
                totb = [sb2("totp%d" % i, [128, 4], F32) for i in range(2)]
                usuf = sb2("usuf", [128, 128], F32)
                onesf = sb2("onesf", [128, 1], F32)
                P.dma("sp", "c_usuf", lambda: nc.sync.dma_start(out=usuf[:], in_=consts[:, 3, :]), writes=["usuf"])
                P.op("dve", lambda: V.memset(onesf[:], 1.0), writes=["onesf"])
                P.op("dve", lambda: V.memset(Z[:].rearrange("p h d -> p (h d)"), 0.0), writes=["Z0", "Z1", "Z2", "Z3"])
                P.dma("pool", "wk", lambda: G.dma_start(out=wk[:], in_=w_in[:, :, C_KG:C_KG + 512]), writes=["wk"])

                def p_load(c):
                    xk, tk = "xpc%d" % (c % 3), "xtk%d" % (c % 3)
                    P.dma("pool", xk, lambda: G.dma_start(out=xpc[c % 3][:], in_=xpre[c, :, :]), writes=[xk])
                    P.dma("pool", tk, lambda: G.dma_start(out=xtk[c % 3][:], in_=xptok[c, :, :]), writes=[tk])

                def p_inproj(c):
                    xb = xpc[c % 3][:].rearrange("p (k t) -> p k t", k=KC)
                    xk = "xpc%d" % (c % 3)
                    sl = c % 2
                    for q2 in range(2):
                        P.op("pe", [(lambda k=k: T.matmul(B[1][0:32, 0:128], lhsT=wgk[:, k, :], rhs=xb[:, k, :],
                                                          start=(k == 0), stop=(k == KC - 1))) for k in range(q2 * 8, q2 * 8 + 8)],
                             reads=["wgk", xk], writes=["B1"])
                        if q2 == 1:
                            P.op("act", lambda: A.copy(out=gkT[0:16, sl * 128:(sl + 1) * 128], in_=B[1][0:16, 0:128]),
                                 reads=["B1"], writes=["gkT%d" % sl])
                        yield
                    for q4 in range(4):
                        P.op("pe", [(lambda k=k: T.matmul(B[0][:, 0:512], lhsT=xb[:, k, :], rhs=wk[:, k, :],
                                                          start=(k == 0), stop=(k == KC - 1))) for k in range(q4 * 4, q4 * 4 + 4)],
                             reads=["wk", xk], writes=["B0"])
                        if q4 == 3:
                            P.op("act", lambda: A.copy(out=ksb[sl][:], in_=B[0][:, 0:512]), reads=["B0"], writes=["ksb%d" % sl])
                        yield

                def p_tail_head(c):
                    sl = c % 2
                    P.op("pe", lambda: T.matmul(B[2][:, 0:512], lhsT=gkT[0:32, sl * 128:(sl + 1) * 128], rhs=waug[0:32, :],
                                                start=True, stop=True), reads=["gkT%d" % sl, "waug"], writes=["B2"])
                    P.op("act", lambda: A.activation(out=e1[:], in_=B[2][:, 0:512], func=AF.Exp, scale=-1.0),
                         reads=["B2"], writes=["e1"])
                    P.op("act", lambda: A.activation(out=spl[:], in_=e1[:], func=AF.Ln, bias=1.0, scale=1.0),
                         reads=["e1"], writes=["spl"])
                    yield
                    P.op("pe", lambda: T.matmul(B[3][:, 0:512], lhsT=usuf[:], rhs=spl[:], start=True, stop=True),
                         reads=["usuf", "spl"], writes=["B3"])
                    P.op("pe", [(lambda h=h: T.matmul(B[2][:, h:h + 1], lhsT=spl[:, h * 128:(h + 1) * 128], rhs=onesf[:, 0:1],
                                                      start=True, stop=True)) for h in range(4)],
                         reads=["spl", "onesf"], writes=["B2"])
                    P.op("act", lambda: A.activation(out=esuf[:], in_=B[3][:, 0:512], func=AF.Exp, scale=-1.0 / 16.0),
                         reads=["B3"], writes=["esuf"])
                    P.op("act", lambda: A.activation(out=totb[sl][:], in_=B[2][:, 0:4], func=AF.Exp, scale=-1.0 / 16.0),
                         reads=["B2"], writes=["totp%d" % sl])
                    P.op("dve", lambda: V.tensor_tensor(out=kdpb[sl][:], in0=ksb[sl][:], in1=esuf[:], op=ALU.mult),
                         reads=["ksb%d" % sl, "esuf"], writes=["kdp%d" % sl])
                    yield

                def p_tail_z(c):
                    xt_, tk = xtk[c % 3], "xtk%d" % (c % 3)
                    sl = c % 2
                    kdp, tot_ = kdpb[sl], totb[sl]
                    for h in range(4):
                        for half in range(2):
                            keys = ["B4", "B5"] if half == 0 else ["B6", "B7"]
                            c0 = half * 1024
                            P.op("pe", [(lambda nb=nb: T.matmul(BZ[:, c0 + nb * 512:c0 + (nb + 1) * 512],
                                                                lhsT=kdp[:, h * 128:(h + 1) * 128],
                                                                rhs=xt_[:, c0 + nb * 512:c0 + (nb + 1) * 512],
                                                                start=True, stop=True)) for nb in range(2)],
                                 reads=["kdp%d" % sl, tk], writes=keys)
                            P.op("dve", lambda: V.scalar_tensor_tensor(out=Z[:, h, c0:c0 + 1024], in0=Z[:, h, c0:c0 + 1024],
                                                                       scalar=tot_[:, h:h + 1], in1=BZ[:, c0:c0 + 1024],
                                                                       op0=ALU.mult, op1=ALU.add),
                                 reads=["Z%d" % h, "totp%d" % sl] + keys, writes=["Z%d" % h])
                            yield

                def run_seq(order, gens):
                    for gi in order:
                        try:
                            next(gens[gi])
                        except StopIteration:
                            pass

                p_load(0)
                if npre > 1:
                    p_load(1)
                for _ in p_inproj(0):
                    pass
                for _ in p_tail_head(0):
                    pass
                for c in range(npre):
                    if c + 2 < npre:
                        p_load(c + 2)
                    if c % 4 == 1:
                        emit_wcast(1)
                    more = c + 1 < npre
                    gens = {"Z": p_tail_z(c), "I": p_inproj(c + 1) if more else iter(()),
                            "T": p_tail_head(c + 1) if more else iter(())}
                    run_seq(["Z", "I", "Z", "I", "Z", "T", "Z", "I", "Z", "I", "Z", "I", "Z", "I", "Z", "T", "Z", "I", "T"], gens)
                Zb = sb2("Zb", [128, 4, D], BF16)
                ZT = sb2("ZT", [128, 4, KC, 128], BF16)
                wv = sb2("wv", [128, KC, 1024], BF16)
                P.dma("pool", "wv", lambda: G.dma_start(out=wv[:], in_=w_in[:, :, C_VG:C_VG + 1024]), writes=["wv"])
                for h in range(4):
                    eng, E = ("act", A) if h % 2 == 0 else ("dve", V)
                    if eng == "act":
                        P.op("act", (lambda h=h: A.copy(out=Zb[:, h, :], in_=Z[:, h, :])), reads=["Z%d" % h], writes=["Zb"])
                    else:
                        P.op("dve", (lambda h=h: V.tensor_copy(out=Zb[:, h, :], in_=Z[:, h, :])), reads=["Z%d" % h], writes=["Zb"])
                for h in range(4):
                    for qd in range(2):
                        P.op("pe", [(lambda j=j: T.transpose(B4[:, (j % 8) * 128:(j % 8 + 1) * 128], Zb[:, h, j * 128:(j + 1) * 128], ident[:]))
                                    for j in range(qd * 8, qd * 8 + 8)], reads=["Zb", "ident"], writes=["B4"])
                        P.op("act", (lambda h=h, qd=qd: A.copy(out=ZT[:, h, qd * 8:(qd + 1) * 8, :].rearrange("p j t -> p (j t)"),
                                                               in_=B4[:, 0:1024])), reads=["B4"], writes=["ZT"])
                for h in range(4):
                    bk, bkey = B[h // 2], "B%d" % (h // 2)
                    P.op("pe", [(lambda k=k: T.matmul(bk[:, (h % 2) * 256:(h % 2) * 256 + 256], lhsT=ZT[:, h, k, :],
                                                      rhs=wv[:, k, h * 256:(h + 1) * 256], start=(k == 0), stop=(k == KC - 1)))
                                for k in range(KC)], reads=["ZT", "wv"], writes=[bkey])
                for j in range(2):
                    P.op("dve", (lambda j=j: V.tensor_copy(out=S32[:, j * 512:(j + 1) * 512], in_=B[j][:, 0:512])),
                         reads=["B%d" % j], writes=["S32"])
                P.op("act", lambda: A.copy(out=S16[:], in_=S32[:]), reads=["S32"], writes=["S16"])
                P.barrier()

        emit_wcast(len(WBLOCKS))
        xTpb = [sb("xTp%d" % i, [128, KC, TOK], BF16) for i in range(2)]
        xTp = xTpb[0]
        wblk = [sb("wblk%d" % i, [128, KC, 512], BF16) for i in range(2)]
        q32 = sb("q32", [128, TP, 1024], F32)
        k32 = sb("k32", [128, TP, 128], F32)
        qr = sb("qr", [128, TP, 1024], BF16)
        krb = [sb("kr%d" % i, [128, 128], BF16) for i in range(2)]
        ta = sb("ta", [128, 512], F32)
        tb = sb("tb", [128, 512], F32)
        sza = sb("sza", [128, TP, 1024], BF16)
        szg = sb("szg", [128, TP, 1024], BF16)
        vg = sb("vg", [128, TP, 1024], BF16)
        qgT = sb("qgT", [128, 4, TOK], F32)
        kgT = sb("kgT", [128, 4, TOK], F32)
        kTall = sb("kTall", [64, 2, (nt + 1) * 128], BF16)
        vall = sb("vall", [128, nt + 1, 128], BF16)
        qT = sb("qT", [64, 16, 128], BF16)
        pexp = [sb("pexp%d" % i, [128, 2, 256], BF16) for i in range(2)]
        pT = [sb("pT%d" % i, [128, 2, 2, 128], BF16) for i in range(2)]
        mraw = sb("mraw", [128, 16], F32)
        msc = sb("msc", [128, 16], F32)
        negm = sb("negm", [128, 16], F32)
        dsm = sb("dsm", [128, 16], F32)
        esk = sb("esk", [128, 16], F32)
        den = sb("den", [128, 16], F32)
        rden = sb("rden", [128, 16], F32)
        otmp = sb("otmp", [128, 1024], F32)
        eq = sb("eq", [128, 4, 128], F32)
        ek = sb("ek", [128, 4, 128], F32)
        qdec = sb("qdec", [128, 4, 128], BF16)
        kneg = sb("kneg", [128, 4, 128], BF16)
        atm = sb("atm", [128, 4, 128], BF16)
        sqj = sb("sqj", [128, 256], BF16)
        ss = sb("ss", [128, 4], F32)
        rstd = sb("rstd", [128, 4], F32)
        y16 = sb("y16", [128, TP, D], BF16)
        yT = sb("yT", [128, TP, KC, 128], BF16)
        xres = [sb("xres%d" % i, [128, 512], F32) for i in range(4)]
        r32 = [sb("r32_%d" % i, [128, D], F32) for i in range(TP)]
        lnst = sb("lnst", [128, TP, 8], F32)

        wslot = [0]

        def load_wblk(src, c0):
            bi = WB_IDX[(src, c0)]
            ncols = WBLOCKS[bi][2]
            i = wslot[0] % 2
            wslot[0] += 1
            key = "wblk%d" % i
            P.dma("sp", key, lambda: nc.sync.dma_start(out=wblk[i][:, :, 0:ncols], in_=wsc_view(bi)), reads=["wsc"], writes=[key])
            return wblk[i], key

        accs = [0]

        def next_acc():
            i = accs[0] % 2
            accs[0] += 1
            return B[i], "B%d" % i

        def run(gen):
            for _ in gen:
                pass

        def run_mix(fg, bg, ratio=2, start_after=0):
            fg_done = bg_done = False
            for _ in range(start_after):
                try:
                    next(fg)
                except StopIteration:
                    fg_done = True
                    break
            while not (fg_done and bg_done):
                for _ in range(ratio):
                    if fg_done:
                        break
                    try:
                        r = next(fg)
                        if r == "drain":
                            for _ in bg:
                                pass
                            bg_done = True
                    except StopIteration:
                        fg_done = True
                if not bg_done:
                    try:
                        next(bg)
                    except StopIteration:
                        bg_done = True

        def chain(*gens):
            for g_ in gens:
                yield from g_

        def do_rope(E, ename, src, src_key, dst1, dst2, dst_key, nh, slot):
            s4 = src.rearrange("p (h two f) -> p h two f", h=nh, two=2)
            t1, t2 = s4[:, :, 0, :], s4[:, :, 1, :]
            cosb = cst[:, slot, 0, :].unsqueeze(1).to_broadcast([128, nh, 32])
            sinb = cst[:, slot, 1, :].unsqueeze(1).to_broadcast([128, nh, 32])
            ta3 = ta[:, 0:nh * 32].rearrange("p (h f) -> p h f", h=nh)
            tb3 = tb[:, 0:nh * 32].rearrange("p (h f) -> p h f", h=nh)
            P.op(ename, lambda: E.tensor_tensor(out=ta3, in0=t1, in1=cosb, op=ALU.mult), reads=[src_key, "cst"], writes=["ta"])
            P.op(ename, lambda: E.tensor_tensor(out=tb3, in0=t2, in1=sinb, op=ALU.mult), reads=[src_key, "cst"], writes=["tb"])
            P.op(ename, lambda: E.tensor_tensor(out=dst1, in0=ta3, in1=tb3, op=ALU.subtract), reads=["ta", "tb"], writes=[dst_key])
            P.op(ename, lambda: E.tensor_tensor(out=ta3, in0=t2, in1=cosb, op=ALU.mult), reads=[src_key, "cst"], writes=["ta"])
            P.op(ename, lambda: E.tensor_tensor(out=tb3, in0=t1, in1=sinb, op=ALU.mult), reads=[src_key, "cst"], writes=["tb"])
            P.op(ename, lambda: E.tensor_tensor(out=dst2, in0=ta3, in1=tb3, op=ALU.add), reads=["ta", "tb"], writes=[dst_key])

        def k_rope(t_k32, slot, ti):
            kr4 = krb[ti][:].rearrange("p (h two f) -> p h two f", h=2, two=2)
            do_rope(G, "pool", t_k32, "k32", kr4[:, :, 0, :], kr4[:, :, 1, :], "kr%d" % ti, 2, slot)

        def k_transposes(slot, ti):
            kr = krb[ti]
            P.op("pe", [(lambda kv=kv: T.transpose(B4[0:64, kv * 128:(kv + 1) * 128], kr[:, kv * 64:(kv + 1) * 64], ident[:]))
                        for kv in range(2)], reads=["kr%d" % ti, "ident"], writes=["B4"])
            P.op("dve", lambda: V.tensor_copy(out=kTall[:, :, slot * 128:(slot + 1) * 128],
                                              in_=B4[0:64, 0:256].rearrange("p (a t) -> p a t", a=2)),
                 reads=["B4"], writes=["kTall"])

        def load_x(ps_i):
            tok0 = (1 + ps_i * TP) * 128
            xk = "xTp%d" % (ps_i % 2)
            P.dma("pool", xk, lambda: G.dma_start(out=xTpb[ps_i % 2][:, :, :], in_=xT[:, :, tok0:tok0 + TOK]), writes=[xk])

        wb, wkey = load_wblk("in", C_KA)
        P.dma("pool", "xTp1", lambda: G.dma_start(out=xTpb[1][:, :, 0:128], in_=xT[:, :, 0:128]), writes=["xTp1"])
        acc, akey = next_acc()
        P.op("pe", [(lambda k=k: T.matmul(acc[:, 0:256], lhsT=xTpb[1][:, k, 0:128], rhs=wb[:, k, 0:256],
                                          start=(k == 0), stop=(k == KC - 1))) for k in range(KC)],
             reads=["xTp1", wkey], writes=[akey])
        P.op("act", lambda: A.copy(out=k32[:, 0, :], in_=acc[:, 0:128]), reads=[akey], writes=["k32"])
        P.op("act", lambda: A.copy(out=vall[:, 0, :], in_=acc[:, 128:256]), reads=[akey], writes=["vall"])
        k_rope(k32[:, 0, :], 0, 0)
        k_transposes(0, 0)
        load_x(0)

        npass = nt // TP
        pending_stores = []
        deferred_ln = []
        for ps_i in range(npass):
            def bg_tm(blocks, pi=None):
                pi = ps_i if pi is None else pi
                xb_, xk_ = xTpb[pi % 2], "xTp%d" % (pi % 2)
                for col0, ncols, handler in blocks:
                    wb, wkey = load_wblk("in", col0)
                    for t in range(TP):
                        acc, akey = next_acc()
                        P.op("pe", [(lambda k=k: T.matmul(acc[:, 0:ncols], lhsT=xb_[:, k, t * 128:(t + 1) * 128],
                                                          rhs=wb[:, k, 0:ncols], start=(k == 0), stop=(k == KC - 1)))
                                    for k in range(KC)], reads=[xk_, wkey], writes=[akey])
                        handler(t, acc, akey)
                        yield

            def bg_fm():
                xb_, xk_ = xTpb[ps_i % 2], "xTp%d" % (ps_i % 2)
                for (col0, dstT, dkey) in ((C_QG, qgT, "qgT"), (C_KG, kgT, "kgT")):
                    wb, wkey = load_wblk("in", col0)
                    for h in range(4):
                        acc, akey = next_acc()
                        P.op("pe", [(lambda k=k: T.matmul(acc[:, 0:TOK], lhsT=wb[:, k, h * 128:(h + 1) * 128], rhs=xb_[:, k, :],
                                                          start=(k == 0), stop=(k == KC - 1))) for k in range(KC)],
                             reads=[xk_, wkey], writes=[akey])
                        P.op("act", lambda: A.copy(out=dstT[:, h, :], in_=acc[:, 0:TOK]), reads=[akey], writes=[dkey])
                        yield
                acc, akey = next_acc()
                P.op("pe", [(lambda k=k: T.matmul(acc[0:32, 0:TOK], lhsT=wgk[:, k, :], rhs=xb_[:, k, :],
                                                  start=(k == 0), stop=(k == KC - 1))) for k in range(KC)],
                     reads=[xk_, "wgk"], writes=[akey])
                P.op("act", lambda: A.copy(out=gkT[0:16, 0:TOK], in_=acc[0:16, 0:TOK]), reads=[akey], writes=["gkT"])
                yield

            def bg_outproj(pre):
                for cb in range(4):
                    wbo, wkeyo = pre[cb] if cb < len(pre) else load_wblk("out", cb * 512)
                    for t in range(TP):
                        gt = ps_i * TP + t
                        xi = (cb * TP + t) % 4
                        xs, xkey = xres[xi], "xres%d" % xi
                        P.dma("sp", xkey, lambda: nc.sync.dma_start(out=xs[:], in_=xtok[gt, :, cb * 512:(cb + 1) * 512]), writes=[xkey])
                        acc, akey = next_acc()
                        P.op("pe", [(lambda k=k: T.matmul(acc[:, 0:512], lhsT=yT[:, t, k, :], rhs=wbo[:, k, :],
                                                          start=(k == 0), stop=(k == KC - 1))) for k in range(KC)],
                             reads=["yT%d" % t, wkeyo], writes=[akey])
                        P.op("dve", lambda: V.scalar_tensor_tensor(out=r32[t][:, cb * 512:(cb + 1) * 512], in0=xs[:], scalar=ALPHA,
                                                                   in1=acc[:, 0:512], op0=ALU.mult, op1=ALU.add,
                                                                   accum_out=lnst[:, t, cb:cb + 1]),
                             reads=[xkey, akey, "lnst%d" % t], writes=["r32_%d" % t, "lnst%d" % t])
                        yield

            def h_q(j, pi=None):
                def h(t, acc, ak):
                    P.op("act", lambda: A.copy(out=q32[:, t, j * 512:(j + 1) * 512], in_=acc[:, 0:512]),
                         reads=[ak], writes=["q32_%d" % t])
                    if j == 1:
                        g = (ps_i if pi is None else pi) * TP + t + 1
                        q4 = qr[:, t, :].rearrange("p (h two f) -> p h two f", h=16, two=2)
                        do_rope(G, "pool", q32[:, t, :], "q32_%d" % t, q4[:, :, 0, :], q4[:, :, 1, :], "qr%d" % t, 16, g)
                return h

            def h_kv(t, acc, ak, pi=None):
                g = (ps_i if pi is None else pi) * TP + t + 1
                P.op("act", lambda: A.copy(out=k32[:, t, :], in_=acc[:, 0:128]), reads=[ak], writes=["k32"])
                P.op("act", lambda: A.copy(out=vall[:, g, :], in_=acc[:, 128:256]), reads=[ak], writes=["vall"])
                k_rope(k32[:, t, :], g, t)

            def h_silu(dst, dkey, j):
                return lambda t, acc, ak: P.op("act", lambda: A.activation(out=dst[:, t, j * 512:(j + 1) * 512], in_=acc[:, 0:512],
                                                                           func=AF.Silu), reads=[ak], writes=[dkey + "%d" % t])

            def h_vg(j):
                return lambda t, acc, ak: P.op("dve", lambda: V.tensor_copy(out=vg[:, t, j * 512:(j + 1) * 512], in_=acc[:, 0:512]),
                                               reads=[ak], writes=["vg%d" % t])

            def attn(t):
                g = ps_i * TP + t + 1
                mi = 0 if (ps_i == 0 and t == 0) else 1
                k_transposes(g, t)
                yield
                for half in range(2):
                    P.op("pe", [(lambda j=j: T.transpose(B4[0:64, j * 128:(j + 1) * 128],
                                                         qr[:, t, (half * 8 + j) * 64:(half * 8 + j + 1) * 64], ident[:]))
                                for j in range(8)], reads=["qr%d" % t, "ident"], writes=["B4"])
                    P.op("dve", lambda: V.tensor_copy(out=qT[:, half * 8:(half + 1) * 8, :].rearrange("p j t -> p (j t)"),
                                                      in_=B4[0:64, 0:1024]), reads=["B4"], writes=["qT"])
                    yield

                def scores(hp):
                    sbk, skey = B[2 + hp % 2], "B%d" % (2 + hp % 2)
                    kv = hp // 4
                    P.op("pe", [(lambda hh=hh: T.matmul(sbk[:, hh * 256:(hh + 1) * 256], lhsT=qT[:, 2 * hp + hh, :],
                                                        rhs=kTall[:, kv, (g - 1) * 128:(g + 1) * 128],
                                                        start=True, stop=True)) for hh in range(2)],
                         reads=["qT", "kTall"], writes=[skey])

                scores(0)
                for hp in range(8):
                    sbk, skey = B[2 + hp % 2], "B%d" % (2 + hp % 2)
                    kv = hp // 4
                    c2 = slice(2 * hp, 2 * hp + 2)
                    P.op("dve", lambda: V.tensor_reduce(out=mraw[:, c2], in_=sbk[:, 0:512].rearrange("p (h n) -> p h n", h=2),
                                                        axis=AX.X, op=ALU.max), reads=[skey], writes=["mraw"])
                    P.op("dve", lambda: V.scalar_tensor_tensor(out=negm[:, c2], in0=mraw[:, c2], scalar=-0.125, in1=negsink[:, c2],
                                                               op0=ALU.mult, op1=ALU.min),
                         reads=["mraw", "negsink"], writes=["negm"])
                    pe_, pkey = pexp[hp % 2], "pexp%d" % (hp % 2)
                    for hh in range(2):
                        P.op("act", (lambda hh=hh: A.activation(out=pe_[:, hh, :], in_=sbk[:, hh * 256:(hh + 1) * 256],
                                                                func=AF.Exp, bias=negm[:, 2 * hp + hh:2 * hp + hh + 1], scale=0.125)),
                             reads=[skey, "negm"], writes=[pkey])
                    if hp + 1 < 8:
                        scores(hp + 1)
                    yield
                    P.op("pe", [(lambda hh=hh, kt=kt: T.transpose(B4[:, (hh * 2 + kt) * 128:(hh * 2 + kt + 1) * 128],
                                                                  pe_[:, hh, kt * 128:(kt + 1) * 128], ident[:]))
                                for hh in range(2) for kt in range(2)], reads=[pkey, "ident"], writes=["B4"])
                    pt_, ptkey = pT[hp % 2], "pT%d" % (hp % 2)
                    P.op("dve", lambda: V.tensor_tensor(out=pt_[:], in0=B4[:, 0:512].rearrange("p (h k q) -> p h k q", h=2, k=2),
                                                        in1=mask16[:, mi, :, :].unsqueeze(1).to_broadcast([128, 2, 2, 128]),
                                                        op=ALU.mult), reads=["B4", "mask16"], writes=[ptkey])
                    obk, okey = (B5, "B5") if hp < 4 else (B6, "B6")
                    mm = []
                    for hh in range(2):
                        hl = (2 * hp + hh) % 8
                        for kt in range(2):
                            mm.append(lambda hh=hh, kt=kt, hl=hl: T.matmul(obk[:, hl * 64:(hl + 1) * 64], lhsT=pt_[:, hh, kt, :],
                                                                           rhs=vall[:, g - 1 + kt, kv * 64:(kv + 1) * 64],
                                                                           start=(kt == 0), stop=(kt == 1)))
                        for kt in range(2):
                            mm.append(lambda hh=hh, kt=kt: T.matmul(B7[:, 2 * hp + hh:2 * hp + hh + 1], lhsT=pt_[:, hh, kt, :],
                                                                    rhs=ones16[:, 0:1], start=(kt == 0), stop=(kt == 1)))
                    P.op("pe", mm, reads=[ptkey, "vall", "ones16"], writes=[okey, "B7"])
                    yield
                P.op("dve", lambda: V.tensor_tensor(out=dsm[:], in0=sinkb[:], in1=negm[:], op=ALU.add),
                     reads=["sinkb", "negm"], writes=["dsm"])
                P.op("act", lambda: A.activation(out=esk[:], in_=dsm[:], func=AF.Exp), reads=["dsm"], writes=["esk"])
                P.op("dve", lambda: V.tensor_tensor(out=den[:], in0=B7[:, 0:16], in1=esk[:], op=ALU.add),
                     reads=["B7", "esk"], writes=["den"])
                P.op("dve", lambda: V.reciprocal(out=rden[:], in_=den[:]), reads=["den"], writes=["rden"])
                for j, (obk, okey) in enumerate(((B5, "B5"), (B6, "B6"))):
                    P.op("dve", (lambda j=j, obk=obk: V.tensor_tensor(
                        out=otmp[:, j * 512:(j + 1) * 512].rearrange("p (h d) -> p h d", h=8),
                        in0=obk[:, 0:512].rearrange("p (h d) -> p h d", h=8),
                        in1=rden[:, j * 8:(j + 1) * 8].unsqueeze(2).to_broadcast([128, 8, 64]), op=ALU.mult)),
                        reads=[okey, "rden"], writes=["otmp"])
                yield "drain"
                P.op("dve", lambda: V.tensor_tensor(out=y16[:, t, 0:1024], in0=otmp[:], in1=sza[:, t, :], op=ALU.mult),
                     reads=["otmp", "sza%d" % t], writes=["y16_%d" % t])
                yield

            def gla(t):
                ts_ = slice(t * 128, (t + 1) * 128)
                P.op("pe", lambda: T.matmul(B[2][:, 0:512], lhsT=gkT[0:32, ts_], rhs=waug[0:32, :], start=True, stop=True),
                     reads=["gkT", "waug"], writes=["B2"])
                yield
                P.op("act", lambda: A.activation(out=e1[:], in_=B[2][:, 0:512], func=AF.Exp, scale=-1.0), reads=["B2"], writes=["e1"])
                P.op("act", lambda: A.activation(out=spl[:], in_=e1[:], func=AF.Ln, bias=1.0, scale=1.0), reads=["e1"], writes=["spl"])
                P.op("pe", [(lambda h=h: T.matmul(B[3][:, h * 128:(h + 1) * 128], lhsT=spl[:, h * 128:(h + 1) * 128],
                                                  rhs=uincl[:], start=True, stop=True)) for h in range(4)],
                     reads=["spl", "uincl"], writes=["B3"])
                yield
                P.op("dve", lambda: V.tensor_scalar(out=nbl[:], in0=B[3][:, 0:512].rearrange("p (h t) -> p h t", h=4)[:, :, 127],
                                                    scalar1=-1.0 / 16.0, scalar2=None, op0=ALU.mult), reads=["B3"], writes=["nbl"])
                P.op("act", lambda: A.activation(out=eq[:].rearrange("p h t -> p (h t)"), in_=B[3][:, 0:512], func=AF.Exp,
                                                 scale=-1.0 / 16.0), reads=["B3"], writes=["eq"])
                P.op("act", lambda: A.activation(out=ek[:].rearrange("p h t -> p (h t)"), in_=B[3][:, 0:512], func=AF.Exp,
                                                 scale=1.0 / 16.0), reads=["B3"], writes=["ek"])
                P.op("act", lambda: A.activation(out=tot[:], in_=nbl[:], func=AF.Exp), reads=["nbl"], writes=["tot"])
                for h in range(4):
                    P.op("act", (lambda h=h: A.activation(out=ekd[:, h, :], in_=B[3][:, h * 128:(h + 1) * 128], func=AF.Exp,
                                                          bias=nbl[:, h:h + 1], scale=1.0 / 16.0)),
                         reads=["B3", "nbl"], writes=["ekd"])
                P.op("dve", lambda: V.scalar_tensor_tensor(out=qdec[:], in0=qgT[:, :, ts_], scalar=128.0 ** -0.5, in1=eq[:],
                                                           op0=ALU.mult, op1=ALU.mult), reads=["qgT", "eq"], writes=["qdec"])
                P.op("dve", lambda: V.tensor_tensor(out=kneg[:], in0=kgT[:, :, ts_], in1=ek[:], op=ALU.mult),
                     reads=["kgT", "ek"], writes=["kneg"])
                P.op("pe", [(lambda h=h: T.matmul(B[2][:, h * 128:(h + 1) * 128], lhsT=kneg[:, h, :], rhs=qdec[:, h, :],
                                                  start=True, stop=True)) for h in range(4)],
                     reads=["kneg", "qdec"], writes=["B2"])
                P.op("dve", lambda: V.tensor_tensor(out=kdT[:], in0=kgT[:, :, ts_], in1=ekd[:], op=ALU.mult),
                     reads=["kgT", "ekd"], writes=["kdT"])
                P.op("pe", [(lambda h=h: T.transpose(B4[:, h * 128:(h + 1) * 128], kdT[:, h, :], ident[:])) for h in range(4)],
                     reads=["kdT", "ident"], writes=["B4"])
                yield
                P.op("dve", lambda: V.tensor_tensor(out=atm[:], in0=B[2][:, 0:512].rearrange("p (h i) -> p h i", h=4),
                                                    in1=uincl16[:].unsqueeze(1).to_broadcast([128, 4, 128]), op=ALU.mult),
                     reads=["B2", "uincl16"], writes=["atm"])
                P.op("act", lambda: A.copy(out=kdec[:].rearrange("p h t -> p (h t)"), in_=B4[:, 0:512]), reads=["B4"], writes=["kdec"])
                mm = []
                for h in range(4):
                    obk = B5 if h < 2 else B6
                    sl = slice((h % 2) * 256, (h % 2) * 256 + 256)
                    mm.append(lambda h=h, obk=obk, sl=sl: T.matmul(obk[:, sl], lhsT=atm[:, h, :], rhs=vg[:, t, h * 256:(h + 1) * 256],
                                                                   start=True, stop=False))
                    mm.append(lambda h=h, obk=obk, sl=sl: T.matmul(obk[:, sl], lhsT=qdec[:, h, :], rhs=S16[:, h * 256:(h + 1) * 256],
                                                                   start=False, stop=True))
                P.op("pe", mm, reads=["atm", "vg%d" % t, "qdec", "S16"], writes=["B5", "B6"])
                P.op("pe", [(lambda h=h: T.matmul((B7 if h < 2 else B[3])[:, (h % 2) * 256:(h % 2) * 256 + 256], lhsT=kdec[:, h, :],
                                                  rhs=vg[:, t, h * 256:(h + 1) * 256], start=True, stop=True)) for h in range(4)],
                     reads=["kdec", "vg%d" % t], writes=["B7", "B3"])
                yield
                P.op("dve", lambda: V.memset(ss[:], 0.0), writes=["ss"])
                for h in range(4):
                    obk, okey = (B5, "B5") if h < 2 else (B6, "B6")
                    sl = slice((h % 2) * 256, (h % 2) * 256 + 256)
                    P.op("act", (lambda h=h, obk=obk, sl=sl: A.activation(out=sqj[:], in_=obk[:, sl], func=AF.Square,
                                                                          accum_out=ss[:, h:h + 1])),
                         reads=[okey, "ss"], writes=["sqj", "ss"])
                P.op("act", lambda: A.activation(out=rstd[:], in_=ss[:], func=AF.Ln, bias=RMS_EPS, scale=1.0 / 256.0),
                     reads=["ss"], writes=["rstd"])
                P.op("act", lambda: A.activation(out=rstd[:], in_=rstd[:], func=AF.Exp, scale=-0.5), reads=["rstd"], writes=["rstd"])
                for h in range(4):
                    obk, okey = (B5, "B5") if h < 2 else (B6, "B6")
                    sl = slice((h % 2) * 256, (h % 2) * 256 + 256)
                    P.op("dve", (lambda h=h, obk=obk, sl=sl: V.scalar_tensor_tensor(out=otmp[:, h * 256:(h + 1) * 256], in0=obk[:, sl],
                                                                                    scalar=rstd[:, h:h + 1], in1=normwb[:],
                                                                                    op0=ALU.mult, op1=ALU.mult)),
                         reads=[okey, "rstd", "normwb"], writes=["otmp"])
                P.op("dve", lambda: V.tensor_tensor(out=y16[:, t, 1024:2048], in0=otmp[:], in1=szg[:, t, :], op=ALU.mult),
                     reads=["otmp", "szg%d" % t], writes=["y16_%d" % t])
                for h in range(4):
                    src = (B7 if h < 2 else B[3])[:, (h % 2) * 256:(h % 2) * 256 + 256]
                    P.op("dve", (lambda h=h, src=src: V.scalar_tensor_tensor(out=S32[:, h * 256:(h + 1) * 256],
                                                                             in0=S32[:, h * 256:(h + 1) * 256],
                                                                             scalar=tot[:, h:h + 1], in1=src,
                                                                             op0=ALU.mult, op1=ALU.add)),
                         reads=["S32", "tot", "B7", "B3"], writes=["S32"])
                P.op("act", lambda: A.copy(out=S16[:], in_=S32[:]), reads=["S32"], writes=["S16"])
                yield
                for qd in range(2):
                    P.op("pe", [(lambda j=j: T.transpose(B4[:, (j % 8) * 128:(j % 8 + 1) * 128], y16[:, t, j * 128:(j + 1) * 128], ident[:]))
                                for j in range(qd * 8, qd * 8 + 8)], reads=["y16_%d" % t, "ident"], writes=["B4"])
                    P.op("act", (lambda qd=qd: A.copy(out=yT[:, t, qd * 8:(qd + 1) * 8, :].rearrange("p j t -> p (j t)"),
                                                      in_=B4[:, 0:1024])), reads=["B4"], writes=["yT%d" % t])
                    yield

            def ln_gen(t, gt):
                rt, rkey, lk = r32[t], "r32_%d" % t, "lnst%d" % t
                L = lambda a, b: lnst[:, t, a:b]
                P.op("dve", lambda: V.tensor_reduce(out=L(4, 5), in_=L(0, 4), axis=AX.X, op=ALU.add), reads=[lk], writes=[lk])
                P.op("dve", lambda: V.tensor_scalar(out=L(5, 6), in0=L(4, 5), scalar1=-1.0 / D, scalar2=None, op0=ALU.mult),
                     reads=[lk], writes=[lk])
                P.op("act", lambda: A.activation(out=y16[:, t, :], in_=rt[:], func=AF.Square, bias=L(5, 6), scale=1.0,
                                                 accum_out=L(6, 7)), reads=[rkey, lk], writes=["y16_%d" % t, lk])
                yield
                P.op("act", lambda: A.activation(out=L(7, 8), in_=L(6, 7), func=AF.Ln, bias=LN_EPS, scale=1.0 / D),
                     reads=[lk], writes=[lk])
                P.op("act", lambda: A.activation(out=L(7, 8), in_=L(7, 8), func=AF.Exp, scale=-0.5), reads=[lk], writes=[lk])
                P.op("dve", lambda: V.scalar_tensor_tensor(out=rt[:], in0=rt[:], scalar=L(5, 6), in1=lngb[:],
                                                           op0=ALU.add, op1=ALU.mult), reads=[rkey, lk, "lngb"], writes=[rkey])
                yield
                P.op("dve", lambda: V.scalar_tensor_tensor(out=rt[:], in0=rt[:], scalar=L(7, 8), in1=lnbb[:],
                                                           op0=ALU.mult, op1=ALU.add), reads=[rkey, lk, "lnbb"], writes=[rkey])
                pending_stores.append((t, gt))
                yield

            def flush_stores():
                while pending_stores:
                    t_, gt_ = pending_stores.pop(0)
                    P.dma("sp", "out%d" % t_, (lambda t_=t_, gt_=gt_: nc.sync.dma_start(out=out[gt_, :, :], in_=r32[t_][:])),
                          reads=["r32_%d" % t_])

            def s1_blocks(pi):
                return [(C_QA, 512, h_q(0, pi)), (C_QA + 512, 512, h_q(1, pi)),
                        (C_KA, 256, (lambda t, acc, ak, pi=pi: h_kv(t, acc, ak, pi)))]

            if ps_i == 0:
                run(bg_tm(s1_blocks(0), 0))
            if ps_i + 1 < npass:
                load_x(ps_i + 1)
            run_mix(attn(0), chain(deferred_ln.pop(0) if deferred_ln else iter(()),
                                   bg_tm([(C_ZA, 512, h_silu(sza, "sza", 0)), (C_ZA + 512, 512, h_silu(sza, "sza", 1)),
                                          (C_VG, 512, h_vg(0)), (C_VG + 512, 512, h_vg(1))])), ratio=2)
            flush_stores()
            run_mix(attn(1), chain(deferred_ln.pop(0) if deferred_ln else iter(()),
                                   bg_tm([(C_ZG, 512, h_silu(szg, "szg", 0)), (C_ZG + 512, 512, h_silu(szg, "szg", 1))]),
                                   bg_fm()), ratio=2)
            flush_stores()
            for t in range(TP):
                P.op("dve", (lambda t=t: V.memset(lnst[:, t, :], 0.0)), writes=["lnst%d" % t])
            if ps_i + 1 < npass:
                run_mix(gla(0), bg_tm(s1_blocks(ps_i + 1), ps_i + 1), ratio=1)
            else:
                run(gla(0))
            pre = [load_wblk("out", 0), load_wblk("out", 512)]
            run(gla(1))
            run(bg_outproj(pre))
            deferred_ln.append(ln_gen(0, ps_i * TP + 0))
            deferred_ln.append(ln_gen(1, ps_i * TP + 1))
        while deferred_ln:
            run(deferred_ln.pop(0))
        flush_stores()


def _kmajor(a):
    n = a.shape[1]
    return np.ascontiguousarray(a.reshape(KC, 128, n).transpose(1, 0, 2))


def _host_consts():
    ident = np.eye(128, dtype=np.float32)
    uincl = (np.arange(128)[:, None] <= np.arange(128)[None, :]).astype(np.float32)
    ones = np.ones((128, 128), np.float32)
    usuf = (np.arange(128)[:, None] > np.arange(128)[None, :]).astype(np.float32)
    consts = np.ascontiguousarray(np.stack([ident, uincl, ones, usuf], axis=1))
    k = np.arange(128)[:, None, None]
    kt = np.arange(2)[None, :, None]
    q = np.arange(128)[None, None, :]
    kj = kt * 128 + k
    reg = ((kj > q) & (kj <= q + 128)).astype(np.float32)
    first = reg * (kj >= 128)
    return consts, reg, first


def kernel(x, w_in, w_gk_up, b_gk, attn_sinks, gla_norm_w, w_out, ln_g, ln_b, _ncores=NCORES, _npre=NPRE):
    x = np.asarray(x, np.float32)[0]
    w = np.asarray(w_in, np.float32)[0]
    perm = np.concatenate([np.arange(0, 1024), np.arange(1024, 1152), np.arange(1152, 1280), np.arange(1280, 2304),
                           np.arange(3328, 4352), np.arange(4352, 5376), np.arange(2304, 2816), np.arange(2816, 3328),
                           np.arange(5376, 5392)])
    w_l = _kmajor(np.concatenate([w[:, perm], np.zeros((D, 16), np.float32)], axis=1))
    wo_l = _kmajor(np.asarray(w_out, np.float32)[0])
    waug = np.zeros((32, 512), np.float32)
    waug[0:16] = np.asarray(w_gk_up, np.float32)[0]
    waug[16] = np.asarray(b_gk, np.float32)[0]
    consts, mreg, mfirst = _host_consts()
    sinks = np.ascontiguousarray(np.broadcast_to(np.asarray(attn_sinks, np.float32)[0][None, :], (128, 16)))
    normw = np.ascontiguousarray(np.broadcast_to(np.asarray(gla_norm_w, np.float32)[0][None, :], (128, 256)))
    lng = np.ascontiguousarray(np.broadcast_to(np.asarray(ln_g, np.float32)[0][None, :], (128, D)))
    lnb = np.ascontiguousarray(np.broadcast_to(np.asarray(ln_b, np.float32)[0][None, :], (128, D)))
    inv_freq = (1.0 / (10000.0 ** (np.arange(0, 32, dtype=np.float32) * 2.0 / 64.0))).astype(np.float32)
    xTfull = np.ascontiguousarray(x.T)
    in_maps = []
    for c in range(_ncores):
        s0 = c * OWN
        xt = np.zeros((D, OWN + 128), np.float32)
        if c > 0:
            xt[:, 0:128] = xTfull[:, s0 - 128:s0]
        xt[:, 128:] = xTfull[:, s0:s0 + OWN]
        npre_tok = _npre * 128
        xp = np.zeros((D, max(npre_tok, 128)), np.float32)
        if _npre > 0 and s0 > 0:
            xp[:, npre_tok - s0:] = xTfull[:, 0:s0]
        pos = (np.arange(s0 - 128, s0 + OWN)).astype(np.float32)
        ang = pos[:, None] * inv_freq[None, :]
        cs = np.stack([np.cos(ang), np.sin(ang)], axis=1).astype(np.float32)
        cs = np.ascontiguousarray(cs.reshape(NT + 1, 128, 2, 32).transpose(1, 0, 2, 3))
        masks = np.ascontiguousarray(np.stack([mfirst if c == 0 else mreg, mreg], axis=1))
        in_maps.append({
            "xT": _kmajor(xt),
            "xpre": np.ascontiguousarray(_kmajor(xp).reshape(128, KC, max(_npre, 1), 128).transpose(2, 0, 1, 3)
                                         .reshape(max(_npre, 1), 128, KC * 128)),
            "xtok": np.ascontiguousarray(x[s0:s0 + OWN].reshape(NT, 128, D)),
            "xptok": np.ascontiguousarray(xp.T.reshape(max(_npre, 1), 128, D)),
            "w_in": w_l, "w_out": wo_l, "w_aug": waug, "cs": cs, "masks": masks, "consts": consts,
            "sinks": sinks, "normw": normw, "lng": lng, "lnb": lnb,
        })
    nc = build(npre=_npre, nt=NT)
    res = run_bass_kernel_spmd(nc, in_maps, core_ids=list(range(_ncores)))
    outs = [np.asarray(r["out"]).reshape(OWN, D) for r in res.results]
    return np.concatenate(outs, axis=0)[None].astype(np.float32)
```
